# Optimizing a Trainium2 kernel written in Bass

```python
import math
import jax, jax.numpy as jnp
from jax import lax
import numpy as np

D_MODEL = 1024
BATCH = 4
SEQ = 8192
DEPTH = 2

N_MIXERS = 2
N_A_LAYERS = (DEPTH + 1) // 2
N_B_LAYERS = DEPTH // 2

GDN_HEADS = 8
GDN_DK = 128
GDN_DV = 128
GDN_CONV = 4
GDN_CHUNK = 64
GDN_HK = GDN_HEADS * GDN_DK
GDN_HV = GDN_HEADS * GDN_DV
GDN_IN = 2 * GDN_HK + 2 * GDN_HV + 2 * GDN_HEADS
GDN_CONV_CH = 2 * GDN_HK + GDN_HV

DSW_GROUPS = ((128, 1), (512, 4), (2048, 16))
DSW_N_GROUPS = len(DSW_GROUPS)
DSW_HEADS = 8
DSW_DH = 64
DSW_HG = DSW_HEADS * DSW_DH
DSW_IN = 3 * DSW_N_GROUPS * DSW_HG

REL_BUCKETS = 32
REL_MAX_DIST = 2048
REL_HEADS = DSW_N_GROUPS * DSW_HEADS

FFN_HIDDEN = -(-8 * D_MODEL // (3 * 256)) * 256

RMS_EPS = 1e-6

kernel_name = "hybrid_gdn_dilated_swa_adaln"


def rms_norm(x, gain):
    xf = x.astype(jnp.float32)
    y = xf * lax.rsqrt(jnp.mean(xf * xf, axis=-1, keepdims=True) + RMS_EPS)
    return y * gain.astype(jnp.float32)


def l2_norm(x):
    xf = x.astype(jnp.float32)
    return xf * lax.rsqrt(jnp.sum(xf * xf, axis=-1, keepdims=True) + RMS_EPS)


def causal_depthwise_conv(x, w):
    K, C = w.shape
    return lax.conv_general_dilated(
        x, w[:, None, :].astype(x.dtype), window_strides=(1,),
        padding=((K - 1, 0),), dimension_numbers=("NWC", "WIO", "NWC"),
        feature_group_count=C)


def chunk_gated_delta_rule(q, k, v, g, beta):
    Bsz, S, H, dk = q.shape
    dv = v.shape[-1]
    C = GDN_CHUNK
    N = S // C
    to_chunks = lambda t: jnp.transpose(t.reshape(Bsz, N, C, H, t.shape[-1]), (0, 3, 1, 2, 4))
    q = to_chunks(q * (dk ** -0.5))
    k = to_chunks(k)
    v = to_chunks(v.astype(jnp.float32))
    beta = jnp.transpose(beta.reshape(Bsz, N, C, H), (0, 3, 1, 2))
    g = jnp.cumsum(jnp.transpose(g.reshape(Bsz, N, C, H), (0, 3, 1, 2)), axis=-1)

    causal = jnp.tril(jnp.ones((C, C), dtype=bool))
    strict = jnp.tril(jnp.ones((C, C), dtype=bool), -1)
    decay = jnp.exp(jnp.where(causal, g[..., :, None] - g[..., None, :], -jnp.inf))
    kb = k * beta[..., None]
    vb = v * beta[..., None]
    Lmat = jnp.where(strict, jnp.einsum('bhncd,bhnmd->bhncm', kb, k) * decay, 0.0)
    rhs = jnp.concatenate([vb, kb * jnp.exp(g)[..., None]], axis=-1)
    sol = lax.linalg.triangular_solve(Lmat, rhs, left_side=True, lower=True, unit_diagonal=True)
    u = sol[..., :dv]
    w = sol[..., dv:]
    qk = jnp.where(causal, jnp.einsum('bhncd,bhnmd->bhncm', q, k) * decay, 0.0)

    g_last = g[..., -1]
    q_dec = q * jnp.exp(g)[..., None]
    k_dec = k * jnp.exp(g_last[..., None] - g)[..., None]

    def step(state, inp):
        qd, a, uu, ww, kd, gl = inp
        v_new = uu - jnp.einsum('bhck,bhkv->bhcv', ww, state)
        o = jnp.einsum('bhck,bhkv->bhcv', qd, state) + jnp.einsum('bhcm,bhmv->bhcv', a, v_new)
        state = state * jnp.exp(gl)[..., None, None] + jnp.einsum('bhck,bhcv->bhkv', kd, v_new)
        return state, o

    xs = tuple(jnp.moveaxis(t, 2, 0) for t in (q_dec, qk, u, w, k_dec, g_last))
    state0 = jnp.zeros((Bsz, H, dk, dv), jnp.float32)
    _, o = lax.scan(step, state0, xs)
    return jnp.transpose(o, (1, 0, 3, 2, 4)).reshape(Bsz, S, H, dv)


def gated_deltanet_mixer(h, w_in, conv_w, a_log, dt_bias, out_gain, w_out):
    Bsz, S, _ = h.shape
    proj = h @ w_in
    qkv, z, a, b = jnp.split(proj, [GDN_CONV_CH, GDN_CONV_CH + GDN_HV,
                                    GDN_CONV_CH + GDN_HV + GDN_HEADS], axis=-1)
    qkv = jax.nn.silu(causal_depthwise_conv(qkv, conv_w))
    q, k, v = jnp.split(qkv, [GDN_HK, 2 * GDN_HK], axis=-1)
    q = l2_norm(q.reshape(Bsz, S, GDN_HEADS, GDN_DK))
    k = l2_norm(k.reshape(Bsz, S, GDN_HEADS, GDN_DK))
    v = v.reshape(Bsz, S, GDN_HEADS, GDN_DV)
    beta = jax.nn.sigmoid(b.astype(jnp.float32))
    g = -jnp.exp(a_log.astype(jnp.float32)) * jax.nn.softplus(a.astype(jnp.float32) + dt_bias.astype(jnp.float32))
    o = chunk_gated_delta_rule(q, k, v, g, beta)
    zf = z.reshape(Bsz, S, GDN_HEADS, GDN_DV).astype(jnp.float32)
    o = rms_norm(o, out_gain) * jax.nn.silu(zf)
    return o.reshape(Bsz, S, GDN_HV).astype(h.dtype) @ w_out


def t5_causal_bucket(dist):
    max_exact = REL_BUCKETS // 2
    scaled = jnp.log(jnp.maximum(dist, 1).astype(jnp.float32) / max_exact) / math.log(REL_MAX_DIST / max_exact)
    large = max_exact + (scaled * (REL_BUCKETS - max_exact)).astype(jnp.int32)
    large = jnp.minimum(large, REL_BUCKETS - 1)
    return jnp.where(dist < max_exact, dist, large)


def dilated_window_group(q, k, v, bias_table, window, dilation):
    Bsz, S, H, dh = q.shape
    span = window // dilation
    blk = span
    unit = dilation * blk
    S_pad = -(-S // unit) * unit
    nb = S_pad // unit

    def sub(t):
        t = jnp.pad(t, ((0, 0), (0, S_pad - S), (0, 0), (0, 0)))
        return jnp.transpose(t.reshape(Bsz, nb, blk, dilation, H, dh), (0, 3, 1, 2, 4, 5))

    qs, ks, vs = sub(q), sub(k), sub(v)
    shift = lambda t: jnp.concatenate([jnp.zeros_like(t[:, :, :1]), t[:, :, :-1]], axis=2)
    kb = jnp.concatenate([shift(ks), ks], axis=3)
    vb = jnp.concatenate([shift(vs), vs], axis=3)

    qi = jnp.arange(blk)[:, None] + blk
    ki = jnp.arange(2 * blk)[None, :]
    dist = qi - ki
    band = (dist >= 0) & (dist <= span)
    valid = band[None] & ((jnp.arange(nb) > 0)[:, None, None] | (ki >= blk)[None])
    bias = jnp.transpose(bias_table.astype(jnp.float32)[t5_causal_bucket(jnp.maximum(dist, 0) * dilation)], (2, 0, 1))

    logits = jnp.einsum('brnqhd,brnkhd->brnhqk', qs, kb) + bias[None, None, None]
    logits = jnp.where(valid[None, None, :, None], logits, -jnp.inf)
    m = jnp.max(logits, axis=-1, keepdims=True)
    p = jnp.exp(logits - m)
    l = jnp.sum(p, axis=-1, keepdims=True)
    o = jnp.einsum('brnhqk,brnkhd->brnqhd', p / l, vb)
    lse = (m + jnp.log(l))[..., 0]
    o = jnp.transpose(o, (0, 2, 3, 1, 4, 5)).reshape(Bsz, S_pad, H, dh)[:, :S]
    lse = jnp.transpose(lse, (0, 2, 4, 1, 3)).reshape(Bsz, S_pad, H)[:, :S]
    return o, lse


def dilated_attention_mixer(h, w_in, q_gain, k_gain, rel_bias, w_out):
    Bsz, S, _ = h.shape
    proj = (h @ w_in).reshape(Bsz, S, 3, DSW_N_GROUPS, DSW_HEADS, DSW_DH)
    q = rms_norm(proj[:, :, 0], q_gain) * (DSW_DH ** -0.5)
    k = rms_norm(proj[:, :, 1], k_gain)
    v = proj[:, :, 2].astype(jnp.float32)
    outs, lses = [], []
    for gi, (window, dilation) in enumerate(DSW_GROUPS):
        o, lse = dilated_window_group(q[:, :, gi], k[:, :, gi], v[:, :, gi],
                                      rel_bias[:, gi * DSW_HEADS:(gi + 1) * DSW_HEADS], window, dilation)
        outs.append(o)
        lses.append(lse)
    wts = jax.nn.softmax(jnp.stack(lses), axis=0)
    o = jnp.sum(wts[..., None] * jnp.stack(outs), axis=0)
    return o.reshape(Bsz, S, DSW_HG).astype(h.dtype) @ w_out


def swiglu(h, w_in, w_out):
    gate, up = jnp.split(h @ w_in, 2, axis=-1)
    return (jax.nn.silu(gate) * up) @ w_out


def setup_inputs(seed: int = 0) -> dict:
    key = jax.random.key(seed)
    ks = jax.random.split(key, 20)
    nrm = lambda k, shape, s: jax.random.normal(k, shape, jnp.float32) * s
    D = D_MODEL
    a_init = jax.random.uniform(ks[10], (N_A_LAYERS, GDN_HEADS), jnp.float32, 1.0, 16.0)
    dt = jnp.exp(jax.random.uniform(ks[11], (N_A_LAYERS, GDN_HEADS), jnp.float32,
                                    math.log(1e-3), math.log(1e-1)))
    return {
        "x": nrm(ks[0], (BATCH, SEQ, D), 1.0),
        "c": nrm(ks[1], (BATCH, D), 1.0),
        "w_ada": nrm(ks[2], (DEPTH, D, 6 * D), D ** -0.5),
        "b_ada": nrm(ks[3], (DEPTH, 6 * D), 0.02),
        "norm_mix": 1.0 + nrm(ks[4], (DEPTH, D), 0.02),
        "norm_ffn": 1.0 + nrm(ks[5], (DEPTH, D), 0.02),
        "w_ffn_in": nrm(ks[6], (DEPTH, D, 2 * FFN_HIDDEN), D ** -0.5),
        "w_ffn_out": nrm(ks[7], (DEPTH, FFN_HIDDEN, D), FFN_HIDDEN ** -0.5),
        "gdn_w_in": nrm(ks[8], (N_A_LAYERS, D, GDN_IN), D ** -0.5),
        "gdn_conv": nrm(ks[9], (N_A_LAYERS, GDN_CONV, GDN_CONV_CH), GDN_CONV ** -0.5),
        "gdn_a_log": jnp.log(a_init),
        "gdn_dt_bias": dt + jnp.log(-jnp.expm1(-dt)),
        "gdn_out_norm": 1.0 + nrm(ks[12], (N_A_LAYERS, GDN_DV), 0.02),
        "gdn_w_out": nrm(ks[13], (N_A_LAYERS, GDN_HV, D), GDN_HV ** -0.5),
        "dsw_w_in": nrm(ks[14], (N_B_LAYERS, D, DSW_IN), D ** -0.5),
        "dsw_q_norm": 1.0 + nrm(ks[15], (N_B_LAYERS, DSW_DH), 0.02),
        "dsw_k_norm": 1.0 + nrm(ks[16], (N_B_LAYERS, DSW_DH), 0.02),
        "dsw_w_out": nrm(ks[17], (N_B_LAYERS, DSW_HG, D), DSW_HG ** -0.5),
        "rel_bias": nrm(ks[18], (REL_BUCKETS, REL_HEADS), 0.5),
    }


def reference(x, c, w_ada, b_ada, norm_mix, norm_ffn, w_ffn_in, w_ffn_out,
              gdn_w_in, gdn_conv, gdn_a_log, gdn_dt_bias, gdn_out_norm, gdn_w_out,
              dsw_w_in, dsw_q_norm, dsw_k_norm, dsw_w_out, rel_bias):
    cond = jax.nn.silu(c.astype(jnp.float32))
    for layer in range(DEPTH):
        mod = (cond @ w_ada[layer].astype(jnp.float32) + b_ada[layer].astype(jnp.float32)).astype(x.dtype)
        sh1, sc1, g1, sh2, sc2, g2 = jnp.split(mod[:, None, :], 6, axis=-1)

        h = (rms_norm(x, norm_mix[layer]) * (1.0 + sc1) + sh1).astype(x.dtype)
        j = layer // N_MIXERS
        if layer % N_MIXERS == 0:
            y = gated_deltanet_mixer(h, gdn_w_in[j], gdn_conv[j], gdn_a_log[j], gdn_dt_bias[j],
                                     gdn_out_norm[j], gdn_w_out[j])
        else:
            y = dilated_attention_mixer(h, dsw_w_in[j], dsw_q_norm[j], dsw_k_norm[j],
                                        rel_bias, dsw_w_out[j])
        x = x + g1 * y

        h = (rms_norm(x, norm_ffn[layer]) * (1.0 + sc2) + sh2).astype(x.dtype)
        x = x + g2 * swiglu(h, w_ffn_in[layer], w_ffn_out[layer])
    return x
```

```python
import numpy as np
import ml_dtypes
from contextlib import ExitStack
import concourse.bass as bass
import concourse.mybir as mybir
from concourse.bass_utils import run_bass_kernel_spmd

F32 = mybir.dt.float32
BF16 = mybir.dt.bfloat16
AF = mybir.ActivationFunctionType
ALU = mybir.AluOpType
AX = mybir.AxisListType
NPBF = ml_dtypes.bfloat16

D = 1024
SEQ = 8192
BATCH = 4
FH = 2816
RMS_EPS = 1e-6


class Buf:
    __slots__ = ("name", "last_w", "readers", "excl")

    def __init__(self, name, excl=False):
        self.name = name
        self.last_w = None
        self.readers = []
        self.excl = excl


class MK:
    ENGS = ("tensor", "vector", "scalar", "gpsimd", "sync")

    def __init__(self, nc, n_dma_sems=8, sem_es=None, tag=""):
        self.nc = nc
        self.es = ExitStack()
        self.tag = tag
        self.sem_es = sem_es
        ses = sem_es if sem_es is not None else self.es
        self.ops = {e: [] for e in self.ENGS}
        self.cnt = {e: 0 for e in self.ENGS}
        self.sem = {}
        for e in ("tensor", "vector", "scalar", "gpsimd"):
            self.sem[e] = ses.enter_context(nc.semaphore(tag + "s_" + e))
        self.dma_sems = {}
        self.dma_ring = {}
        for q in ("sync", "gpsimd"):
            self.dma_sems[q] = [ses.enter_context(nc.semaphore(f"{tag}d_{q}{i}")) for i in range(n_dma_sems)]
            self.dma_ring[q] = [0, [0] * n_dma_sems]
        self.known = {e: {} for e in self.ENGS}
        self._rr = 0

    def sbuf(self, name, shape, dt):
        return self.es.enter_context(self.nc.sbuf_tensor(self.tag + "sb_" + name, list(shape), dt))

    def psum(self, name, shape, dt):
        return self.es.enter_context(self.nc.psum_tensor(self.tag + "pp_" + name, list(shape), dt))

    def buf(self, name, excl=False):
        return Buf(name, excl)

    def bufs(self, name, n, excl=False):
        return [Buf(f"{name}{i}", excl) for i in range(n)]

    @staticmethod
    def _split(reads, writes):
        ex = [b for b in reads if b.excl]
        if ex:
            reads = [b for b in reads if not b.excl]
            writes = list(writes) + ex
        return reads, writes

    def _deps(self, reads, writes):
        deps = {}

        def add(d):
            if d is None:
                return
            k, v = d
            if deps.get(k, -1) < v:
                deps[k] = v
        for b in reads:
            add(b.last_w)
        for b in writes:
            add(b.last_w)
            for r in b.readers:
                add(r)
        return deps

    def _waits_for(self, eng, deps):
        waits = []
        kn = self.known[eng]
        for k, v in deps.items():
            if kn.get(k, 0) >= v:
                continue
            if k == ("e", eng) and v > self.cnt[eng]:
                continue
            kn[k] = v
            waits.append((k, v))
        return waits

    def op(self, eng, fn, reads=(), writes=(), inc=True):
        reads, writes = self._split(reads, writes)
        deps = self._deps(reads, writes)
        waits = self._waits_for(eng, deps)
        key = ("e", eng)
        if inc:
            self.cnt[eng] += 1
            c = self.cnt[eng]
            self.ops[eng].append((waits, fn, (key, 1)))
        else:
            c = self.cnt[eng] + 1
            self.ops[eng].append((waits, fn, None))
        tag = (key, c)
        for b in reads:
            b.readers.append(tag)
        for b in writes:
            b.last_w = tag
            b.readers = []
        return tag

    def dma(self, q, out, in_, reads=(), writes=()):
        reads, writes = self._split(reads, writes)
        deps = self._deps(reads, writes)
        ring = self.dma_ring[q]
        i = ring[0]
        ring[0] = (i + 1) % len(self.dma_sems[q])
        n_prev = ring[1][i]
        key = ("d", q, i)
        if n_prev > 0 and deps.get(key, -1) < 16 * n_prev:
            deps[key] = 16 * n_prev
        waits = self._waits_for(q, deps)
        ring[1][i] = n_prev + 1
        self.ops[q].append((waits, (lambda e, o=out, s=in_: e.dma_start(out=o, in_=s)), (key, 16)))
        tag = (key, 16 * (n_prev + 1))
        for b in reads:
            b.readers.append(tag)
        for b in writes:
            b.last_w = tag
            b.readers = []
        return tag

    def cc_allgather(self, src, dst, groups, reads=(), writes=()):
        if not hasattr(self, "cc_sem"):
            ses = self.sem_es if self.sem_es is not None else self.es
            self.cc_sem = ses.enter_context(self.nc.semaphore(self.tag + "cc_sem"))
            self.cc_n = 0
        reads, writes = self._split(reads, writes)
        deps = self._deps(reads, writes)
        waits = self._waits_for("gpsimd", deps)
        self.cc_n += 1
        key = ("c",)
        self.ops["gpsimd"].append((waits, (lambda e: e.collective_compute("AllGather", ALU.bypass, replica_groups=groups,
                                                                        ins=[src.opt()], outs=[dst.opt()])), (key, 1)))
        tag = (key, self.cc_n)
        for b in reads:
            b.readers.append(tag)
        for b in writes:
            b.last_w = tag
            b.readers = []
        return tag

    def _semof(self, key):
        if key[0] == "c":
            return self.cc_sem
        if key[0] == "e":
            return self.sem[key[1]]
        return self.dma_sems[key[1]][key[2]]

    def final_wait(self, eng, bufs):
        deps = self._deps(bufs, ())
        waits = self._waits_for(eng, deps)
        self.ops[eng].append((waits, None, None))

    def barrier(self):
        deps = {}
        for e in ("tensor", "vector", "scalar", "gpsimd"):
            if self.cnt[e] > 0:
                deps[("e", e)] = self.cnt[e]
        for q, (nxt, counts) in self.dma_ring.items():
            for i, n in enumerate(counts):
                if n > 0:
                    deps[("d", q, i)] = 16 * n
        if getattr(self, "cc_n", 0) > 0:
            deps[("c",)] = self.cc_n
        for e in self.ENGS:
            waits = self._waits_for(e, dict(deps))
            waits = [(k, v) for (k, v) in waits]
            self.ops[e].append((waits, None, None))

    def build(self, barrier=True):
        nc = self.nc
        if barrier:
            self.barrier()
        with nc.Block() as block:
            def mk(ename):
                def body(e):
                    for waits, fn, inc in self.ops[ename]:
                        for k, v in waits:
                            e.wait_ge(self._semof(k), v)
                        if fn is not None:
                            ins = fn(e)
                            if inc is not None:
                                ins.then_inc(self._semof(inc[0]), inc[1])
                return body
            block.tensor(mk("tensor"))
            block.vector(mk("vector"))
            block.scalar(mk("scalar"))
            block.gpsimd(mk("gpsimd"))
            block.sync(mk("sync"))
        self.es.close()

    def mm(self, out, lhsT, rhs, start, stop, reads, writes, inc=None):
        if inc is None:
            inc = stop
        return self.op("tensor", lambda e: e.matmul(out, lhsT=lhsT, rhs=rhs, start=start, stop=stop),
                       reads, writes, inc=inc)

    def tr(self, out, in_, ident, reads, writes):
        return self.op("tensor", lambda e: e.transpose(out, in_, ident), reads, writes)

    def act(self, out, in_, func, reads, writes, bias=0.0, scale=1.0, accum_out=None):
        if accum_out is None:
            return self.op("scalar", lambda e: e.activation(out=out, in_=in_, func=func, bias=bias, scale=scale),
                           reads, writes)
        return self.op("scalar", lambda e: e.activation(out=out, in_=in_, func=func, bias=bias, scale=scale,
                                                        accum_out=accum_out), reads, writes)

    def tt(self, eng, out, in0, in1, op, reads, writes):
        return self.op(eng, lambda e: e.tensor_tensor(out=out, in0=in0, in1=in1, op=op), reads, writes)

    def ts(self, eng, out, in0, s1, s2, op0, op1, reads, writes):
        if s2 is None:
            return self.op(eng, lambda e: e.tensor_scalar(out=out, in0=in0, scalar1=s1, scalar2=None, op0=op0),
                           reads, writes)
        return self.op(eng, lambda e: e.tensor_scalar(out=out, in0=in0, scalar1=s1, scalar2=s2, op0=op0, op1=op1),
                       reads, writes)

    def stt(self, eng, out, in0, scalar, in1, op0, op1, reads, writes):
        return self.op(eng, lambda e: e.scalar_tensor_tensor(out=out, in0=in0, scalar=scalar, in1=in1,
                                                             op0=op0, op1=op1), reads, writes)

    def copy(self, eng, out, in_, reads, writes):
        if eng == "scalar":
            return self.op(eng, lambda e: e.copy(out=out, in_=in_), reads, writes)
        return self.op(eng, lambda e: e.tensor_copy(out=out, in_=in_), reads, writes)

    def memset(self, eng, ap, val, writes):
        return self.op(eng, lambda e: e.memset(ap, val), (), writes)

    def recip(self, out, in_, reads, writes):
        return self.op("vector", lambda e: e.reciprocal(out=out, in_=in_), reads, writes)

    def rr(self, engs=("vector", "scalar", "vector", "scalar", "vector", "gpsimd", "scalar", "vector")):
        self._rr += 1
        return engs[self._rr % len(engs)]


class Stage:
    def __init__(self, m, width=1024, n=3):
        self.m = m
        self.width = width
        self.t = [m.sbuf(f"stg{i}", [128, width], F32) for i in range(n)]
        self.b = m.bufs("stg", n)
        self.i = 0

    def load_cast(self, dst_ap_fn, src_rows_ap, ncols, wbuf, queue="sync"):
        m = self.m
        for c0 in range(0, ncols, self.width):
            c1 = min(ncols, c0 + self.width)
            i = self.i
            self.i = (self.i + 1) % len(self.t)
            m.dma(queue, self.t[i][:, 0:c1 - c0], src_rows_ap[:, c0:c1], writes=[self.b[i]])
            eng = m.rr()
            m.copy(eng, dst_ap_fn(c0, c1), self.t[i][:, 0:c1 - c0], reads=[self.b[i]], writes=[wbuf])


def emit_mod(m, nchunk, wada_d, bada_d, cT_d, ps_mod, Bps, stg, sfx=""):
    cond = m.sbuf("cond" + sfx, [128, 8], F32)
    Bcond = m.buf("cond")
    m.dma("sync", cond[:], cT_d, writes=[Bcond])
    m.act(cond[:], cond[:], AF.Silu, reads=[Bcond], writes=[Bcond])
    bada = m.sbuf("bada" + sfx, [128, nchunk], F32)
    Bbada = m.buf("bada")
    m.dma("sync", bada[:], bada_d, writes=[Bbada])
    modsb = m.sbuf("modsb" + sfx, [128, nchunk], F32)
    Bmod = m.buf("mod")
    ns = len(stg.t)
    for j in range(nchunk):
        i = j % ns
        wv = stg.t[i][:, 0:1024].rearrange("p (k c) -> p k c", k=8)
        m.dma("gpsimd" if j % 2 else "sync", wv, wada_d[j], writes=[stg.b[i]])
        for kc in range(8):
            m.mm(ps_mod[:, j:j + 1], wv[:, kc, :], cond[:, kc:kc + 1], kc == 0, kc == 7,
                 reads=[stg.b[i], Bcond], writes=[Bps])
    m.tt("vector", modsb[:], ps_mod[:, 0:nchunk], bada[:], ALU.add, reads=[Bps, Bbada], writes=[Bmod])
    return modsb, Bmod


def emit_norm_mod(m, x_t, Bx, T, A, Bc, BAB, ones_bf, Bconst, sq, Bsq, ps_ss, Bps_ss, rs, Brs, tmp, Btmp, h, Bh):
    for kc in range(8):
        eng = "gpsimd" if kc % 2 else "scalar"
        if eng == "scalar":
            m.act(sq[:, kc, 0:T], x_t[:, kc, 0:T], AF.Square, reads=[Bx], writes=[Bsq])
        else:
            m.tt("gpsimd", sq[:, kc, 0:T], x_t[:, kc, 0:T], x_t[:, kc, 0:T], ALU.mult, reads=[Bx], writes=[Bsq])
    for kc in range(8):
        m.mm(ps_ss[:, 0:T], ones_bf[:], sq[:, kc, 0:T], kc == 0, kc == 7, reads=[Bsq, Bconst], writes=[Bps_ss])
    m.act(rs[:, 0:T], ps_ss[:, 0:T], AF.Ln, reads=[Bps_ss], writes=[Brs], bias=RMS_EPS, scale=1.0 / D)
    m.act(rs[:, 0:T], rs[:, 0:T], AF.Exp, reads=[Brs], writes=[Brs], scale=-0.5)
    for kc in range(8):
        i = kc % 2
        m.tt("vector" if kc % 2 else "gpsimd", tmp[i][:, 0:T], x_t[:, kc, 0:T], rs[:, 0:T], ALU.mult,
             reads=[Bx, Brs], writes=[Btmp[i]])
        m.act(h[:, kc, 0:T], tmp[i][:, 0:T], AF.Identity, reads=[Btmp[i], BAB], writes=[Bh],
              bias=Bc[:, kc:kc + 1], scale=A[:, kc:kc + 1])


def ffn_decl(nc, NT, KO, pre=""):
    KC = KO // 128
    A = {}
    A["xT"] = nc.dram_tensor(pre + "xT", [128, 8, NT], F32, kind="ExternalInput").ap()
    A["ogT"] = nc.dram_tensor(pre + "ogT", [128, KC, NT], BF16, kind="ExternalInput").ap()
    A["w_o"] = nc.dram_tensor(pre + "w_o", [KO, D], F32, kind="ExternalInput").ap()
    A["wada"] = nc.dram_tensor(pre + "wada", [32, 128, 8, 128], F32, kind="ExternalInput").ap()
    A["bada"] = nc.dram_tensor(pre + "bada", [128, 32], F32, kind="ExternalInput").ap()
    A["cT"] = nc.dram_tensor(pre + "cT", [128, 8], F32, kind="ExternalInput").ap()
    A["gam"] = nc.dram_tensor(pre + "gam", [128, 8], F32, kind="ExternalInput").ap()
    A["w1"] = nc.dram_tensor(pre + "w1", [D, 2 * FH], F32, kind="ExternalInput").ap()
    A["w2"] = nc.dram_tensor(pre + "w2", [FH, D], F32, kind="ExternalInput").ap()
    A["outT"] = nc.dram_tensor(pre + "outT", [128, 8, NT], F32, kind="ExternalOutput").ap()
    return A


def build_ffn(NT, KO, T=256):
    nc = bass.Bass("TRN2", target_bir_lowering=False)
    A = ffn_decl(nc, NT, KO)
    m = MK(nc)
    outs = emit_ffn(m, A, NT, KO, T)
    m.final_wait("sync", outs)
    m.build(barrier=False)
    return nc


def emit_ffn(m, A, NT, KO, T=256):
    KC = KO // 128
    HC = FH // 128
    xT_d, og_d, wo_d, wada_d, bada_d, cT_d, gam_d, w1_d, w2_d, out_d = [A.get(k) for k in
        ("xT", "ogT", "w_o", "wada", "bada", "cT", "gam", "w1", "w2", "outT")]

    ones_bf = m.sbuf("ones_bf", [128, 128], BF16)
    Bconst = m.buf("const")
    m.memset("vector", ones_bf[:], 1.0, writes=[Bconst])

    ps_mod = m.psum("ps_mod", [128, 512], F32); Bps_mod = m.buf("ps_mod", excl=True)
    ps_ss = m.psum("ps_ss", [128, 512], F32); Bps_ss = m.buf("ps_ss", excl=True)
    ps_a = [m.psum(f"ps_a{i}", [128, 512], F32) for i in range(3)]; Bps_a = m.bufs("ps_a", 3, excl=True)
    ps_b = [m.psum(f"ps_b{i}", [128, 512], F32) for i in range(2)]; Bps_b = m.bufs("ps_b", 2, excl=True)

    stg = Stage(m, width=1024, n=2)
    modsb, Bmod = emit_mod(m, 32, wada_d, bada_d, cT_d, ps_mod, Bps_mod, stg)
    gam = m.sbuf("gam", [128, 8], F32); Bgam = m.buf("gam")
    m.dma("sync", gam[:], gam_d, writes=[Bgam])
    A2 = m.sbuf("A2", [128, 8], F32)
    BAB = m.buf("AB")
    m.stt("vector", A2[:], modsb[:, 16:24], 1.0, gam[:], ALU.add, ALU.mult, reads=[Bmod, Bgam], writes=[BAB])

    HX = A.get("h_extra")
    if HX is not None:
        modx, Bmodx = emit_mod(m, 16, HX["wada"], HX["bada"], cT_d, ps_mod, Bps_mod, stg, sfx="x")
        gamx = m.sbuf("gamx", [128, 8], F32); Bgamx = m.buf("gamx")
        m.dma("sync", gamx[:], HX["gam"], writes=[Bgamx])
        A1x = m.sbuf("A1x", [128, 8], F32); BABx = m.buf("ABx")
        m.stt("vector", A1x[:], modx[:, 8:16], 1.0, gamx[:], ALU.add, ALU.mult, reads=[Bmodx, Bgamx], writes=[BABx])
    wo = m.sbuf("wo", [128, KC, D], BF16); Bwo = m.buf("wo")
    w1 = m.sbuf("w1", [128, 8, 2 * FH], BF16); Bw1 = m.buf("w1")
    w2 = m.sbuf("w2", [128, HC, D], BF16); Bw2 = m.buf("w2")
    for kc in range(KC):
        stg.load_cast(lambda c0, c1, kc=kc: wo[:, kc, c0:c1], wo_d[kc * 128:(kc + 1) * 128, :], D, Bwo,
                      queue="sync" if kc % 2 else "gpsimd")
    for kc in range(8):
        stg.load_cast(lambda c0, c1, kc=kc: w1[:, kc, c0:c1], w1_d[kc * 128:(kc + 1) * 128, :], 2 * FH, Bw1,
                      queue="sync" if kc % 2 else "gpsimd")
    for hc in range(HC):
        stg.load_cast(lambda c0, c1, hc=hc: w2[:, hc, c0:c1], w2_d[hc * 128:(hc + 1) * 128, :], D, Bw2,
                      queue="sync" if hc % 2 else "gpsimd")

    xt = [m.sbuf(f"xt{i}", [128, 8, T], F32) for i in range(2)]; Bxt = m.bufs("xt", 2)
    ogbs = [m.sbuf(f"ogb{i}", [128, KC, T], BF16) for i in range(2)]; Bogbs = m.bufs("ogb", 2)
    sq = m.sbuf("sq", [128, 8, T], BF16); Bsq = m.buf("sq")
    rs = m.sbuf("rs", [128, T], F32); Brs = m.buf("rs")
    tmp = [m.sbuf(f"tmp{i}", [128, T], F32) for i in range(2)]; Btmp = m.bufs("tmp", 2)
    h2 = m.sbuf("h2", [128, 8, T], BF16); Bh2 = m.buf("h2")
    actb = m.sbuf("actb", [128, HC, T], BF16); Bact = m.buf("act")
    sg = tmp; Bsg = Btmp
    Bouts = []

    ntile = NT // T
    pa = 0
    for it in range(ntile):
        t0 = it * T
        x_t = xt[it % 2]; Bx = Bxt[it % 2]
        if "x_load" in A:
            A["x_load"](m, x_t, t0, T, Bx)
        else:
            m.dma("sync", x_t[:], xT_d[:, :, t0:t0 + T], writes=[Bx])
        if "og_load" in A:
            ogb, Bogb = A["og_load"](m, ogbs, Bogbs, t0, T)
        else:
            ogb = ogbs[it % 2]; Bogb = Bogbs[it % 2]
            m.dma("gpsimd", ogb[:], og_d[:, :, t0:t0 + T], writes=[Bogb])
        for dc in range(8):
            p = ps_a[pa % 3]; Bp = Bps_a[pa % 3]; pa += 1
            for kc in range(KC):
                m.mm(p[:, 0:T], wo[:, kc, dc * 128:(dc + 1) * 128], ogb[:, kc, :], kc == 0, kc == KC - 1,
                     reads=[Bwo, Bogb], writes=[Bp])
            m.stt("vector", x_t[:, dc, :], p[:, 0:T], modsb[:, dc:dc + 1], x_t[:, dc, :], ALU.mult, ALU.add,
                  reads=[Bp, Bmod, Bx], writes=[Bx])
        emit_norm_mod(m, x_t, Bx, T, A2, modsb[:, 8:16], BAB, ones_bf, Bconst, sq, Bsq, ps_ss, Bps_ss,
                      rs, Brs, tmp, Btmp, h2, Bh2)
        for hc in range(HC):
            pg = ps_a[pa % 3]; Bpg = Bps_a[pa % 3]; pa += 1
            pu = ps_b[hc % 2]; Bpu = Bps_b[hc % 2]
            for kc in range(8):
                m.mm(pg[:, 0:T], w1[:, kc, hc * 128:(hc + 1) * 128], h2[:, kc, :], kc == 0, kc == 7,
                     reads=[Bw1, Bh2], writes=[Bpg])
            for kc in range(8):
                m.mm(pu[:, 0:T], w1[:, kc, FH + hc * 128:FH + (hc + 1) * 128], h2[:, kc, :], kc == 0, kc == 7,
                     reads=[Bw1, Bh2], writes=[Bpu])
            s = sg[hc % 2]; Bs = Bsg[hc % 2]
            m.act(s[:], pg[:, 0:T], AF.Silu, reads=[Bpg], writes=[Bs])
            m.tt("vector", actb[:, hc, :], pu[:, 0:T], s[:], ALU.mult, reads=[Bpu, Bs], writes=[Bact])
        for dc in range(8):
            p = ps_a[pa % 3]; Bp = Bps_a[pa % 3]; pa += 1
            for hc in range(HC):
                m.mm(p[:, 0:T], w2[:, hc, dc * 128:(dc + 1) * 128], actb[:, hc, :], hc == 0, hc == HC - 1,
                     reads=[Bw2, Bact], writes=[Bp])
            m.stt("vector", x_t[:, dc, :], p[:, 0:T], modsb[:, 24 + dc:25 + dc], x_t[:, dc, :], ALU.mult, ALU.add,
                  reads=[Bp, Bmod, Bx], writes=[Bx])
        if HX is not None:
            emit_norm_mod(m, x_t, Bx, T, A1x, modx[:, 0:8], BABx, ones_bf, Bconst, sq, Bsq, ps_ss, Bps_ss,
                          rs, Brs, tmp, Btmp, h2, Bh2)
            Bouts.append(HX["store"](m, h2, t0, T, Bh2))
        if "out_store" in A:
            Bouts.append(A["out_store"](m, x_t, t0, T, Bx))
        else:
            Bouts.append(m.buf("out"))
            m.dma("sync", out_d[:, :, t0:t0 + T], x_t[:], reads=[Bx], writes=[Bouts[-1]])
    return Bouts


def lay_xT(xb):
    F = xb.shape[1]
    return np.ascontiguousarray(xb.T.reshape(F // 128, 128, -1).transpose(1, 0, 2))


def unlay_xT(a):
    return np.ascontiguousarray(a.transpose(1, 0, 2).reshape(a.shape[1] * 128, -1).T)


def lay_wada(w, c0, c1):
    sel = w[:, c0:c1]
    n = sel.shape[1] // 128
    return np.ascontiguousarray(sel.reshape(8, 128, n, 128).transpose(2, 1, 0, 3))


def lay_vec(v):
    return np.ascontiguousarray(v.reshape(-1, 128).T)


_DBG = {}


def interleave(f, b, k=4):
    fa, ba = f is not None, b is not None
    while fa or ba:
        for _ in range(k):
            if fa:
                try:
                    next(f)
                except StopIteration:
                    fa = False
        if ba:
            try:
                next(b)
            except StopIteration:
                ba = False


def gdn_consts():
    C = 128
    U = np.triu(np.ones((C, C), np.float32))
    SLm = np.tril(np.ones((C, C), np.float32), -1)
    mui = np.triu(np.ones((C, C), np.float32))
    mus = np.triu(np.ones((C, C), np.float32), 1)
    I = np.eye(C, dtype=np.float32)
    O = np.ones((C, C), np.float32)
    i = np.arange(C)[:, None]; j = np.arange(C)[None, :]
    BD16 = (i // 16 == j // 16).astype(np.float32)
    Ms = [((i // s == j // s) & ((i % s) >= s // 2) & ((j % s) < s // 2)).astype(np.float32) for s in (32, 64, 128)]
    MTs = [np.ascontiguousarray(M_.T) for M_ in Ms]
    return np.ascontiguousarray(np.stack([U, SLm, mui, mus, I, O, BD16] + Ms + MTs, axis=1))


def gdn_decl(nc, NT, NP, pre="", og_kind="ExternalOutput"):
    A = {}
    A["xT"] = nc.dram_tensor(pre + "xT", [128, 8, NT], F32, kind="ExternalInput").ap()
    A["wada"] = nc.dram_tensor(pre + "wada", [16, 128, 8, 128], F32, kind="ExternalInput").ap()
    A["bada"] = nc.dram_tensor(pre + "bada", [128, 16], F32, kind="ExternalInput").ap()
    A["cT"] = nc.dram_tensor(pre + "cT", [128, 8], F32, kind="ExternalInput").ap()
    A["gam"] = nc.dram_tensor(pre + "gam", [128, 8], F32, kind="ExternalInput").ap()
    A["w_qkvz"] = nc.dram_tensor(pre + "w_qkvz", [NP, D, 2048], F32, kind="ExternalInput").ap()
    A["w_ab"] = nc.dram_tensor(pre + "w_ab", [NP, D, 8], F32, kind="ExternalInput").ap()
    A["convw"] = nc.dram_tensor(pre + "convw", [NP, 128, 12, 4], F32, kind="ExternalInput").ap()
    A["alog"] = nc.dram_tensor(pre + "alog", [NP, 128, 4], F32, kind="ExternalInput").ap()
    A["dtb"] = nc.dram_tensor(pre + "dtb", [NP, 128, 4], F32, kind="ExternalInput").ap()
    A["onorm"] = nc.dram_tensor(pre + "onorm", [128, 128], F32, kind="ExternalInput").ap()
    A["consts"] = nc.dram_tensor(pre + "consts", [128, 13, 128], F32, kind="ExternalInput").ap()
    A["ogT"] = nc.dram_tensor(pre + "ogT", [128, NP * 4, NT], BF16, kind=og_kind).ap()
    return A


def build_gdn(NT, NP=1, T=512):
    nc = bass.Bass("TRN2", target_bir_lowering=False)
    A = gdn_decl(nc, NT, NP)
    m = MK(nc)
    outs = emit_gdn(m, A, NT, NP, T)
    m.final_wait("sync", outs)
    m.build(barrier=False)
    return nc


def emit_gdn(m, A, NT, NP, T=512):
    E2DT = BF16 if _DBG.get("e2_bf16") else F32
    NCH = NT // 128
    xT_d, wada_d, bada_d, cT_d, gam_d, w_d, wab_d, convw_d, alog_d, dtb_d, onorm_d, const_d, og_d = [A.get(k) for k in
        ("xT", "wada", "bada", "cT", "gam", "w_qkvz", "w_ab", "convw", "alog", "dtb", "onorm", "consts", "ogT")]

    cst = m.sbuf("cst", [128, 13, 128], F32); Bconst = m.buf("const")
    m.dma("sync", cst[:], const_d, writes=[Bconst])
    Um, SLm, MUI, MUS, IDf, ONEf, BD16 = [cst[:, i, :] for i in range(7)]
    MN = [cst[:, 7 + i, :] for i in range(3)]
    MT = [cst[:, 10 + i, :] for i in range(3)]
    ones_bf = m.sbuf("ones_bf", [128, 128], BF16)
    id_bf = m.sbuf("id_bf", [128, 128], BF16)
    m.memset("vector", ones_bf[:], 1.0, writes=[Bconst])
    m.copy("vector", id_bf[:], IDf, reads=[Bconst], writes=[Bconst])
    convw = m.sbuf("convw", [128, 12, 4], F32)
    negA = m.sbuf("negA", [128, 4], F32)
    dtb = m.sbuf("dtb", [128, 4], F32)
    Bpw = m.buf("passw")
    onorm = m.sbuf("onorm", [128, 128], F32)
    m.dma("sync", onorm[:], onorm_d, writes=[Bconst])

    ps_big = [m.psum(f"big{i}", [128, 512], F32) for i in range(2)]; Bbig = m.bufs("big", 2, excl=True)
    ps_q4 = [m.psum(f"q4_{i}", [128, 4, 128], F32) for i in range(4)]; Bq4 = m.bufs("q4", 4, excl=True)
    qslots = [(ps_q4[i][:, j, :], Bq4[i]) for j in range(4) for i in range(4)]
    ps_tb = [m.psum(f"tb{i}", [128, 8, 128], BF16) for i in range(2)]; Btb = m.bufs("tb", 2, excl=True)
    tslots = [(ps_tb[i][:, j, :], Btb[i]) for j in range(8) for i in range(2)]
    cnt = {"big": 0, "q": 0, "t": 0}

    def big():
        i = cnt["big"] % 2; cnt["big"] += 1
        return ps_big[i], Bbig[i]

    bigB = [(ps_big[i], Bbig[i]) for i in range(2)] + [(ps_q4[i][:].rearrange("p a b -> p (a b)"), Bq4[i]) for i in range(4)]
    cnt["bigB"] = 0

    def bigb():
        i = cnt["bigB"] % len(bigB); cnt["bigB"] += 1
        return bigB[i]

    def qslot():
        i = cnt["q"] % len(qslots); cnt["q"] += 1
        return qslots[i]

    def tslot():
        i = cnt["t"] % len(tslots); cnt["t"] += 1
        return tslots[i]

    stg = Stage(m, width=1024, n=2)
    pm, Bpm = big()
    modsb, Bmod = emit_mod(m, 16, wada_d, bada_d, cT_d, pm, Bpm, stg)
    gam = m.sbuf("gam", [128, 8], F32); Bgam = m.buf("gam")
    m.dma("sync", gam[:], gam_d, writes=[Bgam])
    A1 = m.sbuf("A1", [128, 8], F32); BAB = m.buf("AB")
    m.stt("vector", A1[:], modsb[:, 8:16], 1.0, gam[:], ALU.add, ALU.mult, reads=[Bmod, Bgam], writes=[BAB])

    w = m.sbuf("w", [128, 8, 2048], BF16); Bw = m.buf("w")
    wab = m.sbuf("wab", [128, 8, 8], BF16)

    xt = [m.sbuf(f"xt{i}", [128, 8, T // 2], F32) for i in range(2)]; Bxt = m.bufs("xt", 2)
    sq = m.sbuf("sq", [128, 8, T], BF16); Bsq = m.buf("sq")
    rs = m.sbuf("rs", [128, T], F32); Brs = m.buf("rs")
    tmp = [m.sbuf(f"tmp{i}", [128, T], F32) for i in range(2)]; Btmp = m.bufs("tmp", 2)
    h = m.sbuf("h", [128, 8, T], BF16); Bh = m.buf("h")
    pcb = [m.sbuf(f"pcb{j}", [128, T + 3], BF16) for j in range(12)]; Bpcb = m.bufs("pcb", 12)
    diag = m.sbuf("diag", [128, 48, 128], BF16); Bdiag = m.buf("diag")
    sqq = [m.sbuf(f"sqq{i}", [128, T], BF16) for i in range(2)]; Bsqq = m.bufs("sqq", 2)
    rn = [m.sbuf(f"rn{i}", [128, T], F32) for i in range(2)]; Brn = m.bufs("rn", 2)
    qkn = m.sbuf("qkn", [128, 8, T], BF16); Bqkn = m.bufs("qkn", 8)
    vT = m.sbuf("vT", [128, 4, T], BF16); BvT = m.bufs("vT", 4)
    gz = m.sbuf("gz", [128, T // 128, 512], BF16); Bgz = m.bufs("gz", T // 128)
    absm = m.sbuf("absm", [128, 8], F32); Bab = m.buf("ab")
    sc1 = m.sbuf("sc1", [128, 4], F32); sc2 = m.sbuf("sc2", [128, 4], F32); Bsc = m.buf("sc")
    graw = m.sbuf("graw", [128, 4], F32); Bgraw = m.buf("graw")
    betaP = [m.sbuf(f"beta{i}", [128, 4], F32) for i in range(2)]; negb = m.sbuf("negb", [128, 4], F32); BbetaP = m.bufs("beta", 2)
    GU = m.sbuf("GU", [128, 4, 128], F32); BGU = m.buf("GU")
    gsb = m.sbuf("gsb", [128, 8], F32); Bgsb = m.buf("gsb")
    egP = [m.sbuf(f"eg{i}", [128, 4], F32) for i in range(2)]; negegP = [m.sbuf(f"negeg{i}", [128, 4], F32) for i in range(2)]
    edl = m.sbuf("edl", [128, 4], F32); eglP = [m.sbuf(f"egl{i}", [128, 4], F32) for i in range(2)]; BscaP = m.bufs("sca", 2)
    GT = m.sbuf("GT", [128, 4, 128], F32); BGT = m.buf("GT")
    GTui = m.sbuf("GTui", [128, 4, 128], F32); GTus = m.sbuf("GTus", [128, 4, 128], F32); BGTm = m.buf("GTm")
    vtP = [[m.sbuf(f"vt{p}_{i}", [128, 128], F32) for i in range(4)] for p in range(2)]; BvtP = [m.bufs(f"vt{p}_", 4) for p in range(2)]
    kdecP = [[m.sbuf(f"kdec{p}_{i}", [128, 128], BF16) for i in range(4)] for p in range(2)]; BkdecP = [m.bufs(f"kdec{p}_", 4) for p in range(2)]
    Pm = [[m.sbuf(f"P{i}_{r}", [128, 128], E2DT) for r in range(2)] for i in range(4)]
    Qm = [[m.sbuf(f"Q{i}_{r}", [128, 128], E2DT) for r in range(2)] for i in range(4)]
    Wm = [[m.sbuf(f"W{i}_{r}", [128, 128], E2DT) for r in range(2)] for i in range(4)]
    Dm = [[m.sbuf(f"Dm{i}_{r}", [128, 128], E2DT) for r in range(2)] for i in range(4)]
    BD = [m.bufs(f"Dm{i}_", 2) for i in range(4)]
    Qf = [m.sbuf(f"Qf{i}", [128, 128], E2DT) for i in range(4)]; BQf = m.bufs("Qf", 4)
    Pf = [m.sbuf(f"Pf{i}", [128, 128], E2DT) for i in range(4)]; BPf = m.bufs("Pf", 4)
    Yt = [m.sbuf(f"Yt{i}", [128, 128], E2DT) for i in range(4)]; BYt = m.bufs("Yt", 4)
    Ym = [m.sbuf(f"Ym{i}", [128, 128], E2DT) for i in range(4)]; BYm = m.bufs("Ym", 4)
    Yt2 = [m.sbuf(f"Yt2{i}", [128, 128], E2DT) for i in range(4)]; BYt2 = m.bufs("Yt2", 4)
    Ym2 = [m.sbuf(f"Ym2{i}", [128, 128], E2DT) for i in range(4)]; BYm2 = m.bufs("Ym2", 4)
    BP = [m.bufs(f"P{i}_", 2) for i in range(4)]
    BQ = [m.bufs(f"Q{i}_", 2) for i in range(4)]
    BW = [m.bufs(f"W{i}_", 2) for i in range(4)]
    WbP = [[m.sbuf(f"Wb{p}_{i}", [128, 128], BF16) for i in range(4)] for p in range(2)]; BWbP = [m.bufs(f"Wb{p}_", 4) for p in range(2)]
    ATP = [[m.sbuf(f"AT{p}_{i}", [128, 128], BF16) for i in range(4)] for p in range(2)]; BATP = [m.bufs(f"AT{p}_", 4) for p in range(2)]
    Rm = [m.sbuf(f"Rm{i}", [128, 128], BF16) for i in range(4)]; BRm = m.bufs("Rm", 4)
    vnew = [m.sbuf(f"vnew{i}", [128, 128], BF16) for i in range(4)]; Bvnew = m.bufs("vnew", 4)
    o1e = [m.sbuf(f"o1e{i}", [128, 128], F32) for i in range(4)]; Bo1e = m.bufs("o1e", 4)
    osb = [m.sbuf(f"osb{i}", [128, 128], F32) for i in range(4)]; Bosb = m.bufs("osb", 4)
    junk = m.sbuf("junk", [128, 128], F32); Bjunk = m.buf("junk")
    ssum = m.sbuf("ssum", [128, 4], F32); Bssum = m.buf("ssum")
    rno = m.sbuf("rno", [128, 4], F32); Brno = m.buf("rno")
    Sf = [m.sbuf(f"Sf{i}", [128, 128], F32) for i in range(4)]; BSf = m.bufs("Sf", 4)
    Sb = [m.sbuf(f"Sb{i}", [128, 128], BF16) for i in range(4)]; BSb = m.bufs("Sb", 4)
    ogt = [m.sbuf(f"ogt{i}", [128, 512], BF16) for i in range(2)]; Bogt = m.bufs("ogt", 2)
    ogTt = [m.sbuf(f"ogTt{i}", [128, 4, 128], BF16) for i in range(2)]; BogTt = m.bufs("ogTt", 2)
    Bouts = []
    ntile = NT // T
    for ps, it in [(ps, it) for ps in range(NP) for it in range(ntile)]:
        if it == 0:
            m.dma("sync", convw[:], convw_d[ps], writes=[Bpw])
            m.dma("sync", negA[:], alog_d[ps], writes=[Bpw])
            m.act(negA[:], negA[:], AF.Exp, reads=[Bpw], writes=[Bpw])
            m.ts("vector", negA[:], negA[:], -1.0, None, ALU.mult, None, reads=[Bpw], writes=[Bpw])
            m.dma("sync", dtb[:], dtb_d[ps], writes=[Bpw])
            for kc in range(8):
                stg.load_cast(lambda c0, c1, kc=kc: w[:, kc, c0:c1], w_d[ps, kc * 128:(kc + 1) * 128, :], 2048, Bw,
                              queue="sync" if kc % 2 else "gpsimd")
                stg.load_cast(lambda c0, c1, kc=kc: wab[:, kc, c0:c1], wab_d[ps, kc * 128:(kc + 1) * 128, :], 8, Bw, queue="sync")
            for j in range(12):
                m.memset("gpsimd", pcb[j][:, 0:3], 0.0, writes=[Bpcb[j]])
                for tap in range(4):
                    m.ts("vector", diag[:, j * 4 + tap, :], id_bf[:], convw[:, j, tap:tap + 1], None, ALU.mult, None,
                         reads=[Bconst, Bpw], writes=[Bdiag])
            for i in range(4):
                m.memset("gpsimd", Sf[i][:], 0.0, writes=[BSf[i]])
                m.memset("gpsimd", Sb[i][:], 0.0, writes=[BSb[i]])
        t0 = it * T
        HTL = T // 2
        for hf in range(2):
            x_t = xt[hf]; Bx = Bxt[hf]
            m.dma("sync" if hf else "gpsimd", x_t[:], xT_d[:, :, t0 + hf * HTL:t0 + (hf + 1) * HTL], writes=[Bx])
            pn_, Bpn_ = big()
            emit_norm_mod(m, x_t, Bx, HTL, A1, modsb[:, 0:8], BAB, ones_bf, Bconst, sq, Bsq, pn_, Bpn_,
                          rs, Brs, tmp, Btmp, h[:, :, hf * HTL:(hf + 1) * HTL], Bh)
        st = {}

        def b_s1(j):
            p, Bp = bigb()
            for kc in range(8):
                m.mm(p[:, 0:T], w[:, kc, j * 128:(j + 1) * 128], h[:, kc, :], kc == 0, kc == 7, reads=[Bw, Bh], writes=[Bp])
            st[j] = (p, Bp)

        def b_s2(j):
            p, Bp = st[j]
            if it > 0:
                m.copy("vector", pcb[j][:, 0:3], pcb[j][:, T:T + 3], reads=[Bpcb[j]], writes=[Bpcb[j]])
            m.copy("vector" if j % 3 else "scalar", pcb[j][:, 3:T + 3], p[:, 0:T], reads=[Bp], writes=[Bpcb[j]])
            p2, Bp2 = bigb()
            for tap in range(4):
                m.mm(p2[:, 0:T], diag[:, j * 4 + tap, :], pcb[j][:, tap:tap + T], tap == 0, tap == 3,
                     reads=[Bdiag, Bpcb[j]], writes=[Bp2])
            st[j] = (p2, Bp2)

        def b_s3(j):
            p2, Bp2 = st[j]
            if j >= 8:
                m.act(vT[:, j - 8, :], p2[:, 0:T], AF.Silu, reads=[Bp2], writes=[BvT[j - 8]])
            else:
                m.act(qkn[:, j, :], p2[:, 0:T], AF.Silu, reads=[Bp2], writes=[Bqkn[j]])

        for j in range(12 + 2):
            if j < 12:
                b_s1(j)
            if 1 <= j < 13:
                b_s2(j - 1)
            if j >= 2:
                b_s3(j - 2)
        for c in range(T // 128):
            p, Bp = bigb()
            for kc in range(8):
                m.mm(p[:, :], h[:, kc, c * 128:(c + 1) * 128], w[:, kc, 1536:2048], kc == 0, kc == 7,
                     reads=[Bw, Bh], writes=[Bp])
            m.act(gz[:, c, :], p[:, :], AF.Silu, reads=[Bp], writes=[Bgz[c]])
            for hd in range(4):
                m.tt("gpsimd", gz[:, c, hd * 128:(hd + 1) * 128], gz[:, c, hd * 128:(hd + 1) * 128], onorm[:], ALU.mult,
                     reads=[Bgz[c], Bconst], writes=[Bgz[c]])
        st2 = {}

        def l_s1(j):
            s_ = sqq[j % 2]; Bs_ = Bsqq[j % 2]
            m.tt("vector", s_[:], qkn[:, j, :], qkn[:, j, :], ALU.mult, reads=[Bqkn[j]], writes=[Bs_])
            p2, Bp2 = bigb()
            m.mm(p2[:, 0:T], ones_bf[:], s_[:], True, True, reads=[Bconst, Bs_], writes=[Bp2])
            st2[j] = (p2, Bp2)

        def l_s2(j):
            p2, Bp2 = st2[j]
            r_ = rn[j % 2]; Br_ = Brn[j % 2]
            m.act(r_[:], p2[:, 0:T], AF.Ln, reads=[Bp2], writes=[Br_], bias=RMS_EPS, scale=1.0)
            m.act(r_[:], r_[:], AF.Exp, reads=[Br_], writes=[Br_], scale=-0.5)
            qscale = (128.0 ** -0.5) if j < 4 else 1.0
            m.stt("vector", qkn[:, j, :], qkn[:, j, :], qscale, r_[:], ALU.mult, ALU.mult, reads=[Bqkn[j], Br_], writes=[Bqkn[j]])

        for j in range(8 + 1):
            if j < 8:
                l_s1(j)
            if j >= 1:
                l_s2(j - 1)
        def chunk_front(c):
            ch = it * (T // 128) + c
            csl = slice(c * 128, (c + 1) * 128)
            par = ch % 2
            eg, negeg, egl, beta, Bsca, Bbeta = egP[par], negegP[par], eglP[par], betaP[par], BscaP[par], BbetaP[par]
            vt, Bvt, kdec, Bkdec, AT, BAT, Wb, BWb = vtP[par], BvtP[par], kdecP[par], BkdecP[par], ATP[par], BATP[par], WbP[par], BWbP[par]
            H4 = range(4)
            qTs = [qkn[:, hd, csl] for hd in H4]; kTs = [qkn[:, 4 + hd, csl] for hd in H4]
            Bqs_ = [Bqkn[hd] for hd in H4]; Bks_ = [Bqkn[4 + hd] for hd in H4]
            pab, Bpab = qslot()
            for kc in range(8):
                m.mm(pab[:, 0:8], h[:, kc, csl], wab[:, kc, :], kc == 0, kc == 7, reads=[Bw, Bh], writes=[Bpab])
            m.copy("vector", absm[:], pab[:, 0:8], reads=[Bpab], writes=[Bab])
            m.tt("vector", sc1[:], absm[:, 0:4], dtb[:], ALU.add, reads=[Bab, Bpw], writes=[Bsc])
            m.ts("vector", sc2[:], sc1[:], -1.0, None, ALU.mult, None, reads=[Bsc], writes=[Bsc])
            m.tt("vector", sc2[:], sc2[:], sc1[:], ALU.max, reads=[Bsc], writes=[Bsc])
            m.act(sc2[:], sc2[:], AF.Exp, reads=[Bsc], writes=[Bsc], scale=-1.0)
            m.act(sc2[:], sc2[:], AF.Ln, reads=[Bsc], writes=[Bsc], bias=1.0)
            m.ts("vector", sc1[:], sc1[:], 0.0, None, ALU.max, None, reads=[Bsc], writes=[Bsc])
            m.tt("vector", sc1[:], sc1[:], sc2[:], ALU.add, reads=[Bsc], writes=[Bsc])
            m.tt("vector", graw[:], sc1[:], negA[:], ALU.mult, reads=[Bsc, Bpw], writes=[Bgraw])
            m.act(beta[:], absm[:, 4:8], AF.Exp, reads=[Bab], writes=[Bbeta], scale=-1.0)
            m.act(beta[:], beta[:], AF.Ln, reads=[Bbeta], writes=[Bbeta], bias=1.0)
            m.act(beta[:], beta[:], AF.Exp, reads=[Bbeta], writes=[Bbeta], scale=-1.0)
            m.ts("vector", negb[:], beta[:], -1.0, None, ALU.mult, None, reads=[Bbeta], writes=[Bbeta])
            for hd in range(4):
                m.tt("gpsimd", GU[:, hd, :], Um, graw[:, hd:hd + 1].to_broadcast([128, 128]), ALU.mult, reads=[Bconst, Bgraw], writes=[BGU])
            pg, Bpg = qslot()
            m.mm(pg[:, 0:4], Um, graw[:], True, True, reads=[Bconst, Bgraw], writes=[Bpg])
            m.mm(pg[:, 4:8], ONEf, graw[:], True, True, reads=[Bconst, Bgraw], writes=[Bpg])
            m.copy("vector", gsb[:], pg[:, 0:8], reads=[Bpg], writes=[Bgsb])
            m.act(eg[:], gsb[:, 0:4], AF.Exp, reads=[Bgsb], writes=[Bsca])
            m.ts("vector", negeg[:], eg[:], -1.0, None, ALU.mult, None, reads=[Bsca], writes=[Bsca])
            m.tt("vector", edl[:], gsb[:, 4:8], gsb[:, 0:4], ALU.subtract, reads=[Bgsb], writes=[Bsca])
            m.act(edl[:], edl[:], AF.Exp, reads=[Bsca], writes=[Bsca])
            m.act(egl[:], gsb[:, 4:8], AF.Exp, reads=[Bgsb], writes=[Bsca])
            pD, BpD = big()
            m.mm(pD[:, :], SLm, GU[:].rearrange("p h i -> p (h i)"), True, True, reads=[Bconst, BGU], writes=[BpD])
            m.act(GT[:].rearrange("p h i -> p (h i)"), pD[:, :], AF.Exp, reads=[BpD], writes=[BGT])
            for hd in range(4):
                m.tt("gpsimd", GTui[:, hd, :], GT[:, hd, :], MUI, ALU.mult, reads=[BGT, Bconst], writes=[BGTm])
                m.tt("gpsimd", GTus[:, hd, :], GT[:, hd, :], MUS, ALU.mult, reads=[BGT, Bconst], writes=[BGTm])
            yield
            H4 = range(4)
            qTs = [qkn[:, hd, csl] for hd in H4]; kTs = [qkn[:, 4 + hd, csl] for hd in H4]
            Bqs_ = [Bqkn[hd] for hd in H4]; Bks_ = [Bqkn[4 + hd] for hd in H4]
            yield
            sl1 = []
            yield
            for hd in H4:
                pt, Bpt = tslot()
                m.tr(pt, vT[:, hd, csl], id_bf[:], reads=[BvT[hd], Bconst], writes=[Bpt])
                pt2, Bpt2 = tslot()
                m.tr(pt2, kTs[hd], id_bf[:], reads=[Bks_[hd], Bconst], writes=[Bpt2])
                pkk, Bpkk = qslot()
                m.mm(pkk, kTs[hd], kTs[hd], True, True, reads=[Bks_[hd]], writes=[Bpkk])
                pqk, Bpqk = qslot()
                m.mm(pqk, kTs[hd], qTs[hd], True, True, reads=[Bks_[hd], Bqs_[hd]], writes=[Bpqk])
                sl1.append((pt, Bpt, pt2, Bpt2, pkk, Bpkk, pqk, Bpqk))
            yield
            for hd in H4:
                pt, Bpt, pt2, Bpt2, pkk, Bpkk, pqk, Bpqk = sl1[hd]
                m.copy("scalar", vt[hd][:], pt, reads=[Bpt], writes=[Bvt[hd]])
                m.ts("vector", kdec[hd][:], pt2, edl[:, hd:hd + 1], None, ALU.mult, None, reads=[Bpt2, Bsca], writes=[Bkdec[hd]])
                m.stt("vector", Qf[hd][:], pkk, negb[:, hd:hd + 1], GTus[:, hd, :], ALU.mult, ALU.mult,
                      reads=[Bpkk, Bbeta, BGTm], writes=[BQf[hd]])
                m.tt("vector", AT[hd][:], pqk, GTui[:, hd, :], ALU.mult, reads=[Bpqk, BGTm], writes=[BAT[hd]])
            yield
            sl2 = []
            yield
            for hd in H4:
                if E2DT == F32:
                    pp, Bpp = qslot()
                    m.tr(pp, Qf[hd][:], IDf, reads=[BQf[hd], Bconst], writes=[Bpp])
                else:
                    pp, Bpp = tslot()
                    m.tr(pp, Qf[hd][:], id_bf[:], reads=[BQf[hd], Bconst], writes=[Bpp])
                sl2.append((pp, Bpp))
            yield
            for hd in H4:
                pp, Bpp = sl2[hd]
                m.copy("scalar", Pf[hd][:], pp, reads=[Bpp], writes=[BPf[hd]])
                m.tt("gpsimd", Qm[hd][0][:], Qf[hd][:], BD16, ALU.mult, reads=[BQf[hd], Bconst], writes=[BQ[hd][0]])
                m.tt("gpsimd", Wm[hd][0][:], Qm[hd][0][:], IDf, ALU.add, reads=[BQ[hd][0], Bconst], writes=[BW[hd][0]])
            yield
            for hd in H4:
                m.tt("gpsimd", Pm[hd][0][:], Pf[hd][:], BD16, ALU.mult, reads=[BPf[hd], Bconst], writes=[BP[hd][0]])
                m.tt("gpsimd", Dm[hd][0][:], Pm[hd][0][:], IDf, ALU.add, reads=[BP[hd][0], Bconst], writes=[BD[hd][0]])
            for lev in range(1, 4):
                r0 = (lev - 1) % 2; r1 = lev % 2
                yield
                sl = []
                yield
                for hd in H4:
                    pP, BpP = qslot()
                    m.mm(pP, Qm[hd][r0][:], Pm[hd][r0][:], True, True, reads=[BQ[hd][r0], BP[hd][r0]], writes=[BpP])
                    pQ, BpQ = qslot()
                    m.mm(pQ, Pm[hd][r0][:], Qm[hd][r0][:], True, True, reads=[BQ[hd][r0], BP[hd][r0]], writes=[BpQ])
                    sl.append((pP, BpP, pQ, BpQ))
                yield
                for hd in H4:
                    pP, BpP, pQ, BpQ = sl[hd]
                    m.copy("scalar", Pm[hd][r1][:], pP, reads=[BpP], writes=[BP[hd][r1]])
                    m.copy("vector" if hd % 2 else "scalar", Qm[hd][r1][:], pQ, reads=[BpQ], writes=[BQ[hd][r1]])
                yield
                sl = []
                yield
                for hd in H4:
                    pW, BpW = qslot()
                    m.mm(pW, Pm[hd][r1][:], Wm[hd][r0][:], True, True, reads=[BP[hd][r1], BW[hd][r0]], writes=[BpW])
                    pD, BpD_ = qslot()
                    m.mm(pD, Qm[hd][r1][:], Dm[hd][r0][:], True, True, reads=[BQ[hd][r1], BD[hd][r0]], writes=[BpD_])
                    sl.append((pW, BpW, pD, BpD_))
                yield
                for hd in H4:
                    pW, BpW, pD, BpD_ = sl[hd]
                    m.tt("vector", Wm[hd][r1][:], pW, Wm[hd][r0][:], ALU.add, reads=[BpW, BW[hd][r0]], writes=[BW[hd][r1]])
                    m.tt("vector", Dm[hd][r1][:], pD, Dm[hd][r0][:], ALU.add, reads=[BpD_, BD[hd][r0]], writes=[BD[hd][r1]])
            for si in range(3):
                r0 = (3 + si) % 2; r1 = (4 + si) % 2
                yield
                sl = []
                yield
                for hd in H4:
                    pY, BpY = qslot()
                    m.mm(pY, Pf[hd][:], Wm[hd][r0][:], True, True, reads=[BPf[hd], BW[hd][r0]], writes=[BpY])
                    if si < 2:
                        pY2, BpY2 = qslot()
                        m.mm(pY2, Qf[hd][:], Dm[hd][r0][:], True, True, reads=[BQf[hd], BD[hd][r0]], writes=[BpY2])
                    else:
                        pY2, BpY2 = None, None
                    sl.append((pY, BpY, pY2, BpY2))
                yield
                for hd in H4:
                    pY, BpY, pY2, BpY2 = sl[hd]
                    m.copy("scalar", Yt[hd][:], pY, reads=[BpY], writes=[BYt[hd]])
                    m.tt("gpsimd", Ym[hd][:], Yt[hd][:], MT[si], ALU.mult, reads=[BYt[hd], Bconst], writes=[BYm[hd]])
                    if si < 2:
                        m.copy("scalar", Yt2[hd][:], pY2, reads=[BpY2], writes=[BYt2[hd]])
                        m.tt("gpsimd", Ym2[hd][:], Yt2[hd][:], MN[si], ALU.mult, reads=[BYt2[hd], Bconst], writes=[BYm2[hd]])
                yield
                sl = []
                yield
                for hd in H4:
                    pZ, BpZ = qslot()
                    m.mm(pZ, Dm[hd][r0][:], Ym[hd][:], True, True, reads=[BD[hd][r0], BYm[hd]], writes=[BpZ])
                    if si < 2:
                        pZ2, BpZ2 = qslot()
                        m.mm(pZ2, Wm[hd][r0][:], Ym2[hd][:], True, True, reads=[BW[hd][r0], BYm2[hd]], writes=[BpZ2])
                    else:
                        pZ2, BpZ2 = None, None
                    sl.append((pZ, BpZ, pZ2, BpZ2))
                yield
                for hd in H4:
                    pZ, BpZ, pZ2, BpZ2 = sl[hd]
                    if si < 2:
                        m.tt("vector", Wm[hd][r1][:], pZ, Wm[hd][r0][:], ALU.add, reads=[BpZ, BW[hd][r0]], writes=[BW[hd][r1]])
                        m.tt("vector", Dm[hd][r1][:], pZ2, Dm[hd][r0][:], ALU.add, reads=[BpZ2, BD[hd][r0]], writes=[BD[hd][r1]])
                    else:
                        m.tt("vector", Wb[hd][:], pZ, Wm[hd][r0][:], ALU.add, reads=[BpZ, BW[hd][r0]], writes=[BWb[hd]])
            yield

        def chunk_back(c):
            ch = it * (T // 128) + c
            csl = slice(c * 128, (c + 1) * 128)
            par = ch % 2
            eg, negeg, egl, beta, Bsca, Bbeta = egP[par], negegP[par], eglP[par], betaP[par], BscaP[par], BbetaP[par]
            vt, Bvt, kdec, Bkdec, AT, BAT, Wb, BWb = vtP[par], BvtP[par], kdecP[par], BkdecP[par], ATP[par], BATP[par], WbP[par], BWbP[par]
            H4 = range(4)
            qTs = [qkn[:, hd, csl] for hd in H4]; kTs = [qkn[:, 4 + hd, csl] for hd in H4]
            Bqs_ = [Bqkn[hd] for hd in H4]; Bks_ = [Bqkn[4 + hd] for hd in H4]
            yield
            sl = []
            yield
            for hd in H4:
                pks, Bpks = qslot()
                m.mm(pks, kTs[hd], Sb[hd][:], True, True, reads=[Bks_[hd], BSb[hd]], writes=[Bpks])
                po1, Bpo1 = qslot()
                m.mm(po1, qTs[hd], Sb[hd][:], True, True, reads=[Bqs_[hd], BSb[hd]], writes=[Bpo1])
                sl.append((pks, Bpks, po1, Bpo1))
            yield
            for hd in H4:
                pks, Bpks, po1, Bpo1 = sl[hd]
                m.stt("vector", Rm[hd][:], pks, negeg[:, hd:hd + 1], vt[hd][:], ALU.mult, ALU.add,
                      reads=[Bpks, Bsca, Bvt[hd]], writes=[BRm[hd]])
                m.act(o1e[hd][:], po1, AF.Copy, reads=[Bpo1, Bsca], writes=[Bo1e[hd]], scale=eg[:, hd:hd + 1])
            yield
            sl = []
            yield
            for hd in H4:
                ptr, Bptr = qslot()
                m.mm(ptr, Wb[hd][:], Rm[hd][:], True, True, reads=[BWb[hd], BRm[hd]], writes=[Bptr])
                sl.append((ptr, Bptr))
            yield
            for hd in H4:
                ptr, Bptr = sl[hd]
                m.ts("vector", vnew[hd][:], ptr, beta[:, hd:hd + 1], None, ALU.mult, None, reads=[Bptr, Bbeta], writes=[Bvnew[hd]])
            yield
            sl = []
            yield
            for hd in H4:
                po2, Bpo2 = qslot()
                m.mm(po2, AT[hd][:], vnew[hd][:], True, True, reads=[BAT[hd], Bvnew[hd]], writes=[Bpo2])
                pS, BpS = qslot()
                m.mm(pS, kdec[hd][:], vnew[hd][:], True, True, reads=[Bkdec[hd], Bvnew[hd]], writes=[BpS])
                sl.append((po2, Bpo2, pS, BpS))
            yield
            for hd in H4:
                po2, Bpo2, pS, BpS = sl[hd]
                m.stt("vector", Sf[hd][:], Sf[hd][:], egl[:, hd:hd + 1], pS, ALU.mult, ALU.add,
                      reads=[BSf[hd], Bsca, BpS], writes=[BSf[hd]])
                m.copy("gpsimd", Sb[hd][:], Sf[hd][:], reads=[BSf[hd]], writes=[BSb[hd]])
                m.tt("vector", osb[hd][:], po2, o1e[hd][:], ALU.add, reads=[Bpo2, Bo1e[hd]], writes=[Bosb[hd]])
                m.act(junk[:], osb[hd][:], AF.Square, reads=[Bosb[hd]], writes=[Bjunk, Bssum], accum_out=ssum[:, hd:hd + 1])
            yield
            m.act(rno[:], ssum[:], AF.Ln, reads=[Bssum], writes=[Brno], bias=RMS_EPS, scale=1.0 / 128)
            m.act(rno[:], rno[:], AF.Exp, reads=[Brno], writes=[Brno], scale=-0.5)
            og_t = ogt[ch % 2]; Bog_t = Bogt[ch % 2]
            for hd in range(4):
                m.stt("vector", og_t[:, hd * 128:(hd + 1) * 128], osb[hd][:], rno[:, hd:hd + 1], gz[:, c, hd * 128:(hd + 1) * 128],
                      ALU.mult, ALU.mult, reads=[Bosb[hd], Brno, Bgz[c]], writes=[Bog_t])
            ot = ogTt[ch % 2]; Bot = BogTt[ch % 2]
            for hd in range(4):
                pt3, Bpt3 = tslot()
                m.tr(pt3, og_t[:, hd * 128:(hd + 1) * 128], id_bf[:], reads=[Bog_t, Bconst], writes=[Bpt3])
                m.copy("scalar", ot[:, hd, :], pt3, reads=[Bpt3], writes=[Bot])
            if "og_store" in A:
                Bouts.append(A["og_store"](m, ot, ps, ch, Bot))
            else:
                Bouts.append(m.buf("out"))
                m.dma("sync", og_d[:, ps * 4:(ps + 1) * 4, ch * 128:(ch + 1) * 128], ot[:], reads=[Bot], writes=[Bouts[-1]])

            yield

        prev_back = None
        for c in range(0 if _DBG.get("skip_chunks") else T // 128):
            interleave(chunk_front(c), prev_back)
            prev_back = chunk_back(c)
        interleave(None, prev_back)
    return Bouts


def gdn_inputs(hhs, xb, cb, wada0, bada0, gam, w_in, conv, a_log, dtb, onorm, xT=None):
    W, WAB, CW, AL, DT = [], [], [], [], []
    for hh in hhs:
        hs = [hh * 4 + i for i in range(4)]
        cols = []
        for tsr in range(4):
            for hd in hs:
                cols.extend(range(tsr * 1024 + hd * 128, tsr * 1024 + (hd + 1) * 128))
        W.append(w_in[:, cols])
        WAB.append(w_in[:, [4096 + hd for hd in hs] + [4104 + hd for hd in hs]])
        cw = np.stack([conv[:, tsr * 1024 + hd * 128: tsr * 1024 + (hd + 1) * 128] for tsr in range(3) for hd in hs], 0)
        CW.append(cw.transpose(2, 0, 1))
        AL.append(np.broadcast_to(a_log[hs][None, :], (128, 4)))
        DT.append(np.broadcast_to(dtb[hs][None, :], (128, 4)))
    c_ = lambda l: np.ascontiguousarray(np.stack(l, 0), dtype=np.float32)
    return {
        "xT": lay_xT(xb) if xT is None else xT, "wada": lay_wada(wada0, 0, 2048), "bada": lay_vec(bada0[0:2048]), "cT": lay_vec(cb),
        "gam": lay_vec(gam), "w_qkvz": c_(W), "w_ab": c_(WAB), "convw": c_(CW), "alog": c_(AL), "dtb": c_(DT),
        "onorm": np.ascontiguousarray(np.broadcast_to(onorm[None, :], (128, 128))),
        "consts": gdn_consts(),
    }


DSW_GROUPS = ((128, 1), (512, 4), (2048, 16))
NEG = -30000.0


def t5_bucket(dist):
    dist = np.asarray(dist, np.int32)
    x = (np.maximum(dist, 1).astype(np.float32) / np.float32(16)).astype(np.float32)
    scaled = (np.log(x).astype(np.float32) / np.float32(np.log(2048 / 16))).astype(np.float32)
    large = 16 + (scaled * np.float32(16)).astype(np.float32).astype(np.int32)
    large = np.minimum(large, 31)
    return np.where(dist < 16, dist, large)


def attn_tables():
    ki = np.arange(128)[:, None, None]
    kb = np.arange(2)[None, :, None]
    qi = np.arange(128)[None, None, :]
    dist = qi + 128 * (1 - kb) - ki
    valid = (dist >= 0) & (dist <= 128)
    return dist, valid


def attn_decl(nc, NT, NP, pre="", og_kind="ExternalOutput", x_kind="ExternalInput"):
    A = {}
    A["xT"] = nc.dram_tensor(pre + "xT", [128, 8, NT], F32, kind=x_kind).ap()
    A["wada"] = nc.dram_tensor(pre + "wada", [16, 128, 8, 128], F32, kind="ExternalInput").ap()
    A["bada"] = nc.dram_tensor(pre + "bada", [128, 16], F32, kind="ExternalInput").ap()
    A["cT"] = nc.dram_tensor(pre + "cT", [128, 8], F32, kind="ExternalInput").ap()
    A["gam"] = nc.dram_tensor(pre + "gam", [128, 8], F32, kind="ExternalInput").ap()
    A["w_qkv"] = nc.dram_tensor(pre + "w_qkv", [NP, D, 1152], F32, kind="ExternalInput").ap()
    A["gains"] = nc.dram_tensor(pre + "gains", [128, 2], F32, kind="ExternalInput").ap()
    A["biasT"] = nc.dram_tensor(pre + "biasT", [NP, 128, 6, 256], F32, kind="ExternalInput").ap()
    A["consts"] = nc.dram_tensor(pre + "consts", [128, 2, 256], F32, kind="ExternalInput").ap()
    A["ogT"] = nc.dram_tensor(pre + "ogT", [128, NP, NT], BF16, kind=og_kind).ap()
    return A


def build_attn(NT, NP=2, T=256):
    nc = bass.Bass("TRN2", target_bir_lowering=False)
    A = attn_decl(nc, NT, NP)
    m = MK(nc)
    outs = emit_attn(m, A, NT, NP, T)
    m.final_wait("sync", outs)
    m.build(barrier=False)
    return nc


def emit_attn(m, A, NT, NP, T=256):
    UN = 2048
    NU = NT // UN
    xT_d, wada_d, bada_d, cT_d, gam_d, w_d, gains_d, bias_d, cst_d, og_d = [A.get(k) for k in
        ("xT", "wada", "bada", "cT", "gam", "w_qkv", "gains", "biasT", "consts", "ogT")]

    cst = m.sbuf("cst", [128, 2, 256], F32); Bconst = m.buf("const")
    m.dma("sync", cst[:], cst_d, writes=[Bconst])
    negmask = cst[:, 0, :]
    ones_bf = m.sbuf("ones_bf", [128, 128], BF16)
    bones_bf = m.sbuf("bones_bf", [128, 128], BF16)
    m.memset("vector", ones_bf[:], 1.0, writes=[Bconst])
    m.copy("vector", bones_bf[:], cst[:, 1, 0:128], reads=[Bconst], writes=[Bconst])
    gains = m.sbuf("gains", [128, 2], F32)
    m.dma("sync", gains[:], gains_d, writes=[Bconst])
    m.ts("vector", gains[:, 0:1], gains[:, 0:1], 0.125, None, ALU.mult, None, reads=[Bconst], writes=[Bconst])

    ps_big = [m.psum(f"big{i}", [128, 512], F32) for i in range(2)]; Bbig = m.bufs("big", 2, excl=True)
    ps_sf = [m.psum(f"s{i}", [128, 512], F32) for i in range(4)]; Bps_s = m.bufs("ps_s", 4, excl=True)
    ps_s = [t[:, 0:256].rearrange("p (b q) -> p b q", b=2) for t in ps_sf]
    ps_of = [m.psum(f"o{i}", [128, 512], F32) for i in range(2)]; Bps_o = m.bufs("ps_o", 2, excl=True)
    ps_o = [t[:, 0:128] for t in ps_of]
    ps_v = [ps_sf[2][:, 0:128], ps_sf[3][:, 0:128]]; Bps_v = [Bps_s[2], Bps_s[3]]
    ring = [(ps_big[i], Bbig[i]) for i in range(2)] + [(ps_sf[i], Bps_s[i]) for i in range(4)] + [(ps_of[i], Bps_o[i]) for i in range(2)]
    cnt = {"big": 0, "s": 0, "o": 0, "v": 0, "ring": 0}

    def rbank():
        i = cnt["ring"] % len(ring); cnt["ring"] += 1
        return ring[i]


    def big():
        i = cnt["big"] % 2; cnt["big"] += 1
        return ps_big[i], Bbig[i]

    stg = Stage(m, width=1024, n=2)
    HL = A.get("h_load")
    if HL is None:
        pm, Bpm = big()
        modsb, Bmod = emit_mod(m, 16, wada_d, bada_d, cT_d, pm, Bpm, stg)
        gam = m.sbuf("gam", [128, 8], F32); Bgam = m.buf("gam")
        m.dma("sync", gam[:], gam_d, writes=[Bgam])
        A1 = m.sbuf("A1", [128, 8], F32); BAB = m.buf("AB")
        m.stt("vector", A1[:], modsb[:, 8:16], 1.0, gam[:], ALU.add, ALU.mult, reads=[Bmod, Bgam], writes=[BAB])

    w = m.sbuf("w", [128, 8, 1152], BF16); Bw = m.buf("w")
    biasT = m.sbuf("biasT", [128, 6, 256], F32); Bbias = m.buf("bias")
    if HL is None:
        xt = [m.sbuf(f"xt{i}", [128, 8, T], F32) for i in range(1)]; Bxt = m.bufs("xt", 1)
        sq = m.sbuf("sq", [128, 8, T], BF16); Bsq = m.buf("sq")
        rs = m.sbuf("rs", [128, T], F32); Brs = m.buf("rs")
        tmp = [m.sbuf(f"tmp{i}", [128, T], F32) for i in range(2)]; Btmp = m.bufs("tmp", 2)
    hU = m.sbuf("hU", [128, 8, UN], BF16); BhU = m.buf("hU")
    qf = [m.sbuf(f"qf{i}", [128, 512], F32) for i in range(2)]; Bqf = m.bufs("qf", 2)
    sqq = [m.sbuf(f"sqq{i}", [128, 512], BF16) for i in range(2)]; Bsqq = m.bufs("sqq", 2)
    rn = [m.sbuf(f"rn{i}", [128, 512], F32) for i in range(2)]; Brn = m.bufs("rn", 2)
    qn = [m.sbuf(f"qn{g}", [128, UN], BF16) for g in range(3)]; Bqn = m.bufs("qn", 3)
    kn = [[m.sbuf(f"kn{g}_{r}", [128, UN], BF16) for r in range(2)] for g in range(3)]
    Bkn = [m.bufs(f"kn{g}_", 2) for g in range(3)]
    Va = [[m.sbuf(f"Va{g}_{r}", [128, 16, 2, 128], BF16) for r in range(2)] for g in range(3)]
    BVa = [m.bufs(f"Va{g}_", 2) for g in range(3)]
    for g in range(3):
        for r in range(2):
            m.memset("gpsimd" if r else "vector", Va[g][r][:, :, :, 64:128], 1.0, writes=[BVa[g][r]])
    acc = [m.sbuf(f"acc{i}", [128, UN], F32) for i in range(1)]; Bacc = m.bufs("acc", 1)
    sT = [m.sbuf(f"sT{i}", [128, 2, 128], F32) for i in range(3)]; BsT = m.bufs("sT", 3)
    pT = [m.sbuf(f"pT{i}", [128, 2, 128], BF16) for i in range(3)]; BpT = m.bufs("pT", 3)
    rden = m.sbuf("rden", [64, UN], F32); Brden = m.buf("rden")
    obf = [m.sbuf(f"obf{i}", [64, UN], BF16) for i in range(1)]; Bobf = m.bufs("obf", 1)
    Bouts = []
    nacc = 0
    npt = 0

    for pp in range(NP):
        for kc in range(8):
            stg.load_cast(lambda c0, c1, kc=kc: w[:, kc, c0:c1], w_d[pp, kc * 128:(kc + 1) * 128, :], 1152, Bw,
                          queue="sync" if kc % 2 else "gpsimd")
        m.dma("sync", biasT[:], bias_d[pp], writes=[Bbias])
        for i in range(6):
            m.tt("gpsimd", biasT[:, i, :], biasT[:, i, :], negmask, ALU.add, reads=[Bbias, Bconst], writes=[Bbias])
        for u in range(NU):
            ur = u % 2
            if HL is not None:
                HL(m, hU, u, BhU)
            for ti in range(0 if HL is not None else UN // T):
                t0 = u * UN + ti * T
                x_t = xt[0]; Bx = Bxt[0]
                if "x_load" in A:
                    A["x_load"](m, x_t, t0, T, Bx)
                else:
                    m.dma("sync" if ti % 2 else "gpsimd", x_t[:], xT_d[:, :, t0:t0 + T], writes=[Bx])
                pn_, Bpn_ = big()
                emit_norm_mod(m, x_t, Bx, T, A1, modsb[:, 0:8], BAB, ones_bf, Bconst, sq, Bsq, pn_, Bpn_,
                              rs, Brs, tmp, Btmp, hU[:, :, ti * T:(ti + 1) * T], BhU)
            items = [(g_, qk, tl) for g_ in range(3) for qk in range(2) for tl in range(UN // 512)]
            stq = {}

            def q_s1(i):
                g_, qk, tl = items[i]
                c0 = (g_ * 3 + qk) * 128
                p, Bp = rbank()
                for kc in range(8):
                    m.mm(p[:, :], w[:, kc, c0:c0 + 128], hU[:, kc, tl * 512:(tl + 1) * 512], kc == 0, kc == 7,
                         reads=[Bw, BhU], writes=[Bp])
                stq[i] = (p, Bp)

            def q_s2(i):
                p, Bp = stq[i]
                f_ = qf[i % 2]; Bf_ = Bqf[i % 2]
                m.copy("scalar", f_[:], p[:, :], reads=[Bp], writes=[Bf_])
                s_ = sqq[i % 2]; Bs_ = Bsqq[i % 2]
                m.tt("gpsimd", s_[:], f_[:], f_[:], ALU.mult, reads=[Bf_], writes=[Bs_])
                p2, Bp2 = rbank()
                m.mm(p2[:, :], bones_bf[:], s_[:], True, True, reads=[Bconst, Bs_], writes=[Bp2])
                stq[i] = (p2, Bp2)

            def q_s3(i):
                g_, qk, tl = items[i]
                d = DSW_GROUPS[g_][1]
                dst = qn[g_] if qk == 0 else kn[g_][ur]
                Bdst = Bqn[g_] if qk == 0 else Bkn[g_][ur]
                p2, Bp2 = stq.pop(i)
                f_ = qf[i % 2]; Bf_ = Bqf[i % 2]
                r_ = rn[i % 2]; Br_ = Brn[i % 2]
                m.act(r_[:], p2[:, :], AF.Ln, reads=[Bp2], writes=[Br_], bias=RMS_EPS, scale=1.0 / 64)
                m.act(r_[:], r_[:], AF.Exp, reads=[Br_], writes=[Br_], scale=-0.5)
                J = 512 // d
                j0 = tl * J
                if d == 1:
                    o_ap = dst[:, tl * 512:(tl + 1) * 512]
                    i0 = f_[:]; i1 = r_[:]
                else:
                    o_ap = dst[:].rearrange("p (r j) -> p r j", r=d)[:, :, j0:j0 + J].rearrange("p r j -> p j r")
                    i0 = f_[:].rearrange("p (j r) -> p j r", r=d)
                    i1 = r_[:].rearrange("p (j r) -> p j r", r=d)
                m.stt("vector", o_ap, i0, gains[:, qk:qk + 1], i1, ALU.mult, ALU.mult,
                      reads=[Bf_, Br_, Bconst], writes=[Bdst])

            nit = len(items)
            for i in range(nit + 2):
                if i < nit:
                    q_s1(i)
                if 1 <= i < nit + 1:
                    q_s2(i - 1)
                if i >= 2:
                    q_s3(i - 2)
            for g in range(3):
                d = DSW_GROUPS[g][1]
                nb = 16 // d
                c0 = (g * 3 + 2) * 128
                for r in range(d):
                    for n_ in range(nb):
                        blk = r * nb + n_
                        tb = n_ * 128 * d + r
                        i = cnt["v"] % 2; cnt["v"] += 1
                        for kc in range(8):
                            m.mm(ps_v[i], hU[:, kc, tb:tb + 127 * d + 1:d], w[:, kc, c0:c0 + 128], kc == 0, kc == 7,
                                 reads=[BhU, Bw], writes=[Bps_v[i]])
                        m.copy("scalar", Va[g][ur][:, blk, :, 0:64], ps_v[i].rearrange("p (h c) -> p h c", h=2),
                               reads=[Bps_v[i]], writes=[BVa[g][ur]])
            LAG = 2
            for hl in range(2):
                a_ = acc[0]; Ba_ = Bacc[0]; nacc += 1
                hp = slice(hl * 64, (hl + 1) * 64)
                blocks = []
                for g in range(3):
                    d = DSW_GROUPS[g][1]
                    nb = 16 // d
                    for r in range(d):
                        for n_ in range(nb):
                            blocks.append((g, d, nb, r, n_))
                stA = {}

                def stage_a(i):
                    g, d, nb, r, n_ = blocks[i]
                    bt = biasT[:, g * 2 + hl, :].rearrange("p (b q) -> p b q", b=2)
                    blk = r * nb + n_
                    qcol = blk * 128
                    if n_ > 0:
                        kprev = (kn[g][ur], Bkn[g][ur], Va[g][ur], BVa[g][ur], blk - 1)
                    elif u > 0:
                        kprev = (kn[g][1 - ur], Bkn[g][1 - ur], Va[g][1 - ur], BVa[g][1 - ur], r * nb + nb - 1)
                    else:
                        kprev = None
                    si = cnt["s"] % len(ps_s); cnt["s"] += 1
                    pS = ps_s[si]; BpS = Bps_s[si]
                    kb0 = 0 if kprev is not None else 1
                    if kprev is not None:
                        kt, Bkt, _, _, pb = kprev
                        m.mm(pS[:, 0, :], kt[hp, pb * 128:(pb + 1) * 128], qn[g][hp, qcol:qcol + 128], True, True,
                             reads=[Bkt, Bqn[g]], writes=[BpS])
                    m.mm(pS[:, 1, :], kn[g][ur][hp, qcol:qcol + 128], qn[g][hp, qcol:qcol + 128], True, True,
                         reads=[Bkn[g][ur], Bqn[g]], writes=[BpS])
                    k3 = i % 3
                    s_ = sT[k3]; Bs_ = BsT[k3]
                    p_ = pT[k3]; Bp_ = BpT[k3]
                    m.tt("vector", s_[:, kb0:2, :], pS[:, kb0:2, :], bt[:, kb0:2, :], ALU.add,
                         reads=[BpS, Bbias], writes=[Bs_])
                    m.act(p_[:, kb0:2, :], s_[:, kb0:2, :], AF.Exp, reads=[Bs_], writes=[Bp_])
                    stA[i] = (kprev, p_, Bp_)

                def stage_b(i):
                    g, d, nb, r, n_ = blocks[i]
                    kprev, p_, Bp_ = stA.pop(i)
                    blk = r * nb + n_
                    tb = n_ * 128 * d + r
                    oi = cnt["o"] % 2; cnt["o"] += 1
                    pO = ps_o[oi]; BpO = Bps_o[oi]
                    if kprev is not None:
                        _, _, vt_, Bvt_, pb = kprev
                        m.mm(pO[:, :], vt_[:, pb, hl, :], p_[:, 0, :], True, False, reads=[Bvt_, Bp_], writes=[BpO], inc=False)
                    m.mm(pO[:, :], Va[g][ur][:, blk, hl, :], p_[:, 1, :], kprev is None, True,
                         reads=[BVa[g][ur], Bp_], writes=[BpO])
                    a_ap = a_[:, tb:tb + 127 * d + 1:d]
                    if g == 0:
                        m.copy("vector", a_ap, pO[:, :], reads=[BpO], writes=[Ba_])
                    else:
                        m.tt("vector", a_ap, pO[:, :], a_ap, ALU.add, reads=[BpO, Ba_], writes=[Ba_])

                nblk = len(blocks)
                for i in range(nblk + LAG):
                    if i < nblk:
                        stage_a(i)
                    if i >= LAG:
                        stage_b(i - LAG)
                m.act(rden[:], a_[64:128, :], AF.Ln, reads=[Ba_], writes=[Brden])
                m.act(rden[:], rden[:], AF.Exp, reads=[Brden], writes=[Brden], scale=-1.0)
                ob = obf[0]; Bob = Bobf[0]
                m.tt("gpsimd", ob[:], a_[0:64, :], rden[:], ALU.mult, reads=[Ba_, Brden], writes=[Bob])
                if "og_store" in A:
                    Bouts.append(A["og_store"](m, ob, pp, hl, u, Bob))
                else:
                    Bouts.append(m.buf("out"))
                    m.dma("sync", og_d[hl * 64:(hl + 1) * 64, pp, u * UN:(u + 1) * UN], ob[:], reads=[Bob], writes=[Bouts[-1]])
    return Bouts


def attn_inputs(pairs, xb, cb, wada1, bada1, gam, w_in, q_gain, k_gain, rel_bias, xT=None):
    NP = len(pairs)
    wsel = np.empty((NP, D, 1152), np.float32)
    bias = np.empty((NP, 128, 6, 256), np.float32)
    dist, valid = attn_tables()
    for pi, pr in enumerate(pairs):
        for g in range(3):
            d = DSW_GROUPS[g][1]
            idx = t5_bucket(np.clip(dist, 0, 128) * d)
            for t in range(3):
                for hl in range(2):
                    hd = pr * 2 + hl
                    c_src = ((t * 3 + g) * 8 + hd) * 64
                    c_dst = (g * 3 + t) * 128 + hl * 64
                    wsel[pi, :, c_dst:c_dst + 64] = w_in[:, c_src:c_src + 64]
            for hl in range(2):
                hd = pr * 2 + hl
                bias[pi, :, g * 2 + hl, :] = rel_bias[idx, g * 8 + hd].reshape(128, 256)
    cst = np.zeros((128, 2, 256), np.float32)
    cst[:, 0, :] = np.where(valid, 0.0, NEG).reshape(128, 256)
    cst[0:64, 1, 0:64] = 1.0
    cst[64:128, 1, 64:128] = 1.0
    gains = np.stack([np.tile(q_gain, 2), np.tile(k_gain, 2)], axis=1).astype(np.float32)
    d_ = {"wada": lay_wada(wada1, 0, 2048), "bada": lay_vec(bada1[0:2048]), "cT": lay_vec(cb),
          "gam": lay_vec(gam), "w_qkv": wsel, "gains": np.ascontiguousarray(gains), "biasT": bias, "consts": cst}
    if xT is not None or xb is not None:
        d_["xT"] = lay_xT(xb) if xT is None else xT
    return d_


def build_fused(NT=SEQ):
    nc = bass.Bass("TRN2", target_bir_lowering=False)
    ses = ExitStack()
    ext = lambda name, shape, dt=F32: nc.dram_tensor(name, list(shape), dt, kind="ExternalInput").ap()
    xT = ext("xT", [128, 8, NT]); cT = ext("cT", [128, 8])
    ogT0 = nc.dram_tensor("ogT0", [128, 8, NT], BF16, kind="Internal").ap()
    x1T = nc.dram_tensor("x1T", [128, 8, NT], F32, kind="Internal").ap()
    ogT1 = nc.dram_tensor("ogT1", [128, 4, NT], BF16, kind="Internal").ap()
    outT = nc.dram_tensor("outT", [128, 8, NT], F32, kind="ExternalOutput").ap()
    A1 = {"xT": xT, "cT": cT, "wada": ext("g_wada", [16, 128, 8, 128]), "bada": ext("g_bada", [128, 16]),
          "gam": ext("g_gam", [128, 8]), "w_qkvz": ext("g_w_qkvz", [2, D, 2048]), "w_ab": ext("g_w_ab", [2, D, 8]),
          "convw": ext("g_convw", [2, 128, 12, 4]), "alog": ext("g_alog", [2, 128, 4]), "dtb": ext("g_dtb", [2, 128, 4]),
          "onorm": ext("g_onorm", [128, 128]), "consts": ext("g_consts", [128, 13, 128]), "ogT": ogT0}
    A2 = {"xT": xT, "cT": cT, "ogT": ogT0, "w_o": ext("f0_w_o", [1024, D]), "wada": ext("f0_wada", [32, 128, 8, 128]),
          "bada": ext("f0_bada", [128, 32]), "gam": ext("f0_gam", [128, 8]), "w1": ext("f0_w1", [D, 2 * FH]),
          "w2": ext("f0_w2", [FH, D]), "outT": x1T}
    A3 = {"xT": x1T, "cT": cT, "wada": ext("a_wada", [16, 128, 8, 128]), "bada": ext("a_bada", [128, 16]),
          "gam": ext("a_gam", [128, 8]), "w_qkv": ext("a_w_qkv", [4, D, 1152]), "gains": ext("a_gains", [128, 2]),
          "biasT": ext("a_biasT", [4, 128, 6, 256]), "consts": ext("a_consts", [128, 2, 256]), "ogT": ogT1}
    A4 = {"xT": x1T, "cT": cT, "ogT": ogT1, "w_o": ext("f1_w_o", [512, D]), "wada": ext("f1_wada", [32, 128, 8, 128]),
          "bada": ext("f1_bada", [128, 32]), "gam": ext("f1_gam", [128, 8]), "w1": ext("f1_w1", [D, 2 * FH]),
          "w2": ext("f1_w2", [FH, D]), "outT": outT}
    m = MK(nc, sem_es=ses, tag="p1_"); emit_gdn(m, A1, NT, 2); m.build()
    m = MK(nc, sem_es=ses, tag="p2_"); emit_ffn(m, A2, NT, 1024); m.build()
    m = MK(nc, sem_es=ses, tag="p3_"); emit_attn(m, A3, NT, 4); m.build()
    m = MK(nc, sem_es=ses, tag="p4_"); emit_ffn(m, A4, NT, 512); m.build()
    ses.close()
    return nc


PAIRS = [[0, 1], [2, 3], [4, 5], [6, 7]]


def build_fused8(NT=SEQ):
    nc = bass.Bass("TRN2", target_bir_lowering=False)
    ses = ExitStack()
    HT = NT // 2
    ext = lambda name, shape, dt=F32: nc.dram_tensor(name, list(shape), dt, kind="ExternalInput").ap()
    itn = lambda name, shape, dt: nc.dram_tensor(name, list(shape), dt, kind="Internal").ap()
    xT = ext("xT", [128, 8, NT]); xTh = ext("xTh", [128, 8, HT]); cT = ext("cT", [128, 8]); sel_d = ext("sel", [128, 2])
    outT = nc.dram_tensor("outT", [128, 8, HT], F32, kind="ExternalOutput").ap()
    OGC = 2048; X1C = 512; O2C = 4096
    n_og, n_x1, n_o2 = NT // OGC, HT // X1C, NT // O2C
    og_src = [itn(f"og_src{j}", [128 * 4, OGC], BF16) for j in range(n_og)]
    og_gat = [itn(f"og_gat{j}", [2 * 128 * 4, OGC], BF16) for j in range(n_og)]
    x1_src = [itn(f"x1_src{j}", [128 * 8, X1C], F32) for j in range(n_x1)]
    H1C = 1024; n_h1 = HT // H1C
    h1_src = [itn(f"h1_src{j}", [128 * 8, H1C], BF16) for j in range(n_h1)]
    h1_gat = [itn(f"h1_gat{j}", [2 * 128 * 8, H1C], BF16) for j in range(n_h1)]
    h1_sv = [t.rearrange("(p k) t -> p k t", k=8) for t in h1_src]
    h1_gv = [t.rearrange("(r p k) t -> r p k t", r=2, k=8) for t in h1_gat]
    o2_src = [itn(f"o2_src{j}", [128 * 2, O2C], BF16) for j in range(n_o2)]
    o2_gat = [itn(f"o2_gat{j}", [2 * 128 * 2, O2C], BF16) for j in range(n_o2)]
    og_sv = [t.rearrange("(p k) t -> p k t", k=4) for t in og_src]
    og_gv = [t.rearrange("(r p k) t -> r p k t", r=2, k=4) for t in og_gat]
    x1_sv = [t.rearrange("(p k) t -> p k t", k=8) for t in x1_src]
    o2_sv = [t.rearrange("(p k) t -> p k t", k=2) for t in o2_src]
    o2_gv = [t.rearrange("(r p k) t -> r p k t", r=2, k=2) for t in o2_gat]

    A1 = {"xT": xT, "cT": cT, "wada": ext("g_wada", [16, 128, 8, 128]), "bada": ext("g_bada", [128, 16]),
          "gam": ext("g_gam", [128, 8]), "w_qkvz": ext("g_w_qkvz", [1, D, 2048]), "w_ab": ext("g_w_ab", [1, D, 8]),
          "convw": ext("g_convw", [1, 128, 12, 4]), "alog": ext("g_alog", [1, 128, 4]), "dtb": ext("g_dtb", [1, 128, 4]),
          "onorm": ext("g_onorm", [128, 128]), "consts": ext("g_consts", [128, 13, 128])}
    A2 = {"xT": xTh, "cT": cT, "w_o": ext("f0_w_o", [1024, D]), "wada": ext("f0_wada", [32, 128, 8, 128]),
          "bada": ext("f0_bada", [128, 32]), "gam": ext("f0_gam", [128, 8]), "w1": ext("f0_w1", [D, 2 * FH]),
          "w2": ext("f0_w2", [FH, D])}
    A3 = {"cT": cT, "wada": ext("a_wada", [16, 128, 8, 128]), "bada": ext("a_bada", [128, 16]),
          "gam": ext("a_gam", [128, 8]), "w_qkv": ext("a_w_qkv", [2, D, 1152]), "gains": ext("a_gains", [128, 2]),
          "biasT": ext("a_biasT", [2, 128, 6, 256]), "consts": ext("a_consts", [128, 2, 256])}
    A4 = {"cT": cT, "w_o": ext("f1_w_o", [512, D]), "wada": ext("f1_wada", [32, 128, 8, 128]),
          "bada": ext("f1_bada", [128, 32]), "gam": ext("f1_gam", [128, 8]), "w1": ext("f1_w1", [D, 2 * FH]),
          "w2": ext("f1_w2", [FH, D]), "outT": outT}

    class Xchg:
        def __init__(self, m, srcs, gats, need):
            self.m, self.srcs, self.gats, self.need = m, srcs, gats, need
            self.bufs = {j: [] for j in range(len(srcs))}

        def stored(self, j, buf):
            self.bufs[j].append(buf)
            if len(self.bufs[j]) == self.need:
                self.m.cc_allgather(self.srcs[j], self.gats[j], PAIRS, reads=self.bufs[j], writes=[self.m.buf("gat")])

    def make_og_load(gv, kper, chunk, msel):
        def og_load(m, ogbs, Bogbs, t0, T):
            A_, B_ = ogbs[0], ogbs[1]
            for r in range(2):
                ja, oa = divmod(t0, chunk)
                jb, ob_ = divmod(HT + t0, chunk)
                m.dma("gpsimd", A_[:, r * kper:(r + 1) * kper, :], gv[ja][r][:, :, oa:oa + T], writes=[Bogbs[0]])
                m.dma("sync", B_[:, r * kper:(r + 1) * kper, :], gv[jb][r][:, :, ob_:ob_ + T], writes=[Bogbs[1]])
            sel, Bsel = msel["sel"]
            m.ts("vector", A_[:], A_[:], sel[:, 0:1], None, ALU.mult, None, reads=[Bogbs[0], Bsel], writes=[Bogbs[0]])
            m.stt("vector", A_[:], B_[:], sel[:, 1:2], A_[:], ALU.mult, ALU.add, reads=[Bogbs[0], Bogbs[1], Bsel], writes=[Bogbs[0]])
            return A_, Bogbs[0]
        return og_load

    def load_sel(m, msel):
        sel = m.sbuf("sel", [128, 2], F32); Bsel = m.buf("sel")
        m.dma("sync", sel[:], sel_d, writes=[Bsel])
        msel["sel"] = (sel, Bsel)

    m = MK(nc, sem_es=ses, tag="p1_")
    xc = Xchg(m, og_src, og_gat, OGC // 128)

    def og_store1(m_, ot, ps, ch, Bot):
        j, o = divmod(ch * 128, OGC)
        bo = m_.buf("out")
        m_.dma("sync", og_sv[j][:, :, o:o + 128], ot[:], reads=[Bot], writes=[bo])
        xc.stored(j, bo)
        return bo
    A1["og_store"] = og_store1
    emit_gdn(m, A1, NT, 1)
    m.build()
    m = MK(nc, sem_es=ses, tag="p2_")
    ms = {}; load_sel(m, ms)
    xc2 = Xchg(m, h1_src, h1_gat, H1C // 256)

    def h_store2(m_, h_t, t0, T, Bh):
        j, o = divmod(t0, H1C)
        bo = m_.buf("out")
        m_.dma("gpsimd", h1_sv[j][:, :, o:o + T], h_t[:], reads=[Bh], writes=[bo])
        xc2.stored(j, bo)
        return bo
    A2["h_extra"] = {"wada": A3["wada"], "bada": A3["bada"], "gam": A3["gam"], "store": h_store2}
    A2["og_load"] = make_og_load(og_gv, 4, OGC, ms)

    def out_store2(m_, x_t, t0, T, Bx):
        j, o = divmod(t0, X1C)
        bo = m_.buf("out")
        m_.dma("sync", x1_sv[j][:, :, o:o + T], x_t[:], reads=[Bx], writes=[bo])
        return bo
    A2["out_store"] = out_store2
    emit_ffn(m, A2, HT, 1024)
    m.build()
    m = MK(nc, sem_es=ses, tag="p3_")
    xc3 = Xchg(m, o2_src, o2_gat, 2 * 2 * (O2C // 2048))

    def h_load3(m_, hU, u, BhU):
        for q in range(2048 // H1C):
            t0 = u * 2048 + q * H1C
            r, tl = divmod(t0, HT)
            j = tl // H1C
            m_.dma("sync" if q % 2 else "gpsimd", hU[:, :, q * H1C:(q + 1) * H1C], h1_gv[j][r], writes=[BhU])
    A3["h_load"] = h_load3

    def og_store3(m_, ob, pp, hl, u, Bob):
        j, o = divmod(u * 2048, O2C)
        bo = m_.buf("out")
        m_.dma("sync", o2_sv[j][hl * 64:(hl + 1) * 64, pp, o:o + 2048], ob[:], reads=[Bob], writes=[bo])
        xc3.stored(j, bo)
        return bo
    A3["og_store"] = og_store3
    emit_attn(m, A3, NT, 2)
    m.build()
    m = MK(nc, sem_es=ses, tag="p4_")
    ms = {}; load_sel(m, ms)
    A4["og_load"] = make_og_load(o2_gv, 2, O2C, ms)

    def x_load4(m_, x_t, t0, T, Bx):
        j, o = divmod(t0, X1C)
        m_.dma("sync", x_t[:], x1_sv[j][:, :, o:o + T], writes=[Bx])
    A4["x_load"] = x_load4
    emit_ffn(m, A4, HT, 512)
    m.build()
    ses.close()
    return nc


_PROGS = {}


def _prog(key, fn):
    if key not in _PROGS:
        _PROGS[key] = fn()
    return _PROGS[key]


def _f32(a):
    return np.ascontiguousarray(np.asarray(a, dtype=np.float32))


def kernel(x, c, w_ada, b_ada, norm_mix, norm_ffn, w_ffn_in, w_ffn_out,
           gdn_w_in, gdn_conv, gdn_a_log, gdn_dt_bias, gdn_out_norm, gdn_w_out,
           dsw_w_in, dsw_q_norm, dsw_k_norm, dsw_w_out, rel_bias):
    (x, c, w_ada, b_ada, norm_mix, norm_ffn, w_ffn_in, w_ffn_out, gdn_w_in, gdn_conv, gdn_a_log, gdn_dt_bias,
     gdn_out_norm, gdn_w_out, dsw_w_in, dsw_q_norm, dsw_k_norm, dsw_w_out, rel_bias) = map(_f32, (
        x, c, w_ada, b_ada, norm_mix, norm_ffn, w_ffn_in, w_ffn_out, gdn_w_in, gdn_conv, gdn_a_log, gdn_dt_bias,
        gdn_out_norm, gdn_w_out, dsw_w_in, dsw_q_norm, dsw_k_norm, dsw_w_out, rel_bias))
    nc = _prog("fused8", build_fused8)
    maps = []
    for b in range(BATCH):
        xTb = lay_xT(x[b])
        for hh in range(2):
            g = gdn_inputs([hh], None, c[b], w_ada[0], b_ada[0], norm_mix[0], gdn_w_in[0], gdn_conv[0], gdn_a_log[0],
                           gdn_dt_bias[0], gdn_out_norm[0], xT=0)
            a = attn_inputs([2 * hh, 2 * hh + 1], None, c[b], w_ada[1], b_ada[1], norm_mix[1], dsw_w_in[0], dsw_q_norm[0],
                            dsw_k_norm[0], rel_bias)
            d_ = {}
            for k in ("wada", "bada", "gam", "w_qkvz", "w_ab", "convw", "alog", "dtb", "onorm", "consts"):
                d_["g_" + k] = g[k]
            for k in ("wada", "bada", "gam", "w_qkv", "gains", "biasT", "consts"):
                d_["a_" + k] = a[k]
            for L, w_o in ((0, gdn_w_out[0]), (1, dsw_w_out[0])):
                p = f"f{L}_"
                d_[p + "w_o"] = w_o
                d_[p + "wada"] = lay_wada(w_ada[L], 2048, 6144)
                d_[p + "bada"] = lay_vec(b_ada[L][2048:])
                d_[p + "gam"] = lay_vec(norm_ffn[L])
                d_[p + "w1"] = w_ffn_in[L]
                d_[p + "w2"] = w_ffn_out[L]
            d_["xT"] = xTb
            d_["xTh"] = np.ascontiguousarray(xTb[:, :, hh * (SEQ // 2):(hh + 1) * (SEQ // 2)])
            d_["cT"] = lay_vec(c[b])
            sel = np.zeros((128, 2), np.float32); sel[:, hh] = 1.0
            d_["sel"] = sel
            maps.append(d_)
    res = run_bass_kernel_spmd(nc, maps, core_ids=list(range(8))).results
    out = np.empty((BATCH, SEQ, D), np.float32)
    for b in range(BATCH):
        for hh in range(2):
            out[b, hh * (SEQ // 2):(hh + 1) * (SEQ // 2)] = unlay_xT(np.asarray(res[b * 2 + hh]["outT"]))
    return out


def kernel_4core(x, c, w_ada, b_ada, norm_mix, norm_ffn, w_ffn_in, w_ffn_out,
           gdn_w_in, gdn_conv, gdn_a_log, gdn_dt_bias, gdn_out_norm, gdn_w_out,
           dsw_w_in, dsw_q_norm, dsw_k_norm, dsw_w_out, rel_bias):
    (x, c, w_ada, b_ada, norm_mix, norm_ffn, w_ffn_in, w_ffn_out, gdn_w_in, gdn_conv, gdn_a_log, gdn_dt_bias,
     gdn_out_norm, gdn_w_out, dsw_w_in, dsw_q_norm, dsw_k_norm, dsw_w_out, rel_bias) = map(_f32, (
        x, c, w_ada, b_ada, norm_mix, norm_ffn, w_ffn_in, w_ffn_out, gdn_w_in, gdn_conv, gdn_a_log, gdn_dt_bias,
        gdn_out_norm, gdn_w_out, dsw_w_in, dsw_q_norm, dsw_k_norm, dsw_w_out, rel_bias))
    nc = _prog("fused", build_fused)
    g = gdn_inputs([0, 1], None, c[0], w_ada[0], b_ada[0], norm_mix[0], gdn_w_in[0], gdn_conv[0], gdn_a_log[0],
                   gdn_dt_bias[0], gdn_out_norm[0], xT=0)
    a = attn_inputs([0, 1, 2, 3], None, c[0], w_ada[1], b_ada[1], norm_mix[1], dsw_w_in[0], dsw_q_norm[0], dsw_k_norm[0], rel_bias)
    shared = {}
    for k in ("wada", "bada", "gam", "w_qkvz", "w_ab", "convw", "alog", "dtb", "onorm", "consts"):
        shared["g_" + k] = g[k]
    for k in ("wada", "bada", "gam", "w_qkv", "gains", "biasT", "consts"):
        shared["a_" + k] = a[k]
    for L, w_o in ((0, gdn_w_out[0]), (1, dsw_w_out[0])):
        p = f"f{L}_"
        shared[p + "w_o"] = w_o
        shared[p + "wada"] = lay_wada(w_ada[L], 2048, 6144)
        shared[p + "bada"] = lay_vec(b_ada[L][2048:])
        shared[p + "gam"] = lay_vec(norm_ffn[L])
        shared[p + "w1"] = w_ffn_in[L]
        shared[p + "w2"] = w_ffn_out[L]
    maps = []
    for b in range(BATCH):
        d_ = dict(shared)
        d_["xT"] = lay_xT(x[b])
        d_["cT"] = lay_vec(c[b])
        maps.append(d_)
    res = run_bass_kernel_spmd(nc, maps, core_ids=list(range(BATCH))).results
    out = np.empty((BATCH, SEQ, D), np.float32)
    for b in range(BATCH):
        out[b] = unlay_xT(np.asarray(res[b]["outT"]))
    return out
```

```python
import numpy as np
import ml_dtypes
from contextlib import ExitStack
import concourse.bass as bass
import concourse.mybir as mybir
from concourse.bass_utils import run_bass_kernel_spmd

F32 = mybir.dt.float32
BF16 = mybir.dt.bfloat16
AF = mybir.ActivationFunctionType
ALU = mybir.AluOpType
AX = mybir.AxisListType
NPBF = ml_dtypes.bfloat16

D = 1024
SEQ = 8192
BATCH = 4
FH = 2816
RMS_EPS = 1e-6


class Buf:
    __slots__ = ("name", "last_w", "readers", "excl")

    def __init__(self, name, excl=False):
        self.name = name
        self.last_w = None
        self.readers = []
        self.excl = excl


class MK:
    ENGS = ("tensor", "vector", "scalar", "gpsimd", "sync")

    def __init__(self, nc, n_dma_sems=8, sem_es=None, tag=""):
        self.nc = nc
        self.es = ExitStack()
        self.tag = tag
        self.sem_es = sem_es
        ses = sem_es if sem_es is not None else self.es
        self.ops = {e: [] for e in self.ENGS}
        self.cnt = {e: 0 for e in self.ENGS}
        self.sem = {}
        for e in ("tensor", "vector", "scalar", "gpsimd"):
            self.sem[e] = ses.enter_context(nc.semaphore(tag + "s_" + e))
        self.dma_sems = {}
        self.dma_ring = {}
        for q in ("sync", "gpsimd"):
            self.dma_sems[q] = [ses.enter_context(nc.semaphore(f"{tag}d_{q}{i}")) for i in range(n_dma_sems)]
            self.dma_ring[q] = [0, [0] * n_dma_sems]
        self.known = {e: {} for e in self.ENGS}
        self._rr = 0

    def sbuf(self, name, shape, dt):
        return self.es.enter_context(self.nc.sbuf_tensor(self.tag + "sb_" + name, list(shape), dt))

    def psum(self, name, shape, dt):
        return self.es.enter_context(self.nc.psum_tensor(self.tag + "pp_" + name, list(shape), dt))

    def buf(self, name, excl=False):
        return Buf(name, excl)

    def bufs(self, name, n, excl=False):
        return [Buf(f"{name}{i}", excl) for i in range(n)]

    @staticmethod
    def _split(reads, writes):
        ex = [b for b in reads if b.excl]
        if ex:
            reads = [b for b in reads if not b.excl]
            writes = list(writes) + ex
        return reads, writes

    def _deps(self, reads, writes):
        deps = {}

        def add(d):
            if d is None:
                return
            k, v = d
            if deps.get(k, -1) < v:
                deps[k] = v
        for b in reads:
            add(b.last_w)
        for b in writes:
            add(b.last_w)
            for r in b.readers:
                add(r)
        return deps

    def _waits_for(self, eng, deps):
        waits = []
        kn = self.known[eng]
        for k, v in deps.items():
            if kn.get(k, 0) >= v:
                continue
            if k == ("e", eng) and v > self.cnt[eng]:
                continue
            kn[k] = v
            waits.append((k, v))
        return waits

    def op(self, eng, fn, reads=(), writes=(), inc=True):
        reads, writes = self._split(reads, writes)
        deps = self._deps(reads, writes)
        waits = self._waits_for(eng, deps)
        key = ("e", eng)
        if inc:
            self.cnt[eng] += 1
            c = self.cnt[eng]
            self.ops[eng].append((waits, fn, (key, 1)))
        else:
            c = self.cnt[eng] + 1
            self.ops[eng].append((waits, fn, None))
        tag = (key, c)
        for b in reads:
            b.readers.append(tag)
        for b in writes:
            b.last_w = tag
            b.readers = []
        return tag

    def dma(self, q, out, in_, reads=(), writes=()):
        reads, writes = self._split(reads, writes)
        deps = self._deps(reads, writes)
        ring = self.dma_ring[q]
        i = ring[0]
        ring[0] = (i + 1) % len(self.dma_sems[q])
        n_prev = ring[1][i]
        key = ("d", q, i)
        if n_prev > 0 and deps.get(key, -1) < 16 * n_prev:
            deps[key] = 16 * n_prev
        waits = self._waits_for(q, deps)
        ring[1][i] = n_prev + 1
        self.ops[q].append((waits, (lambda e, o=out, s=in_: e.dma_start(out=o, in_=s)), (key, 16)))
        tag = (key, 16 * (n_prev + 1))
        for b in reads:
            b.readers.append(tag)
        for b in writes:
            b.last_w = tag
            b.readers = []
        return tag

    def cc_allgather(self, src, dst, groups, reads=(), writes=()):
        if not hasattr(self, "cc_sem"):
            ses = self.sem_es if self.sem_es is not None else self.es
            self.cc_sem = ses.enter_context(self.nc.semaphore(self.tag + "cc_sem"))
            self.cc_n = 0
        reads, writes = self._split(reads, writes)
        deps = self._deps(reads, writes)
        waits = self._waits_for("gpsimd", deps)
        self.cc_n += 1
        key = ("c",)
        self.ops["gpsimd"].append((waits, (lambda e: e.collective_compute("AllGather", ALU.bypass, replica_groups=groups,
                                                                        ins=[src.opt()], outs=[dst.opt()])), (key, 1)))
        tag = (key, self.cc_n)
        for b in reads:
            b.readers.append(tag)
        for b in writes:
            b.last_w = tag
            b.readers = []
        return tag

    def _semof(self, key):
        if key[0] == "c":
            return self.cc_sem
        if key[0] == "e":
            return self.sem[key[1]]
        return self.dma_sems[key[1]][key[2]]

    def final_wait(self, eng, bufs):
        deps = self._deps(bufs, ())
        waits = self._waits_for(eng, deps)
        self.ops[eng].append((waits, None, None))

    def barrier(self):
        deps = {}
        for e in ("tensor", "vector", "scalar", "gpsimd"):
            if self.cnt[e] > 0:
                deps[("e", e)] = self.cnt[e]
        for q, (nxt, counts) in self.dma_ring.items():
            for i, n in enumerate(counts):
                if n > 0:
                    deps[("d", q, i)] = 16 * n
        if getattr(self, "cc_n", 0) > 0:
            deps[("c",)] = self.cc_n
        for e in self.ENGS:
            waits = self._waits_for(e, dict(deps))
            waits = [(k, v) for (k, v) in waits]
            self.ops[e].append((waits, None, None))

    def build(self, barrier=True):
        nc = self.nc
        if barrier:
            self.barrier()
        with nc.Block() as block:
            def mk(ename):
                def body(e):
                    for waits, fn, inc in self.ops[ename]:
                        for k, v in waits:
                            e.wait_ge(self._semof(k), v)
                        if fn is not None:
                            ins = fn(e)
                            if inc is not None:
                                ins.then_inc(self._semof(inc[0]), inc[1])
                return body
            block.tensor(mk("tensor"))
            block.vector(mk("vector"))
            block.scalar(mk("scalar"))
            block.gpsimd(mk("gpsimd"))
            block.sync(mk("sync"))
        self.es.close()

    def mm(self, out, lhsT, rhs, start, stop, reads, writes, inc=None):
        if inc is None:
            inc = stop
        return self.op("tensor", lambda e: e.matmul(out, lhsT=lhsT, rhs=rhs, start=start, stop=stop),
                       reads, writes, inc=inc)

    def tr(self, out, in_, ident, reads, writes):
        return self.op("tensor", lambda e: e.transpose(out, in_, ident), reads, writes)

    def act(self, out, in_, func, reads, writes, bias=0.0, scale=1.0, accum_out=None):
        if accum_out is None:
            return self.op("scalar", lambda e: e.activation(out=out, in_=in_, func=func, bias=bias, scale=scale),
                           reads, writes)
        return self.op("scalar", lambda e: e.activation(out=out, in_=in_, func=func, bias=bias, scale=scale,
                                                        accum_out=accum_out), reads, writes)

    def tt(self, eng, out, in0, in1, op, reads, writes):
        return self.op(eng, lambda e: e.tensor_tensor(out=out, in0=in0, in1=in1, op=op), reads, writes)

    def ts(self, eng, out, in0, s1, s2, op0, op1, reads, writes):
        if s2 is None:
            return self.op(eng, lambda e: e.tensor_scalar(out=out, in0=in0, scalar1=s1, scalar2=None, op0=op0),
                           reads, writes)
        return self.op(eng, lambda e: e.tensor_scalar(out=out, in0=in0, scalar1=s1, scalar2=s2, op0=op0, op1=op1),
                       reads, writes)

    def stt(self, eng, out, in0, scalar, in1, op0, op1, reads, writes):
        return self.op(eng, lambda e: e.scalar_tensor_tensor(out=out, in0=in0, scalar=scalar, in1=in1,
                                                             op0=op0, op1=op1), reads, writes)

    def copy(self, eng, out, in_, reads, writes):
        if eng == "scalar":
            return self.op(eng, lambda e: e.copy(out=out, in_=in_), reads, writes)
        return self.op(eng, lambda e: e.tensor_copy(out=out, in_=in_), reads, writes)

    def memset(self, eng, ap, val, writes):
        return self.op(eng, lambda e: e.memset(ap, val), (), writes)

    def recip(self, out, in_, reads, writes):
        return self.op("vector", lambda e: e.reciprocal(out=out, in_=in_), reads, writes)

    def rr(self, engs=("vector", "scalar", "vector", "scalar", "vector", "gpsimd", "scalar", "vector")):
        self._rr += 1
        return engs[self._rr % len(engs)]


class Stage:
    def __init__(self, m, width=1024, n=3):
        self.m = m
        self.width = width
        self.t = [m.sbuf(f"stg{i}", [128, width], F32) for i in range(n)]
        self.b = m.bufs("stg", n)
        self.i = 0

    def add(self, view, buf):
        self.t.append(view)
        self.b.append(buf)

    def load_cast(self, dst_ap_fn, src_rows_ap, ncols, wbuf, queue="sync"):
        m = self.m
        for c0 in range(0, ncols, self.width):
            c1 = min(ncols, c0 + self.width)
            i = self.i
            self.i = (self.i + 1) % len(self.t)
            m.dma(queue, self.t[i][:, 0:c1 - c0], src_rows_ap[:, c0:c1], writes=[self.b[i]])
            eng = m.rr()
            m.copy(eng, dst_ap_fn(c0, c1), self.t[i][:, 0:c1 - c0], reads=[self.b[i]], writes=[wbuf])


def emit_mod(m, nchunk, wada_d, bada_d, cT_d, ps_mod, Bps, stg, sfx=""):
    cond = m.sbuf("cond" + sfx, [128, 8], F32)
    Bcond = m.buf("cond")
    m.dma("sync", cond[:], cT_d, writes=[Bcond])
    m.act(cond[:], cond[:], AF.Silu, reads=[Bcond], writes=[Bcond])
    bada = m.sbuf("bada" + sfx, [128, nchunk], F32)
    Bbada = m.buf("bada")
    m.dma("sync", bada[:], bada_d, writes=[Bbada])
    modsb = m.sbuf("modsb" + sfx, [128, nchunk], F32)
    Bmod = m.buf("mod")
    ns = len(stg.t)
    for j in range(nchunk):
        i = j % ns
        wv = stg.t[i][:, 0:1024].rearrange("p (k c) -> p k c", k=8)
        m.dma("gpsimd" if j % 2 else "sync", wv, wada_d[j], writes=[stg.b[i]])
        for kc in range(8):
            m.mm(ps_mod[:, j:j + 1], wv[:, kc, :], cond[:, kc:kc + 1], kc == 0, kc == 7,
                 reads=[stg.b[i], Bcond], writes=[Bps])
    m.tt("vector", modsb[:], ps_mod[:, 0:nchunk], bada[:], ALU.add, reads=[Bps, Bbada], writes=[Bmod])
    return modsb, Bmod


def emit_norm_mod(m, x_t, Bx, T, A, Bc, BAB, ones_bf, Bconst, sq, Bsq, ps_ss, Bps_ss, rs, Brs, tmp, Btmp, h, Bh):
    for kc in range(8):
        eng = "gpsimd" if kc % 2 else "scalar"
        if eng == "scalar":
            m.act(sq[:, kc, 0:T], x_t[:, kc, 0:T], AF.Square, reads=[Bx], writes=[Bsq])
        else:
            m.tt("gpsimd", sq[:, kc, 0:T], x_t[:, kc, 0:T], x_t[:, kc, 0:T], ALU.mult, reads=[Bx], writes=[Bsq])
    for kc in range(8):
        m.mm(ps_ss[:, 0:T], ones_bf[:], sq[:, kc, 0:T], kc == 0, kc == 7, reads=[Bsq, Bconst], writes=[Bps_ss])
    m.act(rs[:, 0:T], ps_ss[:, 0:T], AF.Ln, reads=[Bps_ss], writes=[Brs], bias=RMS_EPS, scale=1.0 / D)
    m.act(rs[:, 0:T], rs[:, 0:T], AF.Exp, reads=[Brs], writes=[Brs], scale=-0.5)
    for kc in range(8):
        i = kc % 2
        m.tt("vector" if kc % 2 else "gpsimd", tmp[i][:, 0:T], x_t[:, kc, 0:T], rs[:, 0:T], ALU.mult,
             reads=[Bx, Brs], writes=[Btmp[i]])
        m.act(h[:, kc, 0:T], tmp[i][:, 0:T], AF.Identity, reads=[Btmp[i], BAB], writes=[Bh],
              bias=Bc[:, kc:kc + 1], scale=A[:, kc:kc + 1])


def ffn_decl(nc, NT, KO, pre=""):
    KC = KO // 128
    A = {}
    A["xT"] = nc.dram_tensor(pre + "xT", [128, 8, NT], F32, kind="ExternalInput").ap()
    A["ogT"] = nc.dram_tensor(pre + "ogT", [128, KC, NT], BF16, kind="ExternalInput").ap()
    A["w_o"] = nc.dram_tensor(pre + "w_o", [KO, D], F32, kind="ExternalInput").ap()
    A["wada"] = nc.dram_tensor(pre + "wada", [32, 128, 8, 128], F32, kind="ExternalInput").ap()
    A["bada"] = nc.dram_tensor(pre + "bada", [128, 32], F32, kind="ExternalInput").ap()
    A["cT"] = nc.dram_tensor(pre + "cT", [128, 8], F32, kind="ExternalInput").ap()
    A["gam"] = nc.dram_tensor(pre + "gam", [128, 8], F32, kind="ExternalInput").ap()
    A["w1"] = nc.dram_tensor(pre + "w1", [D, 2 * FH], F32, kind="ExternalInput").ap()
    A["w2"] = nc.dram_tensor(pre + "w2", [FH, D], F32, kind="ExternalInput").ap()
    A["outT"] = nc.dram_tensor(pre + "outT", [128, 8, NT], F32, kind="ExternalOutput").ap()
    return A


def build_ffn(NT, KO, T=256):
    nc = bass.Bass("TRN2", target_bir_lowering=False)
    A = ffn_decl(nc, NT, KO)
    m = MK(nc)
    outs = emit_ffn(m, A, NT, KO, T)
    m.final_wait("sync", outs)
    m.build(barrier=False)
    return nc


def emit_ffn(m, A, NT, KO, T=256):
    KC = KO // 128
    HC = FH // 128
    xT_d, og_d, wo_d, wada_d, bada_d, cT_d, gam_d, w1_d, w2_d, out_d = [A.get(k) for k in
        ("xT", "ogT", "w_o", "wada", "bada", "cT", "gam", "w1", "w2", "outT")]

    ones_bf = m.sbuf("ones_bf", [128, 128], BF16)
    Bconst = m.buf("const")
    m.memset("vector", ones_bf[:], 1.0, writes=[Bconst])

    ps_mod = m.psum("ps_mod", [128, 512], F32); Bps_mod = m.buf("ps_mod", excl=True)
    ps_ss = m.psum("ps_ss", [128, 512], F32); Bps_ss = m.buf("ps_ss", excl=True)
    ps_a = [m.psum(f"ps_a{i}", [128, 512], F32) for i in range(3)]; Bps_a = m.bufs("ps_a", 3, excl=True)
    ps_b = [m.psum(f"ps_b{i}", [128, 512], F32) for i in range(2)]; Bps_b = m.bufs("ps_b", 2, excl=True)

    stg = Stage(m, width=1024, n=2)
    xt = [m.sbuf(f"xt{i}", [128, 8, T], F32) for i in range(2)]; Bxt = m.bufs("xt", 2)
    ogbs = [m.sbuf(f"ogb{i}", [128, KC, T], BF16) for i in range(2)]; Bogbs = m.bufs("ogb", 2)
    sq = m.sbuf("sq", [128, 8, T], BF16); Bsq = m.buf("sq")
    rs = m.sbuf("rs", [128, T], F32); Brs = m.buf("rs")
    tmp = [m.sbuf(f"tmp{i}", [128, T], F32) for i in range(2)]; Btmp = m.bufs("tmp", 2)
    h2 = m.sbuf("h2", [128, 8, T], BF16); Bh2 = m.buf("h2")
    actb = m.sbuf("actb", [128, HC, T], BF16); Bact = m.buf("act")
    sg = tmp; Bsg = Btmp
    if T * 8 >= 2048:
        for i in range(2):
            stg.add(xt[i][:].rearrange("p k t -> p (k t)")[:, 0:1024], Bxt[i])
        lend = [(h2, Bh2), (sq, Bsq), (actb, Bact)] + ([(ogbs[0], Bogbs[0]), (ogbs[1], Bogbs[1])] if KC == 8 else [])
        for t_, b_ in lend:
            v = t_[:].rearrange("p k t -> p (k t)")[:, 0:2048].bitcast(F32)
            stg.add(v, b_)
    modsb, Bmod = emit_mod(m, 32, wada_d, bada_d, cT_d, ps_mod, Bps_mod, stg)
    gam = m.sbuf("gam", [128, 8], F32); Bgam = m.buf("gam")
    m.dma("sync", gam[:], gam_d, writes=[Bgam])
    A2 = m.sbuf("A2", [128, 8], F32)
    BAB = m.buf("AB")
    m.stt("vector", A2[:], modsb[:, 16:24], 1.0, gam[:], ALU.add, ALU.mult, reads=[Bmod, Bgam], writes=[BAB])

    HX = A.get("h_extra")
    if HX is not None:
        modx, Bmodx = emit_mod(m, 16, HX["wada"], HX["bada"], cT_d, ps_mod, Bps_mod, stg, sfx="x")
        gamx = m.sbuf("gamx", [128, 8], F32); Bgamx = m.buf("gamx")
        m.dma("sync", gamx[:], HX["gam"], writes=[Bgamx])
        A1x = m.sbuf("A1x", [128, 8], F32); BABx = m.buf("ABx")
        m.stt("vector", A1x[:], modx[:, 8:16], 1.0, gamx[:], ALU.add, ALU.mult, reads=[Bmodx, Bgamx], writes=[BABx])
    wo = m.sbuf("wo", [128, KC, D], BF16); Bwo = m.buf("wo")
    w1 = m.sbuf("w1", [128, 8, 2 * FH], BF16); Bw1 = m.buf("w1")
    w2 = m.sbuf("w2", [128, HC, D], BF16); Bw2 = m.buf("w2")
    for kc in range(KC):
        stg.load_cast(lambda c0, c1, kc=kc: wo[:, kc, c0:c1], wo_d[kc * 128:(kc + 1) * 128, :], D, Bwo,
                      queue="sync" if kc % 2 else "gpsimd")
    for kc in range(8):
        stg.load_cast(lambda c0, c1, kc=kc: w1[:, kc, c0:c1], w1_d[kc * 128:(kc + 1) * 128, :], 2 * FH, Bw1,
                      queue="sync" if kc % 2 else "gpsimd")
    for hc in range(HC):
        stg.load_cast(lambda c0, c1, hc=hc: w2[:, hc, c0:c1], w2_d[hc * 128:(hc + 1) * 128, :], D, Bw2,
                      queue="sync" if hc % 2 else "gpsimd")

    Bouts = []

    ntile = NT // T
    pa = 0
    for it in range(ntile):
        t0 = it * T
        x_t = xt[it % 2]; Bx = Bxt[it % 2]
        if "x_load" in A:
            A["x_load"](m, x_t, t0, T, Bx)
        else:
            m.dma("sync", x_t[:], xT_d[:, :, t0:t0 + T], writes=[Bx])
        if "og_load" in A:
            ogb, Bogb = A["og_load"](m, ogbs, Bogbs, t0, T)
        else:
            ogb = ogbs[it % 2]; Bogb = Bogbs[it % 2]
            m.dma("gpsimd", ogb[:], og_d[:, :, t0:t0 + T], writes=[Bogb])
        for dc in range(8):
            p = ps_a[pa % 3]; Bp = Bps_a[pa % 3]; pa += 1
            for kc in range(KC):
                m.mm(p[:, 0:T], wo[:, kc, dc * 128:(dc + 1) * 128], ogb[:, kc, :], kc == 0, kc == KC - 1,
                     reads=[Bwo, Bogb], writes=[Bp])
            m.stt("vector", x_t[:, dc, :], p[:, 0:T], modsb[:, dc:dc + 1], x_t[:, dc, :], ALU.mult, ALU.add,
                  reads=[Bp, Bmod, Bx], writes=[Bx])
        emit_norm_mod(m, x_t, Bx, T, A2, modsb[:, 8:16], BAB, ones_bf, Bconst, sq, Bsq, ps_ss, Bps_ss,
                      rs, Brs, tmp, Btmp, h2, Bh2)
        for hc in range(HC):
            pg = ps_a[pa % 3]; Bpg = Bps_a[pa % 3]; pa += 1
            pu = ps_b[hc % 2]; Bpu = Bps_b[hc % 2]
            for kc in range(8):
                m.mm(pg[:, 0:T], w1[:, kc, hc * 128:(hc + 1) * 128], h2[:, kc, :], kc == 0, kc == 7,
                     reads=[Bw1, Bh2], writes=[Bpg])
            for kc in range(8):
                m.mm(pu[:, 0:T], w1[:, kc, FH + hc * 128:FH + (hc + 1) * 128], h2[:, kc, :], kc == 0, kc == 7,
                     reads=[Bw1, Bh2], writes=[Bpu])
            s = sg[hc % 2]; Bs = Bsg[hc % 2]
            m.act(s[:], pg[:, 0:T], AF.Silu, reads=[Bpg], writes=[Bs])
            m.tt("vector", actb[:, hc, :], pu[:, 0:T], s[:], ALU.mult, reads=[Bpu, Bs], writes=[Bact])
        for dc in range(8):
            p = ps_a[pa % 3]; Bp = Bps_a[pa % 3]; pa += 1
            for hc in range(HC):
                m.mm(p[:, 0:T], w2[:, hc, dc * 128:(dc + 1) * 128], actb[:, hc, :], hc == 0, hc == HC - 1,
                     reads=[Bw2, Bact], writes=[Bp])
            m.stt("vector", x_t[:, dc, :], p[:, 0:T], modsb[:, 24 + dc:25 + dc], x_t[:, dc, :], ALU.mult, ALU.add,
                  reads=[Bp, Bmod, Bx], writes=[Bx])
        if HX is not None:
            emit_norm_mod(m, x_t, Bx, T, A1x, modx[:, 0:8], BABx, ones_bf, Bconst, sq, Bsq, ps_ss, Bps_ss,
                          rs, Brs, tmp, Btmp, h2, Bh2)
            Bouts.append(HX["store"](m, h2, t0, T, Bh2))
        if "out_store" in A:
            Bouts.append(A["out_store"](m, x_t, t0, T, Bx))
        else:
            Bouts.append(m.buf("out"))
            m.dma("sync", out_d[:, :, t0:t0 + T], x_t[:], reads=[Bx], writes=[Bouts[-1]])
    return Bouts


def lay_xT(xb):
    F = xb.shape[1]
    return np.ascontiguousarray(xb.T.reshape(F // 128, 128, -1).transpose(1, 0, 2))


def unlay_xT(a):
    return np.ascontiguousarray(a.transpose(1, 0, 2).reshape(a.shape[1] * 128, -1).T)


def lay_wada(w, c0, c1):
    sel = w[:, c0:c1]
    n = sel.shape[1] // 128
    return np.ascontiguousarray(sel.reshape(8, 128, n, 128).transpose(2, 1, 0, 3))


def lay_vec(v):
    return np.ascontiguousarray(v.reshape(-1, 128).T)


_DBG = {}


def interleave(f, b, k=None):
    k = k or _DBG.get("ilk", 4)
    fa, ba = f is not None, b is not None
    while fa or ba:
        for _ in range(k):
            if fa:
                try:
                    next(f)
                except StopIteration:
                    fa = False
        if ba:
            try:
                next(b)
            except StopIteration:
                ba = False


def gdn_consts():
    C = 128
    U = np.triu(np.ones((C, C), np.float32))
    SLm = np.tril(np.ones((C, C), np.float32), -1)
    mui = np.triu(np.ones((C, C), np.float32))
    mus = np.triu(np.ones((C, C), np.float32), 1)
    I = np.eye(C, dtype=np.float32)
    O = np.ones((C, C), np.float32)
    i = np.arange(C)[:, None]; j = np.arange(C)[None, :]
    BD16 = (i // 16 == j // 16).astype(np.float32)
    Ms = [((i // s == j // s) & ((i % s) >= s // 2) & ((j % s) < s // 2)).astype(np.float32) for s in (32, 64, 128)]
    MTs = [np.ascontiguousarray(M_.T) for M_ in Ms]
    return np.ascontiguousarray(np.stack([U, SLm, mui, mus, I, O, BD16] + Ms + MTs, axis=1))


def gdn_decl(nc, NT, NP, pre="", og_kind="ExternalOutput"):
    A = {}
    A["xT"] = nc.dram_tensor(pre + "xT", [128, 8, NT], F32, kind="ExternalInput").ap()
    A["wada"] = nc.dram_tensor(pre + "wada", [16, 128, 8, 128], F32, kind="ExternalInput").ap()
    A["bada"] = nc.dram_tensor(pre + "bada", [128, 16], F32, kind="ExternalInput").ap()
    A["cT"] = nc.dram_tensor(pre + "cT", [128, 8], F32, kind="ExternalInput").ap()
    A["gam"] = nc.dram_tensor(pre + "gam", [128, 8], F32, kind="ExternalInput").ap()
    A["w_qkvz"] = nc.dram_tensor(pre + "w_qkvz", [NP, D, 2048], F32, kind="ExternalInput").ap()
    A["w_ab"] = nc.dram_tensor(pre + "w_ab", [NP, D, 8], F32, kind="ExternalInput").ap()
    A["convw"] = nc.dram_tensor(pre + "convw", [NP, 128, 12, 4], F32, kind="ExternalInput").ap()
    A["alog"] = nc.dram_tensor(pre + "alog", [NP, 128, 4], F32, kind="ExternalInput").ap()
    A["dtb"] = nc.dram_tensor(pre + "dtb", [NP, 128, 4], F32, kind="ExternalInput").ap()
    A["onorm"] = nc.dram_tensor(pre + "onorm", [128, 128], F32, kind="ExternalInput").ap()
    A["consts"] = nc.dram_tensor(pre + "consts", [128, 13, 128], F32, kind="ExternalInput").ap()
    A["ogT"] = nc.dram_tensor(pre + "ogT", [128, NP * 4, NT], BF16, kind=og_kind).ap()
    return A


def build_gdn(NT, NP=1, T=512):
    nc = bass.Bass("TRN2", target_bir_lowering=False)
    A = gdn_decl(nc, NT, NP)
    m = MK(nc)
    outs = emit_gdn(m, A, NT, NP, T)
    m.final_wait("sync", outs)
    m.build(barrier=False)
    return nc


def emit_gdn(m, A, NT, NP, T=512):
    E2DT = BF16 if _DBG.get("e2_bf16") else F32
    NCH = NT // 128
    xT_d, wada_d, bada_d, cT_d, gam_d, w_d, wab_d, convw_d, alog_d, dtb_d, onorm_d, const_d, og_d = [A.get(k) for k in
        ("xT", "wada", "bada", "cT", "gam", "w_qkvz", "w_ab", "convw", "alog", "dtb", "onorm", "consts", "ogT")]

    cst = m.sbuf("cst", [128, 13, 128], F32); Bconst = m.buf("const")
    m.dma("sync", cst[:], const_d, writes=[Bconst])
    Um, SLm, MUI, MUS, IDf, ONEf, BD16 = [cst[:, i, :] for i in range(7)]
    MN = [cst[:, 7 + i, :] for i in range(3)]
    MT = [cst[:, 10 + i, :] for i in range(3)]
    ones_bf = m.sbuf("ones_bf", [128, 128], BF16)
    id_bf = m.sbuf("id_bf", [128, 128], BF16)
    m.memset("vector", ones_bf[:], 1.0, writes=[Bconst])
    m.copy("vector", id_bf[:], IDf, reads=[Bconst], writes=[Bconst])
    convw = m.sbuf("convw", [128, 12, 4], F32)
    negA = m.sbuf("negA", [128, 4], F32)
    dtb = m.sbuf("dtb", [128, 4], F32)
    Bpw = m.buf("passw")
    onorm = m.sbuf("onorm", [128, 128], F32)
    m.dma("sync", onorm[:], onorm_d, writes=[Bconst])

    ps_big = [m.psum(f"big{i}", [128, 512], F32) for i in range(2)]; Bbig = m.bufs("big", 2, excl=True)
    ps_q4 = [m.psum(f"q4_{i}", [128, 4, 128], F32) for i in range(4)]; Bq4 = m.bufs("q4", 4, excl=True)
    qslots = [(ps_q4[i][:, j, :], Bq4[i]) for j in range(4) for i in range(4)]
    ps_tb = [m.psum(f"tb{i}", [128, 8, 128], BF16) for i in range(2)]; Btb = m.bufs("tb", 2, excl=True)
    tslots = [(ps_tb[i][:, j, :], Btb[i]) for j in range(8) for i in range(2)]
    cnt = {"big": 0, "q": 0, "t": 0}

    def big():
        i = cnt["big"] % 2; cnt["big"] += 1
        return ps_big[i], Bbig[i]

    bigB = [(ps_big[i], Bbig[i]) for i in range(2)] + [(ps_q4[i][:].rearrange("p a b -> p (a b)"), Bq4[i]) for i in range(4)]
    cnt["bigB"] = 0

    def bigb():
        i = cnt["bigB"] % len(bigB); cnt["bigB"] += 1
        return bigB[i]

    def qslot():
        i = cnt["q"] % len(qslots); cnt["q"] += 1
        return qslots[i]

    def tslot():
        i = cnt["t"] % len(tslots); cnt["t"] += 1
        return tslots[i]

    stg = Stage(m, width=1024, n=2)
    pm, Bpm = big()
    modsb, Bmod = emit_mod(m, 16, wada_d, bada_d, cT_d, pm, Bpm, stg)
    gam = m.sbuf("gam", [128, 8], F32); Bgam = m.buf("gam")
    m.dma("sync", gam[:], gam_d, writes=[Bgam])
    A1 = m.sbuf("A1", [128, 8], F32); BAB = m.buf("AB")
    m.stt("vector", A1[:], modsb[:, 8:16], 1.0, gam[:], ALU.add, ALU.mult, reads=[Bmod, Bgam], writes=[BAB])

    w = m.sbuf("w", [128, 8, 2048], BF16); Bw = m.buf("w")
    wab = m.sbuf("wab", [128, 8, 8], BF16)

    xt = [m.sbuf(f"xt{i}", [128, 8, T // 2], F32) for i in range(2)]; Bxt = m.bufs("xt", 2)
    sq = m.sbuf("sq", [128, 8, T], BF16); Bsq = m.buf("sq")
    rs = m.sbuf("rs", [128, T], F32); Brs = m.buf("rs")
    tmp = [m.sbuf(f"tmp{i}", [128, T], F32) for i in range(2)]; Btmp = m.bufs("tmp", 2)
    h = m.sbuf("h", [128, 8, T], BF16); Bh = m.buf("h")
    pcb = [m.sbuf(f"pcb{j}", [128, T + 3], BF16) for j in range(12)]; Bpcb = m.bufs("pcb", 12)
    diag = m.sbuf("diag", [128, 48, 128], BF16); Bdiag = m.buf("diag")
    sqq = [m.sbuf(f"sqq{i}", [128, T], BF16) for i in range(2)]; Bsqq = m.bufs("sqq", 2)
    rn = [m.sbuf(f"rn{i}", [128, T], F32) for i in range(2)]; Brn = m.bufs("rn", 2)
    qkn = m.sbuf("qkn", [128, 8, T], BF16); Bqkn = m.bufs("qkn", 8)
    vT = m.sbuf("vT", [128, 4, T], BF16); BvT = m.bufs("vT", 4)
    gz = m.sbuf("gz", [128, T // 128, 512], BF16); Bgz = m.bufs("gz", T // 128)
    absm = m.sbuf("absm", [128, 8], F32); Bab = m.buf("ab")
    sc1 = m.sbuf("sc1", [128, 4], F32); sc2 = m.sbuf("sc2", [128, 4], F32); Bsc = m.buf("sc")
    graw = m.sbuf("graw", [128, 4], F32); Bgraw = m.buf("graw")
    betaP = [m.sbuf(f"beta{i}", [128, 4], F32) for i in range(2)]; negb = m.sbuf("negb", [128, 4], F32); BbetaP = m.bufs("beta", 2)
    GU = m.sbuf("GU", [128, 4, 128], F32); BGU = m.buf("GU")
    gsb = m.sbuf("gsb", [128, 8], F32); Bgsb = m.buf("gsb")
    egP = [m.sbuf(f"eg{i}", [128, 4], F32) for i in range(2)]; negegP = [m.sbuf(f"negeg{i}", [128, 4], F32) for i in range(2)]
    edl = m.sbuf("edl", [128, 4], F32); eglP = [m.sbuf(f"egl{i}", [128, 4], F32) for i in range(2)]; BscaP = m.bufs("sca", 2)
    GT = m.sbuf("GT", [128, 4, 128], F32); BGT = m.buf("GT")
    GTui = m.sbuf("GTui", [128, 4, 128], F32); GTus = m.sbuf("GTus", [128, 4, 128], F32); BGTm = m.buf("GTm")
    vtP = [[m.sbuf(f"vt{p}_{i}", [128, 128], F32) for i in range(4)] for p in range(2)]; BvtP = [m.bufs(f"vt{p}_", 4) for p in range(2)]
    kdecP = [[m.sbuf(f"kdec{p}_{i}", [128, 128], BF16) for i in range(4)] for p in range(2)]; BkdecP = [m.bufs(f"kdec{p}_", 4) for p in range(2)]
    Pm = [[m.sbuf(f"P{i}_{r}", [128, 128], E2DT) for r in range(2)] for i in range(4)]
    Qm = [[m.sbuf(f"Q{i}_{r}", [128, 128], E2DT) for r in range(2)] for i in range(4)]
    Wm = [[m.sbuf(f"W{i}_{r}", [128, 128], E2DT) for r in range(2)] for i in range(4)]
    Dm = [[m.sbuf(f"Dm{i}_{r}", [128, 128], E2DT) for r in range(2)] for i in range(4)]
    BD = [m.bufs(f"Dm{i}_", 2) for i in range(4)]
    Qf = [m.sbuf(f"Qf{i}", [128, 128], E2DT) for i in range(4)]; BQf = m.bufs("Qf", 4)
    Pf = [m.sbuf(f"Pf{i}", [128, 128], E2DT) for i in range(4)]; BPf = m.bufs("Pf", 4)
    Yt = [m.sbuf(f"Yt{i}", [128, 128], E2DT) for i in range(4)]; BYt = m.bufs("Yt", 4)
    Ym = [m.sbuf(f"Ym{i}", [128, 128], E2DT) for i in range(4)]; BYm = m.bufs("Ym", 4)
    Yt2 = [m.sbuf(f"Yt2{i}", [128, 128], E2DT) for i in range(4)]; BYt2 = m.bufs("Yt2", 4)
    Ym2 = [m.sbuf(f"Ym2{i}", [128, 128], E2DT) for i in range(4)]; BYm2 = m.bufs("Ym2", 4)
    BP = [m.bufs(f"P{i}_", 2) for i in range(4)]
    BQ = [m.bufs(f"Q{i}_", 2) for i in range(4)]
    BW = [m.bufs(f"W{i}_", 2) for i in range(4)]
    WbP = [[m.sbuf(f"Wb{p}_{i}", [128, 128], BF16) for i in range(4)] for p in range(2)]; BWbP = [m.bufs(f"Wb{p}_", 4) for p in range(2)]
    ATP = [[m.sbuf(f"AT{p}_{i}", [128, 128], BF16) for i in range(4)] for p in range(2)]; BATP = [m.bufs(f"AT{p}_", 4) for p in range(2)]
    Rm = [m.sbuf(f"Rm{i}", [128, 128], BF16) for i in range(4)]; BRm = m.bufs("Rm", 4)
    vnew = [m.sbuf(f"vnew{i}", [128, 128], BF16) for i in range(4)]; Bvnew = m.bufs("vnew", 4)
    o1e = [m.sbuf(f"o1e{i}", [128, 128], F32) for i in range(4)]; Bo1e = m.bufs("o1e", 4)
    osb = [m.sbuf(f"osb{i}", [128, 128], F32) for i in range(4)]; Bosb = m.bufs("osb", 4)
    junk = m.sbuf("junk", [128, 128], F32); Bjunk = m.buf("junk")
    ssum = m.sbuf("ssum", [128, 4], F32); Bssum = m.buf("ssum")
    rno = m.sbuf("rno", [128, 4], F32); Brno = m.buf("rno")
    Sf = [m.sbuf(f"Sf{i}", [128, 128], F32) for i in range(4)]; BSf = m.bufs("Sf", 4)
    Sb = [m.sbuf(f"Sb{i}", [128, 128], BF16) for i in range(4)]; BSb = m.bufs("Sb", 4)
    ogt = [m.sbuf(f"ogt{i}", [128, 512], BF16) for i in range(2)]; Bogt = m.bufs("ogt", 2)
    ogTt = [m.sbuf(f"ogTt{i}", [128, 4, 128], BF16) for i in range(2)]; BogTt = m.bufs("ogTt", 2)
    Bouts = []
    ntile = NT // T
    for ps, it in [(ps, it) for ps in range(NP) for it in range(ntile)]:
        if it == 0:
            m.dma("sync", convw[:], convw_d[ps], writes=[Bpw])
            m.dma("sync", negA[:], alog_d[ps], writes=[Bpw])
            m.act(negA[:], negA[:], AF.Exp, reads=[Bpw], writes=[Bpw])
            m.ts("vector", negA[:], negA[:], -1.0, None, ALU.mult, None, reads=[Bpw], writes=[Bpw])
            m.dma("sync", dtb[:], dtb_d[ps], writes=[Bpw])
            for kc in range(8):
                stg.load_cast(lambda c0, c1, kc=kc: w[:, kc, c0:c1], w_d[ps, kc * 128:(kc + 1) * 128, :], 2048, Bw,
                              queue="sync" if kc % 2 else "gpsimd")
                stg.load_cast(lambda c0, c1, kc=kc: wab[:, kc, c0:c1], wab_d[ps, kc * 128:(kc + 1) * 128, :], 8, Bw, queue="sync")
            for j in range(12):
                m.memset("gpsimd", pcb[j][:, 0:3], 0.0, writes=[Bpcb[j]])
                for tap in range(4):
                    m.ts("vector", diag[:, j * 4 + tap, :], id_bf[:], convw[:, j, tap:tap + 1], None, ALU.mult, None,
                         reads=[Bconst, Bpw], writes=[Bdiag])
            for i in range(4):
                m.memset("gpsimd", Sf[i][:], 0.0, writes=[BSf[i]])
                m.memset("gpsimd", Sb[i][:], 0.0, writes=[BSb[i]])
        t0 = it * T
        HTL = T // 2
        for hf in range(2):
            x_t = xt[hf]; Bx = Bxt[hf]
            m.dma("sync" if hf else "gpsimd", x_t[:], xT_d[:, :, t0 + hf * HTL:t0 + (hf + 1) * HTL], writes=[Bx])
            pn_, Bpn_ = big()
            emit_norm_mod(m, x_t, Bx, HTL, A1, modsb[:, 0:8], BAB, ones_bf, Bconst, sq, Bsq, pn_, Bpn_,
                          rs, Brs, tmp, Btmp, h[:, :, hf * HTL:(hf + 1) * HTL], Bh)
        st = {}

        def b_s1(j):
            p, Bp = bigb()
            for kc in range(8):
                m.mm(p[:, 0:T], w[:, kc, j * 128:(j + 1) * 128], h[:, kc, :], kc == 0, kc == 7, reads=[Bw, Bh], writes=[Bp])
            st[j] = (p, Bp)

        def b_s2(j):
            p, Bp = st[j]
            if it > 0:
                m.copy("vector", pcb[j][:, 0:3], pcb[j][:, T:T + 3], reads=[Bpcb[j]], writes=[Bpcb[j]])
            m.copy("vector" if j % 3 else "scalar", pcb[j][:, 3:T + 3], p[:, 0:T], reads=[Bp], writes=[Bpcb[j]])
            p2, Bp2 = bigb()
            for tap in range(4):
                m.mm(p2[:, 0:T], diag[:, j * 4 + tap, :], pcb[j][:, tap:tap + T], tap == 0, tap == 3,
                     reads=[Bdiag, Bpcb[j]], writes=[Bp2])
            st[j] = (p2, Bp2)

        def b_s3(j):
            p2, Bp2 = st[j]
            if j >= 8:
                m.act(vT[:, j - 8, :], p2[:, 0:T], AF.Silu, reads=[Bp2], writes=[BvT[j - 8]])
            else:
                m.act(qkn[:, j, :], p2[:, 0:T], AF.Silu, reads=[Bp2], writes=[Bqkn[j]])

        for j in range(12 + 2):
            if j < 12:
                b_s1(j)
            if 1 <= j < 13:
                b_s2(j - 1)
            if j >= 2:
                b_s3(j - 2)
        for c in range(T // 128):
            p, Bp = bigb()
            for kc in range(8):
                m.mm(p[:, :], h[:, kc, c * 128:(c + 1) * 128], w[:, kc, 1536:2048], kc == 0, kc == 7,
                     reads=[Bw, Bh], writes=[Bp])
            m.act(gz[:, c, :], p[:, :], AF.Silu, reads=[Bp], writes=[Bgz[c]])
            for hd in range(4):
                m.tt("gpsimd", gz[:, c, hd * 128:(hd + 1) * 128], gz[:, c, hd * 128:(hd + 1) * 128], onorm[:], ALU.mult,
                     reads=[Bgz[c], Bconst], writes=[Bgz[c]])
        st2 = {}

        def l_s1(j):
            s_ = sqq[j % 2]; Bs_ = Bsqq[j % 2]
            m.tt("vector", s_[:], qkn[:, j, :], qkn[:, j, :], ALU.mult, reads=[Bqkn[j]], writes=[Bs_])
            p2, Bp2 = bigb()
            m.mm(p2[:, 0:T], ones_bf[:], s_[:], True, True, reads=[Bconst, Bs_], writes=[Bp2])
            st2[j] = (p2, Bp2)

        def l_s2(j):
            p2, Bp2 = st2[j]
            r_ = rn[j % 2]; Br_ = Brn[j % 2]
            m.act(r_[:], p2[:, 0:T], AF.Ln, reads=[Bp2], writes=[Br_], bias=RMS_EPS, scale=1.0)
            m.act(r_[:], r_[:], AF.Exp, reads=[Br_], writes=[Br_], scale=-0.5)
            qscale = (128.0 ** -0.5) if j < 4 else 1.0
            m.stt("vector", qkn[:, j, :], qkn[:, j, :], qscale, r_[:], ALU.mult, ALU.mult, reads=[Bqkn[j], Br_], writes=[Bqkn[j]])

        for j in range(8 + 1):
            if j < 8:
                l_s1(j)
            if j >= 1:
                l_s2(j - 1)
        def chunk_front(c):
            ch = it * (T // 128) + c
            csl = slice(c * 128, (c + 1) * 128)
            par = ch % 2
            eg, negeg, egl, beta, Bsca, Bbeta = egP[par], negegP[par], eglP[par], betaP[par], BscaP[par], BbetaP[par]
            vt, Bvt, kdec, Bkdec, AT, BAT, Wb, BWb = vtP[par], BvtP[par], kdecP[par], BkdecP[par], ATP[par], BATP[par], WbP[par], BWbP[par]
            H4 = range(4)
            qTs = [qkn[:, hd, csl] for hd in H4]; kTs = [qkn[:, 4 + hd, csl] for hd in H4]
            Bqs_ = [Bqkn[hd] for hd in H4]; Bks_ = [Bqkn[4 + hd] for hd in H4]
            pab, Bpab = qslot()
            for kc in range(8):
                m.mm(pab[:, 0:8], h[:, kc, csl], wab[:, kc, :], kc == 0, kc == 7, reads=[Bw, Bh], writes=[Bpab])
            m.copy("vector", absm[:], pab[:, 0:8], reads=[Bpab], writes=[Bab])
            m.tt("vector", sc1[:], absm[:, 0:4], dtb[:], ALU.add, reads=[Bab, Bpw], writes=[Bsc])
            m.ts("vector", sc2[:], sc1[:], -1.0, None, ALU.mult, None, reads=[Bsc], writes=[Bsc])
            m.tt("vector", sc2[:], sc2[:], sc1[:], ALU.max, reads=[Bsc], writes=[Bsc])
            m.act(sc2[:], sc2[:], AF.Exp, reads=[Bsc], writes=[Bsc], scale=-1.0)
            m.act(sc2[:], sc2[:], AF.Ln, reads=[Bsc], writes=[Bsc], bias=1.0)
            m.ts("vector", sc1[:], sc1[:], 0.0, None, ALU.max, None, reads=[Bsc], writes=[Bsc])
            m.tt("vector", sc1[:], sc1[:], sc2[:], ALU.add, reads=[Bsc], writes=[Bsc])
            m.tt("vector", graw[:], sc1[:], negA[:], ALU.mult, reads=[Bsc, Bpw], writes=[Bgraw])
            m.act(beta[:], absm[:, 4:8], AF.Exp, reads=[Bab], writes=[Bbeta], scale=-1.0)
            m.act(beta[:], beta[:], AF.Ln, reads=[Bbeta], writes=[Bbeta], bias=1.0)
            m.act(beta[:], beta[:], AF.Exp, reads=[Bbeta], writes=[Bbeta], scale=-1.0)
            m.ts("vector", negb[:], beta[:], -1.0, None, ALU.mult, None, reads=[Bbeta], writes=[Bbeta])
            for hd in range(4):
                m.tt("gpsimd", GU[:, hd, :], Um, graw[:, hd:hd + 1].to_broadcast([128, 128]), ALU.mult, reads=[Bconst, Bgraw], writes=[BGU])
            pg, Bpg = qslot()
            m.mm(pg[:, 0:4], Um, graw[:], True, True, reads=[Bconst, Bgraw], writes=[Bpg])
            m.mm(pg[:, 4:8], ONEf, graw[:], True, True, reads=[Bconst, Bgraw], writes=[Bpg])
            m.copy("vector", gsb[:], pg[:, 0:8], reads=[Bpg], writes=[Bgsb])
            m.act(eg[:], gsb[:, 0:4], AF.Exp, reads=[Bgsb], writes=[Bsca])
            m.ts("vector", negeg[:], eg[:], -1.0, None, ALU.mult, None, reads=[Bsca], writes=[Bsca])
            m.tt("vector", edl[:], gsb[:, 4:8], gsb[:, 0:4], ALU.subtract, reads=[Bgsb], writes=[Bsca])
            m.act(edl[:], edl[:], AF.Exp, reads=[Bsca], writes=[Bsca])
            m.act(egl[:], gsb[:, 4:8], AF.Exp, reads=[Bgsb], writes=[Bsca])
            pD, BpD = big()
            m.mm(pD[:, :], SLm, GU[:].rearrange("p h i -> p (h i)"), True, True, reads=[Bconst, BGU], writes=[BpD])
            m.act(GT[:].rearrange("p h i -> p (h i)"), pD[:, :], AF.Exp, reads=[BpD], writes=[BGT])
            for hd in range(4):
                m.tt("gpsimd", GTui[:, hd, :], GT[:, hd, :], MUI, ALU.mult, reads=[BGT, Bconst], writes=[BGTm])
                m.tt("gpsimd", GTus[:, hd, :], GT[:, hd, :], MUS, ALU.mult, reads=[BGT, Bconst], writes=[BGTm])
            yield
            H4 = range(4)
            qTs = [qkn[:, hd, csl] for hd in H4]; kTs = [qkn[:, 4 + hd, csl] for hd in H4]
            Bqs_ = [Bqkn[hd] for hd in H4]; Bks_ = [Bqkn[4 + hd] for hd in H4]
            yield
            sl1 = []
            yield
            for hd in H4:
                pt, Bpt = tslot()
                m.tr(pt, vT[:, hd, csl], id_bf[:], reads=[BvT[hd], Bconst], writes=[Bpt])
                pt2, Bpt2 = tslot()
                m.tr(pt2, kTs[hd], id_bf[:], reads=[Bks_[hd], Bconst], writes=[Bpt2])
                pkk, Bpkk = qslot()
                m.mm(pkk, kTs[hd], kTs[hd], True, True, reads=[Bks_[hd]], writes=[Bpkk])
                pqk, Bpqk = qslot()
                m.mm(pqk, kTs[hd], qTs[hd], True, True, reads=[Bks_[hd], Bqs_[hd]], writes=[Bpqk])
                sl1.append((pt, Bpt, pt2, Bpt2, pkk, Bpkk, pqk, Bpqk))
            yield
            for hd in H4:
                pt, Bpt, pt2, Bpt2, pkk, Bpkk, pqk, Bpqk = sl1[hd]
                m.copy("scalar", vt[hd][:], pt, reads=[Bpt], writes=[Bvt[hd]])
                m.ts("vector", kdec[hd][:], pt2, edl[:, hd:hd + 1], None, ALU.mult, None, reads=[Bpt2, Bsca], writes=[Bkdec[hd]])
                m.stt("vector", Qf[hd][:], pkk, negb[:, hd:hd + 1], GTus[:, hd, :], ALU.mult, ALU.mult,
                      reads=[Bpkk, Bbeta, BGTm], writes=[BQf[hd]])
                m.tt("vector", AT[hd][:], pqk, GTui[:, hd, :], ALU.mult, reads=[Bpqk, BGTm], writes=[BAT[hd]])
            yield
            sl2 = []
            yield
            for hd in H4:
                if E2DT == F32:
                    pp, Bpp = qslot()
                    m.tr(pp, Qf[hd][:], IDf, reads=[BQf[hd], Bconst], writes=[Bpp])
                else:
                    pp, Bpp = tslot()
                    m.tr(pp, Qf[hd][:], id_bf[:], reads=[BQf[hd], Bconst], writes=[Bpp])
                sl2.append((pp, Bpp))
            yield
            for hd in H4:
                pp, Bpp = sl2[hd]
                m.copy("scalar", Pf[hd][:], pp, reads=[Bpp], writes=[BPf[hd]])
                m.tt("gpsimd", Qm[hd][0][:], Qf[hd][:], BD16, ALU.mult, reads=[BQf[hd], Bconst], writes=[BQ[hd][0]])
                m.tt("gpsimd", Wm[hd][0][:], Qm[hd][0][:], IDf, ALU.add, reads=[BQ[hd][0], Bconst], writes=[BW[hd][0]])
            yield
            for hd in H4:
                m.tt("gpsimd", Pm[hd][0][:], Pf[hd][:], BD16, ALU.mult, reads=[BPf[hd], Bconst], writes=[BP[hd][0]])
            def tr_stage(src, Bsrc, dst, Bdst):
                sl_ = []
                for hd in H4:
                    pp_, Bpp_ = qslot()
                    m.tr(pp_, src[hd][:], IDf, reads=[Bsrc[hd], Bconst], writes=[Bpp_])
                    sl_.append((pp_, Bpp_))
                return sl_

            for lev in range(1, 4):
                r0 = (lev - 1) % 2; r1 = lev % 2
                yield
                sl = []
                for hd in H4:
                    pQ, BpQ = qslot()
                    m.mm(pQ, Pm[hd][r0][:], Qm[hd][r0][:], True, True, reads=[BQ[hd][r0], BP[hd][r0]], writes=[BpQ])
                    sl.append((pQ, BpQ))
                yield
                for hd in H4:
                    pQ, BpQ = sl[hd]
                    m.copy("vector" if hd % 2 else "scalar", Qm[hd][r1][:], pQ, reads=[BpQ], writes=[BQ[hd][r1]])
                yield
                sl = tr_stage([Qm[hd][r1] for hd in H4], [BQ[hd][r1] for hd in H4], None, None)
                yield
                for hd in H4:
                    pp_, Bpp_ = sl[hd]
                    m.copy("scalar" if hd % 2 else "vector", Pm[hd][r1][:], pp_, reads=[Bpp_], writes=[BP[hd][r1]])
                yield
                sl = []
                for hd in H4:
                    pW, BpW = qslot()
                    m.mm(pW, Pm[hd][r1][:], Wm[hd][r0][:], True, True, reads=[BP[hd][r1], BW[hd][r0]], writes=[BpW])
                    sl.append((pW, BpW))
                yield
                for hd in H4:
                    pW, BpW = sl[hd]
                    m.tt("vector", Wm[hd][r1][:], pW, Wm[hd][r0][:], ALU.add, reads=[BpW, BW[hd][r0]], writes=[BW[hd][r1]])
            yield
            sl = tr_stage([Wm[hd][1] for hd in H4], [BW[hd][1] for hd in H4], None, None)
            yield
            for hd in H4:
                pp_, Bpp_ = sl[hd]
                m.copy("scalar", Dm[hd][0][:], pp_, reads=[Bpp_], writes=[BD[hd][0]])
            for si in range(3):
                r0 = (3 + si) % 2; r1 = (4 + si) % 2
                d0 = si % 2; d1 = (si + 1) % 2
                yield
                sl = []
                for hd in H4:
                    pY, BpY = qslot()
                    m.mm(pY, Pf[hd][:], Wm[hd][r0][:], True, True, reads=[BPf[hd], BW[hd][r0]], writes=[BpY])
                    sl.append((pY, BpY))
                yield
                for hd in H4:
                    pY, BpY = sl[hd]
                    m.copy("scalar", Yt[hd][:], pY, reads=[BpY], writes=[BYt[hd]])
                    m.tt("gpsimd", Ym[hd][:], Yt[hd][:], MT[si], ALU.mult, reads=[BYt[hd], Bconst], writes=[BYm[hd]])
                yield
                sl = []
                for hd in H4:
                    pZ, BpZ = qslot()
                    m.mm(pZ, Dm[hd][d0][:], Ym[hd][:], True, True, reads=[BD[hd][d0], BYm[hd]], writes=[BpZ])
                    sl.append((pZ, BpZ))
                yield
                for hd in H4:
                    pZ, BpZ = sl[hd]
                    if si < 2:
                        m.tt("vector", Wm[hd][r1][:], pZ, Wm[hd][r0][:], ALU.add, reads=[BpZ, BW[hd][r0]], writes=[BW[hd][r1]])
                    else:
                        m.tt("vector", Wb[hd][:], pZ, Wm[hd][r0][:], ALU.add, reads=[BpZ, BW[hd][r0]], writes=[BWb[hd]])
                if si < 2:
                    yield
                    sl = tr_stage([Wm[hd][r1] for hd in H4], [BW[hd][r1] for hd in H4], None, None)
                    yield
                    for hd in H4:
                        pp_, Bpp_ = sl[hd]
                        m.copy("scalar", Dm[hd][d1][:], pp_, reads=[Bpp_], writes=[BD[hd][d1]])
            yield

        def chunk_back(c):
            ch = it * (T // 128) + c
            csl = slice(c * 128, (c + 1) * 128)
            par = ch % 2
            eg, negeg, egl, beta, Bsca, Bbeta = egP[par], negegP[par], eglP[par], betaP[par], BscaP[par], BbetaP[par]
            vt, Bvt, kdec, Bkdec, AT, BAT, Wb, BWb = vtP[par], BvtP[par], kdecP[par], BkdecP[par], ATP[par], BATP[par], WbP[par], BWbP[par]
            H4 = range(4)
            qTs = [qkn[:, hd, csl] for hd in H4]; kTs = [qkn[:, 4 + hd, csl] for hd in H4]
            Bqs_ = [Bqkn[hd] for hd in H4]; Bks_ = [Bqkn[4 + hd] for hd in H4]
            yield
            sl = []
            yield
            for hd in H4:
                pks, Bpks = qslot()
                m.mm(pks, kTs[hd], Sb[hd][:], True, True, reads=[Bks_[hd], BSb[hd]], writes=[Bpks])
                po1, Bpo1 = qslot()
                m.mm(po1, qTs[hd], Sb[hd][:], True, True, reads=[Bqs_[hd], BSb[hd]], writes=[Bpo1])
                sl.append((pks, Bpks, po1, Bpo1))
            yield
            for hd in H4:
                pks, Bpks, po1, Bpo1 = sl[hd]
                m.stt("vector", Rm[hd][:], pks, negeg[:, hd:hd + 1], vt[hd][:], ALU.mult, ALU.add,
                      reads=[Bpks, Bsca, Bvt[hd]], writes=[BRm[hd]])
                m.act(o1e[hd][:], po1, AF.Copy, reads=[Bpo1, Bsca], writes=[Bo1e[hd]], scale=eg[:, hd:hd + 1])
            yield
            sl = []
            yield
            for hd in H4:
                ptr, Bptr = qslot()
                m.mm(ptr, Wb[hd][:], Rm[hd][:], True, True, reads=[BWb[hd], BRm[hd]], writes=[Bptr])
                sl.append((ptr, Bptr))
            yield
            for hd in H4:
                ptr, Bptr = sl[hd]
                m.ts("vector", vnew[hd][:], ptr, beta[:, hd:hd + 1], None, ALU.mult, None, reads=[Bptr, Bbeta], writes=[Bvnew[hd]])
            yield
            sl = []
            yield
            for hd in H4:
                po2, Bpo2 = qslot()
                m.mm(po2, AT[hd][:], vnew[hd][:], True, True, reads=[BAT[hd], Bvnew[hd]], writes=[Bpo2])
                pS, BpS = qslot()
                m.mm(pS, kdec[hd][:], vnew[hd][:], True, True, reads=[Bkdec[hd], Bvnew[hd]], writes=[BpS])
                sl.append((po2, Bpo2, pS, BpS))
            yield
            for hd in H4:
                po2, Bpo2, pS, BpS = sl[hd]
                m.stt("vector", Sf[hd][:], Sf[hd][:], egl[:, hd:hd + 1], pS, ALU.mult, ALU.add,
                      reads=[BSf[hd], Bsca, BpS], writes=[BSf[hd]])
                m.copy("gpsimd", Sb[hd][:], Sf[hd][:], reads=[BSf[hd]], writes=[BSb[hd]])
                m.tt("vector", osb[hd][:], po2, o1e[hd][:], ALU.add, reads=[Bpo2, Bo1e[hd]], writes=[Bosb[hd]])
                m.act(junk[:], osb[hd][:], AF.Square, reads=[Bosb[hd]], writes=[Bjunk, Bssum], accum_out=ssum[:, hd:hd + 1])
            yield
            m.act(rno[:], ssum[:], AF.Ln, reads=[Bssum], writes=[Brno], bias=RMS_EPS, scale=1.0 / 128)
            m.act(rno[:], rno[:], AF.Exp, reads=[Brno], writes=[Brno], scale=-0.5)
            og_t = ogt[ch % 2]; Bog_t = Bogt[ch % 2]
            for hd in range(4):
                m.stt("vector", og_t[:, hd * 128:(hd + 1) * 128], osb[hd][:], rno[:, hd:hd + 1], gz[:, c, hd * 128:(hd + 1) * 128],
                      ALU.mult, ALU.mult, reads=[Bosb[hd], Brno, Bgz[c]], writes=[Bog_t])
            ot = ogTt[ch % 2]; Bot = BogTt[ch % 2]
            for hd in range(4):
                pt3, Bpt3 = tslot()
                m.tr(pt3, og_t[:, hd * 128:(hd + 1) * 128], id_bf[:], reads=[Bog_t, Bconst], writes=[Bpt3])
                m.copy("scalar", ot[:, hd, :], pt3, reads=[Bpt3], writes=[Bot])
            if "og_store" in A:
                Bouts.append(A["og_store"](m, ot, ps, ch, Bot))
            else:
                Bouts.append(m.buf("out"))
                m.dma("sync", og_d[:, ps * 4:(ps + 1) * 4, ch * 128:(ch + 1) * 128], ot[:], reads=[Bot], writes=[Bouts[-1]])

            yield

        prev_back = None
        for c in range(0 if _DBG.get("skip_chunks") else T // 128):
            interleave(chunk_front(c), prev_back)
            prev_back = chunk_back(c)
        interleave(None, prev_back)
    return Bouts


def gdn_inputs(hhs, xb, cb, wada0, bada0, gam, w_in, conv, a_log, dtb, onorm, xT=None):
    W, WAB, CW, AL, DT = [], [], [], [], []
    for hh in hhs:
        hs = [hh * 4 + i for i in range(4)]
        cols = []
        for tsr in range(4):
            for hd in hs:
                cols.extend(range(tsr * 1024 + hd * 128, tsr * 1024 + (hd + 1) * 128))
        W.append(w_in[:, cols])
        WAB.append(w_in[:, [4096 + hd for hd in hs] + [4104 + hd for hd in hs]])
        cw = np.stack([conv[:, tsr * 1024 + hd * 128: tsr * 1024 + (hd + 1) * 128] for tsr in range(3) for hd in hs], 0)
        CW.append(cw.transpose(2, 0, 1))
        AL.append(np.broadcast_to(a_log[hs][None, :], (128, 4)))
        DT.append(np.broadcast_to(dtb[hs][None, :], (128, 4)))
    c_ = lambda l: np.ascontiguousarray(np.stack(l, 0), dtype=np.float32)
    return {
        "xT": lay_xT(xb) if xT is None else xT, "wada": lay_wada(wada0, 0, 2048), "bada": lay_vec(bada0[0:2048]), "cT": lay_vec(cb),
        "gam": lay_vec(gam), "w_qkvz": c_(W), "w_ab": c_(WAB), "convw": c_(CW), "alog": c_(AL), "dtb": c_(DT),
        "onorm": np.ascontiguousarray(np.broadcast_to(onorm[None, :], (128, 128))),
        "consts": gdn_consts(),
    }


DSW_GROUPS = ((128, 1), (512, 4), (2048, 16))
NEG = -30000.0


def t5_bucket(dist):
    dist = np.asarray(dist, np.int32)
    x = (np.maximum(dist, 1).astype(np.float32) / np.float32(16)).astype(np.float32)
    scaled = (np.log(x).astype(np.float32) / np.float32(np.log(2048 / 16))).astype(np.float32)
    large = 16 + (scaled * np.float32(16)).astype(np.float32).astype(np.int32)
    large = np.minimum(large, 31)
    return np.where(dist < 16, dist, large)


def attn_tables():
    ki = np.arange(128)[:, None, None]
    kb = np.arange(2)[None, :, None]
    qi = np.arange(128)[None, None, :]
    dist = qi + 128 * (1 - kb) - ki
    valid = (dist >= 0) & (dist <= 128)
    return dist, valid


def attn_decl(nc, NT, NP, pre="", og_kind="ExternalOutput", x_kind="ExternalInput"):
    A = {}
    A["xT"] = nc.dram_tensor(pre + "xT", [128, 8, NT], F32, kind=x_kind).ap()
    A["wada"] = nc.dram_tensor(pre + "wada", [16, 128, 8, 128], F32, kind="ExternalInput").ap()
    A["bada"] = nc.dram_tensor(pre + "bada", [128, 16], F32, kind="ExternalInput").ap()
    A["cT"] = nc.dram_tensor(pre + "cT", [128, 8], F32, kind="ExternalInput").ap()
    A["gam"] = nc.dram_tensor(pre + "gam", [128, 8], F32, kind="ExternalInput").ap()
    A["w_qkv"] = nc.dram_tensor(pre + "w_qkv", [NP, D, 1152], F32, kind="ExternalInput").ap()
    A["gains"] = nc.dram_tensor(pre + "gains", [128, 2], F32, kind="ExternalInput").ap()
    A["biasT"] = nc.dram_tensor(pre + "biasT", [NP, 128, 6, 256], F32, kind="ExternalInput").ap()
    A["consts"] = nc.dram_tensor(pre + "consts", [128, 2, 256], F32, kind="ExternalInput").ap()
    A["ogT"] = nc.dram_tensor(pre + "ogT", [128, NP, NT], BF16, kind=og_kind).ap()
    return A


def build_attn(NT, NP=2, T=256):
    nc = bass.Bass("TRN2", target_bir_lowering=False)
    A = attn_decl(nc, NT, NP)
    m = MK(nc)
    outs = emit_attn(m, A, NT, NP, T)
    m.final_wait("sync", outs)
    m.build(barrier=False)
    return nc


def emit_attn(m, A, NT, NP, T=256):
    UN = 2048
    NU = NT // UN
    xT_d, wada_d, bada_d, cT_d, gam_d, w_d, gains_d, bias_d, cst_d, og_d = [A.get(k) for k in
        ("xT", "wada", "bada", "cT", "gam", "w_qkv", "gains", "biasT", "consts", "ogT")]

    cst = m.sbuf("cst", [128, 2, 256], F32); Bconst = m.buf("const")
    m.dma("sync", cst[:], cst_d, writes=[Bconst])
    negmask = cst[:, 0, :]
    ones_bf = m.sbuf("ones_bf", [128, 128], BF16)
    bones_bf = m.sbuf("bones_bf", [128, 128], BF16)
    m.memset("vector", ones_bf[:], 1.0, writes=[Bconst])
    m.copy("vector", bones_bf[:], cst[:, 1, 0:128], reads=[Bconst], writes=[Bconst])
    gains = m.sbuf("gains", [128, 2], F32)
    m.dma("sync", gains[:], gains_d, writes=[Bconst])
    m.ts("vector", gains[:, 0:1], gains[:, 0:1], 0.125, None, ALU.mult, None, reads=[Bconst], writes=[Bconst])

    ps_big = [m.psum(f"big{i}", [128, 512], F32) for i in range(2)]; Bbig = m.bufs("big", 2, excl=True)
    ps_sf = [m.psum(f"s{i}", [128, 512], F32) for i in range(4)]; Bps_s = m.bufs("ps_s", 4, excl=True)
    ps_s = [t[:, 0:256].rearrange("p (b q) -> p b q", b=2) for t in ps_sf]
    ps_of = [m.psum(f"o{i}", [128, 512], F32) for i in range(2)]; Bps_o = m.bufs("ps_o", 2, excl=True)
    ps_o = [t[:, 0:128] for t in ps_of]
    ps_v = [ps_sf[2][:, 0:128], ps_sf[3][:, 0:128]]; Bps_v = [Bps_s[2], Bps_s[3]]
    ring = [(ps_big[i], Bbig[i]) for i in range(2)] + [(ps_sf[i], Bps_s[i]) for i in range(4)] + [(ps_of[i], Bps_o[i]) for i in range(2)]
    cnt = {"big": 0, "s": 0, "o": 0, "v": 0, "ring": 0}

    def rbank():
        i = cnt["ring"] % len(ring); cnt["ring"] += 1
        return ring[i]


    def big():
        i = cnt["big"] % 2; cnt["big"] += 1
        return ps_big[i], Bbig[i]

    stg = Stage(m, width=1024, n=2)
    HL = A.get("h_load")
    if HL is None:
        pm, Bpm = big()
        modsb, Bmod = emit_mod(m, 16, wada_d, bada_d, cT_d, pm, Bpm, stg)
        gam = m.sbuf("gam", [128, 8], F32); Bgam = m.buf("gam")
        m.dma("sync", gam[:], gam_d, writes=[Bgam])
        A1 = m.sbuf("A1", [128, 8], F32); BAB = m.buf("AB")
        m.stt("vector", A1[:], modsb[:, 8:16], 1.0, gam[:], ALU.add, ALU.mult, reads=[Bmod, Bgam], writes=[BAB])

    w = m.sbuf("w", [128, 8, 1152], BF16); Bw = m.buf("w")
    biasT = m.sbuf("biasT", [128, 6, 256], F32); Bbias = m.buf("bias")
    if HL is None:
        xt = [m.sbuf(f"xt{i}", [128, 8, T], F32) for i in range(1)]; Bxt = m.bufs("xt", 1)
        sq = m.sbuf("sq", [128, 8, T], BF16); Bsq = m.buf("sq")
        rs = m.sbuf("rs", [128, T], F32); Brs = m.buf("rs")
        tmp = [m.sbuf(f"tmp{i}", [128, T], F32) for i in range(2)]; Btmp = m.bufs("tmp", 2)
    hU = m.sbuf("hU", [128, 8, UN], BF16); BhU = m.buf("hU")
    qf = [m.sbuf(f"qf{i}", [128, 512], F32) for i in range(2)]; Bqf = m.bufs("qf", 2)
    sqq = [m.sbuf(f"sqq{i}", [128, 512], BF16) for i in range(2)]; Bsqq = m.bufs("sqq", 2)
    rn = [m.sbuf(f"rn{i}", [128, 512], F32) for i in range(2)]; Brn = m.bufs("rn", 2)
    qn = [m.sbuf(f"qn{g}", [128, UN], BF16) for g in range(3)]; Bqn = m.bufs("qn", 3)
    kn = [[m.sbuf(f"kn{g}_{r}", [128, UN], BF16) for r in range(2)] for g in range(3)]
    Bkn = [m.bufs(f"kn{g}_", 2) for g in range(3)]
    Va = [[m.sbuf(f"Va{g}_{r}", [128, 16, 2, 128], BF16) for r in range(2)] for g in range(3)]
    BVa = [m.bufs(f"Va{g}_", 2) for g in range(3)]
    for g in range(3):
        for r in range(2):
            m.memset("gpsimd" if r else "vector", Va[g][r][:, :, :, 64:128], 1.0, writes=[BVa[g][r]])
    acc = [m.sbuf(f"acc{i}", [128, UN], F32) for i in range(1)]; Bacc = m.bufs("acc", 1)
    sT = [m.sbuf(f"sT{i}", [128, 2, 128], F32) for i in range(3)]; BsT = m.bufs("sT", 3)
    pT = [m.sbuf(f"pT{i}", [128, 2, 128], BF16) for i in range(3)]; BpT = m.bufs("pT", 3)
    rden = m.sbuf("rden", [64, UN], F32); Brden = m.buf("rden")
    obf = [m.sbuf(f"obf{i}", [64, UN], BF16) for i in range(1)]; Bobf = m.bufs("obf", 1)
    Bouts = []
    nacc = 0
    npt = 0

    for pp in range(NP):
        for kc in range(8):
            stg.load_cast(lambda c0, c1, kc=kc: w[:, kc, c0:c1], w_d[pp, kc * 128:(kc + 1) * 128, :], 1152, Bw,
                          queue="sync" if kc % 2 else "gpsimd")
        m.dma("sync", biasT[:], bias_d[pp], writes=[Bbias])
        for i in range(6):
            m.tt("gpsimd", biasT[:, i, :], biasT[:, i, :], negmask, ALU.add, reads=[Bbias, Bconst], writes=[Bbias])
        for u in range(NU):
            ur = u % 2
            if HL is not None:
                HL(m, hU, u, BhU)
            for ti in range(0 if HL is not None else UN // T):
                t0 = u * UN + ti * T
                x_t = xt[0]; Bx = Bxt[0]
                if "x_load" in A:
                    A["x_load"](m, x_t, t0, T, Bx)
                else:
                    m.dma("sync" if ti % 2 else "gpsimd", x_t[:], xT_d[:, :, t0:t0 + T], writes=[Bx])
                pn_, Bpn_ = big()
                emit_norm_mod(m, x_t, Bx, T, A1, modsb[:, 0:8], BAB, ones_bf, Bconst, sq, Bsq, pn_, Bpn_,
                              rs, Brs, tmp, Btmp, hU[:, :, ti * T:(ti + 1) * T], BhU)
            items = [(g_, qk, tl) for g_ in range(3) for qk in range(2) for tl in range(UN // 512)]
            stq = {}

            def q_s1(i):
                g_, qk, tl = items[i]
                c0 = (g_ * 3 + qk) * 128
                p, Bp = rbank()
                for kc in range(8):
                    m.mm(p[:, :], w[:, kc, c0:c0 + 128], hU[:, kc, tl * 512:(tl + 1) * 512], kc == 0, kc == 7,
                         reads=[Bw, BhU], writes=[Bp])
                stq[i] = (p, Bp)

            def q_s2(i):
                p, Bp = stq[i]
                f_ = qf[i % 2]; Bf_ = Bqf[i % 2]
                m.copy("scalar", f_[:], p[:, :], reads=[Bp], writes=[Bf_])
                s_ = sqq[i % 2]; Bs_ = Bsqq[i % 2]
                m.tt("gpsimd", s_[:], f_[:], f_[:], ALU.mult, reads=[Bf_], writes=[Bs_])
                p2, Bp2 = rbank()
                m.mm(p2[:, :], bones_bf[:], s_[:], True, True, reads=[Bconst, Bs_], writes=[Bp2])
                stq[i] = (p2, Bp2)

            def q_s3(i):
                g_, qk, tl = items[i]
                d = DSW_GROUPS[g_][1]
                dst = qn[g_] if qk == 0 else kn[g_][ur]
                Bdst = Bqn[g_] if qk == 0 else Bkn[g_][ur]
                p2, Bp2 = stq.pop(i)
                f_ = qf[i % 2]; Bf_ = Bqf[i % 2]
                r_ = rn[i % 2]; Br_ = Brn[i % 2]
                m.act(r_[:], p2[:, :], AF.Ln, reads=[Bp2], writes=[Br_], bias=RMS_EPS, scale=1.0 / 64)
                m.act(r_[:], r_[:], AF.Exp, reads=[Br_], writes=[Br_], scale=-0.5)
                J = 512 // d
                j0 = tl * J
                if d == 1:
                    o_ap = dst[:, tl * 512:(tl + 1) * 512]
                    i0 = f_[:]; i1 = r_[:]
                else:
                    o_ap = dst[:].rearrange("p (r j) -> p r j", r=d)[:, :, j0:j0 + J].rearrange("p r j -> p j r")
                    i0 = f_[:].rearrange("p (j r) -> p j r", r=d)
                    i1 = r_[:].rearrange("p (j r) -> p j r", r=d)
                m.stt("vector", o_ap, i0, gains[:, qk:qk + 1], i1, ALU.mult, ALU.mult,
                      reads=[Bf_, Br_, Bconst], writes=[Bdst])

            nit = len(items)
            for i in range(nit + 2):
                if i < nit:
                    q_s1(i)
                if 1 <= i < nit + 1:
                    q_s2(i - 1)
                if i >= 2:
                    q_s3(i - 2)
            for g in range(3):
                d = DSW_GROUPS[g][1]
                nb = 16 // d
                c0 = (g * 3 + 2) * 128
                for r in range(d):
                    for n_ in range(nb):
                        blk = r * nb + n_
                        tb = n_ * 128 * d + r
                        i = cnt["v"] % 2; cnt["v"] += 1
                        for kc in range(8):
                            m.mm(ps_v[i], hU[:, kc, tb:tb + 127 * d + 1:d], w[:, kc, c0:c0 + 128], kc == 0, kc == 7,
                                 reads=[BhU, Bw], writes=[Bps_v[i]])
                        m.copy("scalar", Va[g][ur][:, blk, :, 0:64], ps_v[i].rearrange("p (h c) -> p h c", h=2),
                               reads=[Bps_v[i]], writes=[BVa[g][ur]])
            LAG = 2
            for hl in range(2):
                a_ = acc[0]; Ba_ = Bacc[0]; nacc += 1
                hp = slice(hl * 64, (hl + 1) * 64)
                blocks = []
                for g in range(3):
                    d = DSW_GROUPS[g][1]
                    nb = 16 // d
                    for r in range(d):
                        for n_ in range(nb):
                            blocks.append((g, d, nb, r, n_))
                stA = {}

                def stage_a(i):
                    g, d, nb, r, n_ = blocks[i]
                    bt = biasT[:, g * 2 + hl, :].rearrange("p (b q) -> p b q", b=2)
                    blk = r * nb + n_
                    qcol = blk * 128
                    if n_ > 0:
                        kprev = (kn[g][ur], Bkn[g][ur], Va[g][ur], BVa[g][ur], blk - 1)
                    elif u > 0:
                        kprev = (kn[g][1 - ur], Bkn[g][1 - ur], Va[g][1 - ur], BVa[g][1 - ur], r * nb + nb - 1)
                    else:
                        kprev = None
                    si = cnt["s"] % len(ps_s); cnt["s"] += 1
                    pS = ps_s[si]; BpS = Bps_s[si]
                    kb0 = 0 if kprev is not None else 1
                    if kprev is not None:
                        kt, Bkt, _, _, pb = kprev
                        m.mm(pS[:, 0, :], kt[hp, pb * 128:(pb + 1) * 128], qn[g][hp, qcol:qcol + 128], True, True,
                             reads=[Bkt, Bqn[g]], writes=[BpS])
                    m.mm(pS[:, 1, :], kn[g][ur][hp, qcol:qcol + 128], qn[g][hp, qcol:qcol + 128], True, True,
                         reads=[Bkn[g][ur], Bqn[g]], writes=[BpS])
                    k3 = i % 3
                    s_ = sT[k3]; Bs_ = BsT[k3]
                    p_ = pT[k3]; Bp_ = BpT[k3]
                    m.tt("vector", s_[:, kb0:2, :], pS[:, kb0:2, :], bt[:, kb0:2, :], ALU.add,
                         reads=[BpS, Bbias], writes=[Bs_])
                    m.act(p_[:, kb0:2, :], s_[:, kb0:2, :], AF.Exp, reads=[Bs_], writes=[Bp_])
                    stA[i] = (kprev, p_, Bp_)

                def stage_b(i):
                    g, d, nb, r, n_ = blocks[i]
                    kprev, p_, Bp_ = stA.pop(i)
                    blk = r * nb + n_
                    tb = n_ * 128 * d + r
                    oi = cnt["o"] % 2; cnt["o"] += 1
                    pO = ps_o[oi]; BpO = Bps_o[oi]
                    if kprev is not None:
                        _, _, vt_, Bvt_, pb = kprev
                        m.mm(pO[:, :], vt_[:, pb, hl, :], p_[:, 0, :], True, False, reads=[Bvt_, Bp_], writes=[BpO], inc=False)
                    m.mm(pO[:, :], Va[g][ur][:, blk, hl, :], p_[:, 1, :], kprev is None, True,
                         reads=[BVa[g][ur], Bp_], writes=[BpO])
                    a_ap = a_[:, tb:tb + 127 * d + 1:d]
                    if g == 0:
                        m.copy("vector", a_ap, pO[:, :], reads=[BpO], writes=[Ba_])
                    else:
                        m.tt("vector", a_ap, pO[:, :], a_ap, ALU.add, reads=[BpO, Ba_], writes=[Ba_])

                nblk = len(blocks)
                for i in range(nblk + LAG):
                    if i < nblk:
                        stage_a(i)
                    if i >= LAG:
                        stage_b(i - LAG)
                m.act(rden[:], a_[64:128, :], AF.Ln, reads=[Ba_], writes=[Brden])
                m.act(rden[:], rden[:], AF.Exp, reads=[Brden], writes=[Brden], scale=-1.0)
                ob = obf[0]; Bob = Bobf[0]
                m.tt("gpsimd", ob[:], a_[0:64, :], rden[:], ALU.mult, reads=[Ba_, Brden], writes=[Bob])
                if "og_store" in A:
                    Bouts.append(A["og_store"](m, ob, pp, hl, u, Bob))
                else:
                    Bouts.append(m.buf("out"))
                    m.dma("sync", og_d[hl * 64:(hl + 1) * 64, pp, u * UN:(u + 1) * UN], ob[:], reads=[Bob], writes=[Bouts[-1]])
    return Bouts


def attn_inputs(pairs, xb, cb, wada1, bada1, gam, w_in, q_gain, k_gain, rel_bias, xT=None):
    NP = len(pairs)
    wsel = np.empty((NP, D, 1152), np.float32)
    bias = np.empty((NP, 128, 6, 256), np.float32)
    dist, valid = attn_tables()
    for pi, pr in enumerate(pairs):
        for g in range(3):
            d = DSW_GROUPS[g][1]
            idx = t5_bucket(np.clip(dist, 0, 128) * d)
            for t in range(3):
                for hl in range(2):
                    hd = pr * 2 + hl
                    c_src = ((t * 3 + g) * 8 + hd) * 64
                    c_dst = (g * 3 + t) * 128 + hl * 64
                    wsel[pi, :, c_dst:c_dst + 64] = w_in[:, c_src:c_src + 64]
            for hl in range(2):
                hd = pr * 2 + hl
                bias[pi, :, g * 2 + hl, :] = rel_bias[idx, g * 8 + hd].reshape(128, 256)
    cst = np.zeros((128, 2, 256), np.float32)
    cst[:, 0, :] = np.where(valid, 0.0, NEG).reshape(128, 256)
    cst[0:64, 1, 0:64] = 1.0
    cst[64:128, 1, 64:128] = 1.0
    gains = np.stack([np.tile(q_gain, 2), np.tile(k_gain, 2)], axis=1).astype(np.float32)
    d_ = {"wada": lay_wada(wada1, 0, 2048), "bada": lay_vec(bada1[0:2048]), "cT": lay_vec(cb),
          "gam": lay_vec(gam), "w_qkv": wsel, "gains": np.ascontiguousarray(gains), "biasT": bias, "consts": cst}
    if xT is not None or xb is not None:
        d_["xT"] = lay_xT(xb) if xT is None else xT
    return d_


def build_fused(NT=SEQ):
    nc = bass.Bass("TRN2", target_bir_lowering=False)
    ses = ExitStack()
    ext = lambda name, shape, dt=F32: nc.dram_tensor(name, list(shape), dt, kind="ExternalInput").ap()
    xT = ext("xT", [128, 8, NT]); cT = ext("cT", [128, 8])
    ogT0 = nc.dram_tensor("ogT0", [128, 8, NT], BF16, kind="Internal").ap()
    x1T = nc.dram_tensor("x1T", [128, 8, NT], F32, kind="Internal").ap()
    ogT1 = nc.dram_tensor("ogT1", [128, 4, NT], BF16, kind="Internal").ap()
    outT = nc.dram_tensor("outT", [128, 8, NT], F32, kind="ExternalOutput").ap()
    A1 = {"xT": xT, "cT": cT, "wada": ext("g_wada", [16, 128, 8, 128]), "bada": ext("g_bada", [128, 16]),
          "gam": ext("g_gam", [128, 8]), "w_qkvz": ext("g_w_qkvz", [2, D, 2048]), "w_ab": ext("g_w_ab", [2, D, 8]),
          "convw": ext("g_convw", [2, 128, 12, 4]), "alog": ext("g_alog", [2, 128, 4]), "dtb": ext("g_dtb", [2, 128, 4]),
          "onorm": ext("g_onorm", [128, 128]), "consts": ext("g_consts", [128, 13, 128]), "ogT": ogT0}
    A2 = {"xT": xT, "cT": cT, "ogT": ogT0, "w_o": ext("f0_w_o", [1024, D]), "wada": ext("f0_wada", [32, 128, 8, 128]),
          "bada": ext("f0_bada", [128, 32]), "gam": ext("f0_gam", [128, 8]), "w1": ext("f0_w1", [D, 2 * FH]),
          "w2": ext("f0_w2", [FH, D]), "outT": x1T}
    A3 = {"xT": x1T, "cT": cT, "wada": ext("a_wada", [16, 128, 8, 128]), "bada": ext("a_bada", [128, 16]),
          "gam": ext("a_gam", [128, 8]), "w_qkv": ext("a_w_qkv", [4, D, 1152]), "gains": ext("a_gains", [128, 2]),
          "biasT": ext("a_biasT", [4, 128, 6, 256]), "consts": ext("a_consts", [128, 2, 256]), "ogT": ogT1}
    A4 = {"xT": x1T, "cT": cT, "ogT": ogT1, "w_o": ext("f1_w_o", [512, D]), "wada": ext("f1_wada", [32, 128, 8, 128]),
          "bada": ext("f1_bada", [128, 32]), "gam": ext("f1_gam", [128, 8]), "w1": ext("f1_w1", [D, 2 * FH]),
          "w2": ext("f1_w2", [FH, D]), "outT": outT}
    m = MK(nc, sem_es=ses, tag="p1_"); emit_gdn(m, A1, NT, 2); m.build()
    m = MK(nc, sem_es=ses, tag="p2_"); emit_ffn(m, A2, NT, 1024); m.build()
    m = MK(nc, sem_es=ses, tag="p3_"); emit_attn(m, A3, NT, 4); m.build()
    m = MK(nc, sem_es=ses, tag="p4_"); emit_ffn(m, A4, NT, 512); m.build()
    ses.close()
    return nc


PAIRS = [[0, 1], [2, 3], [4, 5], [6, 7]]


def build_fused8(NT=SEQ):
    nc = bass.Bass("TRN2", target_bir_lowering=False)
    ses = ExitStack()
    HT = NT // 2
    ext = lambda name, shape, dt=F32: nc.dram_tensor(name, list(shape), dt, kind="ExternalInput").ap()
    itn = lambda name, shape, dt: nc.dram_tensor(name, list(shape), dt, kind="Internal").ap()
    xT = ext("xT", [128, 8, NT]); xTh = ext("xTh", [128, 8, HT]); cT = ext("cT", [128, 8]); sel_d = ext("sel", [128, 2])
    outT = nc.dram_tensor("outT", [128, 8, HT], F32, kind="ExternalOutput").ap()
    OGC = 2048; X1C = 512; O2C = 4096
    n_og, n_x1, n_o2 = NT // OGC, HT // X1C, NT // O2C
    og_src = [itn(f"og_src{j}", [128 * 4, OGC], BF16) for j in range(n_og)]
    og_gat = [itn(f"og_gat{j}", [2 * 128 * 4, OGC], BF16) for j in range(n_og)]
    x1_src = [itn(f"x1_src{j}", [128 * 8, X1C], F32) for j in range(n_x1)]
    H1C = 1024; n_h1 = HT // H1C
    h1_src = [itn(f"h1_src{j}", [128 * 8, H1C], BF16) for j in range(n_h1)]
    h1_gat = [itn(f"h1_gat{j}", [2 * 128 * 8, H1C], BF16) for j in range(n_h1)]
    h1_sv = [t.rearrange("(p k) t -> p k t", k=8) for t in h1_src]
    h1_gv = [t.rearrange("(r p k) t -> r p k t", r=2, k=8) for t in h1_gat]
    o2_src = [itn(f"o2_src{j}", [128 * 2, O2C], BF16) for j in range(n_o2)]
    o2_gat = [itn(f"o2_gat{j}", [2 * 128 * 2, O2C], BF16) for j in range(n_o2)]
    og_sv = [t.rearrange("(p k) t -> p k t", k=4) for t in og_src]
    og_gv = [t.rearrange("(r p k) t -> r p k t", r=2, k=4) for t in og_gat]
    x1_sv = [t.rearrange("(p k) t -> p k t", k=8) for t in x1_src]
    o2_sv = [t.rearrange("(p k) t -> p k t", k=2) for t in o2_src]
    o2_gv = [t.rearrange("(r p k) t -> r p k t", r=2, k=2) for t in o2_gat]

    A1 = {"xT": xT, "cT": cT, "wada": ext("g_wada", [16, 128, 8, 128]), "bada": ext("g_bada", [128, 16]),
          "gam": ext("g_gam", [128, 8]), "w_qkvz": ext("g_w_qkvz", [1, D, 2048]), "w_ab": ext("g_w_ab", [1, D, 8]),
          "convw": ext("g_convw", [1, 128, 12, 4]), "alog": ext("g_alog", [1, 128, 4]), "dtb": ext("g_dtb", [1, 128, 4]),
          "onorm": ext("g_onorm", [128, 128]), "consts": ext("g_consts", [128, 13, 128])}
    A2 = {"xT": xTh, "cT": cT, "w_o": ext("f0_w_o", [1024, D]), "wada": ext("f0_wada", [32, 128, 8, 128]),
          "bada": ext("f0_bada", [128, 32]), "gam": ext("f0_gam", [128, 8]), "w1": ext("f0_w1", [D, 2 * FH]),
          "w2": ext("f0_w2", [FH, D])}
    A3 = {"cT": cT, "wada": ext("a_wada", [16, 128, 8, 128]), "bada": ext("a_bada", [128, 16]),
          "gam": ext("a_gam", [128, 8]), "w_qkv": ext("a_w_qkv", [2, D, 1152]), "gains": ext("a_gains", [128, 2]),
          "biasT": ext("a_biasT", [2, 128, 6, 256]), "consts": ext("a_consts", [128, 2, 256])}
    A4 = {"cT": cT, "w_o": ext("f1_w_o", [512, D]), "wada": ext("f1_wada", [32, 128, 8, 128]),
          "bada": ext("f1_bada", [128, 32]), "gam": ext("f1_gam", [128, 8]), "w1": ext("f1_w1", [D, 2 * FH]),
          "w2": ext("f1_w2", [FH, D]), "outT": outT}

    class Xchg:
        def __init__(self, m, srcs, gats, need):
            self.m, self.srcs, self.gats, self.need = m, srcs, gats, need
            self.bufs = {j: [] for j in range(len(srcs))}

        def stored(self, j, buf):
            self.bufs[j].append(buf)
            if len(self.bufs[j]) == self.need:
                self.m.cc_allgather(self.srcs[j], self.gats[j], PAIRS, reads=self.bufs[j], writes=[self.m.buf("gat")])

    def make_og_load(gv, kper, chunk, msel):
        def og_load(m, ogbs, Bogbs, t0, T):
            A_, B_ = ogbs[0], ogbs[1]
            for r in range(2):
                ja, oa = divmod(t0, chunk)
                jb, ob_ = divmod(HT + t0, chunk)
                m.dma("gpsimd", A_[:, r * kper:(r + 1) * kper, :], gv[ja][r][:, :, oa:oa + T], writes=[Bogbs[0]])
                m.dma("sync", B_[:, r * kper:(r + 1) * kper, :], gv[jb][r][:, :, ob_:ob_ + T], writes=[Bogbs[1]])
            sel, Bsel = msel["sel"]
            m.ts("vector", A_[:], A_[:], sel[:, 0:1], None, ALU.mult, None, reads=[Bogbs[0], Bsel], writes=[Bogbs[0]])
            m.stt("vector", A_[:], B_[:], sel[:, 1:2], A_[:], ALU.mult, ALU.add, reads=[Bogbs[0], Bogbs[1], Bsel], writes=[Bogbs[0]])
            return A_, Bogbs[0]
        return og_load

    def load_sel(m, msel):
        sel = m.sbuf("sel", [128, 2], F32); Bsel = m.buf("sel")
        m.dma("sync", sel[:], sel_d, writes=[Bsel])
        msel["sel"] = (sel, Bsel)

    m = MK(nc, sem_es=ses, tag="p1_")
    xc = Xchg(m, og_src, og_gat, OGC // 128)

    def og_store1(m_, ot, ps, ch, Bot):
        j, o = divmod(ch * 128, OGC)
        bo = m_.buf("out")
        m_.dma("sync", og_sv[j][:, :, o:o + 128], ot[:], reads=[Bot], writes=[bo])
        xc.stored(j, bo)
        return bo
    A1["og_store"] = og_store1
    emit_gdn(m, A1, NT, 1)
    m.build()
    m = MK(nc, sem_es=ses, tag="p2_")
    ms = {}; load_sel(m, ms)
    xc2 = Xchg(m, h1_src, h1_gat, H1C // 256)

    def h_store2(m_, h_t, t0, T, Bh):
        j, o = divmod(t0, H1C)
        bo = m_.buf("out")
        m_.dma("gpsimd", h1_sv[j][:, :, o:o + T], h_t[:], reads=[Bh], writes=[bo])
        xc2.stored(j, bo)
        return bo
    A2["h_extra"] = {"wada": A3["wada"], "bada": A3["bada"], "gam": A3["gam"], "store": h_store2}
    A2["og_load"] = make_og_load(og_gv, 4, OGC, ms)

    def out_store2(m_, x_t, t0, T, Bx):
        j, o = divmod(t0, X1C)
        bo = m_.buf("out")
        m_.dma("sync", x1_sv[j][:, :, o:o + T], x_t[:], reads=[Bx], writes=[bo])
        return bo
    A2["out_store"] = out_store2
    emit_ffn(m, A2, HT, 1024)
    m.build()
    m = MK(nc, sem_es=ses, tag="p3_")
    xc3 = Xchg(m, o2_src, o2_gat, 2 * 2 * (O2C // 2048))

    def h_load3(m_, hU, u, BhU):
        for q in range(2048 // H1C):
            t0 = u * 2048 + q * H1C
            r, tl = divmod(t0, HT)
            j = tl // H1C
            m_.dma("sync" if q % 2 else "gpsimd", hU[:, :, q * H1C:(q + 1) * H1C], h1_gv[j][r], writes=[BhU])
    A3["h_load"] = h_load3

    def og_store3(m_, ob, pp, hl, u, Bob):
        j, o = divmod(u * 2048, O2C)
        bo = m_.buf("out")
        m_.dma("sync", o2_sv[j][hl * 64:(hl + 1) * 64, pp, o:o + 2048], ob[:], reads=[Bob], writes=[bo])
        xc3.stored(j, bo)
        return bo
    A3["og_store"] = og_store3
    emit_attn(m, A3, NT, 2)
    m.build()
    m = MK(nc, sem_es=ses, tag="p4_")
    ms = {}; load_sel(m, ms)
    A4["og_load"] = make_og_load(o2_gv, 2, O2C, ms)

    def x_load4(m_, x_t, t0, T, Bx):
        j, o = divmod(t0, X1C)
        m_.dma("sync", x_t[:], x1_sv[j][:, :, o:o + T], writes=[Bx])
    A4["x_load"] = x_load4
    emit_ffn(m, A4, HT, 512)
    m.build()
    ses.close()
    return nc


_PROGS = {}


def _prog(key, fn):
    if key not in _PROGS:
        _PROGS[key] = fn()
    return _PROGS[key]


def _f32(a):
    return np.ascontiguousarray(np.asarray(a, dtype=np.float32))


def kernel(x, c, w_ada, b_ada, norm_mix, norm_ffn, w_ffn_in, w_ffn_out,
           gdn_w_in, gdn_conv, gdn_a_log, gdn_dt_bias, gdn_out_norm, gdn_w_out,
           dsw_w_in, dsw_q_norm, dsw_k_norm, dsw_w_out, rel_bias):
    (x, c, w_ada, b_ada, norm_mix, norm_ffn, w_ffn_in, w_ffn_out, gdn_w_in, gdn_conv, gdn_a_log, gdn_dt_bias,
     gdn_out_norm, gdn_w_out, dsw_w_in, dsw_q_norm, dsw_k_norm, dsw_w_out, rel_bias) = map(_f32, (
        x, c, w_ada, b_ada, norm_mix, norm_ffn, w_ffn_in, w_ffn_out, gdn_w_in, gdn_conv, gdn_a_log, gdn_dt_bias,
        gdn_out_norm, gdn_w_out, dsw_w_in, dsw_q_norm, dsw_k_norm, dsw_w_out, rel_bias))
    nc = _prog("fused8", build_fused8)
    maps = []
    for b in range(BATCH):
        xTb = lay_xT(x[b])
        for hh in range(2):
            g = gdn_inputs([hh], None, c[b], w_ada[0], b_ada[0], norm_mix[0], gdn_w_in[0], gdn_conv[0], gdn_a_log[0],
                           gdn_dt_bias[0], gdn_out_norm[0], xT=0)
            a = attn_inputs([2 * hh, 2 * hh + 1], None, c[b], w_ada[1], b_ada[1], norm_mix[1], dsw_w_in[0], dsw_q_norm[0],
                            dsw_k_norm[0], rel_bias)
            d_ = {}
            for k in ("wada", "bada", "gam", "w_qkvz", "w_ab", "convw", "alog", "dtb", "onorm", "consts"):
                d_["g_" + k] = g[k]
            for k in ("wada", "bada", "gam", "w_qkv", "gains", "biasT", "consts"):
                d_["a_" + k] = a[k]
            for L, w_o in ((0, gdn_w_out[0]), (1, dsw_w_out[0])):
                p = f"f{L}_"
                d_[p + "w_o"] = w_o
                d_[p + "wada"] = lay_wada(w_ada[L], 2048, 6144)
                d_[p + "bada"] = lay_vec(b_ada[L][2048:])
                d_[p + "gam"] = lay_vec(norm_ffn[L])
                d_[p + "w1"] = w_ffn_in[L]
                d_[p + "w2"] = w_ffn_out[L]
            d_["xT"] = xTb
            d_["xTh"] = np.ascontiguousarray(xTb[:, :, hh * (SEQ // 2):(hh + 1) * (SEQ // 2)])
            d_["cT"] = lay_vec(c[b])
            sel = np.zeros((128, 2), np.float32); sel[:, hh] = 1.0
            d_["sel"] = sel
            maps.append(d_)
    res = run_bass_kernel_spmd(nc, maps, core_ids=list(range(8))).results
    out = np.empty((BATCH, SEQ, D), np.float32)
    for b in range(BATCH):
        for hh in range(2):
            out[b, hh * (SEQ // 2):(hh + 1) * (SEQ // 2)] = unlay_xT(np.asarray(res[b * 2 + hh]["outT"]))
    return out


def kernel_4core(x, c, w_ada, b_ada, norm_mix, norm_ffn, w_ffn_in, w_ffn_out,
           gdn_w_in, gdn_conv, gdn_a_log, gdn_dt_bias, gdn_out_norm, gdn_w_out,
           dsw_w_in, dsw_q_norm, dsw_k_norm, dsw_w_out, rel_bias):
    (x, c, w_ada, b_ada, norm_mix, norm_ffn, w_ffn_in, w_ffn_out, gdn_w_in, gdn_conv, gdn_a_log, gdn_dt_bias,
     gdn_out_norm, gdn_w_out, dsw_w_in, dsw_q_norm, dsw_k_norm, dsw_w_out, rel_bias) = map(_f32, (
        x, c, w_ada, b_ada, norm_mix, norm_ffn, w_ffn_in, w_ffn_out, gdn_w_in, gdn_conv, gdn_a_log, gdn_dt_bias,
        gdn_out_norm, gdn_w_out, dsw_w_in, dsw_q_norm, dsw_k_norm, dsw_w_out, rel_bias))
    nc = _prog("fused", build_fused)
    g = gdn_inputs([0, 1], None, c[0], w_ada[0], b_ada[0], norm_mix[0], gdn_w_in[0], gdn_conv[0], gdn_a_log[0],
                   gdn_dt_bias[0], gdn_out_norm[0], xT=0)
    a = attn_inputs([0, 1, 2, 3], None, c[0], w_ada[1], b_ada[1], norm_mix[1], dsw_w_in[0], dsw_q_norm[0], dsw_k_norm[0], rel_bias)
    shared = {}
    for k in ("wada", "bada", "gam", "w_qkvz", "w_ab", "convw", "alog", "dtb", "onorm", "consts"):
        shared["g_" + k] = g[k]
    for k in ("wada", "bada", "gam", "w_qkv", "gains", "biasT", "consts"):
        shared["a_" + k] = a[k]
    for L, w_o in ((0, gdn_w_out[0]), (1, dsw_w_out[0])):
        p = f"f{L}_"
        shared[p + "w_o"] = w_o
        shared[p + "wada"] = lay_wada(w_ada[L], 2048, 6144)
        shared[p + "bada"] = lay_vec(b_ada[L][2048:])
        shared[p + "gam"] = lay_vec(norm_ffn[L])
        shared[p + "w1"] = w_ffn_in[L]
        shared[p + "w2"] = w_ffn_out[L]
    maps = []
    for b in range(BATCH):
        d_ = dict(shared)
        d_["xT"] = lay_xT(x[b])
        d_["cT"] = lay_vec(c[b])
        maps.append(d_)
    res = run_bass_kernel_spmd(nc, maps, core_ids=list(range(BATCH))).results
    out = np.empty((BATCH, SEQ, D), np.float32)
    for b in range(BATCH):
        out[b] = unlay_xT(np.asarray(res[b]["outT"]))
    return out
```

```python
import numpy as np
import ml_dtypes
from contextlib import ExitStack
import concourse.bass as bass
import concourse.mybir as mybir
from concourse.bass_utils import run_bass_kernel_spmd

F32 = mybir.dt.float32
BF16 = mybir.dt.bfloat16
AF = mybir.ActivationFunctionType
ALU = mybir.AluOpType
AX = mybir.AxisListType
NPBF = ml_dtypes.bfloat16

D = 1024
SEQ = 8192
BATCH = 4
FH = 2816
RMS_EPS = 1e-6


class Buf:
    __slots__ = ("name", "last_w", "readers", "excl")

    def __init__(self, name, excl=False):
        self.name = name
        self.last_w = None
        self.readers = []
        self.excl = excl


class MK:
    ENGS = ("tensor", "vector", "scalar", "gpsimd", "sync")

    def __init__(self, nc, n_dma_sems=8, sem_es=None, tag=""):
        self.nc = nc
        self.es = ExitStack()
        self.tag = tag
        self.sem_es = sem_es
        ses = sem_es if sem_es is not None else self.es
        self.ops = {e: [] for e in self.ENGS}
        self.cnt = {e: 0 for e in self.ENGS}
        self.sem = {}
        for e in ("tensor", "vector", "scalar", "gpsimd"):
            self.sem[e] = ses.enter_context(nc.semaphore(tag + "s_" + e))
        self.dma_sems = {}
        self.dma_ring = {}
        for q in ("sync", "gpsimd"):
            self.dma_sems[q] = [ses.enter_context(nc.semaphore(f"{tag}d_{q}{i}")) for i in range(n_dma_sems)]
            self.dma_ring[q] = [0, [0] * n_dma_sems]
        self.known = {e: {} for e in self.ENGS}
        self._rr = 0

    def sbuf(self, name, shape, dt):
        return self.es.enter_context(self.nc.sbuf_tensor(self.tag + "sb_" + name, list(shape), dt))

    def psum(self, name, shape, dt):
        return self.es.enter_context(self.nc.psum_tensor(self.tag + "pp_" + name, list(shape), dt))

    def buf(self, name, excl=False):
        return Buf(name, excl)

    def bufs(self, name, n, excl=False):
        return [Buf(f"{name}{i}", excl) for i in range(n)]

    @staticmethod
    def _split(reads, writes):
        ex = [b for b in reads if b.excl]
        if ex:
            reads = [b for b in reads if not b.excl]
            writes = list(writes) + ex
        return reads, writes

    def _deps(self, reads, writes):
        deps = {}

        def add(d):
            if d is None:
                return
            k, v = d
            if deps.get(k, -1) < v:
                deps[k] = v
        for b in reads:
            add(b.last_w)
        for b in writes:
            add(b.last_w)
            for r in b.readers:
                add(r)
        return deps

    def _waits_for(self, eng, deps):
        waits = []
        kn = self.known[eng]
        for k, v in deps.items():
            if kn.get(k, 0) >= v:
                continue
            if k == ("e", eng) and v > self.cnt[eng]:
                continue
            kn[k] = v
            waits.append((k, v))
        return waits

    def op(self, eng, fn, reads=(), writes=(), inc=True):
        reads, writes = self._split(reads, writes)
        deps = self._deps(reads, writes)
        waits = self._waits_for(eng, deps)
        key = ("e", eng)
        if inc:
            self.cnt[eng] += 1
            c = self.cnt[eng]
            self.ops[eng].append((waits, fn, (key, 1)))
        else:
            c = self.cnt[eng] + 1
            self.ops[eng].append((waits, fn, None))
        tag = (key, c)
        for b in reads:
            b.readers.append(tag)
        for b in writes:
            b.last_w = tag
            b.readers = []
        return tag

    def dma(self, q, out, in_, reads=(), writes=()):
        reads, writes = self._split(reads, writes)
        deps = self._deps(reads, writes)
        ring = self.dma_ring[q]
        i = ring[0]
        ring[0] = (i + 1) % len(self.dma_sems[q])
        n_prev = ring[1][i]
        key = ("d", q, i)
        if n_prev > 0 and deps.get(key, -1) < 16 * n_prev:
            deps[key] = 16 * n_prev
        waits = self._waits_for(q, deps)
        ring[1][i] = n_prev + 1
        self.ops[q].append((waits, (lambda e, o=out, s=in_: e.dma_start(out=o, in_=s)), (key, 16)))
        tag = (key, 16 * (n_prev + 1))
        for b in reads:
            b.readers.append(tag)
        for b in writes:
            b.last_w = tag
            b.readers = []
        return tag

    def cc_allgather(self, src, dst, groups, reads=(), writes=()):
        if not hasattr(self, "cc_sem"):
            ses = self.sem_es if self.sem_es is not None else self.es
            self.cc_sem = ses.enter_context(self.nc.semaphore(self.tag + "cc_sem"))
            self.cc_n = 0
        reads, writes = self._split(reads, writes)
        deps = self._deps(reads, writes)
        waits = self._waits_for("gpsimd", deps)
        self.cc_n += 1
        key = ("c",)
        self.ops["gpsimd"].append((waits, (lambda e: e.collective_compute("AllGather", ALU.bypass, replica_groups=groups,
                                                                        ins=[src.opt()], outs=[dst.opt()])), (key, 1)))
        tag = (key, self.cc_n)
        for b in reads:
            b.readers.append(tag)
        for b in writes:
            b.last_w = tag
            b.readers = []
        return tag

    def _semof(self, key):
        if key[0] == "c":
            return self.cc_sem
        if key[0] == "e":
            return self.sem[key[1]]
        return self.dma_sems[key[1]][key[2]]

    def final_wait(self, eng, bufs):
        deps = self._deps(bufs, ())
        waits = self._waits_for(eng, deps)
        self.ops[eng].append((waits, None, None))

    def barrier(self):
        deps = {}
        for e in ("tensor", "vector", "scalar", "gpsimd"):
            if self.cnt[e] > 0:
                deps[("e", e)] = self.cnt[e]
        for q, (nxt, counts) in self.dma_ring.items():
            for i, n in enumerate(counts):
                if n > 0:
                    deps[("d", q, i)] = 16 * n
        if getattr(self, "cc_n", 0) > 0:
            deps[("c",)] = self.cc_n
        for e in self.ENGS:
            waits = self._waits_for(e, dict(deps))
            waits = [(k, v) for (k, v) in waits]
            self.ops[e].append((waits, None, None))

    def build(self, barrier=True):
        nc = self.nc
        if barrier:
            self.barrier()
        with nc.Block() as block:
            def mk(ename):
                def body(e):
                    for waits, fn, inc in self.ops[ename]:
                        for k, v in waits:
                            e.wait_ge(self._semof(k), v)
                        if fn is not None:
                            ins = fn(e)
                            if inc is not None:
                                ins.then_inc(self._semof(inc[0]), inc[1])
                return body
            block.tensor(mk("tensor"))
            block.vector(mk("vector"))
            block.scalar(mk("scalar"))
            block.gpsimd(mk("gpsimd"))
            block.sync(mk("sync"))
        self.es.close()

    def mm(self, out, lhsT, rhs, start, stop, reads, writes, inc=None):
        if inc is None:
            inc = stop
        return self.op("tensor", lambda e: e.matmul(out, lhsT=lhsT, rhs=rhs, start=start, stop=stop),
                       reads, writes, inc=inc)

    def tr(self, out, in_, ident, reads, writes):
        return self.op("tensor", lambda e: e.transpose(out, in_, ident), reads, writes)

    def act(self, out, in_, func, reads, writes, bias=0.0, scale=1.0, accum_out=None):
        if accum_out is None:
            return self.op("scalar", lambda e: e.activation(out=out, in_=in_, func=func, bias=bias, scale=scale),
                           reads, writes)
        return self.op("scalar", lambda e: e.activation(out=out, in_=in_, func=func, bias=bias, scale=scale,
                                                        accum_out=accum_out), reads, writes)

    def tt(self, eng, out, in0, in1, op, reads, writes):
        return self.op(eng, lambda e: e.tensor_tensor(out=out, in0=in0, in1=in1, op=op), reads, writes)

    def ts(self, eng, out, in0, s1, s2, op0, op1, reads, writes):
        if s2 is None:
            return self.op(eng, lambda e: e.tensor_scalar(out=out, in0=in0, scalar1=s1, scalar2=None, op0=op0),
                           reads, writes)
        return self.op(eng, lambda e: e.tensor_scalar(out=out, in0=in0, scalar1=s1, scalar2=s2, op0=op0, op1=op1),
                       reads, writes)

    def stt(self, eng, out, in0, scalar, in1, op0, op1, reads, writes):
        return self.op(eng, lambda e: e.scalar_tensor_tensor(out=out, in0=in0, scalar=scalar, in1=in1,
                                                             op0=op0, op1=op1), reads, writes)

    def copy(self, eng, out, in_, reads, writes):
        if eng == "scalar":
            return self.op(eng, lambda e: e.copy(out=out, in_=in_), reads, writes)
        return self.op(eng, lambda e: e.tensor_copy(out=out, in_=in_), reads, writes)

    def memset(self, eng, ap, val, writes):
        return self.op(eng, lambda e: e.memset(ap, val), (), writes)

    def recip(self, out, in_, reads, writes):
        return self.op("vector", lambda e: e.reciprocal(out=out, in_=in_), reads, writes)

    def rr(self, engs=("vector", "scalar", "vector", "scalar", "vector", "gpsimd", "scalar", "vector")):
        self._rr += 1
        return engs[self._rr % len(engs)]


class Stage:
    def __init__(self, m, width=1024, n=3):
        self.m = m
        self.width = width
        self.t = [m.sbuf(f"stg{i}", [128, width], F32) for i in range(n)]
        self.b = m.bufs("stg", n)
        self.i = 0

    def add(self, view, buf):
        self.t.append(view)
        self.b.append(buf)

    def load_cast(self, dst_ap_fn, src_rows_ap, ncols, wbuf, queue="sync"):
        m = self.m
        for c0 in range(0, ncols, self.width):
            c1 = min(ncols, c0 + self.width)
            i = self.i
            self.i = (self.i + 1) % len(self.t)
            m.dma(queue, self.t[i][:, 0:c1 - c0], src_rows_ap[:, c0:c1], writes=[self.b[i]])
            eng = m.rr()
            m.copy(eng, dst_ap_fn(c0, c1), self.t[i][:, 0:c1 - c0], reads=[self.b[i]], writes=[wbuf])


def emit_mod(m, nchunk, wada_d, bada_d, cT_d, ps_mod, Bps, stg, sfx=""):
    cond = m.sbuf("cond" + sfx, [128, 8], F32)
    Bcond = m.buf("cond")
    m.dma("sync", cond[:], cT_d, writes=[Bcond])
    m.act(cond[:], cond[:], AF.Silu, reads=[Bcond], writes=[Bcond])
    bada = m.sbuf("bada" + sfx, [128, nchunk], F32)
    Bbada = m.buf("bada")
    m.dma("sync", bada[:], bada_d, writes=[Bbada])
    modsb = m.sbuf("modsb" + sfx, [128, nchunk], F32)
    Bmod = m.buf("mod")
    ns = len(stg.t)
    for j in range(nchunk):
        i = j % ns
        wv = stg.t[i][:, 0:1024].rearrange("p (k c) -> p k c", k=8)
        m.dma("gpsimd" if j % 2 else "sync", wv, wada_d[j], writes=[stg.b[i]])
        for kc in range(8):
            m.mm(ps_mod[:, j:j + 1], wv[:, kc, :], cond[:, kc:kc + 1], kc == 0, kc == 7,
                 reads=[stg.b[i], Bcond], writes=[Bps])
    m.tt("vector", modsb[:], ps_mod[:, 0:nchunk], bada[:], ALU.add, reads=[Bps, Bbada], writes=[Bmod])
    return modsb, Bmod


def emit_norm_mod(m, x_t, Bx, T, A, Bc, BAB, ones_bf, Bconst, sq, Bsq, ps_ss, Bps_ss, rs, Brs, tmp, Btmp, h, Bh):
    for kc in range(8):
        eng = "gpsimd" if kc % 2 else "scalar"
        if eng == "scalar":
            m.act(sq[:, kc, 0:T], x_t[:, kc, 0:T], AF.Square, reads=[Bx], writes=[Bsq])
        else:
            m.tt("gpsimd", sq[:, kc, 0:T], x_t[:, kc, 0:T], x_t[:, kc, 0:T], ALU.mult, reads=[Bx], writes=[Bsq])
    for kc in range(8):
        m.mm(ps_ss[:, 0:T], ones_bf[:], sq[:, kc, 0:T], kc == 0, kc == 7, reads=[Bsq, Bconst], writes=[Bps_ss])
    m.act(rs[:, 0:T], ps_ss[:, 0:T], AF.Ln, reads=[Bps_ss], writes=[Brs], bias=RMS_EPS, scale=1.0 / D)
    m.act(rs[:, 0:T], rs[:, 0:T], AF.Exp, reads=[Brs], writes=[Brs], scale=-0.5)
    for kc in range(8):
        i = kc % 2
        m.tt("vector" if kc % 2 else "gpsimd", tmp[i][:, 0:T], x_t[:, kc, 0:T], rs[:, 0:T], ALU.mult,
             reads=[Bx, Brs], writes=[Btmp[i]])
        m.act(h[:, kc, 0:T], tmp[i][:, 0:T], AF.Identity, reads=[Btmp[i], BAB], writes=[Bh],
              bias=Bc[:, kc:kc + 1], scale=A[:, kc:kc + 1])


def ffn_decl(nc, NT, KO, pre=""):
    KC = KO // 128
    A = {}
    A["xT"] = nc.dram_tensor(pre + "xT", [128, 8, NT], F32, kind="ExternalInput").ap()
    A["ogT"] = nc.dram_tensor(pre + "ogT", [128, KC, NT], BF16, kind="ExternalInput").ap()
    A["w_o"] = nc.dram_tensor(pre + "w_o", [KO, D], F32, kind="ExternalInput").ap()
    A["wada"] = nc.dram_tensor(pre + "wada", [32, 128, 8, 128], F32, kind="ExternalInput").ap()
    A["bada"] = nc.dram_tensor(pre + "bada", [128, 32], F32, kind="ExternalInput").ap()
    A["cT"] = nc.dram_tensor(pre + "cT", [128, 8], F32, kind="ExternalInput").ap()
    A["gam"] = nc.dram_tensor(pre + "gam", [128, 8], F32, kind="ExternalInput").ap()
    A["w1"] = nc.dram_tensor(pre + "w1", [D, 2 * FH], F32, kind="ExternalInput").ap()
    A["w2"] = nc.dram_tensor(pre + "w2", [FH, D], F32, kind="ExternalInput").ap()
    A["outT"] = nc.dram_tensor(pre + "outT", [128, 8, NT], F32, kind="ExternalOutput").ap()
    return A


def build_ffn(NT, KO, T=256):
    nc = bass.Bass("TRN2", target_bir_lowering=False)
    A = ffn_decl(nc, NT, KO)
    m = MK(nc)
    outs = emit_ffn(m, A, NT, KO, T)
    m.final_wait("sync", outs)
    m.build(barrier=False)
    return nc


def emit_ffn(m, A, NT, KO, T=256):
    KC = KO // 128
    HC = FH // 128
    xT_d, og_d, wo_d, wada_d, bada_d, cT_d, gam_d, w1_d, w2_d, out_d = [A.get(k) for k in
        ("xT", "ogT", "w_o", "wada", "bada", "cT", "gam", "w1", "w2", "outT")]

    ones_bf = m.sbuf("ones_bf", [128, 128], BF16)
    Bconst = m.buf("const")
    m.memset("vector", ones_bf[:], 1.0, writes=[Bconst])

    ps_mod = m.psum("ps_mod", [128, 512], F32); Bps_mod = m.buf("ps_mod", excl=True)
    ps_ss = m.psum("ps_ss", [128, 512], F32); Bps_ss = m.buf("ps_ss", excl=True)
    ps_a = [m.psum(f"ps_a{i}", [128, 512], F32) for i in range(3)]; Bps_a = m.bufs("ps_a", 3, excl=True)
    ps_b = [m.psum(f"ps_b{i}", [128, 512], F32) for i in range(2)]; Bps_b = m.bufs("ps_b", 2, excl=True)

    stg = Stage(m, width=1024, n=2)
    xt = [m.sbuf(f"xt{i}", [128, 8, T], F32) for i in range(2)]; Bxt = m.bufs("xt", 2)
    ogbs = [m.sbuf(f"ogb{i}", [128, KC, T], BF16) for i in range(2)]; Bogbs = m.bufs("ogb", 2)
    sq = m.sbuf("sq", [128, 8, T], BF16); Bsq = m.buf("sq")
    rs = m.sbuf("rs", [128, T], F32); Brs = m.buf("rs")
    tmp = [m.sbuf(f"tmp{i}", [128, T], F32) for i in range(2)]; Btmp = m.bufs("tmp", 2)
    h2 = m.sbuf("h2", [128, 8, T], BF16); Bh2 = m.buf("h2")
    actb = m.sbuf("actb", [128, HC, T], BF16); Bact = m.buf("act")
    sg = tmp; Bsg = Btmp
    if T * 8 >= 2048:
        for i in range(2):
            stg.add(xt[i][:].rearrange("p k t -> p (k t)")[:, 0:1024], Bxt[i])
        lend = [(h2, Bh2), (sq, Bsq), (actb, Bact)] + ([(ogbs[0], Bogbs[0]), (ogbs[1], Bogbs[1])] if KC == 8 else [])
        for t_, b_ in lend:
            v = t_[:].rearrange("p k t -> p (k t)")[:, 0:2048].bitcast(F32)
            stg.add(v, b_)
    modsb, Bmod = emit_mod(m, 32, wada_d, bada_d, cT_d, ps_mod, Bps_mod, stg)
    gam = m.sbuf("gam", [128, 8], F32); Bgam = m.buf("gam")
    m.dma("sync", gam[:], gam_d, writes=[Bgam])
    A2 = m.sbuf("A2", [128, 8], F32)
    BAB = m.buf("AB")
    m.stt("vector", A2[:], modsb[:, 16:24], 1.0, gam[:], ALU.add, ALU.mult, reads=[Bmod, Bgam], writes=[BAB])

    HX = A.get("h_extra")
    if HX is not None:
        modx, Bmodx = emit_mod(m, 16, HX["wada"], HX["bada"], cT_d, ps_mod, Bps_mod, stg, sfx="x")
        gamx = m.sbuf("gamx", [128, 8], F32); Bgamx = m.buf("gamx")
        m.dma("sync", gamx[:], HX["gam"], writes=[Bgamx])
        A1x = m.sbuf("A1x", [128, 8], F32); BABx = m.buf("ABx")
        m.stt("vector", A1x[:], modx[:, 8:16], 1.0, gamx[:], ALU.add, ALU.mult, reads=[Bmodx, Bgamx], writes=[BABx])
    wo = m.sbuf("wo", [128, KC, D], BF16); Bwo = m.buf("wo")
    w1 = m.sbuf("w1", [128, 8, 2 * FH], BF16); Bw1 = m.buf("w1")
    w2 = m.sbuf("w2", [128, HC, D], BF16); Bw2 = m.buf("w2")
    for kc in range(KC):
        stg.load_cast(lambda c0, c1, kc=kc: wo[:, kc, c0:c1], wo_d[kc * 128:(kc + 1) * 128, :], D, Bwo,
                      queue="sync" if kc % 2 else "gpsimd")
    for kc in range(8):
        stg.load_cast(lambda c0, c1, kc=kc: w1[:, kc, c0:c1], w1_d[kc * 128:(kc + 1) * 128, :], 2 * FH, Bw1,
                      queue="sync" if kc % 2 else "gpsimd")
    for hc in range(HC):
        stg.load_cast(lambda c0, c1, hc=hc: w2[:, hc, c0:c1], w2_d[hc * 128:(hc + 1) * 128, :], D, Bw2,
                      queue="sync" if hc % 2 else "gpsimd")

    Bouts = []

    ntile = NT // T
    pa = 0
    for it in range(ntile):
        t0 = it * T
        x_t = xt[it % 2]; Bx = Bxt[it % 2]
        if "x_load" in A:
            A["x_load"](m, x_t, t0, T, Bx)
        else:
            m.dma("sync", x_t[:], xT_d[:, :, t0:t0 + T], writes=[Bx])
        if "og_load" in A:
            ogb, Bogb = A["og_load"](m, ogbs, Bogbs, t0, T)
        else:
            ogb = ogbs[it % 2]; Bogb = Bogbs[it % 2]
            m.dma("gpsimd", ogb[:], og_d[:, :, t0:t0 + T], writes=[Bogb])
        for dc in range(8):
            p = ps_a[pa % 3]; Bp = Bps_a[pa % 3]; pa += 1
            for kc in range(KC):
                m.mm(p[:, 0:T], wo[:, kc, dc * 128:(dc + 1) * 128], ogb[:, kc, :], kc == 0, kc == KC - 1,
                     reads=[Bwo, Bogb], writes=[Bp])
            m.stt("vector", x_t[:, dc, :], p[:, 0:T], modsb[:, dc:dc + 1], x_t[:, dc, :], ALU.mult, ALU.add,
                  reads=[Bp, Bmod, Bx], writes=[Bx])
        emit_norm_mod(m, x_t, Bx, T, A2, modsb[:, 8:16], BAB, ones_bf, Bconst, sq, Bsq, ps_ss, Bps_ss,
                      rs, Brs, tmp, Btmp, h2, Bh2)
        for hc in range(HC):
            pg = ps_a[pa % 3]; Bpg = Bps_a[pa % 3]; pa += 1
            pu = ps_b[hc % 2]; Bpu = Bps_b[hc % 2]
            for kc in range(8):
                m.mm(pg[:, 0:T], w1[:, kc, hc * 128:(hc + 1) * 128], h2[:, kc, :], kc == 0, kc == 7,
                     reads=[Bw1, Bh2], writes=[Bpg])
            for kc in range(8):
                m.mm(pu[:, 0:T], w1[:, kc, FH + hc * 128:FH + (hc + 1) * 128], h2[:, kc, :], kc == 0, kc == 7,
                     reads=[Bw1, Bh2], writes=[Bpu])
            s = sg[hc % 2]; Bs = Bsg[hc % 2]
            m.act(s[:], pg[:, 0:T], AF.Silu, reads=[Bpg], writes=[Bs])
            m.tt("vector", actb[:, hc, :], pu[:, 0:T], s[:], ALU.mult, reads=[Bpu, Bs], writes=[Bact])
        for dc in range(8):
            p = ps_a[pa % 3]; Bp = Bps_a[pa % 3]; pa += 1
            for hc in range(HC):
                m.mm(p[:, 0:T], w2[:, hc, dc * 128:(dc + 1) * 128], actb[:, hc, :], hc == 0, hc == HC - 1,
                     reads=[Bw2, Bact], writes=[Bp])
            m.stt("vector", x_t[:, dc, :], p[:, 0:T], modsb[:, 24 + dc:25 + dc], x_t[:, dc, :], ALU.mult, ALU.add,
                  reads=[Bp, Bmod, Bx], writes=[Bx])
        if HX is not None:
            emit_norm_mod(m, x_t, Bx, T, A1x, modx[:, 0:8], BABx, ones_bf, Bconst, sq, Bsq, ps_ss, Bps_ss,
                          rs, Brs, tmp, Btmp, h2, Bh2)
            Bouts.append(HX["store"](m, h2, t0, T, Bh2))
        if "out_store" in A:
            Bouts.append(A["out_store"](m, x_t, t0, T, Bx))
        else:
            Bouts.append(m.buf("out"))
            m.dma("sync", out_d[:, :, t0:t0 + T], x_t[:], reads=[Bx], writes=[Bouts[-1]])
    return Bouts


def lay_xT(xb):
    F = xb.shape[1]
    return np.ascontiguousarray(xb.T.reshape(F // 128, 128, -1).transpose(1, 0, 2))


def unlay_xT(a):
    return np.ascontiguousarray(a.transpose(1, 0, 2).reshape(a.shape[1] * 128, -1).T)


def lay_wada(w, c0, c1):
    sel = w[:, c0:c1]
    n = sel.shape[1] // 128
    return np.ascontiguousarray(sel.reshape(8, 128, n, 128).transpose(2, 1, 0, 3))


def lay_vec(v):
    return np.ascontiguousarray(v.reshape(-1, 128).T)


_DBG = {}


def interleave(f, b, k=None):
    k = k or _DBG.get("ilk", 4)
    fa, ba = f is not None, b is not None
    while fa or ba:
        for _ in range(k):
            if fa:
                try:
                    next(f)
                except StopIteration:
                    fa = False
        if ba:
            try:
                next(b)
            except StopIteration:
                ba = False


def interleave3(f, b, p, k=4):
    fa, ba, pa = f is not None, b is not None, p is not None
    while fa or ba or pa:
        for _ in range(k):
            if fa:
                try:
                    next(f)
                except StopIteration:
                    fa = False
        if ba:
            try:
                next(b)
            except StopIteration:
                ba = False
        if pa:
            try:
                next(p)
            except StopIteration:
                pa = False


def gdn_consts():
    C = 128
    U = np.triu(np.ones((C, C), np.float32))
    SLm = np.tril(np.ones((C, C), np.float32), -1)
    mui = np.triu(np.ones((C, C), np.float32))
    mus = np.triu(np.ones((C, C), np.float32), 1)
    I = np.eye(C, dtype=np.float32)
    O = np.ones((C, C), np.float32)
    i = np.arange(C)[:, None]; j = np.arange(C)[None, :]
    BD16 = (i // 16 == j // 16).astype(np.float32)
    Ms = [((i // s == j // s) & ((i % s) >= s // 2) & ((j % s) < s // 2)).astype(np.float32) for s in (32, 64, 128)]
    MTs = [np.ascontiguousarray(M_.T) for M_ in Ms]
    return np.ascontiguousarray(np.stack([U, SLm, mui, mus, I, O, BD16] + Ms + MTs, axis=1))


def gdn_decl(nc, NT, NP, pre="", og_kind="ExternalOutput"):
    A = {}
    A["xT"] = nc.dram_tensor(pre + "xT", [128, 8, NT], F32, kind="ExternalInput").ap()
    A["wada"] = nc.dram_tensor(pre + "wada", [16, 128, 8, 128], F32, kind="ExternalInput").ap()
    A["bada"] = nc.dram_tensor(pre + "bada", [128, 16], F32, kind="ExternalInput").ap()
    A["cT"] = nc.dram_tensor(pre + "cT", [128, 8], F32, kind="ExternalInput").ap()
    A["gam"] = nc.dram_tensor(pre + "gam", [128, 8], F32, kind="ExternalInput").ap()
    A["w_qkvz"] = nc.dram_tensor(pre + "w_qkvz", [NP, D, 2048], F32, kind="ExternalInput").ap()
    A["w_ab"] = nc.dram_tensor(pre + "w_ab", [NP, D, 8], F32, kind="ExternalInput").ap()
    A["convw"] = nc.dram_tensor(pre + "convw", [NP, 128, 12, 4], F32, kind="ExternalInput").ap()
    A["alog"] = nc.dram_tensor(pre + "alog", [NP, 128, 4], F32, kind="ExternalInput").ap()
    A["dtb"] = nc.dram_tensor(pre + "dtb", [NP, 128, 4], F32, kind="ExternalInput").ap()
    A["onorm"] = nc.dram_tensor(pre + "onorm", [128, 128], F32, kind="ExternalInput").ap()
    A["consts"] = nc.dram_tensor(pre + "consts", [128, 13, 128], F32, kind="ExternalInput").ap()
    A["ogT"] = nc.dram_tensor(pre + "ogT", [128, NP * 4, NT], BF16, kind=og_kind).ap()
    return A


def build_gdn(NT, NP=1, T=512):
    nc = bass.Bass("TRN2", target_bir_lowering=False)
    A = gdn_decl(nc, NT, NP)
    m = MK(nc)
    outs = emit_gdn(m, A, NT, NP, T)
    m.final_wait("sync", outs)
    m.build(barrier=False)
    return nc


def emit_gdn(m, A, NT, NP, T=512):
    E2DT = BF16 if _DBG.get("e2_bf16") else F32
    NCH = NT // 128
    xT_d, wada_d, bada_d, cT_d, gam_d, w_d, wab_d, convw_d, alog_d, dtb_d, onorm_d, const_d, og_d = [A.get(k) for k in
        ("xT", "wada", "bada", "cT", "gam", "w_qkvz", "w_ab", "convw", "alog", "dtb", "onorm", "consts", "ogT")]

    cst = m.sbuf("cst", [128, 13, 128], F32); Bconst = m.buf("const")
    m.dma("sync", cst[:], const_d, writes=[Bconst])
    Um, SLm, MUI, MUS, IDf, ONEf, BD16 = [cst[:, i, :] for i in range(7)]
    MN = [cst[:, 7 + i, :] for i in range(3)]
    MT = [cst[:, 10 + i, :] for i in range(3)]
    ones_bf = m.sbuf("ones_bf", [128, 128], BF16)
    id_bf = m.sbuf("id_bf", [128, 128], BF16)
    m.memset("vector", ones_bf[:], 1.0, writes=[Bconst])
    m.copy("vector", id_bf[:], IDf, reads=[Bconst], writes=[Bconst])
    convw = m.sbuf("convw", [128, 12, 4], F32)
    negA = m.sbuf("negA", [128, 4], F32)
    dtb = m.sbuf("dtb", [128, 4], F32)
    Bpw = m.buf("passw")
    onorm = m.sbuf("onorm", [128, 128], F32)
    m.dma("sync", onorm[:], onorm_d, writes=[Bconst])

    ps_big = [m.psum(f"big{i}", [128, 512], F32) for i in range(2)]; Bbig = m.bufs("big", 2, excl=True)
    ps_q4 = [m.psum(f"q4_{i}", [128, 4, 128], F32) for i in range(4)]; Bq4 = m.bufs("q4", 4, excl=True)
    qslots = [(ps_q4[i][:, j, :], Bq4[i]) for j in range(4) for i in range(4)]
    ps_tb = [m.psum(f"tb{i}", [128, 8, 128], BF16) for i in range(2)]; Btb = m.bufs("tb", 2, excl=True)
    tslots = [(ps_tb[i][:, j, :], Btb[i]) for j in range(8) for i in range(2)]
    cnt = {"big": 0, "q": 0, "t": 0}

    def big():
        i = cnt["big"] % 2; cnt["big"] += 1
        return ps_big[i], Bbig[i]

    bigB = [(ps_big[i], Bbig[i]) for i in range(2)] + [(ps_q4[i][:].rearrange("p a b -> p (a b)"), Bq4[i]) for i in range(4)]
    cnt["bigB"] = 0

    def bigb():
        i = cnt["bigB"] % len(bigB); cnt["bigB"] += 1
        return bigB[i]

    def qslot():
        i = cnt["q"] % len(qslots); cnt["q"] += 1
        return qslots[i]

    def tslot():
        i = cnt["t"] % len(tslots); cnt["t"] += 1
        return tslots[i]

    stg = Stage(m, width=1024, n=2)
    pm, Bpm = big()
    modsb, Bmod = emit_mod(m, 16, wada_d, bada_d, cT_d, pm, Bpm, stg)
    gam = m.sbuf("gam", [128, 8], F32); Bgam = m.buf("gam")
    m.dma("sync", gam[:], gam_d, writes=[Bgam])
    A1 = m.sbuf("A1", [128, 8], F32); BAB = m.buf("AB")
    m.stt("vector", A1[:], modsb[:, 8:16], 1.0, gam[:], ALU.add, ALU.mult, reads=[Bmod, Bgam], writes=[BAB])

    w = m.sbuf("w", [128, 8, 2048], BF16); Bw = m.buf("w")
    wab = m.sbuf("wab", [128, 8, 8], BF16)

    xt = [m.sbuf(f"xt{i}", [128, 8, T // 2], F32) for i in range(2)]; Bxt = m.bufs("xt", 2)
    sq = m.sbuf("sq", [128, 8, T], BF16); Bsq = m.buf("sq")
    rs = m.sbuf("rs", [128, T], F32); Brs = m.buf("rs")
    tmp = [m.sbuf(f"tmp{i}", [128, T], F32) for i in range(2)]; Btmp = m.bufs("tmp", 2)
    h = m.sbuf("h", [128, 8, T], BF16); Bh = m.buf("h")
    pcb = [m.sbuf(f"pcb{j}", [128, T + 3], BF16) for j in range(12)]; Bpcb = m.bufs("pcb", 12)
    diag = m.sbuf("diag", [128, 48, 128], BF16); Bdiag = m.buf("diag")
    sqq = [m.sbuf(f"sqq{i}", [128, T], BF16) for i in range(2)]; Bsqq = m.bufs("sqq", 2)
    rn = [m.sbuf(f"rn{i}", [128, T], F32) for i in range(2)]; Brn = m.bufs("rn", 2)
    qkn = m.sbuf("qkn", [128, 8, T], BF16); Bqkn = m.bufs("qkn", 8)
    vT = m.sbuf("vT", [128, 4, T], BF16); BvT = m.bufs("vT", 4)
    gz = m.sbuf("gz", [128, T // 128, 512], BF16); Bgz = m.bufs("gz", T // 128)
    absm = m.sbuf("absm", [128, 8], F32); Bab = m.buf("ab")
    sc1 = m.sbuf("sc1", [128, 4], F32); sc2 = m.sbuf("sc2", [128, 4], F32); Bsc = m.buf("sc")
    graw = m.sbuf("graw", [128, 4], F32); Bgraw = m.buf("graw")
    betaP = [m.sbuf(f"beta{i}", [128, 4], F32) for i in range(3)]; BbetaP = m.bufs("beta", 3)
    negbP = [m.sbuf(f"negb{i}", [128, 4], F32) for i in range(2)]; BnegbP = m.bufs("negb", 2)
    GU = m.sbuf("GU", [128, 4, 128], F32); BGU = m.buf("GU")
    gsb = m.sbuf("gsb", [128, 8], F32); Bgsb = m.buf("gsb")
    egP = [m.sbuf(f"eg{i}", [128, 4], F32) for i in range(3)]; negegP = [m.sbuf(f"negeg{i}", [128, 4], F32) for i in range(3)]
    eglP = [m.sbuf(f"egl{i}", [128, 4], F32) for i in range(3)]; BscaP = m.bufs("sca", 3)
    edlP = [m.sbuf(f"edl{i}", [128, 4], F32) for i in range(2)]; BedlP = m.bufs("edl", 2)
    GT = m.sbuf("GT", [128, 4, 128], F32); BGT = m.buf("GT")
    GTuiP = [m.sbuf(f"GTui{i}", [128, 4, 128], F32) for i in range(2)]; GTusP = [m.sbuf(f"GTus{i}", [128, 4, 128], F32) for i in range(2)]
    BGTmP = m.bufs("GTm", 2)
    vtP = [[m.sbuf(f"vt{p}_{i}", [128, 128], F32) for i in range(4)] for p in range(2)]; BvtP = [m.bufs(f"vt{p}_", 4) for p in range(2)]
    kdecP = [[m.sbuf(f"kdec{p}_{i}", [128, 128], BF16) for i in range(4)] for p in range(2)]; BkdecP = [m.bufs(f"kdec{p}_", 4) for p in range(2)]
    Pm = [[m.sbuf(f"P{i}_{r}", [128, 128], E2DT) for r in range(2)] for i in range(4)]
    Qm = [[m.sbuf(f"Q{i}_{r}", [128, 128], E2DT) for r in range(2)] for i in range(4)]
    Wm = [[m.sbuf(f"W{i}_{r}", [128, 128], E2DT) for r in range(2)] for i in range(4)]
    Dm = [[m.sbuf(f"Dm{i}_{r}", [128, 128], E2DT) for r in range(2)] for i in range(4)]
    BD = [m.bufs(f"Dm{i}_", 2) for i in range(4)]
    Qf = [m.sbuf(f"Qf{i}", [128, 128], E2DT) for i in range(4)]; BQf = m.bufs("Qf", 4)
    Pf = [m.sbuf(f"Pf{i}", [128, 128], E2DT) for i in range(4)]; BPf = m.bufs("Pf", 4)
    Yt = [m.sbuf(f"Yt{i}", [128, 128], E2DT) for i in range(4)]; BYt = m.bufs("Yt", 4)
    Ym = [m.sbuf(f"Ym{i}", [128, 128], E2DT) for i in range(4)]; BYm = m.bufs("Ym", 4)
    BP = [m.bufs(f"P{i}_", 2) for i in range(4)]
    BQ = [m.bufs(f"Q{i}_", 2) for i in range(4)]
    BW = [m.bufs(f"W{i}_", 2) for i in range(4)]
    WbP = [[m.sbuf(f"Wb{p}_{i}", [128, 128], BF16) for i in range(4)] for p in range(2)]; BWbP = [m.bufs(f"Wb{p}_", 4) for p in range(2)]
    ATP = [[m.sbuf(f"AT{p}_{i}", [128, 128], BF16) for i in range(4)] for p in range(2)]; BATP = [m.bufs(f"AT{p}_", 4) for p in range(2)]
    Rm = [m.sbuf(f"Rm{i}", [128, 128], BF16) for i in range(4)]; BRm = m.bufs("Rm", 4)
    vnew = [m.sbuf(f"vnew{i}", [128, 128], BF16) for i in range(4)]; Bvnew = m.bufs("vnew", 4)
    o1e = [m.sbuf(f"o1e{i}", [128, 128], F32) for i in range(4)]; Bo1e = m.bufs("o1e", 4)
    osb = [m.sbuf(f"osb{i}", [128, 128], F32) for i in range(4)]; Bosb = m.bufs("osb", 4)
    junk = m.sbuf("junk", [128, 128], F32); Bjunk = m.buf("junk")
    ssum = m.sbuf("ssum", [128, 4], F32); Bssum = m.buf("ssum")
    rno = m.sbuf("rno", [128, 4], F32); Brno = m.buf("rno")
    Sf = [m.sbuf(f"Sf{i}", [128, 128], F32) for i in range(4)]; BSf = m.bufs("Sf", 4)
    Sb = [m.sbuf(f"Sb{i}", [128, 128], BF16) for i in range(4)]; BSb = m.bufs("Sb", 4)
    ogt = [m.sbuf(f"ogt{i}", [128, 512], BF16) for i in range(2)]; Bogt = m.bufs("ogt", 2)
    ogTt = [m.sbuf(f"ogTt{i}", [128, 4, 128], BF16) for i in range(2)]; BogTt = m.bufs("ogTt", 2)
    Bouts = []
    ntile = NT // T
    for ps, it in [(ps, it) for ps in range(NP) for it in range(ntile)]:
        if it == 0:
            m.dma("sync", convw[:], convw_d[ps], writes=[Bpw])
            m.dma("sync", negA[:], alog_d[ps], writes=[Bpw])
            m.act(negA[:], negA[:], AF.Exp, reads=[Bpw], writes=[Bpw])
            m.ts("vector", negA[:], negA[:], -1.0, None, ALU.mult, None, reads=[Bpw], writes=[Bpw])
            m.dma("sync", dtb[:], dtb_d[ps], writes=[Bpw])
            for kc in range(8):
                stg.load_cast(lambda c0, c1, kc=kc: w[:, kc, c0:c1], w_d[ps, kc * 128:(kc + 1) * 128, :], 2048, Bw,
                              queue="sync" if kc % 2 else "gpsimd")
                stg.load_cast(lambda c0, c1, kc=kc: wab[:, kc, c0:c1], wab_d[ps, kc * 128:(kc + 1) * 128, :], 8, Bw, queue="sync")
            for j in range(12):
                m.memset("gpsimd", pcb[j][:, 0:3], 0.0, writes=[Bpcb[j]])
                for tap in range(4):
                    m.ts("vector", diag[:, j * 4 + tap, :], id_bf[:], convw[:, j, tap:tap + 1], None, ALU.mult, None,
                         reads=[Bconst, Bpw], writes=[Bdiag])
            for i in range(4):
                m.memset("gpsimd", Sf[i][:], 0.0, writes=[BSf[i]])
                m.memset("gpsimd", Sb[i][:], 0.0, writes=[BSb[i]])
        t0 = it * T
        HTL = T // 2
        for hf in range(2):
            x_t = xt[hf]; Bx = Bxt[hf]
            m.dma("sync" if hf else "gpsimd", x_t[:], xT_d[:, :, t0 + hf * HTL:t0 + (hf + 1) * HTL], writes=[Bx])
            pn_, Bpn_ = big()
            emit_norm_mod(m, x_t, Bx, HTL, A1, modsb[:, 0:8], BAB, ones_bf, Bconst, sq, Bsq, pn_, Bpn_,
                          rs, Brs, tmp, Btmp, h[:, :, hf * HTL:(hf + 1) * HTL], Bh)
        st = {}

        def b_s1(j):
            p, Bp = bigb()
            for kc in range(8):
                m.mm(p[:, 0:T], w[:, kc, j * 128:(j + 1) * 128], h[:, kc, :], kc == 0, kc == 7, reads=[Bw, Bh], writes=[Bp])
            st[j] = (p, Bp)

        def b_s2(j):
            p, Bp = st[j]
            if it > 0:
                m.copy("vector", pcb[j][:, 0:3], pcb[j][:, T:T + 3], reads=[Bpcb[j]], writes=[Bpcb[j]])
            m.copy("vector" if j % 3 else "scalar", pcb[j][:, 3:T + 3], p[:, 0:T], reads=[Bp], writes=[Bpcb[j]])
            p2, Bp2 = bigb()
            for tap in range(4):
                m.mm(p2[:, 0:T], diag[:, j * 4 + tap, :], pcb[j][:, tap:tap + T], tap == 0, tap == 3,
                     reads=[Bdiag, Bpcb[j]], writes=[Bp2])
            st[j] = (p2, Bp2)

        def b_s3(j):
            p2, Bp2 = st[j]
            if j >= 8:
                m.act(vT[:, j - 8, :], p2[:, 0:T], AF.Silu, reads=[Bp2], writes=[BvT[j - 8]])
            else:
                m.act(qkn[:, j, :], p2[:, 0:T], AF.Silu, reads=[Bp2], writes=[Bqkn[j]])

        for j in range(12 + 2):
            if j < 12:
                b_s1(j)
            if 1 <= j < 13:
                b_s2(j - 1)
            if j >= 2:
                b_s3(j - 2)
        for c in range(T // 128):
            p, Bp = bigb()
            for kc in range(8):
                m.mm(p[:, :], h[:, kc, c * 128:(c + 1) * 128], w[:, kc, 1536:2048], kc == 0, kc == 7,
                     reads=[Bw, Bh], writes=[Bp])
            m.act(gz[:, c, :], p[:, :], AF.Silu, reads=[Bp], writes=[Bgz[c]])
            for hd in range(4):
                m.tt("gpsimd", gz[:, c, hd * 128:(hd + 1) * 128], gz[:, c, hd * 128:(hd + 1) * 128], onorm[:], ALU.mult,
                     reads=[Bgz[c], Bconst], writes=[Bgz[c]])
        st2 = {}

        def l_s1(j):
            s_ = sqq[j % 2]; Bs_ = Bsqq[j % 2]
            m.tt("vector", s_[:], qkn[:, j, :], qkn[:, j, :], ALU.mult, reads=[Bqkn[j]], writes=[Bs_])
            p2, Bp2 = bigb()
            m.mm(p2[:, 0:T], ones_bf[:], s_[:], True, True, reads=[Bconst, Bs_], writes=[Bp2])
            st2[j] = (p2, Bp2)

        def l_s2(j):
            p2, Bp2 = st2[j]
            r_ = rn[j % 2]; Br_ = Brn[j % 2]
            m.act(r_[:], p2[:, 0:T], AF.Ln, reads=[Bp2], writes=[Br_], bias=RMS_EPS, scale=1.0)
            m.act(r_[:], r_[:], AF.Exp, reads=[Br_], writes=[Br_], scale=-0.5)
            qscale = (128.0 ** -0.5) if j < 4 else 1.0
            m.stt("vector", qkn[:, j, :], qkn[:, j, :], qscale, r_[:], ALU.mult, ALU.mult, reads=[Bqkn[j], Br_], writes=[Bqkn[j]])

        for j in range(8 + 1):
            if j < 8:
                l_s1(j)
            if j >= 1:
                l_s2(j - 1)
        def chunk_pre(c):
            ch = it * (T // 128) + c
            csl = slice(c * 128, (c + 1) * 128)
            par = ch % 2; p3 = ch % 3
            eg, negeg, egl, beta, Bsca, Bbeta = egP[p3], negegP[p3], eglP[p3], betaP[p3], BscaP[p3], BbetaP[p3]
            negb, Bnegb, edl, Bedl = negbP[par], BnegbP[par], edlP[par], BedlP[par]
            GTui, GTus, BGTm = GTuiP[par], GTusP[par], BGTmP[par]
            vt, Bvt, kdec, Bkdec, AT, BAT, Wb, BWb = vtP[par], BvtP[par], kdecP[par], BkdecP[par], ATP[par], BATP[par], WbP[par], BWbP[par]
            H4 = range(4)
            qTs = [qkn[:, hd, csl] for hd in H4]; kTs = [qkn[:, 4 + hd, csl] for hd in H4]
            Bqs_ = [Bqkn[hd] for hd in H4]; Bks_ = [Bqkn[4 + hd] for hd in H4]
            pab, Bpab = qslot()
            for kc in range(8):
                m.mm(pab[:, 0:8], h[:, kc, csl], wab[:, kc, :], kc == 0, kc == 7, reads=[Bw, Bh], writes=[Bpab])
            m.copy("vector", absm[:], pab[:, 0:8], reads=[Bpab], writes=[Bab])
            m.tt("vector", sc1[:], absm[:, 0:4], dtb[:], ALU.add, reads=[Bab, Bpw], writes=[Bsc])
            m.ts("vector", sc2[:], sc1[:], -1.0, None, ALU.mult, None, reads=[Bsc], writes=[Bsc])
            m.tt("vector", sc2[:], sc2[:], sc1[:], ALU.max, reads=[Bsc], writes=[Bsc])
            m.act(sc2[:], sc2[:], AF.Exp, reads=[Bsc], writes=[Bsc], scale=-1.0)
            m.act(sc2[:], sc2[:], AF.Ln, reads=[Bsc], writes=[Bsc], bias=1.0)
            m.ts("vector", sc1[:], sc1[:], 0.0, None, ALU.max, None, reads=[Bsc], writes=[Bsc])
            m.tt("vector", sc1[:], sc1[:], sc2[:], ALU.add, reads=[Bsc], writes=[Bsc])
            m.tt("vector", graw[:], sc1[:], negA[:], ALU.mult, reads=[Bsc, Bpw], writes=[Bgraw])
            m.act(beta[:], absm[:, 4:8], AF.Exp, reads=[Bab], writes=[Bbeta], scale=-1.0)
            m.act(beta[:], beta[:], AF.Ln, reads=[Bbeta], writes=[Bbeta], bias=1.0)
            m.act(beta[:], beta[:], AF.Exp, reads=[Bbeta], writes=[Bbeta], scale=-1.0)
            m.ts("vector", negb[:], beta[:], -1.0, None, ALU.mult, None, reads=[Bbeta], writes=[Bnegb])
            for hd in range(4):
                m.tt("gpsimd", GU[:, hd, :], Um, graw[:, hd:hd + 1].to_broadcast([128, 128]), ALU.mult, reads=[Bconst, Bgraw], writes=[BGU])
            yield
            pg, Bpg = qslot()
            m.mm(pg[:, 0:4], Um, graw[:], True, True, reads=[Bconst, Bgraw], writes=[Bpg])
            m.mm(pg[:, 4:8], ONEf, graw[:], True, True, reads=[Bconst, Bgraw], writes=[Bpg])
            m.copy("vector", gsb[:], pg[:, 0:8], reads=[Bpg], writes=[Bgsb])
            m.act(eg[:], gsb[:, 0:4], AF.Exp, reads=[Bgsb], writes=[Bsca])
            m.ts("vector", negeg[:], eg[:], -1.0, None, ALU.mult, None, reads=[Bsca], writes=[Bsca])
            m.tt("vector", edl[:], gsb[:, 4:8], gsb[:, 0:4], ALU.subtract, reads=[Bgsb], writes=[Bedl])
            m.act(edl[:], edl[:], AF.Exp, reads=[Bedl], writes=[Bedl])
            m.act(egl[:], gsb[:, 4:8], AF.Exp, reads=[Bgsb], writes=[Bsca])
            yield
            pD, BpD = big()
            m.mm(pD[:, :], SLm, GU[:].rearrange("p h i -> p (h i)"), True, True, reads=[Bconst, BGU], writes=[BpD])
            m.act(GT[:].rearrange("p h i -> p (h i)"), pD[:, :], AF.Exp, reads=[BpD], writes=[BGT])
            for hd in range(4):
                m.tt("gpsimd", GTui[:, hd, :], GT[:, hd, :], MUI, ALU.mult, reads=[BGT, Bconst], writes=[BGTm])
                m.tt("gpsimd", GTus[:, hd, :], GT[:, hd, :], MUS, ALU.mult, reads=[BGT, Bconst], writes=[BGTm])
            yield

        def chunk_front(c):
            ch = it * (T // 128) + c
            csl = slice(c * 128, (c + 1) * 128)
            par = ch % 2; p3 = ch % 3
            eg, negeg, egl, beta, Bsca, Bbeta = egP[p3], negegP[p3], eglP[p3], betaP[p3], BscaP[p3], BbetaP[p3]
            negb, Bnegb, edl, Bedl = negbP[par], BnegbP[par], edlP[par], BedlP[par]
            GTui, GTus, BGTm = GTuiP[par], GTusP[par], BGTmP[par]
            vt, Bvt, kdec, Bkdec, AT, BAT, Wb, BWb = vtP[par], BvtP[par], kdecP[par], BkdecP[par], ATP[par], BATP[par], WbP[par], BWbP[par]
            H4 = range(4)
            qTs = [qkn[:, hd, csl] for hd in H4]; kTs = [qkn[:, 4 + hd, csl] for hd in H4]
            Bqs_ = [Bqkn[hd] for hd in H4]; Bks_ = [Bqkn[4 + hd] for hd in H4]
            H4 = range(4)
            qTs = [qkn[:, hd, csl] for hd in H4]; kTs = [qkn[:, 4 + hd, csl] for hd in H4]
            Bqs_ = [Bqkn[hd] for hd in H4]; Bks_ = [Bqkn[4 + hd] for hd in H4]
            yield
            sl1 = []
            yield
            for hd in H4:
                pt, Bpt = tslot()
                m.tr(pt, vT[:, hd, csl], id_bf[:], reads=[BvT[hd], Bconst], writes=[Bpt])
                pt2, Bpt2 = tslot()
                m.tr(pt2, kTs[hd], id_bf[:], reads=[Bks_[hd], Bconst], writes=[Bpt2])
                pkk, Bpkk = qslot()
                m.mm(pkk, kTs[hd], kTs[hd], True, True, reads=[Bks_[hd]], writes=[Bpkk])
                pqk, Bpqk = qslot()
                m.mm(pqk, kTs[hd], qTs[hd], True, True, reads=[Bks_[hd], Bqs_[hd]], writes=[Bpqk])
                sl1.append((pt, Bpt, pt2, Bpt2, pkk, Bpkk, pqk, Bpqk))
            yield
            for hd in H4:
                pt, Bpt, pt2, Bpt2, pkk, Bpkk, pqk, Bpqk = sl1[hd]
                m.copy("scalar", vt[hd][:], pt, reads=[Bpt], writes=[Bvt[hd]])
                m.ts("vector", kdec[hd][:], pt2, edl[:, hd:hd + 1], None, ALU.mult, None, reads=[Bpt2, Bedl], writes=[Bkdec[hd]])
                m.stt("vector", Qf[hd][:], pkk, negb[:, hd:hd + 1], GTus[:, hd, :], ALU.mult, ALU.mult,
                      reads=[Bpkk, Bnegb, BGTm], writes=[BQf[hd]])
                m.tt("vector", AT[hd][:], pqk, GTui[:, hd, :], ALU.mult, reads=[Bpqk, BGTm], writes=[BAT[hd]])
            yield
            sl2 = []
            yield
            for hd in H4:
                if E2DT == F32:
                    pp, Bpp = qslot()
                    m.tr(pp, Qf[hd][:], IDf, reads=[BQf[hd], Bconst], writes=[Bpp])
                else:
                    pp, Bpp = tslot()
                    m.tr(pp, Qf[hd][:], id_bf[:], reads=[BQf[hd], Bconst], writes=[Bpp])
                sl2.append((pp, Bpp))
            yield
            for hd in H4:
                pp, Bpp = sl2[hd]
                m.copy("scalar", Pf[hd][:], pp, reads=[Bpp], writes=[BPf[hd]])
                m.tt("gpsimd", Qm[hd][0][:], Qf[hd][:], BD16, ALU.mult, reads=[BQf[hd], Bconst], writes=[BQ[hd][0]])
                m.tt("gpsimd", Wm[hd][0][:], Qm[hd][0][:], IDf, ALU.add, reads=[BQ[hd][0], Bconst], writes=[BW[hd][0]])
            yield
            for hd in H4:
                m.tt("gpsimd", Pm[hd][0][:], Pf[hd][:], BD16, ALU.mult, reads=[BPf[hd], Bconst], writes=[BP[hd][0]])
            def tr_stage(src, Bsrc, dst, Bdst):
                sl_ = []
                for hd in H4:
                    pp_, Bpp_ = qslot()
                    m.tr(pp_, src[hd][:], IDf, reads=[Bsrc[hd], Bconst], writes=[Bpp_])
                    sl_.append((pp_, Bpp_))
                return sl_

            for lev in range(1, 4):
                r0 = (lev - 1) % 2; r1 = lev % 2
                yield
                sl = []
                for hd in H4:
                    pQ, BpQ = qslot()
                    m.mm(pQ, Pm[hd][r0][:], Qm[hd][r0][:], True, True, reads=[BQ[hd][r0], BP[hd][r0]], writes=[BpQ])
                    sl.append((pQ, BpQ))
                yield
                for hd in H4:
                    pQ, BpQ = sl[hd]
                    m.copy("vector" if hd % 2 else "scalar", Qm[hd][r1][:], pQ, reads=[BpQ], writes=[BQ[hd][r1]])
                yield
                sl = tr_stage([Qm[hd][r1] for hd in H4], [BQ[hd][r1] for hd in H4], None, None)
                yield
                for hd in H4:
                    pp_, Bpp_ = sl[hd]
                    m.copy("scalar" if hd % 2 else "vector", Pm[hd][r1][:], pp_, reads=[Bpp_], writes=[BP[hd][r1]])
                yield
                sl = []
                for hd in H4:
                    pW, BpW = qslot()
                    m.mm(pW, Pm[hd][r1][:], Wm[hd][r0][:], True, True, reads=[BP[hd][r1], BW[hd][r0]], writes=[BpW])
                    sl.append((pW, BpW))
                yield
                for hd in H4:
                    pW, BpW = sl[hd]
                    m.tt("vector", Wm[hd][r1][:], pW, Wm[hd][r0][:], ALU.add, reads=[BpW, BW[hd][r0]], writes=[BW[hd][r1]])
            yield
            sl = tr_stage([Wm[hd][1] for hd in H4], [BW[hd][1] for hd in H4], None, None)
            yield
            for hd in H4:
                pp_, Bpp_ = sl[hd]
                m.copy("scalar", Dm[hd][0][:], pp_, reads=[Bpp_], writes=[BD[hd][0]])
            for si in range(3):
                r0 = (3 + si) % 2; r1 = (4 + si) % 2
                d0 = si % 2; d1 = (si + 1) % 2
                yield
                sl = []
                for hd in H4:
                    pY, BpY = qslot()
                    m.mm(pY, Pf[hd][:], Wm[hd][r0][:], True, True, reads=[BPf[hd], BW[hd][r0]], writes=[BpY])
                    sl.append((pY, BpY))
                yield
                for hd in H4:
                    pY, BpY = sl[hd]
                    m.copy("scalar", Yt[hd][:], pY, reads=[BpY], writes=[BYt[hd]])
                    m.tt("gpsimd", Ym[hd][:], Yt[hd][:], MT[si], ALU.mult, reads=[BYt[hd], Bconst], writes=[BYm[hd]])
                yield
                sl = []
                for hd in H4:
                    pZ, BpZ = qslot()
                    m.mm(pZ, Dm[hd][d0][:], Ym[hd][:], True, True, reads=[BD[hd][d0], BYm[hd]], writes=[BpZ])
                    sl.append((pZ, BpZ))
                yield
                for hd in H4:
                    pZ, BpZ = sl[hd]
                    if si < 2:
                        m.tt("vector", Wm[hd][r1][:], pZ, Wm[hd][r0][:], ALU.add, reads=[BpZ, BW[hd][r0]], writes=[BW[hd][r1]])
                    else:
                        m.tt("vector", Wb[hd][:], pZ, Wm[hd][r0][:], ALU.add, reads=[BpZ, BW[hd][r0]], writes=[BWb[hd]])
                if si < 2:
                    yield
                    sl = tr_stage([Wm[hd][r1] for hd in H4], [BW[hd][r1] for hd in H4], None, None)
                    yield
                    for hd in H4:
                        pp_, Bpp_ = sl[hd]
                        m.copy("scalar", Dm[hd][d1][:], pp_, reads=[Bpp_], writes=[BD[hd][d1]])
            yield

        def chunk_back(c):
            ch = it * (T // 128) + c
            csl = slice(c * 128, (c + 1) * 128)
            par = ch % 2; p3 = ch % 3
            eg, negeg, egl, beta, Bsca, Bbeta = egP[p3], negegP[p3], eglP[p3], betaP[p3], BscaP[p3], BbetaP[p3]
            negb, Bnegb, edl, Bedl = negbP[par], BnegbP[par], edlP[par], BedlP[par]
            GTui, GTus, BGTm = GTuiP[par], GTusP[par], BGTmP[par]
            vt, Bvt, kdec, Bkdec, AT, BAT, Wb, BWb = vtP[par], BvtP[par], kdecP[par], BkdecP[par], ATP[par], BATP[par], WbP[par], BWbP[par]
            H4 = range(4)
            qTs = [qkn[:, hd, csl] for hd in H4]; kTs = [qkn[:, 4 + hd, csl] for hd in H4]
            Bqs_ = [Bqkn[hd] for hd in H4]; Bks_ = [Bqkn[4 + hd] for hd in H4]
            yield
            sl = []
            yield
            for hd in H4:
                pks, Bpks = qslot()
                m.mm(pks, kTs[hd], Sb[hd][:], True, True, reads=[Bks_[hd], BSb[hd]], writes=[Bpks])
                po1, Bpo1 = qslot()
                m.mm(po1, qTs[hd], Sb[hd][:], True, True, reads=[Bqs_[hd], BSb[hd]], writes=[Bpo1])
                sl.append((pks, Bpks, po1, Bpo1))
            yield
            for hd in H4:
                pks, Bpks, po1, Bpo1 = sl[hd]
                m.stt("vector", Rm[hd][:], pks, negeg[:, hd:hd + 1], vt[hd][:], ALU.mult, ALU.add,
                      reads=[Bpks, Bsca, Bvt[hd]], writes=[BRm[hd]])
                m.act(o1e[hd][:], po1, AF.Copy, reads=[Bpo1, Bsca], writes=[Bo1e[hd]], scale=eg[:, hd:hd + 1])
            yield
            sl = []
            yield
            for hd in H4:
                ptr, Bptr = qslot()
                m.mm(ptr, Wb[hd][:], Rm[hd][:], True, True, reads=[BWb[hd], BRm[hd]], writes=[Bptr])
                sl.append((ptr, Bptr))
            yield
            for hd in H4:
                ptr, Bptr = sl[hd]
                m.ts("vector", vnew[hd][:], ptr, beta[:, hd:hd + 1], None, ALU.mult, None, reads=[Bptr, Bbeta], writes=[Bvnew[hd]])
            yield
            sl = []
            yield
            for hd in H4:
                po2, Bpo2 = qslot()
                m.mm(po2, AT[hd][:], vnew[hd][:], True, True, reads=[BAT[hd], Bvnew[hd]], writes=[Bpo2])
                pS, BpS = qslot()
                m.mm(pS, kdec[hd][:], vnew[hd][:], True, True, reads=[Bkdec[hd], Bvnew[hd]], writes=[BpS])
                sl.append((po2, Bpo2, pS, BpS))
            yield
            for hd in H4:
                po2, Bpo2, pS, BpS = sl[hd]
                m.stt("vector", Sf[hd][:], Sf[hd][:], egl[:, hd:hd + 1], pS, ALU.mult, ALU.add,
                      reads=[BSf[hd], Bsca, BpS], writes=[BSf[hd]])
                m.copy("gpsimd", Sb[hd][:], Sf[hd][:], reads=[BSf[hd]], writes=[BSb[hd]])
                m.tt("vector", osb[hd][:], po2, o1e[hd][:], ALU.add, reads=[Bpo2, Bo1e[hd]], writes=[Bosb[hd]])
                m.act(junk[:], osb[hd][:], AF.Square, reads=[Bosb[hd]], writes=[Bjunk, Bssum], accum_out=ssum[:, hd:hd + 1])
            yield
            m.act(rno[:], ssum[:], AF.Ln, reads=[Bssum], writes=[Brno], bias=RMS_EPS, scale=1.0 / 128)
            m.act(rno[:], rno[:], AF.Exp, reads=[Brno], writes=[Brno], scale=-0.5)
            og_t = ogt[ch % 2]; Bog_t = Bogt[ch % 2]
            for hd in range(4):
                m.stt("vector", og_t[:, hd * 128:(hd + 1) * 128], osb[hd][:], rno[:, hd:hd + 1], gz[:, c, hd * 128:(hd + 1) * 128],
                      ALU.mult, ALU.mult, reads=[Bosb[hd], Brno, Bgz[c]], writes=[Bog_t])
            ot = ogTt[ch % 2]; Bot = BogTt[ch % 2]
            for hd in range(4):
                pt3, Bpt3 = tslot()
                m.tr(pt3, og_t[:, hd * 128:(hd + 1) * 128], id_bf[:], reads=[Bog_t, Bconst], writes=[Bpt3])
                m.copy("scalar", ot[:, hd, :], pt3, reads=[Bpt3], writes=[Bot])
            if "og_store" in A:
                Bouts.append(A["og_store"](m, ot, ps, ch, Bot))
            else:
                Bouts.append(m.buf("out"))
                m.dma("sync", og_d[:, ps * 4:(ps + 1) * 4, ch * 128:(ch + 1) * 128], ot[:], reads=[Bot], writes=[Bouts[-1]])

            yield

        NCK = 0 if _DBG.get("skip_chunks") else T // 128
        prev_back = None
        for c in range(NCK):
            if c == 0:
                interleave(chunk_pre(0), None)
            nxt = chunk_pre(c + 1) if c + 1 < NCK else None
            interleave3(chunk_front(c), prev_back, nxt)
            prev_back = chunk_back(c)
        interleave(None, prev_back)
    return Bouts


def gdn_inputs(hhs, xb, cb, wada0, bada0, gam, w_in, conv, a_log, dtb, onorm, xT=None):
    W, WAB, CW, AL, DT = [], [], [], [], []
    for hh in hhs:
        hs = [hh * 4 + i for i in range(4)]
        cols = []
        for tsr in range(4):
            for hd in hs:
                cols.extend(range(tsr * 1024 + hd * 128, tsr * 1024 + (hd + 1) * 128))
        W.append(w_in[:, cols])
        WAB.append(w_in[:, [4096 + hd for hd in hs] + [4104 + hd for hd in hs]])
        cw = np.stack([conv[:, tsr * 1024 + hd * 128: tsr * 1024 + (hd + 1) * 128] for tsr in range(3) for hd in hs], 0)
        CW.append(cw.transpose(2, 0, 1))
        AL.append(np.broadcast_to(a_log[hs][None, :], (128, 4)))
        DT.append(np.broadcast_to(dtb[hs][None, :], (128, 4)))
    c_ = lambda l: np.ascontiguousarray(np.stack(l, 0), dtype=np.float32)
    return {
        "xT": lay_xT(xb) if xT is None else xT, "wada": lay_wada(wada0, 0, 2048), "bada": lay_vec(bada0[0:2048]), "cT": lay_vec(cb),
        "gam": lay_vec(gam), "w_qkvz": c_(W), "w_ab": c_(WAB), "convw": c_(CW), "alog": c_(AL), "dtb": c_(DT),
        "onorm": np.ascontiguousarray(np.broadcast_to(onorm[None, :], (128, 128))),
        "consts": gdn_consts(),
    }


DSW_GROUPS = ((128, 1), (512, 4), (2048, 16))
NEG = -30000.0


def t5_bucket(dist):
    dist = np.asarray(dist, np.int32)
    x = (np.maximum(dist, 1).astype(np.float32) / np.float32(16)).astype(np.float32)
    scaled = (np.log(x).astype(np.float32) / np.float32(np.log(2048 / 16))).astype(np.float32)
    large = 16 + (scaled * np.float32(16)).astype(np.float32).astype(np.int32)
    large = np.minimum(large, 31)
    return np.where(dist < 16, dist, large)


def attn_tables():
    ki = np.arange(128)[:, None, None]
    kb = np.arange(2)[None, :, None]
    qi = np.arange(128)[None, None, :]
    dist = qi + 128 * (1 - kb) - ki
    valid = (dist >= 0) & (dist <= 128)
    return dist, valid


def attn_decl(nc, NT, NP, pre="", og_kind="ExternalOutput", x_kind="ExternalInput"):
    A = {}
    A["xT"] = nc.dram_tensor(pre + "xT", [128, 8, NT], F32, kind=x_kind).ap()
    A["wada"] = nc.dram_tensor(pre + "wada", [16, 128, 8, 128], F32, kind="ExternalInput").ap()
    A["bada"] = nc.dram_tensor(pre + "bada", [128, 16], F32, kind="ExternalInput").ap()
    A["cT"] = nc.dram_tensor(pre + "cT", [128, 8], F32, kind="ExternalInput").ap()
    A["gam"] = nc.dram_tensor(pre + "gam", [128, 8], F32, kind="ExternalInput").ap()
    A["w_qkv"] = nc.dram_tensor(pre + "w_qkv", [NP, D, 1152], F32, kind="ExternalInput").ap()
    A["gains"] = nc.dram_tensor(pre + "gains", [128, 2], F32, kind="ExternalInput").ap()
    A["biasT"] = nc.dram_tensor(pre + "biasT", [NP, 128, 6, 256], F32, kind="ExternalInput").ap()
    A["consts"] = nc.dram_tensor(pre + "consts", [128, 2, 256], F32, kind="ExternalInput").ap()
    A["ogT"] = nc.dram_tensor(pre + "ogT", [128, NP, NT], BF16, kind=og_kind).ap()
    return A


def build_attn(NT, NP=2, T=256):
    nc = bass.Bass("TRN2", target_bir_lowering=False)
    A = attn_decl(nc, NT, NP)
    m = MK(nc)
    outs = emit_attn(m, A, NT, NP, T)
    m.final_wait("sync", outs)
    m.build(barrier=False)
    return nc


def emit_attn(m, A, NT, NP, T=256):
    UN = 2048
    NU = NT // UN
    xT_d, wada_d, bada_d, cT_d, gam_d, w_d, gains_d, bias_d, cst_d, og_d = [A.get(k) for k in
        ("xT", "wada", "bada", "cT", "gam", "w_qkv", "gains", "biasT", "consts", "ogT")]

    cst = m.sbuf("cst", [128, 2, 256], F32); Bconst = m.buf("const")
    m.dma("sync", cst[:], cst_d, writes=[Bconst])
    negmask = cst[:, 0, :]
    ones_bf = m.sbuf("ones_bf", [128, 128], BF16)
    bones_bf = m.sbuf("bones_bf", [128, 128], BF16)
    m.memset("vector", ones_bf[:], 1.0, writes=[Bconst])
    m.copy("vector", bones_bf[:], cst[:, 1, 0:128], reads=[Bconst], writes=[Bconst])
    gains = m.sbuf("gains", [128, 2], F32)
    m.dma("sync", gains[:], gains_d, writes=[Bconst])
    m.ts("vector", gains[:, 0:1], gains[:, 0:1], 0.125, None, ALU.mult, None, reads=[Bconst], writes=[Bconst])

    ps_big = [m.psum(f"big{i}", [128, 512], F32) for i in range(2)]; Bbig = m.bufs("big", 2, excl=True)
    ps_sf = [m.psum(f"s{i}", [128, 512], F32) for i in range(4)]; Bps_s = m.bufs("ps_s", 4, excl=True)
    ps_s = [t[:, 0:256].rearrange("p (b q) -> p b q", b=2) for t in ps_sf]
    ps_of = [m.psum(f"o{i}", [128, 512], F32) for i in range(2)]; Bps_o = m.bufs("ps_o", 2, excl=True)
    ps_o = [t[:, 0:128] for t in ps_of]
    ps_v = [ps_sf[2][:, 0:128], ps_sf[3][:, 0:128]]; Bps_v = [Bps_s[2], Bps_s[3]]
    ring = [(ps_big[i], Bbig[i]) for i in range(2)] + [(ps_sf[i], Bps_s[i]) for i in range(4)] + [(ps_of[i], Bps_o[i]) for i in range(2)]
    cnt = {"big": 0, "s": 0, "o": 0, "v": 0, "ring": 0}

    def rbank():
        i = cnt["ring"] % len(ring); cnt["ring"] += 1
        return ring[i]


    def big():
        i = cnt["big"] % 2; cnt["big"] += 1
        return ps_big[i], Bbig[i]

    stg = Stage(m, width=1024, n=2)
    HL = A.get("h_load")
    if HL is None:
        pm, Bpm = big()
        modsb, Bmod = emit_mod(m, 16, wada_d, bada_d, cT_d, pm, Bpm, stg)
        gam = m.sbuf("gam", [128, 8], F32); Bgam = m.buf("gam")
        m.dma("sync", gam[:], gam_d, writes=[Bgam])
        A1 = m.sbuf("A1", [128, 8], F32); BAB = m.buf("AB")
        m.stt("vector", A1[:], modsb[:, 8:16], 1.0, gam[:], ALU.add, ALU.mult, reads=[Bmod, Bgam], writes=[BAB])

    w = m.sbuf("w", [128, 8, 1152], BF16); Bw = m.buf("w")
    biasT = m.sbuf("biasT", [128, 6, 256], F32); Bbias = m.buf("bias")
    if HL is None:
        xt = [m.sbuf(f"xt{i}", [128, 8, T], F32) for i in range(1)]; Bxt = m.bufs("xt", 1)
        sq = m.sbuf("sq", [128, 8, T], BF16); Bsq = m.buf("sq")
        rs = m.sbuf("rs", [128, T], F32); Brs = m.buf("rs")
        tmp = [m.sbuf(f"tmp{i}", [128, T], F32) for i in range(2)]; Btmp = m.bufs("tmp", 2)
    hU = m.sbuf("hU", [128, 8, UN], BF16); BhU = m.buf("hU")
    qf = [m.sbuf(f"qf{i}", [128, 512], F32) for i in range(2)]; Bqf = m.bufs("qf", 2)
    sqq = [m.sbuf(f"sqq{i}", [128, 512], BF16) for i in range(2)]; Bsqq = m.bufs("sqq", 2)
    rn = [m.sbuf(f"rn{i}", [128, 512], F32) for i in range(2)]; Brn = m.bufs("rn", 2)
    qn = [m.sbuf(f"qn{g}", [128, UN], BF16) for g in range(3)]; Bqn = m.bufs("qn", 3)
    kn = [[m.sbuf(f"kn{g}_{r}", [128, UN], BF16) for r in range(2)] for g in range(3)]
    Bkn = [m.bufs(f"kn{g}_", 2) for g in range(3)]
    Va = [[m.sbuf(f"Va{g}_{r}", [128, 16, 2, 128], BF16) for r in range(2)] for g in range(3)]
    BVa = [m.bufs(f"Va{g}_", 2) for g in range(3)]
    for g in range(3):
        for r in range(2):
            m.memset("gpsimd" if r else "vector", Va[g][r][:, :, :, 64:128], 1.0, writes=[BVa[g][r]])
    acc = [m.sbuf(f"acc{i}", [128, UN], F32) for i in range(1)]; Bacc = m.bufs("acc", 1)
    sT = [m.sbuf(f"sT{i}", [128, 2, 128], F32) for i in range(3)]; BsT = m.bufs("sT", 3)
    pT = [m.sbuf(f"pT{i}", [128, 2, 128], BF16) for i in range(3)]; BpT = m.bufs("pT", 3)
    rden = m.sbuf("rden", [64, UN], F32); Brden = m.buf("rden")
    obf = [m.sbuf(f"obf{i}", [64, UN], BF16) for i in range(1)]; Bobf = m.bufs("obf", 1)
    Bouts = []
    nacc = 0
    npt = 0

    for pp in range(NP):
        for kc in range(8):
            stg.load_cast(lambda c0, c1, kc=kc: w[:, kc, c0:c1], w_d[pp, kc * 128:(kc + 1) * 128, :], 1152, Bw,
                          queue="sync" if kc % 2 else "gpsimd")
        m.dma("sync", biasT[:], bias_d[pp], writes=[Bbias])
        for i in range(6):
            m.tt("gpsimd", biasT[:, i, :], biasT[:, i, :], negmask, ALU.add, reads=[Bbias, Bconst], writes=[Bbias])
        for u in range(NU):
            ur = u % 2
            if HL is not None:
                HL(m, hU, u, BhU)
            for ti in range(0 if HL is not None else UN // T):
                t0 = u * UN + ti * T
                x_t = xt[0]; Bx = Bxt[0]
                if "x_load" in A:
                    A["x_load"](m, x_t, t0, T, Bx)
                else:
                    m.dma("sync" if ti % 2 else "gpsimd", x_t[:], xT_d[:, :, t0:t0 + T], writes=[Bx])
                pn_, Bpn_ = big()
                emit_norm_mod(m, x_t, Bx, T, A1, modsb[:, 0:8], BAB, ones_bf, Bconst, sq, Bsq, pn_, Bpn_,
                              rs, Brs, tmp, Btmp, hU[:, :, ti * T:(ti + 1) * T], BhU)
            items = [(g_, qk, tl) for g_ in range(3) for qk in range(2) for tl in range(UN // 512)]
            stq = {}

            def q_s1(i):
                g_, qk, tl = items[i]
                c0 = (g_ * 3 + qk) * 128
                p, Bp = rbank()
                for kc in range(8):
                    m.mm(p[:, :], w[:, kc, c0:c0 + 128], hU[:, kc, tl * 512:(tl + 1) * 512], kc == 0, kc == 7,
                         reads=[Bw, BhU], writes=[Bp])
                stq[i] = (p, Bp)

            def q_s2(i):
                p, Bp = stq[i]
                f_ = qf[i % 2]; Bf_ = Bqf[i % 2]
                m.copy("scalar", f_[:], p[:, :], reads=[Bp], writes=[Bf_])
                s_ = sqq[i % 2]; Bs_ = Bsqq[i % 2]
                m.tt("gpsimd", s_[:], f_[:], f_[:], ALU.mult, reads=[Bf_], writes=[Bs_])
                p2, Bp2 = rbank()
                m.mm(p2[:, :], bones_bf[:], s_[:], True, True, reads=[Bconst, Bs_], writes=[Bp2])
                stq[i] = (p2, Bp2)

            def q_s3(i):
                g_, qk, tl = items[i]
                d = DSW_GROUPS[g_][1]
                dst = qn[g_] if qk == 0 else kn[g_][ur]
                Bdst = Bqn[g_] if qk == 0 else Bkn[g_][ur]
                p2, Bp2 = stq.pop(i)
                f_ = qf[i % 2]; Bf_ = Bqf[i % 2]
                r_ = rn[i % 2]; Br_ = Brn[i % 2]
                m.act(r_[:], p2[:, :], AF.Ln, reads=[Bp2], writes=[Br_], bias=RMS_EPS, scale=1.0 / 64)
                m.act(r_[:], r_[:], AF.Exp, reads=[Br_], writes=[Br_], scale=-0.5)
                J = 512 // d
                j0 = tl * J
                if d == 1:
                    o_ap = dst[:, tl * 512:(tl + 1) * 512]
                    i0 = f_[:]; i1 = r_[:]
                else:
                    o_ap = dst[:].rearrange("p (r j) -> p r j", r=d)[:, :, j0:j0 + J].rearrange("p r j -> p j r")
                    i0 = f_[:].rearrange("p (j r) -> p j r", r=d)
                    i1 = r_[:].rearrange("p (j r) -> p j r", r=d)
                m.stt("vector", o_ap, i0, gains[:, qk:qk + 1], i1, ALU.mult, ALU.mult,
                      reads=[Bf_, Br_, Bconst], writes=[Bdst])

            nit = len(items)
            for i in range(nit + 2):
                if i < nit:
                    q_s1(i)
                if 1 <= i < nit + 1:
                    q_s2(i - 1)
                if i >= 2:
                    q_s3(i - 2)
            for g in range(3):
                d = DSW_GROUPS[g][1]
                nb = 16 // d
                c0 = (g * 3 + 2) * 128
                for r in range(d):
                    for n_ in range(nb):
                        blk = r * nb + n_
                        tb = n_ * 128 * d + r
                        i = cnt["v"] % 2; cnt["v"] += 1
                        for kc in range(8):
                            m.mm(ps_v[i], hU[:, kc, tb:tb + 127 * d + 1:d], w[:, kc, c0:c0 + 128], kc == 0, kc == 7,
                                 reads=[BhU, Bw], writes=[Bps_v[i]])
                        m.copy("scalar", Va[g][ur][:, blk, :, 0:64], ps_v[i].rearrange("p (h c) -> p h c", h=2),
                               reads=[Bps_v[i]], writes=[BVa[g][ur]])
            LAG = 2
            for hl in range(2):
                a_ = acc[0]; Ba_ = Bacc[0]; nacc += 1
                hp = slice(hl * 64, (hl + 1) * 64)
                blocks = []
                for g in range(3):
                    d = DSW_GROUPS[g][1]
                    nb = 16 // d
                    for r in range(d):
                        for n_ in range(nb):
                            blocks.append((g, d, nb, r, n_))
                stA = {}

                def stage_a(i):
                    g, d, nb, r, n_ = blocks[i]
                    bt = biasT[:, g * 2 + hl, :].rearrange("p (b q) -> p b q", b=2)
                    blk = r * nb + n_
                    qcol = blk * 128
                    if n_ > 0:
                        kprev = (kn[g][ur], Bkn[g][ur], Va[g][ur], BVa[g][ur], blk - 1)
                    elif u > 0:
                        kprev = (kn[g][1 - ur], Bkn[g][1 - ur], Va[g][1 - ur], BVa[g][1 - ur], r * nb + nb - 1)
                    else:
                        kprev = None
                    si = cnt["s"] % len(ps_s); cnt["s"] += 1
                    pS = ps_s[si]; BpS = Bps_s[si]
                    kb0 = 0 if kprev is not None else 1
                    if kprev is not None:
                        kt, Bkt, _, _, pb = kprev
                        m.mm(pS[:, 0, :], kt[hp, pb * 128:(pb + 1) * 128], qn[g][hp, qcol:qcol + 128], True, True,
                             reads=[Bkt, Bqn[g]], writes=[BpS])
                    m.mm(pS[:, 1, :], kn[g][ur][hp, qcol:qcol + 128], qn[g][hp, qcol:qcol + 128], True, True,
                         reads=[Bkn[g][ur], Bqn[g]], writes=[BpS])
                    k3 = i % 3
                    s_ = sT[k3]; Bs_ = BsT[k3]
                    p_ = pT[k3]; Bp_ = BpT[k3]
                    m.tt("vector", s_[:, kb0:2, :], pS[:, kb0:2, :], bt[:, kb0:2, :], ALU.add,
                         reads=[BpS, Bbias], writes=[Bs_])
                    m.act(p_[:, kb0:2, :], s_[:, kb0:2, :], AF.Exp, reads=[Bs_], writes=[Bp_])
                    stA[i] = (kprev, p_, Bp_)

                def stage_b(i):
                    g, d, nb, r, n_ = blocks[i]
                    kprev, p_, Bp_ = stA.pop(i)
                    blk = r * nb + n_
                    tb = n_ * 128 * d + r
                    oi = cnt["o"] % 2; cnt["o"] += 1
                    pO = ps_o[oi]; BpO = Bps_o[oi]
                    if kprev is not None:
                        _, _, vt_, Bvt_, pb = kprev
                        m.mm(pO[:, :], vt_[:, pb, hl, :], p_[:, 0, :], True, False, reads=[Bvt_, Bp_], writes=[BpO], inc=False)
                    m.mm(pO[:, :], Va[g][ur][:, blk, hl, :], p_[:, 1, :], kprev is None, True,
                         reads=[BVa[g][ur], Bp_], writes=[BpO])
                    a_ap = a_[:, tb:tb + 127 * d + 1:d]
                    if g == 0:
                        m.copy("vector", a_ap, pO[:, :], reads=[BpO], writes=[Ba_])
                    else:
                        m.tt("vector", a_ap, pO[:, :], a_ap, ALU.add, reads=[BpO, Ba_], writes=[Ba_])

                nblk = len(blocks)
                for i in range(nblk + LAG):
                    if i < nblk:
                        stage_a(i)
                    if i >= LAG:
                        stage_b(i - LAG)
                m.act(rden[:], a_[64:128, :], AF.Ln, reads=[Ba_], writes=[Brden])
                m.act(rden[:], rden[:], AF.Exp, reads=[Brden], writes=[Brden], scale=-1.0)
                ob = obf[0]; Bob = Bobf[0]
                m.tt("gpsimd", ob[:], a_[0:64, :], rden[:], ALU.mult, reads=[Ba_, Brden], writes=[Bob])
                if "og_store" in A:
                    Bouts.append(A["og_store"](m, ob, pp, hl, u, Bob))
                else:
                    Bouts.append(m.buf("out"))
                    m.dma("sync", og_d[hl * 64:(hl + 1) * 64, pp, u * UN:(u + 1) * UN], ob[:], reads=[Bob], writes=[Bouts[-1]])
    return Bouts


def attn_inputs(pairs, xb, cb, wada1, bada1, gam, w_in, q_gain, k_gain, rel_bias, xT=None):
    NP = len(pairs)
    wsel = np.empty((NP, D, 1152), np.float32)
    bias = np.empty((NP, 128, 6, 256), np.float32)
    dist, valid = attn_tables()
    for pi, pr in enumerate(pairs):
        for g in range(3):
            d = DSW_GROUPS[g][1]
            idx = t5_bucket(np.clip(dist, 0, 128) * d)
            for t in range(3):
                for hl in range(2):
                    hd = pr * 2 + hl
                    c_src = ((t * 3 + g) * 8 + hd) * 64
                    c_dst = (g * 3 + t) * 128 + hl * 64
                    wsel[pi, :, c_dst:c_dst + 64] = w_in[:, c_src:c_src + 64]
            for hl in range(2):
                hd = pr * 2 + hl
                bias[pi, :, g * 2 + hl, :] = rel_bias[idx, g * 8 + hd].reshape(128, 256)
    cst = np.zeros((128, 2, 256), np.float32)
    cst[:, 0, :] = np.where(valid, 0.0, NEG).reshape(128, 256)
    cst[0:64, 1, 0:64] = 1.0
    cst[64:128, 1, 64:128] = 1.0
    gains = np.stack([np.tile(q_gain, 2), np.tile(k_gain, 2)], axis=1).astype(np.float32)
    d_ = {"wada": lay_wada(wada1, 0, 2048), "bada": lay_vec(bada1[0:2048]), "cT": lay_vec(cb),
          "gam": lay_vec(gam), "w_qkv": wsel, "gains": np.ascontiguousarray(gains), "biasT": bias, "consts": cst}
    if xT is not None or xb is not None:
        d_["xT"] = lay_xT(xb) if xT is None else xT
    return d_


def build_fused(NT=SEQ):
    nc = bass.Bass("TRN2", target_bir_lowering=False)
    ses = ExitStack()
    ext = lambda name, shape, dt=F32: nc.dram_tensor(name, list(shape), dt, kind="ExternalInput").ap()
    xT = ext("xT", [128, 8, NT]); cT = ext("cT", [128, 8])
    ogT0 = nc.dram_tensor("ogT0", [128, 8, NT], BF16, kind="Internal").ap()
    x1T = nc.dram_tensor("x1T", [128, 8, NT], F32, kind="Internal").ap()
    ogT1 = nc.dram_tensor("ogT1", [128, 4, NT], BF16, kind="Internal").ap()
    outT = nc.dram_tensor("outT", [128, 8, NT], F32, kind="ExternalOutput").ap()
    A1 = {"xT": xT, "cT": cT, "wada": ext("g_wada", [16, 128, 8, 128]), "bada": ext("g_bada", [128, 16]),
          "gam": ext("g_gam", [128, 8]), "w_qkvz": ext("g_w_qkvz", [2, D, 2048]), "w_ab": ext("g_w_ab", [2, D, 8]),
          "convw": ext("g_convw", [2, 128, 12, 4]), "alog": ext("g_alog", [2, 128, 4]), "dtb": ext("g_dtb", [2, 128, 4]),
          "onorm": ext("g_onorm", [128, 128]), "consts": ext("g_consts", [128, 13, 128]), "ogT": ogT0}
    A2 = {"xT": xT, "cT": cT, "ogT": ogT0, "w_o": ext("f0_w_o", [1024, D]), "wada": ext("f0_wada", [32, 128, 8, 128]),
          "bada": ext("f0_bada", [128, 32]), "gam": ext("f0_gam", [128, 8]), "w1": ext("f0_w1", [D, 2 * FH]),
          "w2": ext("f0_w2", [FH, D]), "outT": x1T}
    A3 = {"xT": x1T, "cT": cT, "wada": ext("a_wada", [16, 128, 8, 128]), "bada": ext("a_bada", [128, 16]),
          "gam": ext("a_gam", [128, 8]), "w_qkv": ext("a_w_qkv", [4, D, 1152]), "gains": ext("a_gains", [128, 2]),
          "biasT": ext("a_biasT", [4, 128, 6, 256]), "consts": ext("a_consts", [128, 2, 256]), "ogT": ogT1}
    A4 = {"xT": x1T, "cT": cT, "ogT": ogT1, "w_o": ext("f1_w_o", [512, D]), "wada": ext("f1_wada", [32, 128, 8, 128]),
          "bada": ext("f1_bada", [128, 32]), "gam": ext("f1_gam", [128, 8]), "w1": ext("f1_w1", [D, 2 * FH]),
          "w2": ext("f1_w2", [FH, D]), "outT": outT}
    m = MK(nc, sem_es=ses, tag="p1_"); emit_gdn(m, A1, NT, 2); m.build()
    m = MK(nc, sem_es=ses, tag="p2_"); emit_ffn(m, A2, NT, 1024); m.build()
    m = MK(nc, sem_es=ses, tag="p3_"); emit_attn(m, A3, NT, 4); m.build()
    m = MK(nc, sem_es=ses, tag="p4_"); emit_ffn(m, A4, NT, 512); m.build()
    ses.close()
    return nc


PAIRS = [[0, 1], [2, 3], [4, 5], [6, 7]]


def build_fused8(NT=SEQ):
    nc = bass.Bass("TRN2", target_bir_lowering=False)
    ses = ExitStack()
    HT = NT // 2
    ext = lambda name, shape, dt=F32: nc.dram_tensor(name, list(shape), dt, kind="ExternalInput").ap()
    itn = lambda name, shape, dt: nc.dram_tensor(name, list(shape), dt, kind="Internal").ap()
    xT = ext("xT", [128, 8, NT]); xTh = ext("xTh", [128, 8, HT]); cT = ext("cT", [128, 8]); sel_d = ext("sel", [128, 2])
    outT = nc.dram_tensor("outT", [128, 8, HT], F32, kind="ExternalOutput").ap()
    OGC = 2048; X1C = 512; O2C = 4096
    n_og, n_x1, n_o2 = NT // OGC, HT // X1C, NT // O2C
    og_src = [itn(f"og_src{j}", [128 * 4, OGC], BF16) for j in range(n_og)]
    og_gat = [itn(f"og_gat{j}", [2 * 128 * 4, OGC], BF16) for j in range(n_og)]
    x1_src = [itn(f"x1_src{j}", [128 * 8, X1C], F32) for j in range(n_x1)]
    H1C = 1024; n_h1 = HT // H1C
    h1_src = [itn(f"h1_src{j}", [128 * 8, H1C], BF16) for j in range(n_h1)]
    h1_gat = [itn(f"h1_gat{j}", [2 * 128 * 8, H1C], BF16) for j in range(n_h1)]
    h1_sv = [t.rearrange("(p k) t -> p k t", k=8) for t in h1_src]
    h1_gv = [t.rearrange("(r p k) t -> r p k t", r=2, k=8) for t in h1_gat]
    o2_src = [itn(f"o2_src{j}", [128 * 2, O2C], BF16) for j in range(n_o2)]
    o2_gat = [itn(f"o2_gat{j}", [2 * 128 * 2, O2C], BF16) for j in range(n_o2)]
    og_sv = [t.rearrange("(p k) t -> p k t", k=4) for t in og_src]
    og_gv = [t.rearrange("(r p k) t -> r p k t", r=2, k=4) for t in og_gat]
    x1_sv = [t.rearrange("(p k) t -> p k t", k=8) for t in x1_src]
    o2_sv = [t.rearrange("(p k) t -> p k t", k=2) for t in o2_src]
    o2_gv = [t.rearrange("(r p k) t -> r p k t", r=2, k=2) for t in o2_gat]

    A1 = {"xT": xT, "cT": cT, "wada": ext("g_wada", [16, 128, 8, 128]), "bada": ext("g_bada", [128, 16]),
          "gam": ext("g_gam", [128, 8]), "w_qkvz": ext("g_w_qkvz", [1, D, 2048]), "w_ab": ext("g_w_ab", [1, D, 8]),
          "convw": ext("g_convw", [1, 128, 12, 4]), "alog": ext("g_alog", [1, 128, 4]), "dtb": ext("g_dtb", [1, 128, 4]),
          "onorm": ext("g_onorm", [128, 128]), "consts": ext("g_consts", [128, 13, 128])}
    A2 = {"xT": xTh, "cT": cT, "w_o": ext("f0_w_o", [1024, D]), "wada": ext("f0_wada", [32, 128, 8, 128]),
          "bada": ext("f0_bada", [128, 32]), "gam": ext("f0_gam", [128, 8]), "w1": ext("f0_w1", [D, 2 * FH]),
          "w2": ext("f0_w2", [FH, D])}
    A3 = {"cT": cT, "wada": ext("a_wada", [16, 128, 8, 128]), "bada": ext("a_bada", [128, 16]),
          "gam": ext("a_gam", [128, 8]), "w_qkv": ext("a_w_qkv", [2, D, 1152]), "gains": ext("a_gains", [128, 2]),
          "biasT": ext("a_biasT", [2, 128, 6, 256]), "consts": ext("a_consts", [128, 2, 256])}
    A4 = {"cT": cT, "w_o": ext("f1_w_o", [512, D]), "wada": ext("f1_wada", [32, 128, 8, 128]),
          "bada": ext("f1_bada", [128, 32]), "gam": ext("f1_gam", [128, 8]), "w1": ext("f1_w1", [D, 2 * FH]),
          "w2": ext("f1_w2", [FH, D]), "outT": outT}

    class Xchg:
        def __init__(self, m, srcs, gats, need):
            self.m, self.srcs, self.gats, self.need = m, srcs, gats, need
            self.bufs = {j: [] for j in range(len(srcs))}

        def stored(self, j, buf):
            self.bufs[j].append(buf)
            if len(self.bufs[j]) == self.need:
                self.m.cc_allgather(self.srcs[j], self.gats[j], PAIRS, reads=self.bufs[j], writes=[self.m.buf("gat")])

    def make_og_load(gv, kper, chunk, msel):
        def og_load(m, ogbs, Bogbs, t0, T):
            A_, B_ = ogbs[0], ogbs[1]
            for r in range(2):
                ja, oa = divmod(t0, chunk)
                jb, ob_ = divmod(HT + t0, chunk)
                m.dma("gpsimd", A_[:, r * kper:(r + 1) * kper, :], gv[ja][r][:, :, oa:oa + T], writes=[Bogbs[0]])
                m.dma("sync", B_[:, r * kper:(r + 1) * kper, :], gv[jb][r][:, :, ob_:ob_ + T], writes=[Bogbs[1]])
            sel, Bsel = msel["sel"]
            m.ts("vector", A_[:], A_[:], sel[:, 0:1], None, ALU.mult, None, reads=[Bogbs[0], Bsel], writes=[Bogbs[0]])
            m.stt("vector", A_[:], B_[:], sel[:, 1:2], A_[:], ALU.mult, ALU.add, reads=[Bogbs[0], Bogbs[1], Bsel], writes=[Bogbs[0]])
            return A_, Bogbs[0]
        return og_load

    def load_sel(m, msel):
        sel = m.sbuf("sel", [128, 2], F32); Bsel = m.buf("sel")
        m.dma("sync", sel[:], sel_d, writes=[Bsel])
        msel["sel"] = (sel, Bsel)

    m = MK(nc, sem_es=ses, tag="p1_")
    xc = Xchg(m, og_src, og_gat, OGC // 128)

    def og_store1(m_, ot, ps, ch, Bot):
        j, o = divmod(ch * 128, OGC)
        bo = m_.buf("out")
        m_.dma("sync", og_sv[j][:, :, o:o + 128], ot[:], reads=[Bot], writes=[bo])
        xc.stored(j, bo)
        return bo
    A1["og_store"] = og_store1
    emit_gdn(m, A1, NT, 1)
    m.build()
    m = MK(nc, sem_es=ses, tag="p2_")
    ms = {}; load_sel(m, ms)
    xc2 = Xchg(m, h1_src, h1_gat, H1C // 256)

    def h_store2(m_, h_t, t0, T, Bh):
        j, o = divmod(t0, H1C)
        bo = m_.buf("out")
        m_.dma("gpsimd", h1_sv[j][:, :, o:o + T], h_t[:], reads=[Bh], writes=[bo])
        xc2.stored(j, bo)
        return bo
    A2["h_extra"] = {"wada": A3["wada"], "bada": A3["bada"], "gam": A3["gam"], "store": h_store2}
    A2["og_load"] = make_og_load(og_gv, 4, OGC, ms)

    def out_store2(m_, x_t, t0, T, Bx):
        j, o = divmod(t0, X1C)
        bo = m_.buf("out")
        m_.dma("sync", x1_sv[j][:, :, o:o + T], x_t[:], reads=[Bx], writes=[bo])
        return bo
    A2["out_store"] = out_store2
    emit_ffn(m, A2, HT, 1024)
    m.build()
    m = MK(nc, sem_es=ses, tag="p3_")
    xc3 = Xchg(m, o2_src, o2_gat, 2 * 2 * (O2C // 2048))

    def h_load3(m_, hU, u, BhU):
        for q in range(2048 // H1C):
            t0 = u * 2048 + q * H1C
            r, tl = divmod(t0, HT)
            j = tl // H1C
            m_.dma("sync" if q % 2 else "gpsimd", hU[:, :, q * H1C:(q + 1) * H1C], h1_gv[j][r], writes=[BhU])
    A3["h_load"] = h_load3

    def og_store3(m_, ob, pp, hl, u, Bob):
        j, o = divmod(u * 2048, O2C)
        bo = m_.buf("out")
        m_.dma("sync", o2_sv[j][hl * 64:(hl + 1) * 64, pp, o:o + 2048], ob[:], reads=[Bob], writes=[bo])
        xc3.stored(j, bo)
        return bo
    A3["og_store"] = og_store3
    emit_attn(m, A3, NT, 2)
    m.build()
    m = MK(nc, sem_es=ses, tag="p4_")
    ms = {}; load_sel(m, ms)
    A4["og_load"] = make_og_load(o2_gv, 2, O2C, ms)

    def x_load4(m_, x_t, t0, T, Bx):
        j, o = divmod(t0, X1C)
        m_.dma("sync", x_t[:], x1_sv[j][:, :, o:o + T], writes=[Bx])
    A4["x_load"] = x_load4
    emit_ffn(m, A4, HT, 512)
    m.build()
    ses.close()
    return nc


_PROGS = {}


def _prog(key, fn):
    if key not in _PROGS:
        _PROGS[key] = fn()
    return _PROGS[key]


def _f32(a):
    return np.ascontiguousarray(np.asarray(a, dtype=np.float32))


def kernel(x, c, w_ada, b_ada, norm_mix, norm_ffn, w_ffn_in, w_ffn_out,
           gdn_w_in, gdn_conv, gdn_a_log, gdn_dt_bias, gdn_out_norm, gdn_w_out,
           dsw_w_in, dsw_q_norm, dsw_k_norm, dsw_w_out, rel_bias):
    (x, c, w_ada, b_ada, norm_mix, norm_ffn, w_ffn_in, w_ffn_out, gdn_w_in, gdn_conv, gdn_a_log, gdn_dt_bias,
     gdn_out_norm, gdn_w_out, dsw_w_in, dsw_q_norm, dsw_k_norm, dsw_w_out, rel_bias) = map(_f32, (
        x, c, w_ada, b_ada, norm_mix, norm_ffn, w_ffn_in, w_ffn_out, gdn_w_in, gdn_conv, gdn_a_log, gdn_dt_bias,
        gdn_out_norm, gdn_w_out, dsw_w_in, dsw_q_norm, dsw_k_norm, dsw_w_out, rel_bias))
    nc = _prog("fused8", build_fused8)
    maps = []
    for b in range(BATCH):
        xTb = lay_xT(x[b])
        for hh in range(2):
            g = gdn_inputs([hh], None, c[b], w_ada[0], b_ada[0], norm_mix[0], gdn_w_in[0], gdn_conv[0], gdn_a_log[0],
                           gdn_dt_bias[0], gdn_out_norm[0], xT=0)
            a = attn_inputs([2 * hh, 2 * hh + 1], None, c[b], w_ada[1], b_ada[1], norm_mix[1], dsw_w_in[0], dsw_q_norm[0],
                            dsw_k_norm[0], rel_bias)
            d_ = {}
            for k in ("wada", "bada", "gam", "w_qkvz", "w_ab", "convw", "alog", "dtb", "onorm", "consts"):
                d_["g_" + k] = g[k]
            for k in ("wada", "bada", "gam", "w_qkv", "gains", "biasT", "consts"):
                d_["a_" + k] = a[k]
            for L, w_o in ((0, gdn_w_out[0]), (1, dsw_w_out[0])):
                p = f"f{L}_"
                d_[p + "w_o"] = w_o
                d_[p + "wada"] = lay_wada(w_ada[L], 2048, 6144)
                d_[p + "bada"] = lay_vec(b_ada[L][2048:])
                d_[p + "gam"] = lay_vec(norm_ffn[L])
                d_[p + "w1"] = w_ffn_in[L]
                d_[p + "w2"] = w_ffn_out[L]
            d_["xT"] = xTb
            d_["xTh"] = np.ascontiguousarray(xTb[:, :, hh * (SEQ // 2):(hh + 1) * (SEQ // 2)])
            d_["cT"] = lay_vec(c[b])
            sel = np.zeros((128, 2), np.float32); sel[:, hh] = 1.0
            d_["sel"] = sel
            maps.append(d_)
    res = run_bass_kernel_spmd(nc, maps, core_ids=list(range(8))).results
    out = np.empty((BATCH, SEQ, D), np.float32)
    for b in range(BATCH):
        for hh in range(2):
            out[b, hh * (SEQ // 2):(hh + 1) * (SEQ // 2)] = unlay_xT(np.asarray(res[b * 2 + hh]["outT"]))
    return out


def kernel_4core(x, c, w_ada, b_ada, norm_mix, norm_ffn, w_ffn_in, w_ffn_out,
           gdn_w_in, gdn_conv, gdn_a_log, gdn_dt_bias, gdn_out_norm, gdn_w_out,
           dsw_w_in, dsw_q_norm, dsw_k_norm, dsw_w_out, rel_bias):
    (x, c, w_ada, b_ada, norm_mix, norm_ffn, w_ffn_in, w_ffn_out, gdn_w_in, gdn_conv, gdn_a_log, gdn_dt_bias,
     gdn_out_norm, gdn_w_out, dsw_w_in, dsw_q_norm, dsw_k_norm, dsw_w_out, rel_bias) = map(_f32, (
        x, c, w_ada, b_ada, norm_mix, norm_ffn, w_ffn_in, w_ffn_out, gdn_w_in, gdn_conv, gdn_a_log, gdn_dt_bias,
        gdn_out_norm, gdn_w_out, dsw_w_in, dsw_q_norm, dsw_k_norm, dsw_w_out, rel_bias))
    nc = _prog("fused", build_fused)
    g = gdn_inputs([0, 1], None, c[0], w_ada[0], b_ada[0], norm_mix[0], gdn_w_in[0], gdn_conv[0], gdn_a_log[0],
                   gdn_dt_bias[0], gdn_out_norm[0], xT=0)
    a = attn_inputs([0, 1, 2, 3], None, c[0], w_ada[1], b_ada[1], norm_mix[1], dsw_w_in[0], dsw_q_norm[0], dsw_k_norm[0], rel_bias)
    shared = {}
    for k in ("wada", "bada", "gam", "w_qkvz", "w_ab", "convw", "alog", "dtb", "onorm", "consts"):
        shared["g_" + k] = g[k]
    for k in ("wada", "bada", "gam", "w_qkv", "gains", "biasT", "consts"):
        shared["a_" + k] = a[k]
    for L, w_o in ((0, gdn_w_out[0]), (1, dsw_w_out[0])):
        p = f"f{L}_"
        shared[p + "w_o"] = w_o
        shared[p + "wada"] = lay_wada(w_ada[L], 2048, 6144)
        shared[p + "bada"] = lay_vec(b_ada[L][2048:])
        shared[p + "gam"] = lay_vec(norm_ffn[L])
        shared[p + "w1"] = w_ffn_in[L]
        shared[p + "w2"] = w_ffn_out[L]
    maps = []
    for b in range(BATCH):
        d_ = dict(shared)
        d_["xT"] = lay_xT(x[b])
        d_["cT"] = lay_vec(c[b])
        maps.append(d_)
    res = run_bass_kernel_spmd(nc, maps, core_ids=list(range(BATCH))).results
    out = np.empty((BATCH, SEQ, D), np.float32)
    for b in range(BATCH):
        out[b] = unlay_xT(np.asarray(res[b]["outT"]))
    return out
```

```python
import numpy as np
import ml_dtypes
from contextlib import ExitStack
import concourse.bass as bass
import concourse.mybir as mybir
from concourse.bass_utils import run_bass_kernel_spmd

F32 = mybir.dt.float32
BF16 = mybir.dt.bfloat16
AF = mybir.ActivationFunctionType
ALU = mybir.AluOpType
AX = mybir.AxisListType
NPBF = ml_dtypes.bfloat16

D = 1024
SEQ = 8192
BATCH = 4
FH = 2816
RMS_EPS = 1e-6


class Buf:
    __slots__ = ("name", "last_w", "readers", "excl")

    def __init__(self, name, excl=False):
        self.name = name
        self.last_w = None
        self.readers = []
        self.excl = excl


class MK:
    ENGS = ("tensor", "vector", "scalar", "gpsimd", "sync")

    def __init__(self, nc, n_dma_sems=8, sem_es=None, tag=""):
        self.nc = nc
        self.es = ExitStack()
        self.tag = tag
        self.sem_es = sem_es
        ses = sem_es if sem_es is not None else self.es
        self.ops = {e: [] for e in self.ENGS}
        self.cnt = {e: 0 for e in self.ENGS}
        self.sem = {}
        for e in ("tensor", "vector", "scalar", "gpsimd"):
            self.sem[e] = ses.enter_context(nc.semaphore(tag + "s_" + e))
        self.dma_sems = {}
        self.dma_ring = {}
        for q in ("sync", "gpsimd"):
            self.dma_sems[q] = [ses.enter_context(nc.semaphore(f"{tag}d_{q}{i}")) for i in range(n_dma_sems)]
            self.dma_ring[q] = [0, [0] * n_dma_sems]
        self.known = {e: {} for e in self.ENGS}
        self._rr = 0

    def sbuf(self, name, shape, dt):
        return self.es.enter_context(self.nc.sbuf_tensor(self.tag + "sb_" + name, list(shape), dt))

    def psum(self, name, shape, dt):
        return self.es.enter_context(self.nc.psum_tensor(self.tag + "pp_" + name, list(shape), dt))

    def buf(self, name, excl=False):
        return Buf(name, excl)

    def bufs(self, name, n, excl=False):
        return [Buf(f"{name}{i}", excl) for i in range(n)]

    @staticmethod
    def _split(reads, writes):
        ex = [b for b in reads if b.excl]
        if ex:
            reads = [b for b in reads if not b.excl]
            writes = list(writes) + ex
        return reads, writes

    def _deps(self, reads, writes):
        deps = {}

        def add(d):
            if d is None:
                return
            k, v = d
            if deps.get(k, -1) < v:
                deps[k] = v
        for b in reads:
            add(b.last_w)
        for b in writes:
            add(b.last_w)
            for r in b.readers:
                add(r)
        return deps

    def _waits_for(self, eng, deps):
        waits = []
        kn = self.known[eng]
        for k, v in deps.items():
            if kn.get(k, 0) >= v:
                continue
            if k == ("e", eng) and v > self.cnt[eng]:
                continue
            kn[k] = v
            waits.append((k, v))
        return waits

    def op(self, eng, fn, reads=(), writes=(), inc=True):
        reads, writes = self._split(reads, writes)
        deps = self._deps(reads, writes)
        waits = self._waits_for(eng, deps)
        key = ("e", eng)
        if inc:
            self.cnt[eng] += 1
            c = self.cnt[eng]
            self.ops[eng].append((waits, fn, (key, 1)))
        else:
            c = self.cnt[eng] + 1
            self.ops[eng].append((waits, fn, None))
        tag = (key, c)
        for b in reads:
            b.readers.append(tag)
        for b in writes:
            b.last_w = tag
            b.readers = []
        return tag

    def dma(self, q, out, in_, reads=(), writes=()):
        reads, writes = self._split(reads, writes)
        deps = self._deps(reads, writes)
        ring = self.dma_ring[q]
        i = ring[0]
        ring[0] = (i + 1) % len(self.dma_sems[q])
        n_prev = ring[1][i]
        key = ("d", q, i)
        if n_prev > 0 and deps.get(key, -1) < 16 * n_prev:
            deps[key] = 16 * n_prev
        waits = self._waits_for(q, deps)
        ring[1][i] = n_prev + 1
        self.ops[q].append((waits, (lambda e, o=out, s=in_: e.dma_start(out=o, in_=s)), (key, 16)))
        tag = (key, 16 * (n_prev + 1))
        for b in reads:
            b.readers.append(tag)
        for b in writes:
            b.last_w = tag
            b.readers = []
        return tag

    def cc_allgather(self, src, dst, groups, reads=(), writes=()):
        if not hasattr(self, "cc_sem"):
            ses = self.sem_es if self.sem_es is not None else self.es
            self.cc_sem = ses.enter_context(self.nc.semaphore(self.tag + "cc_sem"))
            self.cc_n = 0
        reads, writes = self._split(reads, writes)
        deps = self._deps(reads, writes)
        waits = self._waits_for("gpsimd", deps)
        self.cc_n += 1
        key = ("c",)
        self.ops["gpsimd"].append((waits, (lambda e: e.collective_compute("AllGather", ALU.bypass, replica_groups=groups,
                                                                        ins=[src.opt()], outs=[dst.opt()])), (key, 1)))
        tag = (key, self.cc_n)
        for b in reads:
            b.readers.append(tag)
        for b in writes:
            b.last_w = tag
            b.readers = []
        return tag

    def _semof(self, key):
        if key[0] == "c":
            return self.cc_sem
        if key[0] == "e":
            return self.sem[key[1]]
        return self.dma_sems[key[1]][key[2]]

    def final_wait(self, eng, bufs):
        deps = self._deps(bufs, ())
        waits = self._waits_for(eng, deps)
        self.ops[eng].append((waits, None, None))

    def barrier(self):
        deps = {}
        for e in ("tensor", "vector", "scalar", "gpsimd"):
            if self.cnt[e] > 0:
                deps[("e", e)] = self.cnt[e]
        for q, (nxt, counts) in self.dma_ring.items():
            for i, n in enumerate(counts):
                if n > 0:
                    deps[("d", q, i)] = 16 * n
        if getattr(self, "cc_n", 0) > 0:
            deps[("c",)] = self.cc_n
        for e in self.ENGS:
            waits = self._waits_for(e, dict(deps))
            waits = [(k, v) for (k, v) in waits]
            self.ops[e].append((waits, None, None))

    def build(self, barrier=True):
        nc = self.nc
        if barrier:
            self.barrier()
        with nc.Block() as block:
            def mk(ename):
                def body(e):
                    for waits, fn, inc in self.ops[ename]:
                        for k, v in waits:
                            e.wait_ge(self._semof(k), v)
                        if fn is not None:
                            ins = fn(e)
                            if inc is not None:
                                ins.then_inc(self._semof(inc[0]), inc[1])
                return body
            block.tensor(mk("tensor"))
            block.vector(mk("vector"))
            block.scalar(mk("scalar"))
            block.gpsimd(mk("gpsimd"))
            block.sync(mk("sync"))
        self.es.close()

    def mm(self, out, lhsT, rhs, start, stop, reads, writes, inc=None):
        if inc is None:
            inc = stop
        return self.op("tensor", lambda e: e.matmul(out, lhsT=lhsT, rhs=rhs, start=start, stop=stop),
                       reads, writes, inc=inc)

    def tr(self, out, in_, ident, reads, writes):
        return self.op("tensor", lambda e: e.transpose(out, in_, ident), reads, writes)

    def act(self, out, in_, func, reads, writes, bias=0.0, scale=1.0, accum_out=None):
        if accum_out is None:
            return self.op("scalar", lambda e: e.activation(out=out, in_=in_, func=func, bias=bias, scale=scale),
                           reads, writes)
        return self.op("scalar", lambda e: e.activation(out=out, in_=in_, func=func, bias=bias, scale=scale,
                                                        accum_out=accum_out), reads, writes)

    def tt(self, eng, out, in0, in1, op, reads, writes):
        return self.op(eng, lambda e: e.tensor_tensor(out=out, in0=in0, in1=in1, op=op), reads, writes)

    def ts(self, eng, out, in0, s1, s2, op0, op1, reads, writes):
        if s2 is None:
            return self.op(eng, lambda e: e.tensor_scalar(out=out, in0=in0, scalar1=s1, scalar2=None, op0=op0),
                           reads, writes)
        return self.op(eng, lambda e: e.tensor_scalar(out=out, in0=in0, scalar1=s1, scalar2=s2, op0=op0, op1=op1),
                       reads, writes)

    def stt(self, eng, out, in0, scalar, in1, op0, op1, reads, writes):
        return self.op(eng, lambda e: e.scalar_tensor_tensor(out=out, in0=in0, scalar=scalar, in1=in1,
                                                             op0=op0, op1=op1), reads, writes)

    def copy(self, eng, out, in_, reads, writes):
        if eng == "scalar":
            return self.op(eng, lambda e: e.copy(out=out, in_=in_), reads, writes)
        return self.op(eng, lambda e: e.tensor_copy(out=out, in_=in_), reads, writes)

    def memset(self, eng, ap, val, writes):
        return self.op(eng, lambda e: e.memset(ap, val), (), writes)

    def recip(self, out, in_, reads, writes):
        return self.op("vector", lambda e: e.reciprocal(out=out, in_=in_), reads, writes)

    def rr(self, engs=("vector", "scalar", "vector", "scalar", "vector", "gpsimd", "scalar", "vector")):
        self._rr += 1
        return engs[self._rr % len(engs)]


class Stage:
    def __init__(self, m, width=1024, n=3):
        self.m = m
        self.width = width
        self.t = [m.sbuf(f"stg{i}", [128, width], F32) for i in range(n)]
        self.b = m.bufs("stg", n)
        self.i = 0

    def add(self, view, buf):
        self.t.append(view)
        self.b.append(buf)

    def load_cast(self, dst_ap_fn, src_rows_ap, ncols, wbuf, queue="sync"):
        m = self.m
        for c0 in range(0, ncols, self.width):
            c1 = min(ncols, c0 + self.width)
            i = self.i
            self.i = (self.i + 1) % len(self.t)
            m.dma(queue, self.t[i][:, 0:c1 - c0], src_rows_ap[:, c0:c1], writes=[self.b[i]])
            eng = m.rr()
            m.copy(eng, dst_ap_fn(c0, c1), self.t[i][:, 0:c1 - c0], reads=[self.b[i]], writes=[wbuf])


def emit_mod(m, nchunk, wada_d, bada_d, cT_d, ps_mod, Bps, stg, sfx=""):
    cond = m.sbuf("cond" + sfx, [128, 8], F32)
    Bcond = m.buf("cond")
    m.dma("sync", cond[:], cT_d, writes=[Bcond])
    m.act(cond[:], cond[:], AF.Silu, reads=[Bcond], writes=[Bcond])
    bada = m.sbuf("bada" + sfx, [128, nchunk], F32)
    Bbada = m.buf("bada")
    m.dma("sync", bada[:], bada_d, writes=[Bbada])
    modsb = m.sbuf("modsb" + sfx, [128, nchunk], F32)
    Bmod = m.buf("mod")
    ns = len(stg.t)
    for j in range(nchunk):
        i = j % ns
        wv = stg.t[i][:, 0:1024].rearrange("p (k c) -> p k c", k=8)
        m.dma("gpsimd" if j % 2 else "sync", wv, wada_d[j], writes=[stg.b[i]])
        for kc in range(8):
            m.mm(ps_mod[:, j:j + 1], wv[:, kc, :], cond[:, kc:kc + 1], kc == 0, kc == 7,
                 reads=[stg.b[i], Bcond], writes=[Bps])
    m.tt("vector", modsb[:], ps_mod[:, 0:nchunk], bada[:], ALU.add, reads=[Bps, Bbada], writes=[Bmod])
    return modsb, Bmod


def emit_norm_mod(m, x_t, Bx, T, A, Bc, BAB, ones_bf, Bconst, sq, Bsq, ps_ss, Bps_ss, rs, Brs, tmp, Btmp, h, Bh):
    for kc in range(8):
        eng = "gpsimd" if kc % 2 else "scalar"
        if eng == "scalar":
            m.act(sq[:, kc, 0:T], x_t[:, kc, 0:T], AF.Square, reads=[Bx], writes=[Bsq])
        else:
            m.tt("gpsimd", sq[:, kc, 0:T], x_t[:, kc, 0:T], x_t[:, kc, 0:T], ALU.mult, reads=[Bx], writes=[Bsq])
    for kc in range(8):
        m.mm(ps_ss[:, 0:T], ones_bf[:], sq[:, kc, 0:T], kc == 0, kc == 7, reads=[Bsq, Bconst], writes=[Bps_ss])
    m.act(rs[:, 0:T], ps_ss[:, 0:T], AF.Ln, reads=[Bps_ss], writes=[Brs], bias=RMS_EPS, scale=1.0 / D)
    m.act(rs[:, 0:T], rs[:, 0:T], AF.Exp, reads=[Brs], writes=[Brs], scale=-0.5)
    for kc in range(8):
        i = kc % 2
        m.tt("vector" if kc % 2 else "gpsimd", tmp[i][:, 0:T], x_t[:, kc, 0:T], rs[:, 0:T], ALU.mult,
             reads=[Bx, Brs], writes=[Btmp[i]])
        m.act(h[:, kc, 0:T], tmp[i][:, 0:T], AF.Identity, reads=[Btmp[i], BAB], writes=[Bh],
              bias=Bc[:, kc:kc + 1], scale=A[:, kc:kc + 1])


def ffn_decl(nc, NT, KO, pre=""):
    KC = KO // 128
    A = {}
    A["xT"] = nc.dram_tensor(pre + "xT", [128, 8, NT], F32, kind="ExternalInput").ap()
    A["ogT"] = nc.dram_tensor(pre + "ogT", [128, KC, NT], BF16, kind="ExternalInput").ap()
    A["w_o"] = nc.dram_tensor(pre + "w_o", [KO, D], F32, kind="ExternalInput").ap()
    A["wada"] = nc.dram_tensor(pre + "wada", [32, 128, 8, 128], F32, kind="ExternalInput").ap()
    A["bada"] = nc.dram_tensor(pre + "bada", [128, 32], F32, kind="ExternalInput").ap()
    A["cT"] = nc.dram_tensor(pre + "cT", [128, 8], F32, kind="ExternalInput").ap()
    A["gam"] = nc.dram_tensor(pre + "gam", [128, 8], F32, kind="ExternalInput").ap()
    A["w1"] = nc.dram_tensor(pre + "w1", [D, 2 * FH], F32, kind="ExternalInput").ap()
    A["w2"] = nc.dram_tensor(pre + "w2", [FH, D], F32, kind="ExternalInput").ap()
    A["outT"] = nc.dram_tensor(pre + "outT", [128, 8, NT], F32, kind="ExternalOutput").ap()
    return A


def build_ffn(NT, KO, T=256):
    nc = bass.Bass("TRN2", target_bir_lowering=False)
    A = ffn_decl(nc, NT, KO)
    m = MK(nc)
    outs = emit_ffn(m, A, NT, KO, T)
    m.final_wait("sync", outs)
    m.build(barrier=False)
    return nc


def emit_ffn(m, A, NT, KO, T=256):
    KC = KO // 128
    HC = FH // 128
    xT_d, og_d, wo_d, wada_d, bada_d, cT_d, gam_d, w1_d, w2_d, out_d = [A.get(k) for k in
        ("xT", "ogT", "w_o", "wada", "bada", "cT", "gam", "w1", "w2", "outT")]

    ones_bf = m.sbuf("ones_bf", [128, 128], BF16)
    Bconst = m.buf("const")
    m.memset("vector", ones_bf[:], 1.0, writes=[Bconst])

    ps_mod = m.psum("ps_mod", [128, 512], F32); Bps_mod = m.buf("ps_mod", excl=True)
    ps_ss = m.psum("ps_ss", [128, 512], F32); Bps_ss = m.buf("ps_ss", excl=True)
    ps_a = [m.psum(f"ps_a{i}", [128, 512], F32) for i in range(3)]; Bps_a = m.bufs("ps_a", 3, excl=True)
    ps_b = [m.psum(f"ps_b{i}", [128, 512], F32) for i in range(2)]; Bps_b = m.bufs("ps_b", 2, excl=True)

    stg = Stage(m, width=1024, n=2)
    xt = [m.sbuf(f"xt{i}", [128, 8, T], F32) for i in range(2)]; Bxt = m.bufs("xt", 2)
    ogbs = [m.sbuf(f"ogb{i}", [128, KC, T], BF16) for i in range(2)]; Bogbs = m.bufs("ogb", 2)
    sq = m.sbuf("sq", [128, 8, T], BF16); Bsq = m.buf("sq")
    rs = m.sbuf("rs", [128, T], F32); Brs = m.buf("rs")
    tmp = [m.sbuf(f"tmp{i}", [128, T], F32) for i in range(2)]; Btmp = m.bufs("tmp", 2)
    h2 = m.sbuf("h2", [128, 8, T], BF16); Bh2 = m.buf("h2")
    actb = m.sbuf("actb", [128, HC, T], BF16); Bact = m.buf("act")
    sg = tmp; Bsg = Btmp
    if T * 8 >= 2048:
        for i in range(2):
            stg.add(xt[i][:].rearrange("p k t -> p (k t)")[:, 0:1024], Bxt[i])
        lend = [(h2, Bh2), (sq, Bsq), (actb, Bact)] + ([(ogbs[0], Bogbs[0]), (ogbs[1], Bogbs[1])] if KC == 8 else [])
        for t_, b_ in lend:
            v = t_[:].rearrange("p k t -> p (k t)")[:, 0:2048].bitcast(F32)
            stg.add(v, b_)
    modsb, Bmod = emit_mod(m, 32, wada_d, bada_d, cT_d, ps_mod, Bps_mod, stg)
    gam = m.sbuf("gam", [128, 8], F32); Bgam = m.buf("gam")
    m.dma("sync", gam[:], gam_d, writes=[Bgam])
    A2 = m.sbuf("A2", [128, 8], F32)
    BAB = m.buf("AB")
    m.stt("vector", A2[:], modsb[:, 16:24], 1.0, gam[:], ALU.add, ALU.mult, reads=[Bmod, Bgam], writes=[BAB])

    HX = A.get("h_extra")
    if HX is not None:
        modx, Bmodx = emit_mod(m, 16, HX["wada"], HX["bada"], cT_d, ps_mod, Bps_mod, stg, sfx="x")
        gamx = m.sbuf("gamx", [128, 8], F32); Bgamx = m.buf("gamx")
        m.dma("sync", gamx[:], HX["gam"], writes=[Bgamx])
        A1x = m.sbuf("A1x", [128, 8], F32); BABx = m.buf("ABx")
        m.stt("vector", A1x[:], modx[:, 8:16], 1.0, gamx[:], ALU.add, ALU.mult, reads=[Bmodx, Bgamx], writes=[BABx])
    wo = m.sbuf("wo", [128, KC, D], BF16); Bwo = m.buf("wo")
    w1 = m.sbuf("w1", [128, 8, 2 * FH], BF16); Bw1 = m.buf("w1")
    w2 = m.sbuf("w2", [128, HC, D], BF16); Bw2 = m.buf("w2")
    for kc in range(KC):
        stg.load_cast(lambda c0, c1, kc=kc: wo[:, kc, c0:c1], wo_d[kc * 128:(kc + 1) * 128, :], D, Bwo,
                      queue="sync" if kc % 2 else "gpsimd")
    for kc in range(8):
        stg.load_cast(lambda c0, c1, kc=kc: w1[:, kc, c0:c1], w1_d[kc * 128:(kc + 1) * 128, :], 2 * FH, Bw1,
                      queue="sync" if kc % 2 else "gpsimd")
    for hc in range(HC):
        stg.load_cast(lambda c0, c1, hc=hc: w2[:, hc, c0:c1], w2_d[hc * 128:(hc + 1) * 128, :], D, Bw2,
                      queue="sync" if hc % 2 else "gpsimd")

    Bouts = []

    ntile = NT // T
    pa = 0
    for it in range(ntile):
        t0 = it * T
        x_t = xt[it % 2]; Bx = Bxt[it % 2]
        if "x_load" in A:
            A["x_load"](m, x_t, t0, T, Bx)
        else:
            m.dma("sync", x_t[:], xT_d[:, :, t0:t0 + T], writes=[Bx])
        if "og_load" in A:
            ogb, Bogb = A["og_load"](m, ogbs, Bogbs, t0, T)
        else:
            ogb = ogbs[it % 2]; Bogb = Bogbs[it % 2]
            m.dma("gpsimd", ogb[:], og_d[:, :, t0:t0 + T], writes=[Bogb])
        for dc in range(8):
            p = ps_a[pa % 3]; Bp = Bps_a[pa % 3]; pa += 1
            for kc in range(KC):
                m.mm(p[:, 0:T], wo[:, kc, dc * 128:(dc + 1) * 128], ogb[:, kc, :], kc == 0, kc == KC - 1,
                     reads=[Bwo, Bogb], writes=[Bp])
            m.stt("vector", x_t[:, dc, :], p[:, 0:T], modsb[:, dc:dc + 1], x_t[:, dc, :], ALU.mult, ALU.add,
                  reads=[Bp, Bmod, Bx], writes=[Bx])
        emit_norm_mod(m, x_t, Bx, T, A2, modsb[:, 8:16], BAB, ones_bf, Bconst, sq, Bsq, ps_ss, Bps_ss,
                      rs, Brs, tmp, Btmp, h2, Bh2)
        for hc in range(HC):
            pg = ps_a[pa % 3]; Bpg = Bps_a[pa % 3]; pa += 1
            pu = ps_b[hc % 2]; Bpu = Bps_b[hc % 2]
            for kc in range(8):
                m.mm(pg[:, 0:T], w1[:, kc, hc * 128:(hc + 1) * 128], h2[:, kc, :], kc == 0, kc == 7,
                     reads=[Bw1, Bh2], writes=[Bpg])
            for kc in range(8):
                m.mm(pu[:, 0:T], w1[:, kc, FH + hc * 128:FH + (hc + 1) * 128], h2[:, kc, :], kc == 0, kc == 7,
                     reads=[Bw1, Bh2], writes=[Bpu])
            s = sg[hc % 2]; Bs = Bsg[hc % 2]
            m.act(s[:], pg[:, 0:T], AF.Silu, reads=[Bpg], writes=[Bs])
            m.tt("vector", actb[:, hc, :], pu[:, 0:T], s[:], ALU.mult, reads=[Bpu, Bs], writes=[Bact])
        for dc in range(8):
            p = ps_a[pa % 3]; Bp = Bps_a[pa % 3]; pa += 1
            for hc in range(HC):
                m.mm(p[:, 0:T], w2[:, hc, dc * 128:(dc + 1) * 128], actb[:, hc, :], hc == 0, hc == HC - 1,
                     reads=[Bw2, Bact], writes=[Bp])
            m.stt("vector", x_t[:, dc, :], p[:, 0:T], modsb[:, 24 + dc:25 + dc], x_t[:, dc, :], ALU.mult, ALU.add,
                  reads=[Bp, Bmod, Bx], writes=[Bx])
        if HX is not None:
            emit_norm_mod(m, x_t, Bx, T, A1x, modx[:, 0:8], BABx, ones_bf, Bconst, sq, Bsq, ps_ss, Bps_ss,
                          rs, Brs, tmp, Btmp, h2, Bh2)
            Bouts.append(HX["store"](m, h2, t0, T, Bh2))
        if "out_store" in A:
            Bouts.append(A["out_store"](m, x_t, t0, T, Bx))
        else:
            Bouts.append(m.buf("out"))
            m.dma("sync", out_d[:, :, t0:t0 + T], x_t[:], reads=[Bx], writes=[Bouts[-1]])
    return Bouts


def lay_xT(xb):
    F = xb.shape[1]
    return np.ascontiguousarray(xb.T.reshape(F // 128, 128, -1).transpose(1, 0, 2))


def unlay_xT(a):
    return np.ascontiguousarray(a.transpose(1, 0, 2).reshape(a.shape[1] * 128, -1).T)


def lay_wada(w, c0, c1):
    sel = w[:, c0:c1]
    n = sel.shape[1] // 128
    return np.ascontiguousarray(sel.reshape(8, 128, n, 128).transpose(2, 1, 0, 3))


def lay_vec(v):
    return np.ascontiguousarray(v.reshape(-1, 128).T)


_DBG = {}


def interleave(f, b, k=None):
    k = k or _DBG.get("ilk", 4)
    fa, ba = f is not None, b is not None
    while fa or ba:
        for _ in range(k):
            if fa:
                try:
                    next(f)
                except StopIteration:
                    fa = False
        if ba:
            try:
                next(b)
            except StopIteration:
                ba = False


def interleave3(f, b, p, k=4):
    fa, ba, pa = f is not None, b is not None, p is not None
    while fa or ba or pa:
        for _ in range(k):
            if fa:
                try:
                    next(f)
                except StopIteration:
                    fa = False
        if ba:
            try:
                next(b)
            except StopIteration:
                ba = False
        if pa:
            try:
                next(p)
            except StopIteration:
                pa = False


def gdn_consts():
    C = 128
    U = np.triu(np.ones((C, C), np.float32))
    SLm = np.tril(np.ones((C, C), np.float32), -1)
    mui = np.triu(np.ones((C, C), np.float32))
    mus = np.triu(np.ones((C, C), np.float32), 1)
    I = np.eye(C, dtype=np.float32)
    O = np.ones((C, C), np.float32)
    i = np.arange(C)[:, None]; j = np.arange(C)[None, :]
    BD16 = (i // 16 == j // 16).astype(np.float32)
    Ms = [((i // s == j // s) & ((i % s) >= s // 2) & ((j % s) < s // 2)).astype(np.float32) for s in (32, 64, 128)]
    MTs = [np.ascontiguousarray(M_.T) for M_ in Ms]
    return np.ascontiguousarray(np.stack([U, SLm, mui, mus, I, O, BD16] + Ms + MTs, axis=1))


def gdn_decl(nc, NT, NP, pre="", og_kind="ExternalOutput"):
    A = {}
    A["xT"] = nc.dram_tensor(pre + "xT", [128, 8, NT], F32, kind="ExternalInput").ap()
    A["wada"] = nc.dram_tensor(pre + "wada", [16, 128, 8, 128], F32, kind="ExternalInput").ap()
    A["bada"] = nc.dram_tensor(pre + "bada", [128, 16], F32, kind="ExternalInput").ap()
    A["cT"] = nc.dram_tensor(pre + "cT", [128, 8], F32, kind="ExternalInput").ap()
    A["gam"] = nc.dram_tensor(pre + "gam", [128, 8], F32, kind="ExternalInput").ap()
    A["w_qkvz"] = nc.dram_tensor(pre + "w_qkvz", [NP, D, 2048], F32, kind="ExternalInput").ap()
    A["w_ab"] = nc.dram_tensor(pre + "w_ab", [NP, D, 8], F32, kind="ExternalInput").ap()
    A["convw"] = nc.dram_tensor(pre + "convw", [NP, 128, 12, 4], F32, kind="ExternalInput").ap()
    A["alog"] = nc.dram_tensor(pre + "alog", [NP, 128, 4], F32, kind="ExternalInput").ap()
    A["dtb"] = nc.dram_tensor(pre + "dtb", [NP, 128, 4], F32, kind="ExternalInput").ap()
    A["onorm"] = nc.dram_tensor(pre + "onorm", [128, 128], F32, kind="ExternalInput").ap()
    A["consts"] = nc.dram_tensor(pre + "consts", [128, 13, 128], F32, kind="ExternalInput").ap()
    A["ogT"] = nc.dram_tensor(pre + "ogT", [128, NP * 4, NT], BF16, kind=og_kind).ap()
    return A


def build_gdn(NT, NP=1, T=512):
    nc = bass.Bass("TRN2", target_bir_lowering=False)
    A = gdn_decl(nc, NT, NP)
    m = MK(nc)
    outs = emit_gdn(m, A, NT, NP, T)
    m.final_wait("sync", outs)
    m.build(barrier=False)
    return nc


def emit_gdn(m, A, NT, NP, T=512):
    E2DT = BF16 if _DBG.get("e2_bf16") else F32
    NCH = NT // 128
    xT_d, wada_d, bada_d, cT_d, gam_d, w_d, wab_d, convw_d, alog_d, dtb_d, onorm_d, const_d, og_d = [A.get(k) for k in
        ("xT", "wada", "bada", "cT", "gam", "w_qkvz", "w_ab", "convw", "alog", "dtb", "onorm", "consts", "ogT")]

    cst = m.sbuf("cst", [128, 13, 128], F32); Bconst = m.buf("const")
    m.dma("sync", cst[:], const_d, writes=[Bconst])
    Um, SLm, MUI, MUS, IDf, ONEf, BD16 = [cst[:, i, :] for i in range(7)]
    MN = [cst[:, 7 + i, :] for i in range(3)]
    MT = [cst[:, 10 + i, :] for i in range(3)]
    ones_bf = m.sbuf("ones_bf", [128, 128], BF16)
    id_bf = m.sbuf("id_bf", [128, 128], BF16)
    m.memset("vector", ones_bf[:], 1.0, writes=[Bconst])
    m.copy("vector", id_bf[:], IDf, reads=[Bconst], writes=[Bconst])
    convw = m.sbuf("convw", [128, 12, 4], F32)
    negA = m.sbuf("negA", [128, 4], F32)
    dtb = m.sbuf("dtb", [128, 4], F32)
    Bpw = m.buf("passw")
    onorm = m.sbuf("onorm", [128, 128], F32)
    m.dma("sync", onorm[:], onorm_d, writes=[Bconst])

    ps_big = [m.psum(f"big{i}", [128, 512], F32) for i in range(2)]; Bbig = m.bufs("big", 2, excl=True)
    ps_q4 = [m.psum(f"q4_{i}", [128, 4, 128], F32) for i in range(4)]; Bq4 = m.bufs("q4", 4, excl=True)
    qslots = [(ps_q4[i][:, j, :], Bq4[i]) for j in range(4) for i in range(4)]
    ps_tb = [m.psum(f"tb{i}", [128, 8, 128], BF16) for i in range(2)]; Btb = m.bufs("tb", 2, excl=True)
    tslots = [(ps_tb[i][:, j, :], Btb[i]) for j in range(8) for i in range(2)]
    cnt = {"big": 0, "q": 0, "t": 0}

    def big():
        i = cnt["big"] % 2; cnt["big"] += 1
        return ps_big[i], Bbig[i]

    bigB = [(ps_big[i], Bbig[i]) for i in range(2)] + [(ps_q4[i][:].rearrange("p a b -> p (a b)"), Bq4[i]) for i in range(4)]
    cnt["bigB"] = 0

    def bigb():
        i = cnt["bigB"] % len(bigB); cnt["bigB"] += 1
        return bigB[i]

    def qslot():
        i = cnt["q"] % len(qslots); cnt["q"] += 1
        return qslots[i]

    def tslot():
        i = cnt["t"] % len(tslots); cnt["t"] += 1
        return tslots[i]

    stg = Stage(m, width=1024, n=2)
    pm, Bpm = big()
    modsb, Bmod = emit_mod(m, 16, wada_d, bada_d, cT_d, pm, Bpm, stg)
    gam = m.sbuf("gam", [128, 8], F32); Bgam = m.buf("gam")
    m.dma("sync", gam[:], gam_d, writes=[Bgam])
    A1 = m.sbuf("A1", [128, 8], F32); BAB = m.buf("AB")
    m.stt("vector", A1[:], modsb[:, 8:16], 1.0, gam[:], ALU.add, ALU.mult, reads=[Bmod, Bgam], writes=[BAB])

    w = m.sbuf("w", [128, 8, 2048], BF16); Bw = m.buf("w")
    wab = m.sbuf("wab", [128, 8, 8], BF16)

    xt = [m.sbuf(f"xt{i}", [128, 8, T // 2], F32) for i in range(2)]; Bxt = m.bufs("xt", 2)
    sq = m.sbuf("sq", [128, 8, T], BF16); Bsq = m.buf("sq")
    rs = m.sbuf("rs", [128, T], F32); Brs = m.buf("rs")
    tmp = [m.sbuf(f"tmp{i}", [128, T], F32) for i in range(2)]; Btmp = m.bufs("tmp", 2)
    h = m.sbuf("h", [128, 8, T], BF16); Bh = m.buf("h")
    pcb = [m.sbuf(f"pcb{j}", [128, T + 3], BF16) for j in range(12)]; Bpcb = m.bufs("pcb", 12)
    diag = m.sbuf("diag", [128, 48, 128], BF16); Bdiag = m.buf("diag")
    sqq = [m.sbuf(f"sqq{i}", [128, T], BF16) for i in range(2)]; Bsqq = m.bufs("sqq", 2)
    rn = [m.sbuf(f"rn{i}", [128, T], F32) for i in range(2)]; Brn = m.bufs("rn", 2)
    qkn = m.sbuf("qkn", [128, 8, T], BF16); Bqkn = m.bufs("qkn", 8)
    vT = m.sbuf("vT", [128, 4, T], BF16); BvT = m.bufs("vT", 4)
    gz = m.sbuf("gz", [128, T // 128, 512], BF16); Bgz = m.bufs("gz", T // 128)
    absm = m.sbuf("absm", [128, 8], F32); Bab = m.buf("ab")
    sc1 = m.sbuf("sc1", [128, 4], F32); sc2 = m.sbuf("sc2", [128, 4], F32); Bsc = m.buf("sc")
    graw = m.sbuf("graw", [128, 4], F32); Bgraw = m.buf("graw")
    betaP = [m.sbuf(f"beta{i}", [128, 4], F32) for i in range(3)]; BbetaP = m.bufs("beta", 3)
    negbP = [m.sbuf(f"negb{i}", [128, 4], F32) for i in range(2)]; BnegbP = m.bufs("negb", 2)
    GU = m.sbuf("GU", [128, 4, 128], F32); BGU = m.buf("GU")
    gsb = m.sbuf("gsb", [128, 8], F32); Bgsb = m.buf("gsb")
    egP = [m.sbuf(f"eg{i}", [128, 4], F32) for i in range(3)]; negegP = [m.sbuf(f"negeg{i}", [128, 4], F32) for i in range(3)]
    eglP = [m.sbuf(f"egl{i}", [128, 4], F32) for i in range(3)]; BscaP = m.bufs("sca", 3)
    edlP = [m.sbuf(f"edl{i}", [128, 4], F32) for i in range(2)]; BedlP = m.bufs("edl", 2)
    GT = m.sbuf("GT", [128, 4, 128], F32); BGT = m.buf("GT")
    GTuiP = [m.sbuf(f"GTui{i}", [128, 4, 128], F32) for i in range(2)]; GTusP = [m.sbuf(f"GTus{i}", [128, 4, 128], F32) for i in range(2)]
    BGTmP = m.bufs("GTm", 2)
    vtP = [[m.sbuf(f"vt{p}_{i}", [128, 128], F32) for i in range(4)] for p in range(2)]; BvtP = [m.bufs(f"vt{p}_", 4) for p in range(2)]
    kdecP = [[m.sbuf(f"kdec{p}_{i}", [128, 128], BF16) for i in range(4)] for p in range(2)]; BkdecP = [m.bufs(f"kdec{p}_", 4) for p in range(2)]
    Pm = [[m.sbuf(f"P{i}_{r}", [128, 128], E2DT) for r in range(2)] for i in range(4)]
    Qm = [[m.sbuf(f"Q{i}_{r}", [128, 128], E2DT) for r in range(2)] for i in range(4)]
    Wm = [[m.sbuf(f"W{i}_{r}", [128, 128], E2DT) for r in range(2)] for i in range(4)]
    Dm = [[m.sbuf(f"Dm{i}_{r}", [128, 128], E2DT) for r in range(2)] for i in range(4)]
    BD = [m.bufs(f"Dm{i}_", 2) for i in range(4)]
    Qf = [m.sbuf(f"Qf{i}", [128, 128], E2DT) for i in range(4)]; BQf = m.bufs("Qf", 4)
    Pf = [m.sbuf(f"Pf{i}", [128, 128], E2DT) for i in range(4)]; BPf = m.bufs("Pf", 4)
    Yt = [m.sbuf(f"Yt{i}", [128, 128], E2DT) for i in range(4)]; BYt = m.bufs("Yt", 4)
    Ym = [m.sbuf(f"Ym{i}", [128, 128], E2DT) for i in range(4)]; BYm = m.bufs("Ym", 4)
    BP = [m.bufs(f"P{i}_", 2) for i in range(4)]
    BQ = [m.bufs(f"Q{i}_", 2) for i in range(4)]
    BW = [m.bufs(f"W{i}_", 2) for i in range(4)]
    WbP = [[m.sbuf(f"Wb{p}_{i}", [128, 128], BF16) for i in range(4)] for p in range(2)]; BWbP = [m.bufs(f"Wb{p}_", 4) for p in range(2)]
    ATP = [[m.sbuf(f"AT{p}_{i}", [128, 128], BF16) for i in range(4)] for p in range(2)]; BATP = [m.bufs(f"AT{p}_", 4) for p in range(2)]
    Rm = [m.sbuf(f"Rm{i}", [128, 128], BF16) for i in range(4)]; BRm = m.bufs("Rm", 4)
    vnew = [m.sbuf(f"vnew{i}", [128, 128], BF16) for i in range(4)]; Bvnew = m.bufs("vnew", 4)
    o1e = [m.sbuf(f"o1e{i}", [128, 128], F32) for i in range(4)]; Bo1e = m.bufs("o1e", 4)
    osb = [m.sbuf(f"osb{i}", [128, 128], F32) for i in range(4)]; Bosb = m.bufs("osb", 4)
    junk = m.sbuf("junk", [128, 128], F32); Bjunk = m.buf("junk")
    ssum = m.sbuf("ssum", [128, 4], F32); Bssum = m.buf("ssum")
    rno = m.sbuf("rno", [128, 4], F32); Brno = m.buf("rno")
    Sf = [m.sbuf(f"Sf{i}", [128, 128], F32) for i in range(4)]; BSf = m.bufs("Sf", 4)
    Sb = [m.sbuf(f"Sb{i}", [128, 128], BF16) for i in range(4)]; BSb = m.bufs("Sb", 4)
    ogt = [m.sbuf(f"ogt{i}", [128, 512], BF16) for i in range(2)]; Bogt = m.bufs("ogt", 2)
    ogTt = [m.sbuf(f"ogTt{i}", [128, 4, 128], BF16) for i in range(2)]; BogTt = m.bufs("ogTt", 2)
    Bouts = []
    ntile = NT // T
    for ps, it in [(ps, it) for ps in range(NP) for it in range(ntile)]:
        if it == 0:
            m.dma("sync", convw[:], convw_d[ps], writes=[Bpw])
            m.dma("sync", negA[:], alog_d[ps], writes=[Bpw])
            m.act(negA[:], negA[:], AF.Exp, reads=[Bpw], writes=[Bpw])
            m.ts("vector", negA[:], negA[:], -1.0, None, ALU.mult, None, reads=[Bpw], writes=[Bpw])
            m.dma("sync", dtb[:], dtb_d[ps], writes=[Bpw])
            for kc in range(8):
                stg.load_cast(lambda c0, c1, kc=kc: w[:, kc, c0:c1], w_d[ps, kc * 128:(kc + 1) * 128, :], 2048, Bw,
                              queue="sync" if kc % 2 else "gpsimd")
                stg.load_cast(lambda c0, c1, kc=kc: wab[:, kc, c0:c1], wab_d[ps, kc * 128:(kc + 1) * 128, :], 8, Bw, queue="sync")
            for j in range(12):
                m.memset("gpsimd", pcb[j][:, 0:3], 0.0, writes=[Bpcb[j]])
                for tap in range(4):
                    m.ts("vector", diag[:, j * 4 + tap, :], id_bf[:], convw[:, j, tap:tap + 1], None, ALU.mult, None,
                         reads=[Bconst, Bpw], writes=[Bdiag])
            for i in range(4):
                m.memset("gpsimd", Sf[i][:], 0.0, writes=[BSf[i]])
                m.memset("gpsimd", Sb[i][:], 0.0, writes=[BSb[i]])
        t0 = it * T
        HTL = T // 2
        for hf in range(2):
            x_t = xt[hf]; Bx = Bxt[hf]
            m.dma("sync" if hf else "gpsimd", x_t[:], xT_d[:, :, t0 + hf * HTL:t0 + (hf + 1) * HTL], writes=[Bx])
            pn_, Bpn_ = big()
            emit_norm_mod(m, x_t, Bx, HTL, A1, modsb[:, 0:8], BAB, ones_bf, Bconst, sq, Bsq, pn_, Bpn_,
                          rs, Brs, tmp, Btmp, h[:, :, hf * HTL:(hf + 1) * HTL], Bh)
        st = {}

        def b_s1(j):
            p, Bp = bigb()
            for kc in range(8):
                m.mm(p[:, 0:T], w[:, kc, j * 128:(j + 1) * 128], h[:, kc, :], kc == 0, kc == 7, reads=[Bw, Bh], writes=[Bp])
            st[j] = (p, Bp)

        def b_s2(j):
            p, Bp = st[j]
            if it > 0:
                m.copy("vector", pcb[j][:, 0:3], pcb[j][:, T:T + 3], reads=[Bpcb[j]], writes=[Bpcb[j]])
            m.copy("vector" if j % 3 else "scalar", pcb[j][:, 3:T + 3], p[:, 0:T], reads=[Bp], writes=[Bpcb[j]])
            p2, Bp2 = bigb()
            for tap in range(4):
                m.mm(p2[:, 0:T], diag[:, j * 4 + tap, :], pcb[j][:, tap:tap + T], tap == 0, tap == 3,
                     reads=[Bdiag, Bpcb[j]], writes=[Bp2])
            st[j] = (p2, Bp2)

        def b_s3(j):
            p2, Bp2 = st[j]
            if j >= 8:
                m.act(vT[:, j - 8, :], p2[:, 0:T], AF.Silu, reads=[Bp2], writes=[BvT[j - 8]])
            else:
                m.act(qkn[:, j, :], p2[:, 0:T], AF.Silu, reads=[Bp2], writes=[Bqkn[j]])

        for j in range(12 + 2):
            if j < 12:
                b_s1(j)
            if 1 <= j < 13:
                b_s2(j - 1)
            if j >= 2:
                b_s3(j - 2)
        for c in range(T // 128):
            p, Bp = bigb()
            for kc in range(8):
                m.mm(p[:, :], h[:, kc, c * 128:(c + 1) * 128], w[:, kc, 1536:2048], kc == 0, kc == 7,
                     reads=[Bw, Bh], writes=[Bp])
            m.act(gz[:, c, :], p[:, :], AF.Silu, reads=[Bp], writes=[Bgz[c]])
            for hd in range(4):
                m.tt("gpsimd", gz[:, c, hd * 128:(hd + 1) * 128], gz[:, c, hd * 128:(hd + 1) * 128], onorm[:], ALU.mult,
                     reads=[Bgz[c], Bconst], writes=[Bgz[c]])
        st2 = {}

        def l_s1(j):
            s_ = sqq[j % 2]; Bs_ = Bsqq[j % 2]
            m.tt("vector", s_[:], qkn[:, j, :], qkn[:, j, :], ALU.mult, reads=[Bqkn[j]], writes=[Bs_])
            p2, Bp2 = bigb()
            m.mm(p2[:, 0:T], ones_bf[:], s_[:], True, True, reads=[Bconst, Bs_], writes=[Bp2])
            st2[j] = (p2, Bp2)

        def l_s2(j):
            p2, Bp2 = st2[j]
            r_ = rn[j % 2]; Br_ = Brn[j % 2]
            m.act(r_[:], p2[:, 0:T], AF.Ln, reads=[Bp2], writes=[Br_], bias=RMS_EPS, scale=1.0)
            m.act(r_[:], r_[:], AF.Exp, reads=[Br_], writes=[Br_], scale=-0.5)
            qscale = (128.0 ** -0.5) if j < 4 else 1.0
            m.stt("vector", qkn[:, j, :], qkn[:, j, :], qscale, r_[:], ALU.mult, ALU.mult, reads=[Bqkn[j], Br_], writes=[Bqkn[j]])

        for j in range(8 + 1):
            if j < 8:
                l_s1(j)
            if j >= 1:
                l_s2(j - 1)
        def chunk_pre(c):
            ch = it * (T // 128) + c
            csl = slice(c * 128, (c + 1) * 128)
            par = ch % 2; p3 = ch % 3
            eg, negeg, egl, beta, Bsca, Bbeta = egP[p3], negegP[p3], eglP[p3], betaP[p3], BscaP[p3], BbetaP[p3]
            negb, Bnegb, edl, Bedl = negbP[par], BnegbP[par], edlP[par], BedlP[par]
            GTui, GTus, BGTm = GTuiP[par], GTusP[par], BGTmP[par]
            vt, Bvt, kdec, Bkdec, AT, BAT, Wb, BWb = vtP[par], BvtP[par], kdecP[par], BkdecP[par], ATP[par], BATP[par], WbP[par], BWbP[par]
            H4 = range(4)
            qTs = [qkn[:, hd, csl] for hd in H4]; kTs = [qkn[:, 4 + hd, csl] for hd in H4]
            Bqs_ = [Bqkn[hd] for hd in H4]; Bks_ = [Bqkn[4 + hd] for hd in H4]
            pab, Bpab = qslot()
            for kc in range(8):
                m.mm(pab[:, 0:8], h[:, kc, csl], wab[:, kc, :], kc == 0, kc == 7, reads=[Bw, Bh], writes=[Bpab])
            m.copy("vector", absm[:], pab[:, 0:8], reads=[Bpab], writes=[Bab])
            m.tt("vector", sc1[:], absm[:, 0:4], dtb[:], ALU.add, reads=[Bab, Bpw], writes=[Bsc])
            m.ts("vector", sc2[:], sc1[:], -1.0, None, ALU.mult, None, reads=[Bsc], writes=[Bsc])
            m.tt("vector", sc2[:], sc2[:], sc1[:], ALU.max, reads=[Bsc], writes=[Bsc])
            m.act(sc2[:], sc2[:], AF.Exp, reads=[Bsc], writes=[Bsc], scale=-1.0)
            m.act(sc2[:], sc2[:], AF.Ln, reads=[Bsc], writes=[Bsc], bias=1.0)
            m.ts("vector", sc1[:], sc1[:], 0.0, None, ALU.max, None, reads=[Bsc], writes=[Bsc])
            m.tt("vector", sc1[:], sc1[:], sc2[:], ALU.add, reads=[Bsc], writes=[Bsc])
            m.tt("vector", graw[:], sc1[:], negA[:], ALU.mult, reads=[Bsc, Bpw], writes=[Bgraw])
            m.act(beta[:], absm[:, 4:8], AF.Exp, reads=[Bab], writes=[Bbeta], scale=-1.0)
            m.act(beta[:], beta[:], AF.Ln, reads=[Bbeta], writes=[Bbeta], bias=1.0)
            m.act(beta[:], beta[:], AF.Exp, reads=[Bbeta], writes=[Bbeta], scale=-1.0)
            m.ts("vector", negb[:], beta[:], -1.0, None, ALU.mult, None, reads=[Bbeta], writes=[Bnegb])
            for hd in range(4):
                m.tt("gpsimd", GU[:, hd, :], Um, graw[:, hd:hd + 1].to_broadcast([128, 128]), ALU.mult, reads=[Bconst, Bgraw], writes=[BGU])
            yield
            pg, Bpg = qslot()
            m.mm(pg[:, 0:4], Um, graw[:], True, True, reads=[Bconst, Bgraw], writes=[Bpg])
            m.mm(pg[:, 4:8], ONEf, graw[:], True, True, reads=[Bconst, Bgraw], writes=[Bpg])
            m.copy("vector", gsb[:], pg[:, 0:8], reads=[Bpg], writes=[Bgsb])
            m.act(eg[:], gsb[:, 0:4], AF.Exp, reads=[Bgsb], writes=[Bsca])
            m.ts("vector", negeg[:], eg[:], -1.0, None, ALU.mult, None, reads=[Bsca], writes=[Bsca])
            m.tt("vector", edl[:], gsb[:, 4:8], gsb[:, 0:4], ALU.subtract, reads=[Bgsb], writes=[Bedl])
            m.act(edl[:], edl[:], AF.Exp, reads=[Bedl], writes=[Bedl])
            m.act(egl[:], gsb[:, 4:8], AF.Exp, reads=[Bgsb], writes=[Bsca])
            yield
            pD, BpD = big()
            m.mm(pD[:, :], SLm, GU[:].rearrange("p h i -> p (h i)"), True, True, reads=[Bconst, BGU], writes=[BpD])
            m.act(GT[:].rearrange("p h i -> p (h i)"), pD[:, :], AF.Exp, reads=[BpD], writes=[BGT])
            for hd in range(4):
                m.tt("gpsimd", GTui[:, hd, :], GT[:, hd, :], MUI, ALU.mult, reads=[BGT, Bconst], writes=[BGTm])
                m.tt("gpsimd", GTus[:, hd, :], GT[:, hd, :], MUS, ALU.mult, reads=[BGT, Bconst], writes=[BGTm])
            yield

        def chunk_front(c):
            ch = it * (T // 128) + c
            csl = slice(c * 128, (c + 1) * 128)
            par = ch % 2; p3 = ch % 3
            eg, negeg, egl, beta, Bsca, Bbeta = egP[p3], negegP[p3], eglP[p3], betaP[p3], BscaP[p3], BbetaP[p3]
            negb, Bnegb, edl, Bedl = negbP[par], BnegbP[par], edlP[par], BedlP[par]
            GTui, GTus, BGTm = GTuiP[par], GTusP[par], BGTmP[par]
            vt, Bvt, kdec, Bkdec, AT, BAT, Wb, BWb = vtP[par], BvtP[par], kdecP[par], BkdecP[par], ATP[par], BATP[par], WbP[par], BWbP[par]
            H4 = range(4)
            qTs = [qkn[:, hd, csl] for hd in H4]; kTs = [qkn[:, 4 + hd, csl] for hd in H4]
            Bqs_ = [Bqkn[hd] for hd in H4]; Bks_ = [Bqkn[4 + hd] for hd in H4]
            H4 = range(4)
            qTs = [qkn[:, hd, csl] for hd in H4]; kTs = [qkn[:, 4 + hd, csl] for hd in H4]
            Bqs_ = [Bqkn[hd] for hd in H4]; Bks_ = [Bqkn[4 + hd] for hd in H4]
            yield
            sl1 = []
            yield
            for hd in H4:
                pt, Bpt = tslot()
                m.tr(pt, vT[:, hd, csl], id_bf[:], reads=[BvT[hd], Bconst], writes=[Bpt])
                pt2, Bpt2 = tslot()
                m.tr(pt2, kTs[hd], id_bf[:], reads=[Bks_[hd], Bconst], writes=[Bpt2])
                pkk, Bpkk = qslot()
                m.mm(pkk, kTs[hd], kTs[hd], True, True, reads=[Bks_[hd]], writes=[Bpkk])
                pqk, Bpqk = qslot()
                m.mm(pqk, kTs[hd], qTs[hd], True, True, reads=[Bks_[hd], Bqs_[hd]], writes=[Bpqk])
                sl1.append((pt, Bpt, pt2, Bpt2, pkk, Bpkk, pqk, Bpqk))
            yield
            for hd in H4:
                pt, Bpt, pt2, Bpt2, pkk, Bpkk, pqk, Bpqk = sl1[hd]
                m.copy("scalar", vt[hd][:], pt, reads=[Bpt], writes=[Bvt[hd]])
                m.ts("vector", kdec[hd][:], pt2, edl[:, hd:hd + 1], None, ALU.mult, None, reads=[Bpt2, Bedl], writes=[Bkdec[hd]])
                m.stt("vector", Qf[hd][:], pkk, negb[:, hd:hd + 1], GTus[:, hd, :], ALU.mult, ALU.mult,
                      reads=[Bpkk, Bnegb, BGTm], writes=[BQf[hd]])
                m.tt("vector", AT[hd][:], pqk, GTui[:, hd, :], ALU.mult, reads=[Bpqk, BGTm], writes=[BAT[hd]])
            yield
            sl2 = []
            yield
            for hd in H4:
                if E2DT == F32:
                    pp, Bpp = qslot()
                    m.tr(pp, Qf[hd][:], IDf, reads=[BQf[hd], Bconst], writes=[Bpp])
                else:
                    pp, Bpp = tslot()
                    m.tr(pp, Qf[hd][:], id_bf[:], reads=[BQf[hd], Bconst], writes=[Bpp])
                sl2.append((pp, Bpp))
            yield
            for hd in H4:
                pp, Bpp = sl2[hd]
                m.copy("scalar", Pf[hd][:], pp, reads=[Bpp], writes=[BPf[hd]])
                m.tt("gpsimd", Qm[hd][0][:], Qf[hd][:], BD16, ALU.mult, reads=[BQf[hd], Bconst], writes=[BQ[hd][0]])
                m.tt("gpsimd", Wm[hd][0][:], Qm[hd][0][:], IDf, ALU.add, reads=[BQ[hd][0], Bconst], writes=[BW[hd][0]])
            yield
            for hd in H4:
                m.tt("gpsimd", Pm[hd][0][:], Pf[hd][:], BD16, ALU.mult, reads=[BPf[hd], Bconst], writes=[BP[hd][0]])
            def tr_stage(src, Bsrc, dst, Bdst):
                sl_ = []
                for hd in H4:
                    pp_, Bpp_ = qslot()
                    m.tr(pp_, src[hd][:], IDf, reads=[Bsrc[hd], Bconst], writes=[Bpp_])
                    sl_.append((pp_, Bpp_))
                return sl_

            for lev in range(1, 4):
                r0 = (lev - 1) % 2; r1 = lev % 2
                yield
                sl = []
                for hd in H4:
                    pQ, BpQ = qslot()
                    m.mm(pQ, Pm[hd][r0][:], Qm[hd][r0][:], True, True, reads=[BQ[hd][r0], BP[hd][r0]], writes=[BpQ])
                    sl.append((pQ, BpQ))
                yield
                for hd in H4:
                    pQ, BpQ = sl[hd]
                    m.copy("vector" if hd % 2 else "scalar", Qm[hd][r1][:], pQ, reads=[BpQ], writes=[BQ[hd][r1]])
                yield
                sl = tr_stage([Qm[hd][r1] for hd in H4], [BQ[hd][r1] for hd in H4], None, None)
                yield
                for hd in H4:
                    pp_, Bpp_ = sl[hd]
                    m.copy("scalar" if hd % 2 else "vector", Pm[hd][r1][:], pp_, reads=[Bpp_], writes=[BP[hd][r1]])
                yield
                sl = []
                for hd in H4:
                    pW, BpW = qslot()
                    m.mm(pW, Pm[hd][r1][:], Wm[hd][r0][:], True, True, reads=[BP[hd][r1], BW[hd][r0]], writes=[BpW])
                    sl.append((pW, BpW))
                yield
                for hd in H4:
                    pW, BpW = sl[hd]
                    m.tt("vector", Wm[hd][r1][:], pW, Wm[hd][r0][:], ALU.add, reads=[BpW, BW[hd][r0]], writes=[BW[hd][r1]])
            yield
            sl = tr_stage([Wm[hd][1] for hd in H4], [BW[hd][1] for hd in H4], None, None)
            yield
            for hd in H4:
                pp_, Bpp_ = sl[hd]
                m.copy("scalar", Dm[hd][0][:], pp_, reads=[Bpp_], writes=[BD[hd][0]])
            for si in range(3):
                r0 = (3 + si) % 2; r1 = (4 + si) % 2
                d0 = si % 2; d1 = (si + 1) % 2
                yield
                sl = []
                for hd in H4:
                    pY, BpY = qslot()
                    m.mm(pY, Pf[hd][:], Wm[hd][r0][:], True, True, reads=[BPf[hd], BW[hd][r0]], writes=[BpY])
                    sl.append((pY, BpY))
                yield
                for hd in H4:
                    pY, BpY = sl[hd]
                    m.copy("scalar", Yt[hd][:], pY, reads=[BpY], writes=[BYt[hd]])
                    m.tt("gpsimd", Ym[hd][:], Yt[hd][:], MT[si], ALU.mult, reads=[BYt[hd], Bconst], writes=[BYm[hd]])
                yield
                sl = []
                for hd in H4:
                    pZ, BpZ = qslot()
                    m.mm(pZ, Dm[hd][d0][:], Ym[hd][:], True, True, reads=[BD[hd][d0], BYm[hd]], writes=[BpZ])
                    sl.append((pZ, BpZ))
                yield
                for hd in H4:
                    pZ, BpZ = sl[hd]
                    if si < 2:
                        m.tt("vector", Wm[hd][r1][:], pZ, Wm[hd][r0][:], ALU.add, reads=[BpZ, BW[hd][r0]], writes=[BW[hd][r1]])
                    else:
                        m.tt("vector", Wb[hd][:], pZ, Wm[hd][r0][:], ALU.add, reads=[BpZ, BW[hd][r0]], writes=[BWb[hd]])
                if si < 2:
                    yield
                    sl = tr_stage([Wm[hd][r1] for hd in H4], [BW[hd][r1] for hd in H4], None, None)
                    yield
                    for hd in H4:
                        pp_, Bpp_ = sl[hd]
                        m.copy("scalar", Dm[hd][d1][:], pp_, reads=[Bpp_], writes=[BD[hd][d1]])
            yield

        def chunk_back(c):
            ch = it * (T // 128) + c
            csl = slice(c * 128, (c + 1) * 128)
            par = ch % 2; p3 = ch % 3
            eg, negeg, egl, beta, Bsca, Bbeta = egP[p3], negegP[p3], eglP[p3], betaP[p3], BscaP[p3], BbetaP[p3]
            negb, Bnegb, edl, Bedl = negbP[par], BnegbP[par], edlP[par], BedlP[par]
            GTui, GTus, BGTm = GTuiP[par], GTusP[par], BGTmP[par]
            vt, Bvt, kdec, Bkdec, AT, BAT, Wb, BWb = vtP[par], BvtP[par], kdecP[par], BkdecP[par], ATP[par], BATP[par], WbP[par], BWbP[par]
            H4 = range(4)
            qTs = [qkn[:, hd, csl] for hd in H4]; kTs = [qkn[:, 4 + hd, csl] for hd in H4]
            Bqs_ = [Bqkn[hd] for hd in H4]; Bks_ = [Bqkn[4 + hd] for hd in H4]
            yield
            sl = []
            yield
            for hd in H4:
                pks, Bpks = qslot()
                m.mm(pks, kTs[hd], Sb[hd][:], True, True, reads=[Bks_[hd], BSb[hd]], writes=[Bpks])
                po1, Bpo1 = qslot()
                m.mm(po1, qTs[hd], Sb[hd][:], True, True, reads=[Bqs_[hd], BSb[hd]], writes=[Bpo1])
                sl.append((pks, Bpks, po1, Bpo1))
            yield
            for hd in H4:
                pks, Bpks, po1, Bpo1 = sl[hd]
                m.stt("vector", Rm[hd][:], pks, negeg[:, hd:hd + 1], vt[hd][:], ALU.mult, ALU.add,
                      reads=[Bpks, Bsca, Bvt[hd]], writes=[BRm[hd]])
                m.act(o1e[hd][:], po1, AF.Copy, reads=[Bpo1, Bsca], writes=[Bo1e[hd]], scale=eg[:, hd:hd + 1])
            yield
            sl = []
            yield
            for hd in H4:
                ptr, Bptr = qslot()
                m.mm(ptr, Wb[hd][:], Rm[hd][:], True, True, reads=[BWb[hd], BRm[hd]], writes=[Bptr])
                sl.append((ptr, Bptr))
            yield
            for hd in H4:
                ptr, Bptr = sl[hd]
                m.ts("vector", vnew[hd][:], ptr, beta[:, hd:hd + 1], None, ALU.mult, None, reads=[Bptr, Bbeta], writes=[Bvnew[hd]])
            yield
            sl = []
            yield
            for hd in H4:
                po2, Bpo2 = qslot()
                m.mm(po2, AT[hd][:], vnew[hd][:], True, True, reads=[BAT[hd], Bvnew[hd]], writes=[Bpo2])
                pS, BpS = qslot()
                m.mm(pS, kdec[hd][:], vnew[hd][:], True, True, reads=[Bkdec[hd], Bvnew[hd]], writes=[BpS])
                sl.append((po2, Bpo2, pS, BpS))
            yield
            for hd in H4:
                po2, Bpo2, pS, BpS = sl[hd]
                m.stt("vector", Sf[hd][:], Sf[hd][:], egl[:, hd:hd + 1], pS, ALU.mult, ALU.add,
                      reads=[BSf[hd], Bsca, BpS], writes=[BSf[hd]])
                m.copy("gpsimd", Sb[hd][:], Sf[hd][:], reads=[BSf[hd]], writes=[BSb[hd]])
                m.tt("vector", osb[hd][:], po2, o1e[hd][:], ALU.add, reads=[Bpo2, Bo1e[hd]], writes=[Bosb[hd]])
                m.act(junk[:], osb[hd][:], AF.Square, reads=[Bosb[hd]], writes=[Bjunk, Bssum], accum_out=ssum[:, hd:hd + 1])
            yield
            m.act(rno[:], ssum[:], AF.Ln, reads=[Bssum], writes=[Brno], bias=RMS_EPS, scale=1.0 / 128)
            m.act(rno[:], rno[:], AF.Exp, reads=[Brno], writes=[Brno], scale=-0.5)
            og_t = ogt[ch % 2]; Bog_t = Bogt[ch % 2]
            for hd in range(4):
                m.stt("vector", og_t[:, hd * 128:(hd + 1) * 128], osb[hd][:], rno[:, hd:hd + 1], gz[:, c, hd * 128:(hd + 1) * 128],
                      ALU.mult, ALU.mult, reads=[Bosb[hd], Brno, Bgz[c]], writes=[Bog_t])
            ot = ogTt[ch % 2]; Bot = BogTt[ch % 2]
            for hd in range(4):
                pt3, Bpt3 = tslot()
                m.tr(pt3, og_t[:, hd * 128:(hd + 1) * 128], id_bf[:], reads=[Bog_t, Bconst], writes=[Bpt3])
                m.copy("scalar", ot[:, hd, :], pt3, reads=[Bpt3], writes=[Bot])
            if "og_store" in A:
                Bouts.append(A["og_store"](m, ot, ps, ch, Bot))
            else:
                Bouts.append(m.buf("out"))
                m.dma("sync", og_d[:, ps * 4:(ps + 1) * 4, ch * 128:(ch + 1) * 128], ot[:], reads=[Bot], writes=[Bouts[-1]])

            yield

        NCK = 0 if _DBG.get("skip_chunks") else T // 128
        prev_back = None
        for c in range(NCK):
            if c == 0:
                interleave(chunk_pre(0), None)
            nxt = chunk_pre(c + 1) if c + 1 < NCK else None
            interleave3(chunk_front(c), prev_back, nxt)
            prev_back = chunk_back(c)
        interleave(None, prev_back)
    return Bouts


def gdn_inputs(hhs, xb, cb, wada0, bada0, gam, w_in, conv, a_log, dtb, onorm, xT=None):
    W, WAB, CW, AL, DT = [], [], [], [], []
    for hh in hhs:
        hs = [hh * 4 + i for i in range(4)]
        cols = []
        for tsr in range(4):
            for hd in hs:
                cols.extend(range(tsr * 1024 + hd * 128, tsr * 1024 + (hd + 1) * 128))
        W.append(w_in[:, cols])
        WAB.append(w_in[:, [4096 + hd for hd in hs] + [4104 + hd for hd in hs]])
        cw = np.stack([conv[:, tsr * 1024 + hd * 128: tsr * 1024 + (hd + 1) * 128] for tsr in range(3) for hd in hs], 0)
        CW.append(cw.transpose(2, 0, 1))
        AL.append(np.broadcast_to(a_log[hs][None, :], (128, 4)))
        DT.append(np.broadcast_to(dtb[hs][None, :], (128, 4)))
    c_ = lambda l: np.ascontiguousarray(np.stack(l, 0), dtype=np.float32)
    return {
        "xT": lay_xT(xb) if xT is None else xT, "wada": lay_wada(wada0, 0, 2048), "bada": lay_vec(bada0[0:2048]), "cT": lay_vec(cb),
        "gam": lay_vec(gam), "w_qkvz": c_(W), "w_ab": c_(WAB), "convw": c_(CW), "alog": c_(AL), "dtb": c_(DT),
        "onorm": np.ascontiguousarray(np.broadcast_to(onorm[None, :], (128, 128))),
        "consts": gdn_consts(),
    }


DSW_GROUPS = ((128, 1), (512, 4), (2048, 16))
NEG = -30000.0


def t5_bucket(dist):
    dist = np.asarray(dist, np.int32)
    x = (np.maximum(dist, 1).astype(np.float32) / np.float32(16)).astype(np.float32)
    scaled = (np.log(x).astype(np.float32) / np.float32(np.log(2048 / 16))).astype(np.float32)
    large = 16 + (scaled * np.float32(16)).astype(np.float32).astype(np.int32)
    large = np.minimum(large, 31)
    return np.where(dist < 16, dist, large)


def attn_tables():
    ki = np.arange(128)[:, None, None]
    kb = np.arange(2)[None, :, None]
    qi = np.arange(128)[None, None, :]
    dist = qi + 128 * (1 - kb) - ki
    valid = (dist >= 0) & (dist <= 128)
    return dist, valid


def attn_decl(nc, NT, NP, pre="", og_kind="ExternalOutput", x_kind="ExternalInput"):
    A = {}
    A["xT"] = nc.dram_tensor(pre + "xT", [128, 8, NT], F32, kind=x_kind).ap()
    A["wada"] = nc.dram_tensor(pre + "wada", [16, 128, 8, 128], F32, kind="ExternalInput").ap()
    A["bada"] = nc.dram_tensor(pre + "bada", [128, 16], F32, kind="ExternalInput").ap()
    A["cT"] = nc.dram_tensor(pre + "cT", [128, 8], F32, kind="ExternalInput").ap()
    A["gam"] = nc.dram_tensor(pre + "gam", [128, 8], F32, kind="ExternalInput").ap()
    A["w_qkv"] = nc.dram_tensor(pre + "w_qkv", [NP, D, 1152], F32, kind="ExternalInput").ap()
    A["gains"] = nc.dram_tensor(pre + "gains", [128, 2], F32, kind="ExternalInput").ap()
    A["biasT"] = nc.dram_tensor(pre + "biasT", [NP, 128, 6, 256], F32, kind="ExternalInput").ap()
    A["consts"] = nc.dram_tensor(pre + "consts", [128, 2, 256], F32, kind="ExternalInput").ap()
    A["ogT"] = nc.dram_tensor(pre + "ogT", [128, NP, NT], BF16, kind=og_kind).ap()
    return A


def build_attn(NT, NP=2, T=256):
    nc = bass.Bass("TRN2", target_bir_lowering=False)
    A = attn_decl(nc, NT, NP)
    m = MK(nc)
    outs = emit_attn(m, A, NT, NP, T)
    m.final_wait("sync", outs)
    m.build(barrier=False)
    return nc


def emit_attn(m, A, NT, NP, T=256):
    UN = 2048
    NU = NT // UN
    xT_d, wada_d, bada_d, cT_d, gam_d, w_d, gains_d, bias_d, cst_d, og_d = [A.get(k) for k in
        ("xT", "wada", "bada", "cT", "gam", "w_qkv", "gains", "biasT", "consts", "ogT")]

    cst = m.sbuf("cst", [128, 2, 256], F32); Bconst = m.buf("const")
    m.dma("sync", cst[:], cst_d, writes=[Bconst])
    negmask = cst[:, 0, :]
    ones_bf = m.sbuf("ones_bf", [128, 128], BF16)
    bones_bf = m.sbuf("bones_bf", [128, 128], BF16)
    m.memset("vector", ones_bf[:], 1.0, writes=[Bconst])
    m.copy("vector", bones_bf[:], cst[:, 1, 0:128], reads=[Bconst], writes=[Bconst])
    gains = m.sbuf("gains", [128, 2], F32)
    m.dma("sync", gains[:], gains_d, writes=[Bconst])
    m.ts("vector", gains[:, 0:1], gains[:, 0:1], 0.125, None, ALU.mult, None, reads=[Bconst], writes=[Bconst])

    ps_big = [m.psum(f"big{i}", [128, 512], F32) for i in range(2)]; Bbig = m.bufs("big", 2, excl=True)
    ps_sf = [m.psum(f"s{i}", [128, 512], F32) for i in range(4)]; Bps_s = m.bufs("ps_s", 4, excl=True)
    ps_s = [t[:, 0:256].rearrange("p (b q) -> p b q", b=2) for t in ps_sf]
    ps_of = [m.psum(f"o{i}", [128, 512], F32) for i in range(2)]; Bps_o = m.bufs("ps_o", 2, excl=True)
    ps_o = [t[:, 0:128] for t in ps_of]
    ps_v = [ps_sf[2][:, 0:128], ps_sf[3][:, 0:128]]; Bps_v = [Bps_s[2], Bps_s[3]]
    ring = [(ps_big[i], Bbig[i]) for i in range(2)] + [(ps_sf[i], Bps_s[i]) for i in range(4)] + [(ps_of[i], Bps_o[i]) for i in range(2)]
    cnt = {"big": 0, "s": 0, "o": 0, "v": 0, "ring": 0}

    def rbank():
        i = cnt["ring"] % len(ring); cnt["ring"] += 1
        return ring[i]


    def big():
        i = cnt["big"] % 2; cnt["big"] += 1
        return ps_big[i], Bbig[i]

    stg = Stage(m, width=1024, n=2)
    HL = A.get("h_load")
    if HL is None:
        pm, Bpm = big()
        modsb, Bmod = emit_mod(m, 16, wada_d, bada_d, cT_d, pm, Bpm, stg)
        gam = m.sbuf("gam", [128, 8], F32); Bgam = m.buf("gam")
        m.dma("sync", gam[:], gam_d, writes=[Bgam])
        A1 = m.sbuf("A1", [128, 8], F32); BAB = m.buf("AB")
        m.stt("vector", A1[:], modsb[:, 8:16], 1.0, gam[:], ALU.add, ALU.mult, reads=[Bmod, Bgam], writes=[BAB])

    w = m.sbuf("w", [128, 8, 1152], BF16); Bw = m.buf("w")
    biasT = m.sbuf("biasT", [128, 6, 256], F32); Bbias = m.buf("bias")
    if HL is None:
        xt = [m.sbuf(f"xt{i}", [128, 8, T], F32) for i in range(1)]; Bxt = m.bufs("xt", 1)
        sq = m.sbuf("sq", [128, 8, T], BF16); Bsq = m.buf("sq")
        rs = m.sbuf("rs", [128, T], F32); Brs = m.buf("rs")
        tmp = [m.sbuf(f"tmp{i}", [128, T], F32) for i in range(2)]; Btmp = m.bufs("tmp", 2)
    hU = m.sbuf("hU", [128, 8, UN], BF16); BhU = m.buf("hU")
    qf = [m.sbuf(f"qf{i}", [128, 512], F32) for i in range(2)]; Bqf = m.bufs("qf", 2)
    sqq = [m.sbuf(f"sqq{i}", [128, 512], BF16) for i in range(2)]; Bsqq = m.bufs("sqq", 2)
    rn = [m.sbuf(f"rn{i}", [128, 512], F32) for i in range(2)]; Brn = m.bufs("rn", 2)
    qn = [m.sbuf(f"qn{g}", [128, UN], BF16) for g in range(3)]; Bqn = m.bufs("qn", 3)
    kn = [[m.sbuf(f"kn{g}_{r}", [128, UN], BF16) for r in range(2)] for g in range(3)]
    Bkn = [m.bufs(f"kn{g}_", 2) for g in range(3)]
    Va = [[m.sbuf(f"Va{g}_{r}", [128, 16, 2, 128], BF16) for r in range(2)] for g in range(3)]
    BVa = [m.bufs(f"Va{g}_", 2) for g in range(3)]
    for g in range(3):
        for r in range(2):
            m.memset("gpsimd" if r else "vector", Va[g][r][:, :, :, 64:128], 1.0, writes=[BVa[g][r]])
    NACC = 2 if HL is not None else 1
    acc = [m.sbuf(f"acc{i}", [128, UN], F32) for i in range(NACC)]; Bacc = m.bufs("acc", NACC)
    sT = [m.sbuf(f"sT{i}", [128, 2, 128], F32) for i in range(3)]; BsT = m.bufs("sT", 3)
    pT = [m.sbuf(f"pT{i}", [128, 2, 128], BF16) for i in range(3)]; BpT = m.bufs("pT", 3)
    rdens = [m.sbuf(f"rden{i}", [64, UN], F32) for i in range(NACC)]; Brdens = m.bufs("rden", NACC)
    obf = [m.sbuf(f"obf{i}", [64, UN], BF16) for i in range(NACC)]; Bobf = m.bufs("obf", NACC)
    Bouts = []
    nacc = 0
    npt = 0

    for pp in range(NP):
        for kc in range(8):
            stg.load_cast(lambda c0, c1, kc=kc: w[:, kc, c0:c1], w_d[pp, kc * 128:(kc + 1) * 128, :], 1152, Bw,
                          queue="sync" if kc % 2 else "gpsimd")
        m.dma("sync", biasT[:], bias_d[pp], writes=[Bbias])
        for i in range(6):
            m.tt("gpsimd", biasT[:, i, :], biasT[:, i, :], negmask, ALU.add, reads=[Bbias, Bconst], writes=[Bbias])
        for u in range(NU):
            ur = u % 2
            if HL is not None:
                HL(m, hU, u, BhU)
            for ti in range(0 if HL is not None else UN // T):
                t0 = u * UN + ti * T
                x_t = xt[0]; Bx = Bxt[0]
                if "x_load" in A:
                    A["x_load"](m, x_t, t0, T, Bx)
                else:
                    m.dma("sync" if ti % 2 else "gpsimd", x_t[:], xT_d[:, :, t0:t0 + T], writes=[Bx])
                pn_, Bpn_ = big()
                emit_norm_mod(m, x_t, Bx, T, A1, modsb[:, 0:8], BAB, ones_bf, Bconst, sq, Bsq, pn_, Bpn_,
                              rs, Brs, tmp, Btmp, hU[:, :, ti * T:(ti + 1) * T], BhU)
            items = [(g_, qk, tl) for g_ in range(3) for qk in range(2) for tl in range(UN // 512)]
            stq = {}

            def q_s1(i):
                g_, qk, tl = items[i]
                c0 = (g_ * 3 + qk) * 128
                p, Bp = rbank()
                for kc in range(8):
                    m.mm(p[:, :], w[:, kc, c0:c0 + 128], hU[:, kc, tl * 512:(tl + 1) * 512], kc == 0, kc == 7,
                         reads=[Bw, BhU], writes=[Bp])
                stq[i] = (p, Bp)

            def q_s2(i):
                p, Bp = stq[i]
                f_ = qf[i % 2]; Bf_ = Bqf[i % 2]
                m.copy("scalar", f_[:], p[:, :], reads=[Bp], writes=[Bf_])
                s_ = sqq[i % 2]; Bs_ = Bsqq[i % 2]
                m.tt("gpsimd", s_[:], f_[:], f_[:], ALU.mult, reads=[Bf_], writes=[Bs_])
                p2, Bp2 = rbank()
                m.mm(p2[:, :], bones_bf[:], s_[:], True, True, reads=[Bconst, Bs_], writes=[Bp2])
                stq[i] = (p2, Bp2)

            def q_s3(i):
                g_, qk, tl = items[i]
                d = DSW_GROUPS[g_][1]
                dst = qn[g_] if qk == 0 else kn[g_][ur]
                Bdst = Bqn[g_] if qk == 0 else Bkn[g_][ur]
                p2, Bp2 = stq.pop(i)
                f_ = qf[i % 2]; Bf_ = Bqf[i % 2]
                r_ = rn[i % 2]; Br_ = Brn[i % 2]
                m.act(r_[:], p2[:, :], AF.Ln, reads=[Bp2], writes=[Br_], bias=RMS_EPS, scale=1.0 / 64)
                m.act(r_[:], r_[:], AF.Exp, reads=[Br_], writes=[Br_], scale=-0.5)
                J = 512 // d
                j0 = tl * J
                if d == 1:
                    o_ap = dst[:, tl * 512:(tl + 1) * 512]
                    i0 = f_[:]; i1 = r_[:]
                else:
                    o_ap = dst[:].rearrange("p (r j) -> p r j", r=d)[:, :, j0:j0 + J].rearrange("p r j -> p j r")
                    i0 = f_[:].rearrange("p (j r) -> p j r", r=d)
                    i1 = r_[:].rearrange("p (j r) -> p j r", r=d)
                m.stt("vector", o_ap, i0, gains[:, qk:qk + 1], i1, ALU.mult, ALU.mult,
                      reads=[Bf_, Br_, Bconst], writes=[Bdst])

            nit = len(items)
            for i in range(nit + 2):
                if i < nit:
                    q_s1(i)
                if 1 <= i < nit + 1:
                    q_s2(i - 1)
                if i >= 2:
                    q_s3(i - 2)
            for g in range(3):
                d = DSW_GROUPS[g][1]
                nb = 16 // d
                c0 = (g * 3 + 2) * 128
                for r in range(d):
                    for n_ in range(nb):
                        blk = r * nb + n_
                        tb = n_ * 128 * d + r
                        i = cnt["v"] % 2; cnt["v"] += 1
                        for kc in range(8):
                            m.mm(ps_v[i], hU[:, kc, tb:tb + 127 * d + 1:d], w[:, kc, c0:c0 + 128], kc == 0, kc == 7,
                                 reads=[BhU, Bw], writes=[Bps_v[i]])
                        m.copy("scalar", Va[g][ur][:, blk, :, 0:64], ps_v[i].rearrange("p (h c) -> p h c", h=2),
                               reads=[Bps_v[i]], writes=[BVa[g][ur]])
            LAG = 2
            for hl in range(2):
                a_ = acc[nacc % NACC]; Ba_ = Bacc[nacc % NACC]; nacc += 1
                hp = slice(hl * 64, (hl + 1) * 64)
                blocks = []
                for g in range(3):
                    d = DSW_GROUPS[g][1]
                    nb = 16 // d
                    for r in range(d):
                        for n_ in range(nb):
                            blocks.append((g, d, nb, r, n_))
                stA = {}

                def stage_a(i):
                    g, d, nb, r, n_ = blocks[i]
                    bt = biasT[:, g * 2 + hl, :].rearrange("p (b q) -> p b q", b=2)
                    blk = r * nb + n_
                    qcol = blk * 128
                    if n_ > 0:
                        kprev = (kn[g][ur], Bkn[g][ur], Va[g][ur], BVa[g][ur], blk - 1)
                    elif u > 0:
                        kprev = (kn[g][1 - ur], Bkn[g][1 - ur], Va[g][1 - ur], BVa[g][1 - ur], r * nb + nb - 1)
                    else:
                        kprev = None
                    si = cnt["s"] % len(ps_s); cnt["s"] += 1
                    pS = ps_s[si]; BpS = Bps_s[si]
                    kb0 = 0 if kprev is not None else 1
                    if kprev is not None:
                        kt, Bkt, _, _, pb = kprev
                        m.mm(pS[:, 0, :], kt[hp, pb * 128:(pb + 1) * 128], qn[g][hp, qcol:qcol + 128], True, True,
                             reads=[Bkt, Bqn[g]], writes=[BpS])
                    m.mm(pS[:, 1, :], kn[g][ur][hp, qcol:qcol + 128], qn[g][hp, qcol:qcol + 128], True, True,
                         reads=[Bkn[g][ur], Bqn[g]], writes=[BpS])
                    k3 = i % 3
                    s_ = sT[k3]; Bs_ = BsT[k3]
                    p_ = pT[k3]; Bp_ = BpT[k3]
                    m.tt("vector", s_[:, kb0:2, :], pS[:, kb0:2, :], bt[:, kb0:2, :], ALU.add,
                         reads=[BpS, Bbias], writes=[Bs_])
                    m.act(p_[:, kb0:2, :], s_[:, kb0:2, :], AF.Exp, reads=[Bs_], writes=[Bp_])
                    stA[i] = (kprev, p_, Bp_)

                def stage_b(i):
                    g, d, nb, r, n_ = blocks[i]
                    kprev, p_, Bp_ = stA.pop(i)
                    blk = r * nb + n_
                    tb = n_ * 128 * d + r
                    oi = cnt["o"] % 2; cnt["o"] += 1
                    pO = ps_o[oi]; BpO = Bps_o[oi]
                    if kprev is not None:
                        _, _, vt_, Bvt_, pb = kprev
                        m.mm(pO[:, :], vt_[:, pb, hl, :], p_[:, 0, :], True, False, reads=[Bvt_, Bp_], writes=[BpO], inc=False)
                    m.mm(pO[:, :], Va[g][ur][:, blk, hl, :], p_[:, 1, :], kprev is None, True,
                         reads=[BVa[g][ur], Bp_], writes=[BpO])
                    a_ap = a_[:, tb:tb + 127 * d + 1:d]
                    if g == 0:
                        m.copy("vector", a_ap, pO[:, :], reads=[BpO], writes=[Ba_])
                    else:
                        m.tt("vector", a_ap, pO[:, :], a_ap, ALU.add, reads=[BpO, Ba_], writes=[Ba_])

                nblk = len(blocks)
                for i in range(nblk + LAG):
                    if i < nblk:
                        stage_a(i)
                    if i >= LAG:
                        stage_b(i - LAG)
                rden = rdens[nacc % NACC]; Brden = Brdens[nacc % NACC]
                m.act(rden[:], a_[64:128, :], AF.Ln, reads=[Ba_], writes=[Brden])
                m.act(rden[:], rden[:], AF.Exp, reads=[Brden], writes=[Brden], scale=-1.0)
                ob = obf[nacc % NACC]; Bob = Bobf[nacc % NACC]
                m.tt("gpsimd", ob[:], a_[0:64, :], rden[:], ALU.mult, reads=[Ba_, Brden], writes=[Bob])
                if "og_store" in A:
                    Bouts.append(A["og_store"](m, ob, pp, hl, u, Bob))
                else:
                    Bouts.append(m.buf("out"))
                    m.dma("sync", og_d[hl * 64:(hl + 1) * 64, pp, u * UN:(u + 1) * UN], ob[:], reads=[Bob], writes=[Bouts[-1]])
    return Bouts


def attn_inputs(pairs, xb, cb, wada1, bada1, gam, w_in, q_gain, k_gain, rel_bias, xT=None):
    NP = len(pairs)
    wsel = np.empty((NP, D, 1152), np.float32)
    bias = np.empty((NP, 128, 6, 256), np.float32)
    dist, valid = attn_tables()
    for pi, pr in enumerate(pairs):
        for g in range(3):
            d = DSW_GROUPS[g][1]
            idx = t5_bucket(np.clip(dist, 0, 128) * d)
            for t in range(3):
                for hl in range(2):
                    hd = pr * 2 + hl
                    c_src = ((t * 3 + g) * 8 + hd) * 64
                    c_dst = (g * 3 + t) * 128 + hl * 64
                    wsel[pi, :, c_dst:c_dst + 64] = w_in[:, c_src:c_src + 64]
            for hl in range(2):
                hd = pr * 2 + hl
                bias[pi, :, g * 2 + hl, :] = rel_bias[idx, g * 8 + hd].reshape(128, 256)
    cst = np.zeros((128, 2, 256), np.float32)
    cst[:, 0, :] = np.where(valid, 0.0, NEG).reshape(128, 256)
    cst[0:64, 1, 0:64] = 1.0
    cst[64:128, 1, 64:128] = 1.0
    gains = np.stack([np.tile(q_gain, 2), np.tile(k_gain, 2)], axis=1).astype(np.float32)
    d_ = {"wada": lay_wada(wada1, 0, 2048), "bada": lay_vec(bada1[0:2048]), "cT": lay_vec(cb),
          "gam": lay_vec(gam), "w_qkv": wsel, "gains": np.ascontiguousarray(gains), "biasT": bias, "consts": cst}
    if xT is not None or xb is not None:
        d_["xT"] = lay_xT(xb) if xT is None else xT
    return d_


def build_fused(NT=SEQ):
    nc = bass.Bass("TRN2", target_bir_lowering=False)
    ses = ExitStack()
    ext = lambda name, shape, dt=F32: nc.dram_tensor(name, list(shape), dt, kind="ExternalInput").ap()
    xT = ext("xT", [128, 8, NT]); cT = ext("cT", [128, 8])
    ogT0 = nc.dram_tensor("ogT0", [128, 8, NT], BF16, kind="Internal").ap()
    x1T = nc.dram_tensor("x1T", [128, 8, NT], F32, kind="Internal").ap()
    ogT1 = nc.dram_tensor("ogT1", [128, 4, NT], BF16, kind="Internal").ap()
    outT = nc.dram_tensor("outT", [128, 8, NT], F32, kind="ExternalOutput").ap()
    A1 = {"xT": xT, "cT": cT, "wada": ext("g_wada", [16, 128, 8, 128]), "bada": ext("g_bada", [128, 16]),
          "gam": ext("g_gam", [128, 8]), "w_qkvz": ext("g_w_qkvz", [2, D, 2048]), "w_ab": ext("g_w_ab", [2, D, 8]),
          "convw": ext("g_convw", [2, 128, 12, 4]), "alog": ext("g_alog", [2, 128, 4]), "dtb": ext("g_dtb", [2, 128, 4]),
          "onorm": ext("g_onorm", [128, 128]), "consts": ext("g_consts", [128, 13, 128]), "ogT": ogT0}
    A2 = {"xT": xT, "cT": cT, "ogT": ogT0, "w_o": ext("f0_w_o", [1024, D]), "wada": ext("f0_wada", [32, 128, 8, 128]),
          "bada": ext("f0_bada", [128, 32]), "gam": ext("f0_gam", [128, 8]), "w1": ext("f0_w1", [D, 2 * FH]),
          "w2": ext("f0_w2", [FH, D]), "outT": x1T}
    A3 = {"xT": x1T, "cT": cT, "wada": ext("a_wada", [16, 128, 8, 128]), "bada": ext("a_bada", [128, 16]),
          "gam": ext("a_gam", [128, 8]), "w_qkv": ext("a_w_qkv", [4, D, 1152]), "gains": ext("a_gains", [128, 2]),
          "biasT": ext("a_biasT", [4, 128, 6, 256]), "consts": ext("a_consts", [128, 2, 256]), "ogT": ogT1}
    A4 = {"xT": x1T, "cT": cT, "ogT": ogT1, "w_o": ext("f1_w_o", [512, D]), "wada": ext("f1_wada", [32, 128, 8, 128]),
          "bada": ext("f1_bada", [128, 32]), "gam": ext("f1_gam", [128, 8]), "w1": ext("f1_w1", [D, 2 * FH]),
          "w2": ext("f1_w2", [FH, D]), "outT": outT}
    m = MK(nc, sem_es=ses, tag="p1_"); emit_gdn(m, A1, NT, 2); m.build()
    m = MK(nc, sem_es=ses, tag="p2_"); emit_ffn(m, A2, NT, 1024); m.build()
    m = MK(nc, sem_es=ses, tag="p3_"); emit_attn(m, A3, NT, 4); m.build()
    m = MK(nc, sem_es=ses, tag="p4_"); emit_ffn(m, A4, NT, 512); m.build()
    ses.close()
    return nc


PAIRS = [[0, 1], [2, 3], [4, 5], [6, 7]]


def build_fused8(NT=SEQ):
    nc = bass.Bass("TRN2", target_bir_lowering=False)
    ses = ExitStack()
    HT = NT // 2
    ext = lambda name, shape, dt=F32: nc.dram_tensor(name, list(shape), dt, kind="ExternalInput").ap()
    itn = lambda name, shape, dt: nc.dram_tensor(name, list(shape), dt, kind="Internal").ap()
    xT = ext("xT", [128, 8, NT]); xTh = ext("xTh", [128, 8, HT]); cT = ext("cT", [128, 8]); sel_d = ext("sel", [128, 2])
    outT = nc.dram_tensor("outT", [128, 8, HT], F32, kind="ExternalOutput").ap()
    OGC = 2048; X1C = 512; O2C = 4096
    n_og, n_x1, n_o2 = NT // OGC, HT // X1C, NT // O2C
    og_src = [itn(f"og_src{j}", [128 * 4, OGC], BF16) for j in range(n_og)]
    og_gat = [itn(f"og_gat{j}", [2 * 128 * 4, OGC], BF16) for j in range(n_og)]
    x1_src = [itn(f"x1_src{j}", [128 * 8, X1C], F32) for j in range(n_x1)]
    H1C = 1024; n_h1 = HT // H1C
    h1_src = [itn(f"h1_src{j}", [128 * 8, H1C], BF16) for j in range(n_h1)]
    h1_gat = [itn(f"h1_gat{j}", [2 * 128 * 8, H1C], BF16) for j in range(n_h1)]
    h1_sv = [t.rearrange("(p k) t -> p k t", k=8) for t in h1_src]
    h1_gv = [t.rearrange("(r p k) t -> r p k t", r=2, k=8) for t in h1_gat]
    o2_src = [itn(f"o2_src{j}", [128 * 2, O2C], BF16) for j in range(n_o2)]
    o2_gat = [itn(f"o2_gat{j}", [2 * 128 * 2, O2C], BF16) for j in range(n_o2)]
    og_sv = [t.rearrange("(p k) t -> p k t", k=4) for t in og_src]
    og_gv = [t.rearrange("(r p k) t -> r p k t", r=2, k=4) for t in og_gat]
    x1_sv = [t.rearrange("(p k) t -> p k t", k=8) for t in x1_src]
    o2_sv = [t.rearrange("(p k) t -> p k t", k=2) for t in o2_src]
    o2_gv = [t.rearrange("(r p k) t -> r p k t", r=2, k=2) for t in o2_gat]

    A1 = {"xT": xT, "cT": cT, "wada": ext("g_wada", [16, 128, 8, 128]), "bada": ext("g_bada", [128, 16]),
          "gam": ext("g_gam", [128, 8]), "w_qkvz": ext("g_w_qkvz", [1, D, 2048]), "w_ab": ext("g_w_ab", [1, D, 8]),
          "convw": ext("g_convw", [1, 128, 12, 4]), "alog": ext("g_alog", [1, 128, 4]), "dtb": ext("g_dtb", [1, 128, 4]),
          "onorm": ext("g_onorm", [128, 128]), "consts": ext("g_consts", [128, 13, 128])}
    A2 = {"xT": xTh, "cT": cT, "w_o": ext("f0_w_o", [1024, D]), "wada": ext("f0_wada", [32, 128, 8, 128]),
          "bada": ext("f0_bada", [128, 32]), "gam": ext("f0_gam", [128, 8]), "w1": ext("f0_w1", [D, 2 * FH]),
          "w2": ext("f0_w2", [FH, D])}
    A3 = {"cT": cT, "wada": ext("a_wada", [16, 128, 8, 128]), "bada": ext("a_bada", [128, 16]),
          "gam": ext("a_gam", [128, 8]), "w_qkv": ext("a_w_qkv", [2, D, 1152]), "gains": ext("a_gains", [128, 2]),
          "biasT": ext("a_biasT", [2, 128, 6, 256]), "consts": ext("a_consts", [128, 2, 256])}
    A4 = {"cT": cT, "w_o": ext("f1_w_o", [512, D]), "wada": ext("f1_wada", [32, 128, 8, 128]),
          "bada": ext("f1_bada", [128, 32]), "gam": ext("f1_gam", [128, 8]), "w1": ext("f1_w1", [D, 2 * FH]),
          "w2": ext("f1_w2", [FH, D]), "outT": outT}

    class Xchg:
        def __init__(self, m, srcs, gats, need):
            self.m, self.srcs, self.gats, self.need = m, srcs, gats, need
            self.bufs = {j: [] for j in range(len(srcs))}

        def stored(self, j, buf):
            self.bufs[j].append(buf)
            if len(self.bufs[j]) == self.need:
                self.m.cc_allgather(self.srcs[j], self.gats[j], PAIRS, reads=self.bufs[j], writes=[self.m.buf("gat")])

    def make_og_load(gv, kper, chunk, msel):
        def og_load(m, ogbs, Bogbs, t0, T):
            A_, B_ = ogbs[0], ogbs[1]
            for r in range(2):
                ja, oa = divmod(t0, chunk)
                jb, ob_ = divmod(HT + t0, chunk)
                m.dma("gpsimd", A_[:, r * kper:(r + 1) * kper, :], gv[ja][r][:, :, oa:oa + T], writes=[Bogbs[0]])
                m.dma("sync", B_[:, r * kper:(r + 1) * kper, :], gv[jb][r][:, :, ob_:ob_ + T], writes=[Bogbs[1]])
            sel, Bsel = msel["sel"]
            m.ts("vector", A_[:], A_[:], sel[:, 0:1], None, ALU.mult, None, reads=[Bogbs[0], Bsel], writes=[Bogbs[0]])
            m.stt("vector", A_[:], B_[:], sel[:, 1:2], A_[:], ALU.mult, ALU.add, reads=[Bogbs[0], Bogbs[1], Bsel], writes=[Bogbs[0]])
            return A_, Bogbs[0]
        return og_load

    def load_sel(m, msel):
        sel = m.sbuf("sel", [128, 2], F32); Bsel = m.buf("sel")
        m.dma("sync", sel[:], sel_d, writes=[Bsel])
        msel["sel"] = (sel, Bsel)

    m = MK(nc, sem_es=ses, tag="p1_")
    xc = Xchg(m, og_src, og_gat, OGC // 128)

    def og_store1(m_, ot, ps, ch, Bot):
        j, o = divmod(ch * 128, OGC)
        bo = m_.buf("out")
        m_.dma("sync", og_sv[j][:, :, o:o + 128], ot[:], reads=[Bot], writes=[bo])
        xc.stored(j, bo)
        return bo
    A1["og_store"] = og_store1
    emit_gdn(m, A1, NT, 1)
    m.build()
    m = MK(nc, sem_es=ses, tag="p2_")
    ms = {}; load_sel(m, ms)
    xc2 = Xchg(m, h1_src, h1_gat, H1C // 256)

    def h_store2(m_, h_t, t0, T, Bh):
        j, o = divmod(t0, H1C)
        bo = m_.buf("out")
        m_.dma("gpsimd", h1_sv[j][:, :, o:o + T], h_t[:], reads=[Bh], writes=[bo])
        xc2.stored(j, bo)
        return bo
    A2["h_extra"] = {"wada": A3["wada"], "bada": A3["bada"], "gam": A3["gam"], "store": h_store2}
    A2["og_load"] = make_og_load(og_gv, 4, OGC, ms)

    def out_store2(m_, x_t, t0, T, Bx):
        j, o = divmod(t0, X1C)
        bo = m_.buf("out")
        m_.dma("sync", x1_sv[j][:, :, o:o + T], x_t[:], reads=[Bx], writes=[bo])
        return bo
    A2["out_store"] = out_store2
    emit_ffn(m, A2, HT, 1024)
    m.build()
    m = MK(nc, sem_es=ses, tag="p3_")
    xc3 = Xchg(m, o2_src, o2_gat, 2 * 2 * (O2C // 2048))

    def h_load3(m_, hU, u, BhU):
        for q in range(2048 // H1C):
            t0 = u * 2048 + q * H1C
            r, tl = divmod(t0, HT)
            j = tl // H1C
            m_.dma("sync" if q % 2 else "gpsimd", hU[:, :, q * H1C:(q + 1) * H1C], h1_gv[j][r], writes=[BhU])
    A3["h_load"] = h_load3

    def og_store3(m_, ob, pp, hl, u, Bob):
        j, o = divmod(u * 2048, O2C)
        bo = m_.buf("out")
        m_.dma("sync", o2_sv[j][hl * 64:(hl + 1) * 64, pp, o:o + 2048], ob[:], reads=[Bob], writes=[bo])
        xc3.stored(j, bo)
        return bo
    A3["og_store"] = og_store3
    emit_attn(m, A3, NT, 2)
    m.build()
    m = MK(nc, sem_es=ses, tag="p4_")
    ms = {}; load_sel(m, ms)
    A4["og_load"] = make_og_load(o2_gv, 2, O2C, ms)

    def x_load4(m_, x_t, t0, T, Bx):
        j, o = divmod(t0, X1C)
        m_.dma("sync", x_t[:], x1_sv[j][:, :, o:o + T], writes=[Bx])
    A4["x_load"] = x_load4
    emit_ffn(m, A4, HT, 512)
    m.build()
    ses.close()
    return nc


_PROGS = {}


def _prog(key, fn):
    if key not in _PROGS:
        _PROGS[key] = fn()
    return _PROGS[key]


def _f32(a):
    return np.ascontiguousarray(np.asarray(a, dtype=np.float32))


def kernel(x, c, w_ada, b_ada, norm_mix, norm_ffn, w_ffn_in, w_ffn_out,
           gdn_w_in, gdn_conv, gdn_a_log, gdn_dt_bias, gdn_out_norm, gdn_w_out,
           dsw_w_in, dsw_q_norm, dsw_k_norm, dsw_w_out, rel_bias):
    (x, c, w_ada, b_ada, norm_mix, norm_ffn, w_ffn_in, w_ffn_out, gdn_w_in, gdn_conv, gdn_a_log, gdn_dt_bias,
     gdn_out_norm, gdn_w_out, dsw_w_in, dsw_q_norm, dsw_k_norm, dsw_w_out, rel_bias) = map(_f32, (
        x, c, w_ada, b_ada, norm_mix, norm_ffn, w_ffn_in, w_ffn_out, gdn_w_in, gdn_conv, gdn_a_log, gdn_dt_bias,
        gdn_out_norm, gdn_w_out, dsw_w_in, dsw_q_norm, dsw_k_norm, dsw_w_out, rel_bias))
    nc = _prog("fused8", build_fused8)
    maps = []
    for b in range(BATCH):
        xTb = lay_xT(x[b])
        for hh in range(2):
            g = gdn_inputs([hh], None, c[b], w_ada[0], b_ada[0], norm_mix[0], gdn_w_in[0], gdn_conv[0], gdn_a_log[0],
                           gdn_dt_bias[0], gdn_out_norm[0], xT=0)
            a = attn_inputs([2 * hh, 2 * hh + 1], None, c[b], w_ada[1], b_ada[1], norm_mix[1], dsw_w_in[0], dsw_q_norm[0],
                            dsw_k_norm[0], rel_bias)
            d_ = {}
            for k in ("wada", "bada", "gam", "w_qkvz", "w_ab", "convw", "alog", "dtb", "onorm", "consts"):
                d_["g_" + k] = g[k]
            for k in ("wada", "bada", "gam", "w_qkv", "gains", "biasT", "consts"):
                d_["a_" + k] = a[k]
            for L, w_o in ((0, gdn_w_out[0]), (1, dsw_w_out[0])):
                p = f"f{L}_"
                d_[p + "w_o"] = w_o
                d_[p + "wada"] = lay_wada(w_ada[L], 2048, 6144)
                d_[p + "bada"] = lay_vec(b_ada[L][2048:])
                d_[p + "gam"] = lay_vec(norm_ffn[L])
                d_[p + "w1"] = w_ffn_in[L]
                d_[p + "w2"] = w_ffn_out[L]
            d_["xT"] = xTb
            d_["xTh"] = np.ascontiguousarray(xTb[:, :, hh * (SEQ // 2):(hh + 1) * (SEQ // 2)])
            d_["cT"] = lay_vec(c[b])
            sel = np.zeros((128, 2), np.float32); sel[:, hh] = 1.0
            d_["sel"] = sel
            maps.append(d_)
    res = run_bass_kernel_spmd(nc, maps, core_ids=list(range(8))).results
    out = np.empty((BATCH, SEQ, D), np.float32)
    for b in range(BATCH):
        for hh in range(2):
            out[b, hh * (SEQ // 2):(hh + 1) * (SEQ // 2)] = unlay_xT(np.asarray(res[b * 2 + hh]["outT"]))
    return out


def kernel_4core(x, c, w_ada, b_ada, norm_mix, norm_ffn, w_ffn_in, w_ffn_out,
           gdn_w_in, gdn_conv, gdn_a_log, gdn_dt_bias, gdn_out_norm, gdn_w_out,
           dsw_w_in, dsw_q_norm, dsw_k_norm, dsw_w_out, rel_bias):
    (x, c, w_ada, b_ada, norm_mix, norm_ffn, w_ffn_in, w_ffn_out, gdn_w_in, gdn_conv, gdn_a_log, gdn_dt_bias,
     gdn_out_norm, gdn_w_out, dsw_w_in, dsw_q_norm, dsw_k_norm, dsw_w_out, rel_bias) = map(_f32, (
        x, c, w_ada, b_ada, norm_mix, norm_ffn, w_ffn_in, w_ffn_out, gdn_w_in, gdn_conv, gdn_a_log, gdn_dt_bias,
        gdn_out_norm, gdn_w_out, dsw_w_in, dsw_q_norm, dsw_k_norm, dsw_w_out, rel_bias))
    nc = _prog("fused", build_fused)
    g = gdn_inputs([0, 1], None, c[0], w_ada[0], b_ada[0], norm_mix[0], gdn_w_in[0], gdn_conv[0], gdn_a_log[0],
                   gdn_dt_bias[0], gdn_out_norm[0], xT=0)
    a = attn_inputs([0, 1, 2, 3], None, c[0], w_ada[1], b_ada[1], norm_mix[1], dsw_w_in[0], dsw_q_norm[0], dsw_k_norm[0], rel_bias)
    shared = {}
    for k in ("wada", "bada", "gam", "w_qkvz", "w_ab", "convw", "alog", "dtb", "onorm", "consts"):
        shared["g_" + k] = g[k]
    for k in ("wada", "bada", "gam", "w_qkv", "gains", "biasT", "consts"):
        shared["a_" + k] = a[k]
    for L, w_o in ((0, gdn_w_out[0]), (1, dsw_w_out[0])):
        p = f"f{L}_"
        shared[p + "w_o"] = w_o
        shared[p + "wada"] = lay_wada(w_ada[L], 2048, 6144)
        shared[p + "bada"] = lay_vec(b_ada[L][2048:])
        shared[p + "gam"] = lay_vec(norm_ffn[L])
        shared[p + "w1"] = w_ffn_in[L]
        shared[p + "w2"] = w_ffn_out[L]
    maps = []
    for b in range(BATCH):
        d_ = dict(shared)
        d_["xT"] = lay_xT(x[b])
        d_["cT"] = lay_vec(c[b])
        maps.append(d_)
    res = run_bass_kernel_spmd(nc, maps, core_ids=list(range(BATCH))).results
    out = np.empty((BATCH, SEQ, D), np.float32)
    for b in range(BATCH):
        out[b] = unlay_xT(np.asarray(res[b]["outT"]))
    return out
```
